# Optimizing a Trainium2 kernel written in Bass

```python
import math
import jax, jax.numpy as jnp
from jax import lax
import numpy as np

D_MODEL = 1024
BATCH = 16
SEQ = 256
DEPTH = 2
DEC_BATCH = 4
DEC_SEQ = 4096
PAST_LEN = 512

GRID_W = 64
N_DIR = 2
CHUNK = 64
Q_BLOCK = 128
ROPE_BASE = 10000.0
EPS = 1e-6

A_HEADS = 4
A_QK_DIM = 64
A_V_DIM = 2 * A_QK_DIM
A_WIDTH = A_HEADS * A_V_DIM
A_QK_COLS = A_HEADS * 2 * A_QK_DIM
B_HEADS = 4
B_DIM = 64
B_WIDTH = B_HEADS * B_DIM
CONV_K = 5
C_HEADS = 4
C_DIM = 64
C_WIDTH = C_HEADS * C_DIM

MIX_WIDTH = A_WIDTH + B_WIDTH + C_WIDTH
IN_SIZES = (A_QK_COLS, A_QK_COLS, A_WIDTH,
            B_WIDTH, B_WIDTH, B_WIDTH, B_WIDTH, N_DIR * B_HEADS, N_DIR * B_HEADS,
            C_WIDTH, C_WIDTH, C_WIDTH, C_WIDTH, N_DIR * C_HEADS, N_DIR * C_HEADS)
IN_WIDTH = 2 * A_QK_COLS + A_WIDTH + 4 * B_WIDTH + 2 * N_DIR * B_HEADS + 4 * C_WIDTH + 2 * N_DIR * C_HEADS
FFN_HIDDEN = -((-8 * D_MODEL) // (3 * 256)) * 256

kernel_name = 'hybrid_diffusion_step'


def rms_norm(x, g):
    xf = x.astype(jnp.float32)
    y = xf * lax.rsqrt(jnp.mean(xf * xf, axis=-1, keepdims=True) + EPS)
    return (y * g.astype(jnp.float32)).astype(x.dtype)


def l2_norm(x):
    xf = x.astype(jnp.float32)
    return (xf * lax.rsqrt(jnp.sum(xf * xf, axis=-1, keepdims=True) + EPS)).astype(x.dtype)


def ada_mod(cvec, w, b):
    m = jnp.matmul(jax.nn.silu(cvec), w) + b
    return jnp.split(m, 6, axis=-1)


def swiglu(u, w_gu, w_dn):
    gate, up = jnp.split(jnp.einsum('bld,df->blf', u, w_gu), 2, axis=-1)
    return jnp.einsum('blf,fd->bld', jax.nn.silu(gate) * up, w_dn)


def axial_rope(x):
    L = x.shape[1]
    n_rows = L // GRID_W
    rows = jnp.repeat(jnp.arange(n_rows, dtype=jnp.float32), GRID_W)
    cols = jnp.tile(jnp.arange(GRID_W, dtype=jnp.float32), n_rows)
    n_freq = A_QK_DIM // 4
    inv_freq = ROPE_BASE ** (-jnp.arange(n_freq, dtype=jnp.float32) / n_freq)
    ang = jnp.concatenate([rows[:, None] * inv_freq, cols[:, None] * inv_freq], axis=-1)
    cos = jnp.cos(ang)[None, :, None, None, :].astype(x.dtype)
    sin = jnp.sin(ang)[None, :, None, None, :].astype(x.dtype)
    xp = x.reshape(x.shape[:-1] + (A_QK_DIM // 2, 2))
    x0, x1 = xp[..., 0], xp[..., 1]
    return jnp.stack([x0 * cos - x1 * sin, x0 * sin + x1 * cos], axis=-1).reshape(x.shape)


def centred_dwconv(x, w):
    return lax.conv_general_dilated(
        x, w[:, None, :].astype(x.dtype), window_strides=(1,),
        padding=[(CONV_K // 2, CONV_K // 2)], dimension_numbers=('NWC', 'WIO', 'NWC'),
        feature_group_count=x.shape[-1])


def diff_lambda(lqk, lam_init):
    lf = lqk.astype(jnp.float32)
    return jnp.exp(jnp.sum(lf[0] * lf[1])) - jnp.exp(jnp.sum(lf[2] * lf[3])) + lam_init


def diff_attention(q, k, v, lam):
    Bn, Lq = q.shape[:2]
    nb = Lq // Q_BLOCK
    qb = jnp.moveaxis(q.reshape((Bn, nb, Q_BLOCK) + q.shape[2:]), 1, 0)
    scale = A_QK_DIM ** -0.5

    def block(qi):
        s = jnp.einsum('bqhmd,bkhmd->bhmqk', qi, k).astype(jnp.float32) * scale
        p = jax.nn.softmax(s, axis=-1)
        w = p[:, :, 0] - lam * p[:, :, 1]
        return jnp.einsum('bhqk,bkhe->bqhe', w.astype(v.dtype), v)

    out = lax.map(block, qb)
    return jnp.moveaxis(out, 0, 1).reshape(Bn, Lq, A_HEADS, A_V_DIM)


def to_chunks(x):
    Bn, L, H = x.shape[:3]
    x = x.reshape((Bn, L // CHUNK, CHUNK, H) + x.shape[3:])
    return jnp.moveaxis(x, (1, 3), (0, 2))


def from_chunks(x):
    x = jnp.moveaxis(x, (0, 2), (1, 3))
    Bn, n, c, H, d = x.shape
    return x.reshape(Bn, n * c, H, d)


def chunk_masks():
    idx = jnp.arange(CHUNK)
    return idx[:, None] >= idx[None, :], idx[:, None] > idx[None, :]


def gated_delta_chunked(q, k, v, beta, g, s0):
    f32 = jnp.float32
    incl, strict = chunk_masks()
    q, k, v = to_chunks(q.astype(f32)), to_chunks(k.astype(f32)), to_chunks(v.astype(f32))
    beta, g = to_chunks(beta.astype(f32)), to_chunks(g.astype(f32))
    gc = jnp.cumsum(g, axis=-1)
    dd = gc[..., :, None] - gc[..., None, :]
    kb = k * beta[..., None]
    a_kk = jnp.einsum('nbhid,nbhjd->nbhij', kb, k) * jnp.exp(jnp.where(strict, dd, -jnp.inf))
    eye = jnp.eye(CHUNK, dtype=f32)
    t_mat = lax.linalg.triangular_solve(eye + a_kk, jnp.broadcast_to(eye, a_kk.shape),
                                        left_side=True, lower=True, unit_diagonal=True)
    u = jnp.matmul(t_mat, v * beta[..., None])
    w = jnp.matmul(t_mat, kb * jnp.exp(gc)[..., None])
    a_qk = jnp.einsum('nbhid,nbhjd->nbhij', q, k) * jnp.exp(jnp.where(incl, dd, -jnp.inf))
    q_dec = q * jnp.exp(gc)[..., None]
    k_dec = k * jnp.exp(gc[..., -1:] - gc)[..., None]
    g_last = jnp.exp(gc[..., -1])

    def step(S, inp):
        qd, kd, ui, wi, aqk, gl = inp
        v_new = ui - jnp.einsum('bhck,bhkv->bhcv', wi, S)
        o = jnp.einsum('bhck,bhkv->bhcv', qd, S) + jnp.einsum('bhcs,bhsv->bhcv', aqk, v_new)
        S = S * gl[..., None, None] + jnp.einsum('bhck,bhcv->bhkv', kd, v_new)
        return S, o

    S, o = lax.scan(step, s0.astype(f32), (q_dec, k_dec, u, w, a_qk, g_last))
    return from_chunks(o), S


def mlstm_chunked(q, k, v, ig, lf, c0, n0, m0):
    f32 = jnp.float32
    incl, _ = chunk_masks()
    q, k, v = to_chunks(q.astype(f32)), to_chunks(k.astype(f32)), to_chunks(v.astype(f32))
    ig, lf = to_chunks(ig.astype(f32)), to_chunks(lf.astype(f32))
    b = jnp.cumsum(lf, axis=-1)
    dmat = jnp.where(incl, b[..., :, None] - b[..., None, :] + ig[..., None, :], -jnp.inf)
    kv_log = b[..., -1:] - b + ig
    qk = jnp.einsum('nbhid,nbhjd->nbhij', q, k)

    def step(carry, inp):
        cs, ns, ms = carry
        qi, ki, vi, bi, di, kvl, qki = inp
        inter = bi + ms[..., None]
        m_t = jnp.maximum(inter, jnp.max(di, axis=-1))
        s = qki * jnp.exp(di - m_t[..., None])
        w_inter = jnp.exp(inter - m_t)
        num = w_inter[..., None] * jnp.einsum('bhcd,bhde->bhce', qi, cs) + jnp.einsum('bhcs,bhse->bhce', s, vi)
        den = w_inter * jnp.einsum('bhcd,bhd->bhc', qi, ns) + jnp.sum(s, axis=-1)
        h = num / jnp.maximum(jnp.abs(den), jnp.exp(-m_t))[..., None]
        bl = bi[..., -1]
        m_new = jnp.maximum(bl + ms, jnp.max(kvl, axis=-1))
        wk = jnp.exp(kvl - m_new[..., None])
        dec = jnp.exp(bl + ms - m_new)
        cs = dec[..., None, None] * cs + jnp.einsum('bhc,bhcd,bhce->bhde', wk, ki, vi)
        ns = dec[..., None] * ns + jnp.einsum('bhc,bhcd->bhd', wk, ki)
        return (cs, ns, m_new), h

    state, h = lax.scan(step, (c0.astype(f32), n0.astype(f32), m0.astype(f32)), (q, k, v, b, dmat, kv_log, qk))
    return from_chunks(h), state


def flip(x):
    return jnp.flip(x, axis=1)


def delta_bidir(q, k, v, beta, g, s0):
    o_f, s_f = gated_delta_chunked(q, k, v, beta[:, :, 0], g[:, :, 0], s0[:, 0])
    o_b, s_b = gated_delta_chunked(flip(q), flip(k), flip(v), flip(beta[:, :, 1]), flip(g[:, :, 1]), s0[:, 1])
    return (o_f + flip(o_b)).astype(v.dtype), jnp.stack([s_f, s_b], axis=1)


def mlstm_bidir(q, k, v, ig, lf, c0, n0, m0):
    h_f, st_f = mlstm_chunked(q, k, v, ig[:, :, 0], lf[:, :, 0], c0[:, 0], n0[:, 0], m0[:, 0])
    h_b, st_b = mlstm_chunked(flip(q), flip(k), flip(v), flip(ig[:, :, 1]), flip(lf[:, :, 1]),
                              c0[:, 1], n0[:, 1], m0[:, 1])
    states = (jnp.stack([st_f[0], st_b[0]], axis=1), jnp.stack([st_f[1], st_b[1]], axis=1),
              jnp.stack([st_f[2], st_b[2]], axis=1))
    return (h_f + flip(h_b)).astype(v.dtype), states


def prep_mixers(u, w_in_l, b_in_l, conv_w, a_log, dt_bias, f_bias):
    f32 = jnp.float32
    Bn, L, _ = u.shape
    split_at = [int(s) for s in np.cumsum(IN_SIZES)[:-1]]
    (aq, ak, av, bq, bk, bv, bg, ba, bb, cq, ck, cv, co, ci, cf) = jnp.split(
        jnp.einsum('bld,de->ble', u, w_in_l) + b_in_l, split_at, axis=-1)
    attn = (aq.reshape(Bn, L, A_HEADS, 2, A_QK_DIM), ak.reshape(Bn, L, A_HEADS, 2, A_QK_DIM),
            av.reshape(Bn, L, A_HEADS, A_V_DIM))
    qkv = jax.nn.silu(centred_dwconv(jnp.concatenate([bq, bk, bv], axis=-1), conv_w))
    dq, dk, dv = jnp.split(qkv, 3, axis=-1)
    beta = jax.nn.sigmoid(bb.astype(f32)).reshape(Bn, L, N_DIR, B_HEADS)
    g = -jnp.exp(a_log.astype(f32)) * jax.nn.softplus(ba.astype(f32).reshape(Bn, L, N_DIR, B_HEADS) + dt_bias.astype(f32))
    delta = (l2_norm(dq.reshape(Bn, L, B_HEADS, B_DIM)) * (B_DIM ** -0.5),
             l2_norm(dk.reshape(Bn, L, B_HEADS, B_DIM)), dv.reshape(Bn, L, B_HEADS, B_DIM), beta, g)
    ig = ci.astype(f32).reshape(Bn, L, N_DIR, C_HEADS)
    lf = jax.nn.log_sigmoid(cf.astype(f32).reshape(Bn, L, N_DIR, C_HEADS) + f_bias.astype(f32))
    mlstm = (cq.reshape(Bn, L, C_HEADS, C_DIM), ck.reshape(Bn, L, C_HEADS, C_DIM) * (C_DIM ** -0.5),
             cv.reshape(Bn, L, C_HEADS, C_DIM), ig, lf)
    return attn, delta, mlstm, bg, co


def merge_heads(a_out, o_delta, bg, h_m, co, attn_g, delta_g, mlstm_g, w_out_l, lam_init):
    Bn, L = a_out.shape[:2]
    a = rms_norm(a_out, attn_g) * (1.0 - lam_init)
    d = rms_norm(o_delta, delta_g) * jax.nn.silu(bg.reshape(Bn, L, B_HEADS, B_DIM))
    m = jax.nn.sigmoid(co.reshape(Bn, L, C_HEADS, C_DIM)) * rms_norm(h_m, mlstm_g)
    cat = jnp.concatenate([a.reshape(Bn, L, A_WIDTH), d.reshape(Bn, L, B_WIDTH), m.reshape(Bn, L, C_WIDTH)], axis=-1)
    return jnp.einsum('ble,ed->bld', cat, w_out_l)


def context_mixer(u, lp, lam_init):
    (w_in_l, b_in_l, w_out_l, lqk, attn_g, conv_w, a_log, dt_bias, delta_g, f_bias, mlstm_g) = lp
    (aq, ak, av), dl, ml, bg, co = prep_mixers(u, w_in_l, b_in_l, conv_w, a_log, dt_bias, f_bias)
    Bn = u.shape[0]
    f32 = jnp.float32
    a_out = diff_attention(aq, ak, av, diff_lambda(lqk, lam_init))
    o_d, s_d = delta_bidir(dl[0], dl[1], dl[2], dl[3], dl[4], jnp.zeros((Bn, N_DIR, B_HEADS, B_DIM, B_DIM), f32))
    h_m, (c_m, n_m, m_m) = mlstm_bidir(ml[0], ml[1], ml[2], ml[3], ml[4],
                                       jnp.zeros((Bn, N_DIR, C_HEADS, C_DIM, C_DIM), f32),
                                       jnp.zeros((Bn, N_DIR, C_HEADS, C_DIM), f32),
                                       jnp.zeros((Bn, N_DIR, C_HEADS), f32))
    out = merge_heads(a_out, o_d, bg, h_m, co, attn_g, delta_g, mlstm_g, w_out_l, lam_init)
    return out, ak, av, s_d, c_m, n_m, m_m


def latent_mixer(u, k_ctx, v_ctx, s_delta, c_st, n_st, m_st, lp, lam_init):
    (w_in_l, b_in_l, w_out_l, lqk, attn_g, conv_w, a_log, dt_bias, delta_g, f_bias, mlstm_g) = lp
    (aq, ak, av), dl, ml, bg, co = prep_mixers(u, w_in_l, b_in_l, conv_w, a_log, dt_bias, f_bias)
    k_all = jnp.concatenate([axial_rope(ak), k_ctx.astype(ak.dtype)], axis=1)
    v_all = jnp.concatenate([av, v_ctx.astype(av.dtype)], axis=1)
    a_out = diff_attention(axial_rope(aq), k_all, v_all, diff_lambda(lqk, lam_init))
    o_d, _ = delta_bidir(dl[0], dl[1], dl[2], dl[3], dl[4], s_delta)
    h_m, _ = mlstm_bidir(ml[0], ml[1], ml[2], ml[3], ml[4], c_st, n_st, m_st)
    return merge_heads(a_out, o_d, bg, h_m, co, attn_g, delta_g, mlstm_g, w_out_l, lam_init)


def setup_inputs(seed: int = 0) -> dict:
    key = jax.random.key(seed)
    ks = jax.random.split(key, 32)
    f32 = jnp.float32

    def nrm(k, shape, scale):
        return jax.random.normal(k, shape, f32) * scale

    dt = jnp.exp(jax.random.uniform(ks[21], (DEPTH, N_DIR, B_HEADS), f32, math.log(1e-3), math.log(1e-1)))
    return {
        'x_prompt': nrm(ks[0], (BATCH, SEQ, D_MODEL), 1.0),
        'x_sample': nrm(ks[1], (DEC_BATCH, DEC_SEQ, D_MODEL), 1.0),
        'cache_attn_k': nrm(ks[2], (DEC_BATCH, DEPTH, PAST_LEN, A_HEADS, 2, A_QK_DIM), 1.0),
        'cache_attn_v': nrm(ks[3], (DEC_BATCH, DEPTH, PAST_LEN, A_HEADS, A_V_DIM), 1.0),
        'state_delta': nrm(ks[4], (DEC_BATCH, DEPTH, N_DIR, B_HEADS, B_DIM, B_DIM), 0.3),
        'state_mlstm_C': nrm(ks[5], (DEC_BATCH, DEPTH, N_DIR, C_HEADS, C_DIM, C_DIM), 0.5),
        'state_mlstm_n': nrm(ks[6], (DEC_BATCH, DEPTH, N_DIR, C_HEADS, C_DIM), 0.5),
        'state_mlstm_m': nrm(ks[7], (DEC_BATCH, DEPTH, N_DIR, C_HEADS), 1.0),
        'c': nrm(ks[8], (DEC_BATCH, D_MODEL), 1.0),
        'c_ctx': nrm(ks[9], (D_MODEL,), 1.0),
        'norm1_g': 1.0 + nrm(ks[10], (DEPTH, D_MODEL), 0.05),
        'norm2_g': 1.0 + nrm(ks[11], (DEPTH, D_MODEL), 0.05),
        'w_mod': nrm(ks[12], (DEPTH, D_MODEL, 6 * D_MODEL), 0.5 * D_MODEL ** -0.5),
        'b_mod': nrm(ks[13], (DEPTH, 6 * D_MODEL), 0.02),
        'w_in': nrm(ks[14], (DEPTH, D_MODEL, IN_WIDTH), D_MODEL ** -0.5),
        'b_in': nrm(ks[15], (DEPTH, IN_WIDTH), 0.02),
        'w_out': nrm(ks[16], (DEPTH, MIX_WIDTH, D_MODEL), MIX_WIDTH ** -0.5),
        'lambda_qk': nrm(ks[17], (DEPTH, 4, A_QK_DIM), 0.1),
        'attn_subln_g': 1.0 + nrm(ks[18], (DEPTH, A_V_DIM), 0.05),
        'delta_conv_w': nrm(ks[19], (DEPTH, CONV_K, 3 * B_WIDTH), CONV_K ** -0.5),
        'delta_A_log': jnp.log(jax.random.uniform(ks[20], (DEPTH, N_DIR, B_HEADS), f32, 1.0, 16.0)),
        'delta_dt_bias': dt + jnp.log(-jnp.expm1(-dt)),
        'delta_norm_g': 1.0 + nrm(ks[22], (DEPTH, B_DIM), 0.05),
        'mlstm_f_bias': jax.random.uniform(ks[23], (DEPTH, N_DIR, C_HEADS), f32, 3.0, 6.0),
        'mlstm_norm_g': 1.0 + nrm(ks[24], (DEPTH, C_DIM), 0.05),
        'w_gate_up': nrm(ks[25], (DEPTH, D_MODEL, 2 * FFN_HIDDEN), D_MODEL ** -0.5),
        'w_down': nrm(ks[26], (DEPTH, FFN_HIDDEN, D_MODEL), FFN_HIDDEN ** -0.5),
        'final_norm_g': 1.0 + nrm(ks[27], (D_MODEL,), 0.05),
    }


def reference(x_prompt, x_sample, cache_attn_k, cache_attn_v, state_delta, state_mlstm_C, state_mlstm_n,
              state_mlstm_m, c, c_ctx, norm1_g, norm2_g, w_mod, b_mod, w_in, b_in, w_out, lambda_qk,
              attn_subln_g, delta_conv_w, delta_A_log, delta_dt_bias, delta_norm_g, mlstm_f_bias,
              mlstm_norm_g, w_gate_up, w_down, final_norm_g):
    xp, xs = x_prompt, x_sample
    sdt = x_prompt.dtype
    ks_l, vs_l, sd_l, cm_l, nm_l, mm_l = [], [], [], [], [], []
    for l in range(DEPTH):
        lp = (w_in[l], b_in[l], w_out[l], lambda_qk[l], attn_subln_g[l], delta_conv_w[l], delta_A_log[l],
              delta_dt_bias[l], delta_norm_g[l], mlstm_f_bias[l], mlstm_norm_g[l])
        lam_init = 0.8 - 0.6 * math.exp(-0.3 * l)
        sh1, sc1, g1, sh2, sc2, g2 = ada_mod(c_ctx, w_mod[l], b_mod[l])
        u = rms_norm(xp, norm1_g[l]) * (1.0 + sc1) + sh1
        mix, k_ctx, v_ctx, s_d, c_m, n_m, m_m = context_mixer(u, lp, lam_init)
        xp = xp + g1 * mix
        xp = xp + g2 * swiglu(rms_norm(xp, norm2_g[l]) * (1.0 + sc2) + sh2, w_gate_up[l], w_down[l])
        ks_l.append(k_ctx)
        vs_l.append(v_ctx)
        sd_l.append(s_d.astype(sdt))
        cm_l.append(c_m.astype(sdt))
        nm_l.append(n_m.astype(sdt))
        mm_l.append(m_m.astype(sdt))
        sh1, sc1, g1, sh2, sc2, g2 = ada_mod(c[:, None, :], w_mod[l], b_mod[l])
        u = rms_norm(xs, norm1_g[l]) * (1.0 + sc1) + sh1
        mix = latent_mixer(u, cache_attn_k[:, l], cache_attn_v[:, l], state_delta[:, l], state_mlstm_C[:, l],
                           state_mlstm_n[:, l], state_mlstm_m[:, l], lp, lam_init)
        xs = xs + g1 * mix
        xs = xs + g2 * swiglu(rms_norm(xs, norm2_g[l]) * (1.0 + sc2) + sh2, w_gate_up[l], w_down[l])
    y_prompt = rms_norm(xp, final_norm_g)
    y_sample = rms_norm(xs, final_norm_g)
    new_attn_k = jnp.stack(ks_l, axis=1)
    new_attn_v = jnp.stack(vs_l, axis=1)
    new_delta_S = jnp.stack(sd_l, axis=1)
    new_mlstm_C = jnp.stack(cm_l, axis=1)
    new_mlstm_n = jnp.stack(nm_l, axis=1)
    new_mlstm_m = jnp.stack(mm_l, axis=1)
    return (y_prompt, y_sample, new_attn_k, new_attn_v, new_delta_S, new_mlstm_C, new_mlstm_n, new_mlstm_m)
```

```python
import contextlib
import math
import numpy as np
import concourse.bass as bass
import concourse.mybir as mybir
from concourse.bass_utils import run_bass_kernel_spmd

F32 = mybir.dt.float32
BF16 = mybir.dt.bfloat16
AF = mybir.ActivationFunctionType
ALU = mybir.AluOpType
AX = mybir.AxisListType


class SemSlot:
    __slots__ = ("sem", "v")

    def __init__(self):
        self.sem = None
        self.v = 0


class Trk:
    __slots__ = ("name", "lw", "rd", "ldma", "slot", "psum")

    def __init__(self, name):
        self.name = name
        self.psum = False
        self.lw = None
        self.rd = []
        self.ldma = None
        self.slot = {}


class Op:
    __slots__ = ("eng", "meth", "args", "kw", "deps", "isdma", "dtrk", "needinc", "ev")

    def __init__(self, eng, meth, args, kw, isdma=False, dtrk=None):
        self.eng, self.meth, self.args, self.kw = eng, meth, args, kw
        self.deps = []
        self.isdma = isdma
        self.dtrk = dtrk
        self.needinc = isdma
        self.ev = None


WRITE_KEYS = ("out", "accum_out")


class Sched:
    def __init__(self, nc):
        self.nc = nc
        self.ops = []
        self.trk = {}
        self.stack = contextlib.ExitStack()
        self.engs = {"pe": nc.tensor, "dve": nc.vector, "act": nc.scalar,
                     "pool": nc.gpsimd, "sp": nc.sync}
        self.sb_bytes = 0
        self.sb_peak = 0
        self.uid = 0
        self.all_trks = []
        self.scope_trks = [[]]
        self.free_slots = {"hw": [], "sw": []}
        self.bar = []
        self.bar_pending = {e: False for e in self.engs}
        self.last_eng_op = {e: None for e in self.engs}

    def _newtrk(self, tname, name):
        t = Trk(name)
        self.trk[tname] = t
        self.all_trks.append(t)
        self.scope_trks[-1].append(t)
        return t

    def sb(self, name, shape, dtype=F32):
        self.uid += 1
        t = self.stack.enter_context(self.nc.sbuf_tensor("%s_%d" % (name, self.uid), list(shape), dtype))
        self._newtrk(t.name, name)
        n = 1
        for s in shape[1:]:
            n *= s
        self.sb_bytes += n * (2 if dtype == BF16 else 4)
        self.sb_peak = max(self.sb_peak, self.sb_bytes)
        return t

    def ps(self, name, shape, dtype=F32):
        t = self.stack.enter_context(self.nc.psum_tensor(name, list(shape), dtype))
        self._newtrk(t.name, name).psum = True
        return t

    @contextlib.contextmanager
    def scope(self):
        old = self.stack
        self.stack = contextlib.ExitStack()
        self.scope_trks.append([])
        b0 = self.sb_bytes
        try:
            yield
        finally:
            self.stack.close()
            self.stack = old
            self.sb_bytes = b0
            self.barrier()
            for t in self.scope_trks.pop():
                for cls, sl in t.slot.items():
                    self.free_slots[cls].append(sl)

    def barrier(self):
        bar = [o for o in self.last_eng_op.values() if o is not None]
        bar += [t.ldma for t in self.all_trks if t.ldma is not None]
        self.bar = sorted(set(bar))
        for e in self.bar_pending:
            self.bar_pending[e] = True

    def dram(self, name, shape, dtype=F32, kind="Internal", track=True):
        t = self.nc.dram_tensor(name, list(shape), dtype, kind=kind)
        if track:
            self.trk[t.name] = Trk(name)
        return t

    def _tr(self, ap):
        try:
            return self.trk.get(ap.tensor.name)
        except AttributeError:
            return None

    def _record(self, op, reads, writes):
        oid = len(self.ops)
        deps = set()
        for t in reads:
            if t.lw is not None:
                deps.add((t.lw, "raw"))
            if t.psum:
                for r in t.rd:
                    deps.add((r, "rar"))
        for t in writes:
            if t.lw is not None:
                deps.add((t.lw, "waw"))
            for r in t.rd:
                deps.add((r, "war"))
        if op.isdma and op.dtrk.ldma is not None:
            deps.add((op.dtrk.ldma, "raw"))
        if self.bar_pending[op.eng]:
            self.bar_pending[op.eng] = False
            for b in self.bar:
                deps.add((b, "bar"))
        final = {}
        for d, kind in deps:
            dop = self.ops[d]
            if not dop.isdma and not op.isdma and dop.eng == op.eng:
                if op.eng == "pe":
                    continue
            final[d] = True
        latest = {}
        for d in list(final):
            dop = self.ops[d]
            if not dop.isdma and dop.eng in ("pe", "act", "dve"):
                if dop.eng in latest:
                    lo = min(latest[dop.eng], d)
                    latest[dop.eng] = max(latest[dop.eng], d)
                    del final[lo]
                else:
                    latest[dop.eng] = d
        op.deps = sorted(final)
        for d in op.deps:
            self.ops[d].needinc = True
        self.ops.append(op)
        for t in reads:
            t.rd.append(oid)
        for t in writes:
            t.lw = oid
            t.rd = []
        if op.isdma:
            op.dtrk.ldma = oid
            cls = "sw" if op.eng == "pool" else "hw"
            if cls not in op.dtrk.slot:
                op.dtrk.slot[cls] = self.free_slots[cls].pop() if self.free_slots[cls] else SemSlot()
            sl = op.dtrk.slot[cls]
            if sl.v >= 30000:
                sl = op.dtrk.slot[cls] = SemSlot()
            sl.v += 16
            op.ev = (sl, sl.v)
        else:
            self.last_eng_op[op.eng] = oid
        return oid

    def op(self, eng, meth, *args, **kw):
        reads, writes = [], []
        names = list(kw.items())
        for i, a in enumerate(args):
            names.append(("out" if i == 0 else "in", a))
        for k, v in names:
            t = self._tr(v) if hasattr(v, "tensor") else None
            if t is None:
                continue
            if k in WRITE_KEYS:
                if t not in writes:
                    writes.append(t)
            elif t not in reads:
                reads.append(t)
        return self._record(Op(eng, meth, args, kw), reads, writes)

    def dma(self, q, out, in_, **kw):
        to, ti = self._tr(out), self._tr(in_)
        dtrk = None
        for ap, t in ((out, to), (in_, ti)):
            if t is not None and not type(ap.tensor).__name__.startswith("DRam"):
                dtrk = t
        if dtrk is None:
            dtrk = to if to is not None else ti
        assert dtrk is not None, "dma with no tracked side"
        o = Op(q, "dma_start", (), dict(out=out, in_=in_, **kw), isdma=True, dtrk=dtrk)
        return self._record(o, [ti] if ti is not None else [], [to] if to is not None else [])

    def finalize(self, final_wait_eng="sp"):
        nc = self.nc
        esem = {e: nc.alloc_semaphore("es_" + e) for e in self.engs}
        ecnt = {e: 0 for e in self.engs}
        known = {e: {} for e in self.engs}
        nwait = 0
        nroll = 0
        for op in self.ops:
            eng = self.engs[op.eng]
            kn = known[op.eng]
            need = {}
            for d in op.deps:
                sem, val = self.ops[d].ev
                if isinstance(sem, SemSlot):
                    if sem.sem is None:
                        sem.sem = nc.alloc_semaphore("ds%d" % id(sem))
                    sem = sem.sem
                k = id(sem)
                if kn.get(k, 0) >= val:
                    continue
                if k not in need or need[k][1] < val:
                    need[k] = (sem, val)
            for k, (sem, val) in need.items():
                eng.wait_ge(sem, val)
                kn[k] = val
                nwait += 1
            ins = getattr(eng, op.meth)(*op.args, **op.kw)
            if op.isdma:
                slot = op.ev[0]
                if slot.sem is None:
                    slot.sem = nc.alloc_semaphore("ds%d" % id(slot))
                ins.then_inc(slot.sem, 16)
            elif op.needinc:
                if ecnt[op.eng] >= 30000:
                    nroll += 1
                    esem[op.eng] = nc.alloc_semaphore("es_%s_%d" % (op.eng, nroll))
                    ecnt[op.eng] = 0
                ecnt[op.eng] += 1
                ins.then_inc(esem[op.eng], 1)
                op.ev = (esem[op.eng], ecnt[op.eng])
        eng = self.engs[final_wait_eng]
        seen = set()
        for t in self.all_trks:
            for sl in t.slot.values():
                if sl.sem is not None and id(sl) not in seen:
                    seen.add(id(sl))
                    eng.wait_ge(sl.sem, sl.v)
        for e in self.engs:
            if ecnt[e] and e != final_wait_eng:
                eng.wait_ge(esem[e], ecnt[e])
        self.stats = dict(n_ops=len(self.ops), n_wait=nwait, ecnt=dict(ecnt), sb_peak=self.sb_peak, nsem=len(seen) + 5)
        return self.stats


D = 1024
KC = 8
LS = 4096
LP = 256
NPR = 2
T = LS + NPR * LP
NT = T // 512
NB = T // 128
PAST = 512
DEPTH = 2
FH = 2816
FC = FH // 128
EPS = 1e-6
O_AQ, O_AK, O_AV = 0, 512, 1024
O_BQ, O_BK, O_BV, O_BG, O_BA, O_BB = 1536, 1792, 2048, 2304, 2560, 2568
O_CQ, O_CK, O_CV, O_CO, O_CI, O_CF = 2576, 2832, 3088, 3344, 3600, 3608
O_AQS, O_AKS = 3616, 4128
WIN = 4640
FM_CHUNKS = ([O_AQ + 128 * i for i in range(4)] + [O_AK + 128 * i for i in range(4)]
             + [O_BQ + 128 * i for i in range(6)] + [O_CQ + 128 * i for i in range(4)]
             + [O_AQS + 128 * i for i in range(4)] + [O_AKS + 128 * i for i in range(4)])
TM_GROUPS = [
    ("av", [(O_AV, 512)]),
    ("ckv", [(O_CK, 256), (O_CV, 256)]),
    ("og", [(O_BG, 256), (O_CO, 256)]),
    ("gt", [(O_BA, 16), (O_CI, 16)]),
    ("ak", [(O_AK, 512)]),
]
TM_OFF = {}
_o = 0
for _n, _cols in TM_GROUPS:
    TM_OFF[_n] = _o
    _o += sum(w for _, w in _cols)
TM_W = _o


class StopBuild(Exception):
    pass


DBG = {}


def chk(name):
    if DBG.get("stop") == name:
        raise StopBuild(name)


def lam_init_of(l):
    return 0.8 - 0.6 * math.exp(-0.3 * l)


def host_consts():
    i = np.arange(128)
    same = (i[:, None] // 64) == (i[None, :] // 64)
    c = {}
    c["ident"] = np.eye(128, dtype=np.float32)
    c["ones"] = np.ones((128, 128), np.float32)
    c["bd"] = same.astype(np.float32)
    c["m_ig"] = (same & (i[:, None] > i[None, :])).astype(np.float32)
    c["m_il"] = (same & (i[:, None] < i[None, :])).astype(np.float32)
    c["m_le"] = (same & (i[:, None] <= i[None, :])).astype(np.float32)
    c["m_ge"] = (same & (i[:, None] >= i[None, :])).astype(np.float32)
    order = ["ident", "ones", "bd", "m_ig", "m_il", "m_le", "m_ge"]
    cm = np.stack([c[k] for k in order], axis=1)
    t = np.arange(LS)
    rows = (t // 64).astype(np.float64)
    cols = (t % 64).astype(np.float64)
    nf = 16
    inv = 10000.0 ** (-np.arange(nf, dtype=np.float64) / nf)
    ang = np.concatenate([rows[:, None] * inv, cols[:, None] * inv], axis=-1)
    ang = ang.astype(np.float32).astype(np.float64)
    cos = np.cos(ang).astype(np.float32)
    sin = np.sin(ang).astype(np.float32)
    ct = np.zeros((128, LS), np.float32)
    st = np.zeros((128, LS), np.float32)
    for m in range(2):
        for d in range(64):
            ct[m * 64 + d] = cos[:, d // 2]
            st[m * 64 + d] = sin[:, d // 2] * (-1.0 if d % 2 == 0 else 1.0)
    return np.ascontiguousarray(cm), ct, st


CONST_ORDER = {"ident": 0, "ones": 1, "bd": 2, "m_ig": 3, "m_il": 4, "m_le": 5, "m_ge": 6}


class K:
    pass


def build(debug_outs=()):
    nc = bass.Bass("TRN2", target_bir_lowering=False)
    S = Sched(nc)
    k = K()
    k.nc, k.S = nc, S

    def din(name, shape):
        return nc.dram_tensor(name, list(shape), F32, kind="ExternalInput")

    def dout(name, shape):
        return S.dram(name, shape, F32, kind="ExternalOutput")

    k.x_all = din("x_all", [T, D])
    k.cache_k = din("cache_k", [DEPTH, PAST, 512])
    k.cache_v = din("cache_v", [DEPTH, PAST, 512])
    k.st_d = din("st_d", [DEPTH, 2, 4, 64, 64])
    k.st_c = din("st_c", [DEPTH, 2, 4, 64, 64])
    k.st_n = din("st_n", [DEPTH, 2, 4, 64])
    k.st_m = din("st_m", [DEPTH, 2, 4])
    k.cT = din("cT", [128, KC, 2])
    k.w_mod = din("w_mod", [DEPTH, D, 6 * D])
    k.w_in = din("w_in", [DEPTH, D, WIN])
    k.w_out = din("w_out", [DEPTH, D, D])
    k.w_gu = din("w_gu", [DEPTH, D, 2 * FH])
    k.w_dn = din("w_dn", [DEPTH, FH, D])
    k.consts = din("consts", [128, 7, 128])
    k.rope_c = din("rope_c", [128, LS])
    k.rope_s = din("rope_s", [128, LS])
    k.n1g = din("n1g", [128, DEPTH, KC])
    k.n2g = din("n2g", [128, DEPTH, KC])
    k.fng = din("fng", [1, D])
    k.b_mod = din("b_mod", [128, DEPTH, 48])
    k.b_fm = din("b_fm", [128, DEPTH, len(FM_CHUNKS)])
    k.b_tm = din("b_tm", [1, DEPTH, TM_W])
    k.conv_w = din("conv_w", [128, DEPTH, 6, 5])
    k.lam_qk = din("lam_qk", [DEPTH, 256])
    k.subln_g = din("subln_g", [128, DEPTH])
    k.dn_g = din("dn_g", [DEPTH, 64])
    k.mn_g = din("mn_g", [DEPTH, 64])
    k.a_log = din("a_log", [DEPTH, 8])
    k.dt_b = din("dt_b", [DEPTH, 8])
    k.f_b = din("f_b", [DEPTH, 8])
    k.y_s = dout("y_s", [LS, D])
    k.y_p = dout("y_p", [NPR * LP, D])
    k.o_k = dout("o_k", [NPR, DEPTH, LP, 512])
    k.o_v = dout("o_v", [NPR, DEPTH, LP, 512])
    k.o_S = dout("o_S", [NPR, DEPTH, 2, 4, 64, 64])
    k.o_C = dout("o_C", [NPR, DEPTH, 2, 4, 64, 64])
    k.o_n = dout("o_n", [NPR, DEPTH, 2, 4, 64])
    k.o_m = dout("o_m", [NPR, DEPTH, 2, 4])
    k.XT = S.dram("XT", [KC, 128, T])
    k.QT = S.dram("QT", [4, 128, T], BF16)
    k.KT = S.dram("KT", [4, 128, T + PAST], BF16)
    k.VV = S.dram("VV", [T + PAST, 512], BF16)
    k.BT = S.dram("BT", [6, 128, T])
    k.CQK = S.dram("CQK", [4, 128, T])
    k.TMS = S.dram("TMS", [T, TM_W])
    k.DQK = S.dram("DQK", [8, 64, T])
    k.DKV = S.dram("DKV", [T, 512])
    k.OD = S.dram("OD", [T, 256])
    k.OD2 = S.dram("OD2", [T, 256])
    k.OM2 = S.dram("OM2", [T, 256])
    k.OM = S.dram("OM", [T, 256])
    k.HT = S.dram("HT", [FC, 128, T], BF16)
    k.dbg = {}
    for name, shape in debug_outs:
        k.dbg[name] = dout("dbg_" + name, shape)

    k.cst = S.sb("cst", [128, 7, 128])
    k.cstb = S.sb("cstb", [128, 7, 128], BF16)
    S.dma("sp", k.cst[:], k.consts[:])
    S.op("dve", "tensor_copy", out=k.cstb[:], in_=k.cst[:])
    k.C = lambda name: k.cst[:, CONST_ORDER[name], :]
    k.Cb = lambda name: k.cstb[:, CONST_ORDER[name], :]
    k.BIG = S.sb("BIG", [128, KC, T], BF16)
    k.ps = [S.ps("ps%d" % i, [128, 512]) for i in range(8)]
    k.mod = S.sb("mod", [128, DEPTH, 48, 2])
    k.g1 = S.sb("g1", [128, DEPTH, KC, 2])
    k.g2 = S.sb("g2", [128, DEPTH, KC, 2])
    k.bfm = S.sb("bfm", [128, DEPTH, len(FM_CHUNKS)])
    S.dma("sp", k.bfm[:], k.b_fm[:])
    k.btm = S.sb("btm", [1, DEPTH, TM_W], BF16)
    k.ones1 = S.sb("ones1", [1, 128], BF16)
    S.op("dve", "memset", k.ones1[:], 1.0)
    return k


def V(k, meth, **kw):
    return k.S.op("dve", meth, **kw)


def A(k, **kw):
    return k.S.op("act", "activation", **kw)


def G(k, meth, *a, **kw):
    return k.S.op("pool", meth, *a, **kw)


def MM(k, out, lhsT, rhs, start=True, stop=True):
    return k.S.op("pe", "matmul", out, lhsT=lhsT, rhs=rhs, start=start, stop=stop)


def phase_setup(k):
    with k.S.scope():
        _phase_setup(k)


def _phase_setup(k):
    S = k.S
    csil = S.sb("csil", [128, KC, 2])
    ctmp = S.sb("ctmp", [128, KC, 2])
    S.dma("sp", ctmp[:], k.cT[:])
    A(k, out=csil[:], in_=ctmp[:], func=AF.Silu)
    bm = S.sb("bm", [128, DEPTH, 48])
    S.dma("sp", bm[:], k.b_mod[:])
    n1 = S.sb("n1", [128, DEPTH, KC])
    n2 = S.sb("n2", [128, DEPTH, KC])
    S.dma("sp", n1[:], k.n1g[:])
    S.dma("sp", n2[:], k.n2g[:])
    btmf = S.sb("btmf", [1, DEPTH, TM_W])
    S.dma("sp", btmf[:], k.b_tm[:])
    V(k, "tensor_copy", out=k.btm[:], in_=btmf[:])
    wst = [S.sb("wmst%d" % i, [128, KC, 768]) for i in range(2)]
    pm = k.ps[0]
    n = 0
    for l in range(DEPTH):
        for g in range(8):
            w = wst[n % 2]
            n += 1
            S.dma("sp", w[:], k.w_mod[l, :, g * 768:(g + 1) * 768].rearrange("(c p) n -> p c n", p=128))
            for j in range(6):
                mc = g * 6 + j
                for kc in range(KC):
                    MM(k, pm[:, mc * 2:mc * 2 + 2], lhsT=w[:, kc, j * 128:(j + 1) * 128], rhs=csil[:, kc, :],
                       start=(kc == 0), stop=(kc == KC - 1))
        for r in range(2):
            V(k, "tensor_tensor", out=k.mod[:, l, :, r], in0=pm[:, 0:96].rearrange("p (c r) -> p c r", r=2)[:, :, r],
              in1=bm[:, l, :], op=ALU.add)
        for r in range(2):
            V(k, "scalar_tensor_tensor", out=k.g1[:, l, :, r], in0=k.mod[:, l, 8:16, r], scalar=1.0, in1=n1[:, l, :],
              op0=ALU.add, op1=ALU.mult)
            V(k, "scalar_tensor_tensor", out=k.g2[:, l, :, r], in0=k.mod[:, l, 32:40, r], scalar=1.0, in1=n2[:, l, :],
              op0=ALU.add, op1=ALU.mult)
    xin = [S.sb("xin%d" % i, [128, D]) for i in range(2)]
    xto = [S.sb("xto%d" % i, [128, KC, 128]) for i in range(2)]
    for b in range(NB):
        xi = xin[b % 2]
        xo = xto[b % 2]
        S.dma("sp", xi[:], k.x_all[b * 128:(b + 1) * 128, :])
        for half in range(2):
            p = k.ps[1 + (b % 2) * 2 + half]
            for j in range(4):
                kc = half * 4 + j
                S.op("pe", "transpose", out=p[:, j * 128:(j + 1) * 128], in_=xi[:, kc * 128:(kc + 1) * 128],
                     identity=k.C("ident"))
            if half == 0:
                V(k, "tensor_copy", out=xo[:, 0:4, :], in_=p[:].rearrange("p (c n) -> p c n", n=128))
            else:
                A(k, out=xo[:, 4:8, :], in_=p[:].rearrange("p (c n) -> p c n", n=128), func=AF.Copy)
        S.dma("pool", k.XT[:, :, b * 128:(b + 1) * 128].rearrange("c p n -> p c n"), xo[:])


def seq_r(tile):
    return 0 if tile < LS // 512 else 1


def phase_norm(k, l, which):
    with k.S.scope():
        S = k.S
        k.xt_buf = [S.sb("xt%d" % i, [128, KC, 512]) for i in range(2)]
        k.sq_buf = S.sb("sq", [128, 2, 512])
        k.rstd_buf = S.sb("rstd", [128, 512])
        k.tmp_buf = [S.sb("tmp%d" % i, [128, 512]) for i in range(2)]
        _phase_norm(k, l, which)


def _phase_norm(k, l, which):
    S = k.S
    gg = k.g1 if which == 1 else k.g2
    sh0 = 0 if which == 1 else 24
    for t in range(NT):
        r = seq_r(t)
        xt = k.xt_buf[t % 2]
        S.dma("sp", xt[:], k.XT[:, :, t * 512:(t + 1) * 512].rearrange("c p n -> p c n"))
        sq = k.sq_buf
        pss = k.ps[t % 2]
        for kc in range(KC):
            A(k, out=sq[:, kc % 2, :], in_=xt[:, kc, :], func=AF.Square)
            MM(k, pss[:], lhsT=k.C("ones"), rhs=sq[:, kc % 2, :], start=(kc == 0), stop=(kc == KC - 1))
        rstd = k.rstd_buf
        A(k, out=k.tmp_buf[0][:], in_=pss[:], func=AF.Sqrt, bias=EPS, scale=1.0 / D)
        V(k, "reciprocal", out=rstd[:], in_=k.tmp_buf[0][:])
        for kc in range(KC):
            tmp = k.tmp_buf[kc % 2]
            V(k, "scalar_tensor_tensor", out=tmp[:], in0=xt[:, kc, :], scalar=gg[:, l, kc, r:r + 1], in1=rstd[:],
              op0=ALU.mult, op1=ALU.mult)
            A(k, out=k.BIG[:, kc, t * 512:(t + 1) * 512], in_=tmp[:], func=AF.Identity,
              bias=k.mod[:, l, sh0 + kc, r:r + 1], scale=1.0)


def load_w_bf16(k, dst, src_ap, stage, eng_i):
    S = k.S
    n = src_ap.shape[-1]
    S.dma("sp", stage[:, :, 0:n], src_ap.rearrange("(c p) n -> p c n", p=128))
    if eng_i % 2 == 0:
        V(k, "tensor_copy", out=dst, in_=stage[:, :, 0:n])
    else:
        G(k, "tensor_copy", out=dst, in_=stage[:, :, 0:n])


def phase_inproj(k, l):
    with k.S.scope():
        S = k.S
        k.tmp_buf = [S.sb("tmp%d" % i, [128, 512]) for i in range(2)]
        k.ob_buf = [S.sb("ob%d" % i, [128, 512], BF16) for i in range(2)]
        k.obf_buf = [S.sb("obf%d" % i, [128, 512]) for i in range(2)]
        k.wfm = [S.sb("wfm%d" % i, [128, KC, 128], BF16) for i in range(2)]
        k.wstage = [S.sb("wstage%d" % i, [128, KC, 256]) for i in range(2)]
        k.wtm = S.sb("wtm", [128, KC, TM_W], BF16)
        k.vb_buf = [S.sb("vb%d" % i, [128, 512], BF16) for i in range(2)]
        k.tmf_buf = [S.sb("tmf%d" % i, [128, 512]) for i in range(2)]
        k.ropec = S.sb("ropec", [128, LS])
        k.ropes = S.sb("ropes", [128, LS])
        S.dma("sp", k.ropec[:], k.rope_c[:])
        S.dma("sp", k.ropes[:], k.rope_s[:])
        _phase_inproj(k, l)


def _phase_inproj(k, l):
    S = k.S
    nfm = len(FM_CHUNKS)
    wfm = k.wfm
    stage = k.wstage
    fm_index = {c: i for i, c in enumerate(FM_CHUNKS)}

    def fm_matmul(col, t, ps):
        for kc in range(KC):
            MM(k, ps[:], lhsT=wcur[:, kc, :], rhs=k.BIG[:, kc, t * 512:(t + 1) * 512], start=(kc == 0), stop=(kc == KC - 1))

    cnt = 0
    for which, o_main, o_sw, dst in (("q", O_AQ, O_AQS, k.QT), ("k", O_AK, O_AKS, k.KT)):
        for h in range(4):
            wm = wfm[0]
            ws = wfm[1]
            load_w_bf16(k, wm[:], k.w_in[l, :, o_main + h * 128:o_main + (h + 1) * 128], stage[0], 0)
            load_w_bf16(k, ws[:], k.w_in[l, :, o_sw + h * 128:o_sw + (h + 1) * 128], stage[1], 1)
            bm = k.bfm[:, l, fm_index[o_main + h * 128]:fm_index[o_main + h * 128] + 1]
            bs = k.bfm[:, l, fm_index[o_sw + h * 128]:fm_index[o_sw + h * 128] + 1]
            for t in range(NT):
                p1 = k.ps[(cnt % 2) * 2]
                p2 = k.ps[(cnt % 2) * 2 + 1]
                ob = k.ob_buf[cnt % 2]
                cnt += 1
                for kc in range(KC):
                    MM(k, p1[:], lhsT=wm[:, kc, :], rhs=k.BIG[:, kc, t * 512:(t + 1) * 512], start=(kc == 0), stop=(kc == KC - 1))
                if seq_r(t) == 0:
                    for kc in range(KC):
                        MM(k, p2[:], lhsT=ws[:, kc, :], rhs=k.BIG[:, kc, t * 512:(t + 1) * 512], start=(kc == 0), stop=(kc == KC - 1))
                    t1 = k.tmp_buf[0]
                    t2 = k.tmp_buf[1]
                    V(k, "scalar_tensor_tensor", out=t1[:], in0=p1[:], scalar=bm, in1=k.ropec[:, t * 512:(t + 1) * 512],
                      op0=ALU.add, op1=ALU.mult)
                    V(k, "scalar_tensor_tensor", out=t2[:], in0=p2[:], scalar=bs, in1=k.ropes[:, t * 512:(t + 1) * 512],
                      op0=ALU.add, op1=ALU.mult)
                    G(k, "tensor_tensor", out=ob[:], in0=t1[:], in1=t2[:], op=ALU.add)
                else:
                    A(k, out=ob[:], in_=p1[:], func=AF.Identity, bias=bm, scale=1.0)
                S.dma("pool", dst[h, :, t * 512:(t + 1) * 512], ob[:])
    for o_main, nch, dst in ((O_BQ, 6, k.BT), (O_CQ, 4, k.CQK)):
        for c in range(nch):
            wm = wfm[cnt % 2]
            load_w_bf16(k, wm[:], k.w_in[l, :, o_main + c * 128:o_main + (c + 1) * 128], stage[cnt % 2], cnt)
            bm = k.bfm[:, l, fm_index[o_main + c * 128]:fm_index[o_main + c * 128] + 1]
            for t in range(NT):
                p1 = k.ps[(cnt % 2) * 2]
                ob = k.obf_buf[cnt % 2]
                cnt += 1
                for kc in range(KC):
                    MM(k, p1[:], lhsT=wm[:, kc, :], rhs=k.BIG[:, kc, t * 512:(t + 1) * 512], start=(kc == 0), stop=(kc == KC - 1))
                A(k, out=ob[:], in_=p1[:], func=AF.Identity, bias=bm, scale=1.0)
                S.dma("pool", dst[c, :, t * 512:(t + 1) * 512], ob[:])
    wtm = k.wtm
    for name, cols in TM_GROUPS:
        o = TM_OFF[name]
        for (c0, w) in cols:
            done = 0
            while done < w:
                ww = min(256, w - done)
                st = stage[cnt % 2]
                cnt += 1
                S.dma("sp", st[:, :, 0:ww], k.w_in[l, :, c0 + done:c0 + done + ww].rearrange("(c p) n -> p c n", p=128))
                V(k, "tensor_copy", out=wtm[:, :, o + done:o + done + ww], in_=st[:, :, 0:ww])
                done += ww
            o += w
    for b in range(NB):
        isprompt = b >= LS // 128
        for gi, (name, cols) in enumerate(TM_GROUPS):
            if name == "ak" and not isprompt:
                continue
            o = TM_OFF[name]
            w = sum(x for _, x in cols)
            p = k.ps[4 + (cnt % 2)]
            cnt += 1
            for kc in range(KC):
                MM(k, p[:, 0:w], lhsT=k.BIG[:, kc, b * 128:(b + 1) * 128], rhs=wtm[:, kc, o:o + w], start=(kc == 0), stop=False)
            MM(k, p[:, 0:w], lhsT=k.ones1[:, :], rhs=k.btm[:, l, o:o + w], start=False, stop=True)
            if name == "av":
                vb = k.vb_buf[b % 2]
                V(k, "tensor_copy", out=vb[:], in_=p[:])
                S.dma("pool", k.VV[b * 128:(b + 1) * 128, :], vb[:])
                if isprompt:
                    vf = k.tmf_buf[cnt % 2]
                    A(k, out=vf[:], in_=p[:], func=AF.Copy)
                    pb = b - LS // 128
                    S.dma("pool", k.o_v[pb // 2, l, (pb % 2) * 128:(pb % 2 + 1) * 128, :], vf[:])
            elif name == "ak":
                vf = k.tmf_buf[cnt % 2]
                A(k, out=vf[:], in_=p[:], func=AF.Copy)
                pb = b - LS // 128
                S.dma("pool", k.o_k[pb // 2, l, (pb % 2) * 128:(pb % 2 + 1) * 128, :], vf[:])
            else:
                vf = k.tmf_buf[cnt % 2]
                A(k, out=vf[:, 0:w], in_=p[:, 0:w], func=AF.Copy)
                S.dma("pool", k.TMS[b * 128:(b + 1) * 128, o:o + w], vf[:, 0:w])


def phase_outproj(k, l):
    S = k.S
    with S.scope():
        wo = S.sb("wo", [128, KC, D], BF16)
        stage = [S.sb("ostage%d" % i, [128, KC, 256]) for i in range(2)]
        for j in range(4):
            load_w_bf16(k, wo[:, :, j * 256:(j + 1) * 256], k.w_out[l, :, j * 256:(j + 1) * 256], stage[j % 2], j)
        xt = [S.sb("oxt%d" % i, [128, KC, 512]) for i in range(2)]
        for t in range(NT):
            r = seq_r(t)
            x = xt[t % 2]
            S.dma("sp", x[:], k.XT[:, :, t * 512:(t + 1) * 512].rearrange("c p n -> p c n"))
            for mc in range(KC):
                p = k.ps[mc % 2]
                for kc in range(KC):
                    MM(k, p[:], lhsT=wo[:, kc, mc * 128:(mc + 1) * 128], rhs=k.BIG[:, kc, t * 512:(t + 1) * 512],
                       start=(kc == 0), stop=(kc == KC - 1))
                V(k, "scalar_tensor_tensor", out=x[:, mc, :], in0=p[:], scalar=k.mod[:, l, 16 + mc, r:r + 1], in1=x[:, mc, :],
                  op0=ALU.mult, op1=ALU.add)
            S.dma("pool", k.XT[:, :, t * 512:(t + 1) * 512].rearrange("c p n -> p c n"), x[:])


def phase_ffn(k, l):
    S = k.S
    with S.scope():
        wg = [S.sb("wg%d" % i, [128, KC, 128], BF16) for i in range(2)]
        wu = [S.sb("wu%d" % i, [128, KC, 128], BF16) for i in range(2)]
        stage = [S.sb("fstage%d" % i, [128, KC, 256]) for i in range(2)]
        sil = [S.sb("sil%d" % i, [128, 512]) for i in range(2)]
        hb = [S.sb("hb%d" % i, [128, 512], BF16) for i in range(2)]
        cnt = 0
        for j in range(FC):
            g, u = wg[j % 2], wu[j % 2]
            load_w_bf16(k, g[:], k.w_gu[l, :, j * 128:(j + 1) * 128], stage[0], 0)
            load_w_bf16(k, u[:], k.w_gu[l, :, FH + j * 128:FH + (j + 1) * 128], stage[1], 1)
            for t in range(NT):
                pg = k.ps[(cnt % 2) * 2]
                pu = k.ps[(cnt % 2) * 2 + 1]
                for kc in range(KC):
                    MM(k, pg[:], lhsT=g[:, kc, :], rhs=k.BIG[:, kc, t * 512:(t + 1) * 512], start=(kc == 0), stop=(kc == KC - 1))
                for kc in range(KC):
                    MM(k, pu[:], lhsT=u[:, kc, :], rhs=k.BIG[:, kc, t * 512:(t + 1) * 512], start=(kc == 0), stop=(kc == KC - 1))
                A(k, out=sil[cnt % 2][:], in_=pg[:], func=AF.Silu)
                V(k, "tensor_tensor", out=hb[cnt % 2][:], in0=sil[cnt % 2][:], in1=pu[:], op=ALU.mult)
                S.dma("pool", k.HT[j, :, t * 512:(t + 1) * 512], hb[cnt % 2][:])
                cnt += 1
    with S.scope():
        wd = S.sb("wd", [128, FC, D], BF16)
        stage = S.sb("dstage", [128, KC, 256])
        n = 0
        for cp in range(4):
            for kr in (0, 8, 16):
                nn = min(8, FC - kr)
                S.dma("sp", stage[:, 0:nn, :], k.w_dn[l, kr * 128:(kr + nn) * 128, cp * 256:(cp + 1) * 256].rearrange("(c p) n -> p c n", p=128))
                if n % 2 == 0:
                    V(k, "tensor_copy", out=wd[:, kr:kr + nn, cp * 256:(cp + 1) * 256], in_=stage[:, 0:nn, :])
                else:
                    G(k, "tensor_copy", out=wd[:, kr:kr + nn, cp * 256:(cp + 1) * 256], in_=stage[:, 0:nn, :])
                n += 1
        ht = S.sb("ht", [128, FC, 512], BF16)
        x = S.sb("fxt", [128, KC, 512])
        for t in range(NT):
            r = seq_r(t)
            S.dma("sp", x[:], k.XT[:, :, t * 512:(t + 1) * 512].rearrange("c p n -> p c n"))
            for c4 in range(0, FC, 6):
                c5 = min(FC, c4 + 6)
                S.dma("sp", ht[:, c4:c5, :], k.HT[c4:c5, :, t * 512:(t + 1) * 512].rearrange("c p n -> p c n"))
            for mc in range(KC):
                p = k.ps[mc % 2]
                for kc in range(FC):
                    MM(k, p[:], lhsT=wd[:, kc, mc * 128:(mc + 1) * 128], rhs=ht[:, kc, :], start=(kc == 0), stop=(kc == FC - 1))
                V(k, "scalar_tensor_tensor", out=x[:, mc, :], in0=p[:], scalar=k.mod[:, l, 40 + mc, r:r + 1], in1=x[:, mc, :],
                  op0=ALU.mult, op1=ALU.add)
            S.dma("pool", k.XT[:, :, t * 512:(t + 1) * 512].rearrange("c p n -> p c n"), x[:])


def phase_final(k):
    S = k.S
    with S.scope():
        fr = S.sb("fr", [1, D])
        S.dma("sp", fr[:], k.fng[:])
        fb = S.sb("fb", [128, D])
        for j in range(2):
            MM(k, k.ps[j][:], lhsT=k.cst[0:1, 1, :], rhs=fr[:, j * 512:(j + 1) * 512])
            V(k, "tensor_copy", out=fb[:, j * 512:(j + 1) * 512], in_=k.ps[j][:])
        xb = [S.sb("yx%d" % i, [128, KC, 128]) for i in range(2)]
        xk = [S.sb("yk%d" % i, [128, D]) for i in range(2)]
        junk = S.sb("yjunk", [128, D])
        ss = S.sb("yss", [128, 2])
        yo = [S.sb("yo%d" % i, [128, D]) for i in range(2)]
        for b in range(NB):
            x = xb[b % 2]
            xt = xk[b % 2]
            S.dma("sp", x[:], k.XT[:, :, b * 128:(b + 1) * 128].rearrange("c p n -> p c n"))
            for half in range(2):
                p = k.ps[2 + (b % 2) * 2 + half]
                for j in range(4):
                    S.op("pe", "transpose", out=p[:, j * 128:(j + 1) * 128], in_=x[:, half * 4 + j, :], identity=k.C("ident"))
                if half == 0:
                    V(k, "tensor_copy", out=xt[:, 0:512], in_=p[:])
                else:
                    A(k, out=xt[:, 512:1024], in_=p[:], func=AF.Copy)
            k.S.op("dve", "memset", ss[:, 0:1], 0.0)
            A(k, out=junk[:], in_=xt[:], func=AF.Square, accum_out=ss[:, 0:1])
            A(k, out=ss[:, 1:2], in_=ss[:, 0:1], func=AF.Sqrt, bias=EPS, scale=1.0 / D)
            V(k, "reciprocal", out=ss[:, 1:2], in_=ss[:, 1:2])
            y = yo[b % 2]
            V(k, "scalar_tensor_tensor", out=y[:], in0=xt[:], scalar=ss[:, 1:2], in1=fb[:], op0=ALU.mult, op1=ALU.mult)
            if b < LS // 128:
                S.dma("pool", k.y_s[b * 128:(b + 1) * 128, :], y[:])
            else:
                pb = b - LS // 128
                S.dma("pool", k.y_p[pb * 128:(pb + 1) * 128, :], y[:])


SEQS = [dict(tok0=0, L=LS, sample=True, p=-1)] + [dict(tok0=LS + i * LP, L=LP, sample=False, p=i) for i in range(NPR)]


def phase_attn(k, l):
    S = k.S
    with S.scope():
        lt = S.sb("lamt", [128, 256])
        S.dma("sp", lt[:], k.lam_qk[l:l + 1, :].partition_broadcast(128))
        lp = S.sb("lamp", [128, 256])
        ls = S.sb("lams", [128, 4])
        V(k, "tensor_tensor", out=lp[:, 0:64], in0=lt[:, 0:64], in1=lt[:, 64:128], op=ALU.mult)
        V(k, "tensor_tensor", out=lp[:, 64:128], in0=lt[:, 128:192], in1=lt[:, 192:256], op=ALU.mult)
        V(k, "reduce_sum", out=ls[:, 0:1], in_=lp[:, 0:64], axis=AX.X)
        V(k, "reduce_sum", out=ls[:, 1:2], in_=lp[:, 64:128], axis=AX.X)
        A(k, out=ls[:, 0:2], in_=ls[:, 0:2], func=AF.Exp)
        V(k, "tensor_tensor", out=ls[:, 2:3], in0=ls[:, 1:2], in1=ls[:, 0:1], op=ALU.subtract)
        V(k, "tensor_scalar", out=ls[:, 3:4], in0=ls[:, 2:3], scalar1=-lam_init_of(l), scalar2=None, op0=ALU.add)
        nlam = ls[:, 3:4]
        sg = S.sb("sublg", [128, DEPTH])
        S.dma("sp", sg[:], k.subln_g[:])
        sgl = S.sb("sublgl", [128, 1])
        V(k, "tensor_scalar", out=sgl[:], in0=sg[:, l:l + 1], scalar1=1.0 - lam_init_of(l), scalar2=None, op0=ALU.mult)
        ckf = S.sb("ckf", [128, 512])
        ckb = S.sb("ckb", [128, 4, 128], BF16)
        cvf = S.sb("cvf", [128, 512])
        cvb = S.sb("cvb", [128, 512], BF16)
        for b in range(PAST // 128):
            S.dma("sp", ckf[:], k.cache_k[l, b * 128:(b + 1) * 128, :])
            for h in range(4):
                S.op("pe", "transpose", out=k.ps[7][:, h * 128:(h + 1) * 128], in_=ckf[:, h * 128:(h + 1) * 128], identity=k.C("ident"))
            V(k, "tensor_copy", out=ckb[:], in_=k.ps[7][:].rearrange("p (h n) -> p h n", n=128))
            S.dma("pool", k.KT[:, :, T + b * 128:T + (b + 1) * 128].rearrange("h p n -> p h n"), ckb[:])
            S.dma("sp", cvf[:], k.cache_v[l, b * 128:(b + 1) * 128, :])
            V(k, "tensor_copy", out=cvb[:], in_=cvf[:])
            S.dma("pool", k.VV[T + b * 128:T + (b + 1) * 128, :], cvb[:])
        ktb = S.sb("ktb", [128, LS + PAST], BF16)
        vsb = S.sb("vsb", [128, (LS + PAST) // 128, 128], BF16)
        qsb = [S.sb("qsb%d" % i, [128, LS], BF16) for i in range(2)]
        for i in range(2):
            S.op("dve", "memset", qsb[i][:], 0.0)
        ptb = [S.sb("ptb%d" % i, [128, 512], BF16) for i in range(6)]
        zacc = [S.sb("zacc%d" % i, [128, 512]) for i in range(4)]
        sbank = [k.ps[0], k.ps[1], k.ps[6], k.ps[7]]
        rz = S.sb("rz", [128, 512])
        a0 = S.sb("a0", [128, 512])
        a1 = S.sb("a1", [128, 512])
        sq = S.sb("asq", [128, 512])
        for sq_ in SEQS:
            tok0, L = sq_["tok0"], sq_["L"]
            QW = min(512, L)
            nkt_own = L // 128
            nkt = nkt_own + (PAST // 128 if sq_["sample"] else 0)
            for h in range(4):
                S.dma("sp", ktb[:, 0:L], k.KT[h, :, tok0:tok0 + L])
                for t4 in range(0, nkt_own, 4):
                    t5 = min(nkt_own, t4 + 4)
                    S.dma("sp", vsb[:, t4:t5, :], k.VV[tok0 + t4 * 128:tok0 + t5 * 128, h * 128:(h + 1) * 128].rearrange("(t p) e -> p t e", p=128))
                if sq_["sample"]:
                    S.dma("sp", ktb[:, L:L + PAST], k.KT[h, :, T:T + PAST])
                    S.dma("sp", vsb[:, nkt_own:nkt, :], k.VV[T:T + PAST, h * 128:(h + 1) * 128].rearrange("(t p) e -> p t e", p=128))
                for m in range(2):
                    S.dma("sp", qsb[m][m * 64:(m + 1) * 64, 0:L], k.QT[h, m * 64:(m + 1) * 64, tok0:tok0 + L])
                for qt in range(L // QW):
                    units = [(m, kt) for m in range(2) for kt in range(nkt)]
                    NS = len(sbank)
                    NU = len(units)

                    def qk_mm(i):
                        m, kt = units[i]
                        MM(k, sbank[i % NS][:, 0:QW], lhsT=ktb[:, kt * 128:(kt + 1) * 128], rhs=qsb[m][:, qt * QW:(qt + 1) * QW])

                    for i in range(min(NS, NU)):
                        qk_mm(i)
                    first_pv = [True, True]
                    zused = set()
                    npv = [0, 0]
                    for i0 in range(0, NU, 2):
                        grp = [i for i in (i0, i0 + 1) if i < NU]
                        for i in grp:
                            m, kt = units[i]
                            pt = ptb[i % len(ptb)]
                            A(k, out=pt[:, 0:QW], in_=sbank[i % NS][:, 0:QW], func=AF.Exp, scale=0.125)
                            par = kt % 3
                            if par == 0:
                                MM(k, k.ps[4 + m][:, 0:QW], lhsT=k.Cb("ones"), rhs=pt[:, 0:QW], start=(kt == 0), stop=False)
                            else:
                                eng = "dve" if par == 1 else "pool"
                                za = zacc[m * 2 + par - 1]
                                if kt < 3:
                                    S.op(eng, "tensor_copy", out=za[:, 0:QW], in_=pt[:, 0:QW])
                                    zused.add(m * 2 + par - 1)
                                else:
                                    S.op(eng, "tensor_tensor", out=za[:, 0:QW], in0=za[:, 0:QW], in1=pt[:, 0:QW], op=ALU.add)
                        for i in reversed(grp):
                            m, kt = units[i]
                            pt = ptb[i % len(ptb)]
                            npv[m] += 1
                            MM(k, k.ps[2 + m][:, 0:QW], lhsT=vsb[:, kt, :], rhs=pt[:, 0:QW], start=first_pv[m], stop=(npv[m] == nkt))
                            first_pv[m] = False
                        for i in grp:
                            if i + NS < NU:
                                qk_mm(i + NS)
                    for m in range(2):
                        zl = [z for z in (m * 2, m * 2 + 1) if z in zused]
                        for zi, z in enumerate(zl):
                            MM(k, k.ps[4 + m][:, 0:QW], lhsT=k.C("ones"), rhs=zacc[z][:, 0:QW], start=False, stop=(zi == len(zl) - 1))
                    V(k, "reciprocal", out=rz[:, 0:QW], in_=k.ps[4][:, 0:QW])
                    V(k, "tensor_tensor", out=a0[:, 0:QW], in0=k.ps[2][:, 0:QW], in1=rz[:, 0:QW], op=ALU.mult)
                    V(k, "reciprocal", out=rz[:, 0:QW], in_=k.ps[5][:, 0:QW])
                    V(k, "tensor_tensor", out=a1[:, 0:QW], in0=k.ps[3][:, 0:QW], in1=rz[:, 0:QW], op=ALU.mult)
                    V(k, "scalar_tensor_tensor", out=a0[:, 0:QW], in0=a1[:, 0:QW], scalar=nlam, in1=a0[:, 0:QW], op0=ALU.mult, op1=ALU.add)
                    G(k, "tensor_tensor", out=sq[:, 0:QW], in0=a0[:, 0:QW], in1=a0[:, 0:QW], op=ALU.mult)
                    MM(k, k.ps[6][:, 0:QW], lhsT=k.C("ones"), rhs=sq[:, 0:QW])
                    A(k, out=sq[:, 0:QW], in_=k.ps[6][:, 0:QW], func=AF.Sqrt, bias=EPS, scale=1.0 / 128)
                    V(k, "reciprocal", out=rz[:, 0:QW], in_=sq[:, 0:QW])
                    V(k, "scalar_tensor_tensor", out=k.BIG[:, h, tok0 + qt * QW:tok0 + (qt + 1) * QW], in0=a0[:, 0:QW], scalar=sgl[:, 0:1],
                      in1=rz[:, 0:QW], op0=ALU.mult, op1=ALU.mult)


def run_streams(gens):
    gens = list(gens)
    while gens:
        for g in list(gens):
            try:
                next(g)
            except StopIteration:
                gens.remove(g)


class NS_:
    pass


DT_REC = F32


def phase_mlstm(k, l):
    S = k.S
    LN8 = math.log(0.125)
    I64 = k.cst[0:64, CONST_ORDER["ident"], 0:64]
    O64 = k.cst[0:64, CONST_ORDER["ones"], 0:64]

    def bc3(ap2, n):
        return ap2.unsqueeze(2).to_broadcast([ap2.shape[0], ap2.shape[1], n])

    def bcm(ap2, n):
        return ap2.unsqueeze(1).to_broadcast([ap2.shape[0], n, ap2.shape[1]])

    def r3(ap):
        return ap.rearrange("p (a n) -> p a n", n=64)

    def alloc(tag):
        t = NS_()
        for nm in ("diag", "X", "ET", "ebq", "qb", "sT", "kw"):
            setattr(t, nm, S.sb("m%s%s" % (nm, tag), [64, 8, 64]))
        for nm in ("nlf", "tg", "nbs", "t4", "t4b", "colE", "wk", "dec"):
            setattr(t, nm, S.sb("m%s%s" % (nm, tag), [64, 8]))
        t.gt2 = S.sb("mgt2" + tag, [64, 2, 32])
        t.ckv = S.sb("mckv" + tag, [64, 2, 512])
        t.og = S.sb("mog" + tag, [64, 512])
        t.qk = S.sb("mcqk" + tag, [64, 4, 2, 128])
        t.v1 = S.sb("mv1" + tag, [64, 8, 128])
        S.op("dve", "memset", t.v1[:], 0.0)
        S.op("dve", "memset", t.v1[:, :, 64:65], 1.0)
        t.state = S.sb("mstate" + tag, [64, 4, 128])
        for nm in ("hm", "omf", "tt", "sig"):
            setattr(t, nm, S.sb("m%s%s" % (nm, tag), [64, 256]))
        t.den = S.sb("mden" + tag, [64, 4])
        t.ss4 = S.sb("mss4" + tag, [64, 4])
        t.junk = S.sb("mjunk" + tag, [64, 64])
        t.em0 = S.sb("mem0" + tag, [64, 4])
        t.n0r = S.sb("mn0r" + tag, [4, 64])
        t.kvs = S.sb("mkvs" + tag, [4, 256])
        t.nls = S.sb("mnls" + tag, [4, 256])
        t.nblc = S.sb("mnblc" + tag, [4, 4])
        t.off = S.sb("moff" + tag, [4, 4])
        t.mx = S.sb("mmx" + tag, [4, 2])
        t.mfin = S.sb("mmfin" + tag, [4, 1])
        t.mrow = S.sb("mmrow" + tag, [1, 4])
        t.d4 = S.sb("md4" + tag, [4, 4])
        t.emf = S.sb("memf" + tag, [64, 4])
        t.so = S.sb("mso" + tag, [64, 4, 64])
        t.ncol = S.sb("mncol" + tag, [64, 4])
        t.nrow = S.sb("mnrow" + tag, [4, 64])
        return t

    def stream(sq_, dr, t, B, fbb, mgb):
        tok0, L = sq_["tok0"], sq_["L"]
        nblk = L // 128
        ci = CONST_ORDER["m_le"] if dr == 0 else CONST_ORDER["m_ge"]
        cum64 = k.cst[0:64, ci, 0:64]
        OWN, OTH = (k.OM, k.OM2) if dr == 0 else (k.OM2, k.OM)
        state = t.state
        S.op("dve", "memset", state[:], 0.0)
        if sq_["sample"]:
            S.dma("sp", t.em0[:], k.st_m[l, dr:dr + 1, :].partition_broadcast(64))
            A(k, out=t.em0[:], in_=t.em0[:], func=AF.Exp)
            S.dma("sp", state[:, :, 0:64], k.st_c[l, dr, :, :, :].rearrange("h a b -> a h b"))
            S.dma("sp", t.n0r[:], k.st_n[l, dr, :, :])
            S.op("pe", "transpose", out=B[0][0:64, 32:36], in_=t.n0r[:], identity=k.cst[0:4, CONST_ORDER["ident"], 0:4])
            V(k, "tensor_copy", out=state[:, :, 64], in_=B[0][0:64, 32:36])
            V(k, "tensor_tensor", out=state[:], in0=state[:], in1=bc3(t.em0[:], 128), op=ALU.mult)
        blocks = list(range(nblk)) if dr == 0 else list(range(nblk - 1, -1, -1))
        corder = (0, 1) if dr == 0 else (1, 0)
        for step, bi in enumerate(blocks):
            r0 = tok0 + bi * 128
            for c in range(2):
                S.dma("sp", t.gt2[:, c, :], k.TMS[r0 + c * 64:r0 + (c + 1) * 64, TM_OFF["gt"]:TM_OFF["gt"] + 32])
                S.dma("sp", t.ckv[:, c, :], k.TMS[r0 + c * 64:r0 + (c + 1) * 64, TM_OFF["ckv"]:TM_OFF["ckv"] + 512])
            S.dma("sp", t.qk[:], k.CQK[:, :, r0:r0 + 128].rearrange("c (hh p) n -> p c hh n", p=64))
            yield
            kv4 = t.ckv[:].rearrange("p c (x h e) -> p c x h e", x=2, e=64)
            G(k, "tensor_copy", out=t.v1[:, :, 0:64].rearrange("p (c h) e -> p c h e", c=2), in_=kv4[:, :, 1, :, :])
            t3 = t.tg[:].rearrange("p (c h) -> p c h", c=2)
            V(k, "tensor_tensor", out=t3, in0=t.gt2[:, :, 24 + dr * 4:28 + dr * 4], in1=bcm(fbb[:, dr * 4:dr * 4 + 4], 2), op=ALU.add)
            A(k, out=t.tg[:], in_=t.tg[:], func=AF.Exp, scale=-1.0)
            A(k, out=t.nlf[:], in_=t.tg[:], func=AF.Ln, bias=1.0, scale=1.0)
            yield
            pa = B[0]
            for c in range(2):
                MM(k, pa[0:64, c * 4:(c + 1) * 4], lhsT=cum64, rhs=t.nlf[:, c * 4:(c + 1) * 4])
                MM(k, pa[0:64, 8 + c * 4:12 + c * 4], lhsT=O64, rhs=t.nlf[:, c * 4:(c + 1) * 4])
            yield
            V(k, "tensor_copy", out=t.nbs[:], in_=pa[0:64, 0:8])
            V(k, "tensor_tensor", out=t.t4[:], in0=t.nbs[:], in1=pa[0:64, 8:16], op=ALU.subtract)
            ig3 = t.gt2[:, :, 16 + dr * 4:20 + dr * 4]
            V(k, "tensor_tensor", out=t.t4b[:].rearrange("p (c h) -> p c h", c=2), in0=t.t4[:].rearrange("p (c h) -> p c h", c=2), in1=ig3, op=ALU.add)
            A(k, out=t.wk[:], in_=t.t4b[:], func=AF.Exp, bias=LN8, scale=1.0)
            V(k, "tensor_tensor", out=t.colE[:].rearrange("p (c h) -> p c h", c=2), in0=t.nbs[:].rearrange("p (c h) -> p c h", c=2), in1=ig3, op=ALU.add)
            V(k, "tensor_copy", out=t.dec[:], in_=pa[0:64, 8:16])
            A(k, out=t.dec[:], in_=t.dec[:], func=AF.Exp, scale=-1.0)
            V(k, "tensor_tensor", out=t.diag[:], in0=bcm(I64, 8), in1=bc3(t.nbs[:], 64), op=ALU.mult)
            yield
            if not sq_["sample"]:
                for c in range(2):
                    cc = bi * 2 + c
                    S.op("pe", "transpose", out=B[0][0:4, 256:320], in_=t.t4b[:, c * 4:(c + 1) * 4], identity=I64)
                    S.op("pe", "transpose", out=B[0][0:4, 384:448], in_=t.nlf[:, c * 4:(c + 1) * 4], identity=I64)
                    V(k, "tensor_copy", out=t.kvs[:, cc * 64:(cc + 1) * 64], in_=B[0][0:4, 256:320])
                    V(k, "tensor_copy", out=t.nls[:, cc * 64:(cc + 1) * 64], in_=B[0][0:4, 384:448])
            pb, pc = B[1], B[2]
            for pi in range(8):
                MM(k, pb[0:64, pi * 64:(pi + 1) * 64], lhsT=O64, rhs=t.diag[:, pi, :])
            for c in range(2):
                for h in range(4):
                    pi = c * 4 + h
                    MM(k, pc[0:64, pi * 64:(pi + 1) * 64], lhsT=t.qk[:, 2 + h // 2, h % 2, c * 64:(c + 1) * 64],
                       rhs=t.qk[:, h // 2, h % 2, c * 64:(c + 1) * 64])
            yield
            pb3 = r3(pb[0:64, :])
            V(k, "tensor_tensor", out=t.X[:], in0=bc3(t.colE[:], 64), in1=pb3, op=ALU.subtract)
            A(k, out=t.ebq[:], in_=pb3, func=AF.Exp, scale=-1.0)
            yield
            A(k, out=t.ET[:], in_=t.X[:], func=AF.Exp)
            q4 = t.qk[:, 0:2, :, :].rearrange("p a hh (c n) -> p c (a hh) n", c=2)
            G(k, "tensor_tensor", out=t.qb[:].rearrange("p (c h) n -> p c h n", c=2), in0=q4,
              in1=t.ebq[:].rearrange("p (c h) n -> p c h n", c=2), op=ALU.mult)
            V(k, "tensor_tensor", out=t.kw[:].rearrange("p (c h) e -> p c h e", c=2), in0=kv4[:, :, 0, :, :],
              in1=bc3(t.wk[:], 64).rearrange("p (c h) e -> p c h e", c=2), op=ALU.mult)
            yield
            G(k, "tensor_tensor", out=t.ET[:], in0=t.ET[:], in1=bcm(cum64, 8), op=ALU.mult)
            yield
            V(k, "scalar_tensor_tensor", out=t.sT[:], in0=r3(pc[0:64, :]), scalar=0.125, in1=t.ET[:], op0=ALU.mult, op1=ALU.mult)
            yield
            for c in corder:
                po, pst = B[3], B[1]
                for h in range(4):
                    pi = c * 4 + h
                    for (a0_, a1_) in ((0, 64), (64, 128)):
                        MM(k, po[0:64, h * 128 + a0_:h * 128 + a1_], lhsT=t.qb[:, pi, :], rhs=state[:, h, a0_:a1_], start=True, stop=False)
                        MM(k, po[0:64, h * 128 + a0_:h * 128 + a1_], lhsT=t.sT[:, pi, :], rhs=t.v1[:, pi, a0_:a1_], start=False, stop=True)
                    MM(k, pst[0:64, h * 128:(h + 1) * 128], lhsT=t.kw[:, pi, :], rhs=t.v1[:, pi, :])
                yield
                V(k, "tensor_tensor", out=state[:], in0=state[:], in1=bc3(t.dec[:, c * 4:(c + 1) * 4], 128), op=ALU.mult)
                V(k, "tensor_tensor", out=state[:], in0=state[:], in1=pst[0:64, :].rearrange("p (h e) -> p h e", e=128), op=ALU.add)
                rc = r0 + c * 64
                po3 = po[0:64, :].rearrange("p (h e) -> p h e", e=128)
                A(k, out=t.den[:], in_=po3[:, :, 64], func=AF.Abs)
                yield
                V(k, "tensor_scalar", out=t.den[:], in0=t.den[:], scalar1=1.0, scalar2=None, op0=ALU.max)
                V(k, "reciprocal", out=t.den[:], in_=t.den[:])
                V(k, "tensor_tensor", out=t.hm[:].rearrange("p (h e) -> p h e", e=64), in0=po3[:, :, 0:64], in1=bc3(t.den[:], 64), op=ALU.mult)
                if step < nblk // 2:
                    S.dma("pool", OWN[rc:rc + 64, :], t.hm[:])
                    yield
                else:
                    S.dma("sp", t.omf[:], OTH[rc:rc + 64, :])
                    S.dma("sp", t.og[:], k.TMS[rc:rc + 64, TM_OFF["og"]:TM_OFF["og"] + 512])
                    yield
                    V(k, "tensor_tensor", out=t.hm[:], in0=t.hm[:], in1=t.omf[:], op=ALU.add)
                    S.op("dve", "memset", t.ss4[:], 0.0)
                    for h in range(4):
                        A(k, out=t.junk[:], in_=t.hm[:, h * 64:(h + 1) * 64], func=AF.Square, accum_out=t.ss4[:, h:h + 1])
                    A(k, out=t.ss4[:], in_=t.ss4[:], func=AF.Sqrt, bias=EPS, scale=1.0 / 64)
                    V(k, "reciprocal", out=t.ss4[:], in_=t.ss4[:])
                    yield
                    for h in range(4):
                        V(k, "scalar_tensor_tensor", out=t.tt[:, h * 64:(h + 1) * 64], in0=t.hm[:, h * 64:(h + 1) * 64],
                          scalar=t.ss4[:, h:h + 1], in1=mgb[:], op0=ALU.mult, op1=ALU.mult)
                    A(k, out=t.sig[:], in_=t.og[:, 256:512], func=AF.Sigmoid)
                    V(k, "tensor_tensor", out=t.tt[:], in0=t.tt[:], in1=t.sig[:], op=ALU.mult)
                    yield
                    for j in range(2):
                        S.op("pe", "transpose", out=B[2][:, j * 64:(j + 1) * 64], in_=t.tt[:, j * 128:(j + 1) * 128], identity=I64)
                    yield
                    V(k, "tensor_copy", out=k.BIG[:, 6:8, rc:rc + 64], in_=B[2][:, 0:128].rearrange("p (j n) -> p j n", n=64))
        if not sq_["sample"]:
            p = sq_["p"]
            V(k, "reduce_sum", out=t.nblc[:], in_=t.nls[:].rearrange("p (c s) -> p c s", s=64), axis=AX.X)
            nch = L // 64
            S.op("dve", "memset", t.off[:], 0.0)
            if dr == 0:
                for c in range(nch - 2, -1, -1):
                    V(k, "tensor_tensor", out=t.off[:, c:c + 1], in0=t.off[:, c + 1:c + 2], in1=t.nblc[:, c + 1:c + 2], op=ALU.subtract)
            else:
                for c in range(1, nch):
                    V(k, "tensor_tensor", out=t.off[:, c:c + 1], in0=t.off[:, c - 1:c], in1=t.nblc[:, c - 1:c], op=ALU.subtract)
            V(k, "tensor_tensor", out=t.kvs[:].rearrange("p (c s) -> p c s", s=64), in0=t.kvs[:].rearrange("p (c s) -> p c s", s=64),
              in1=bc3(t.off[:], 64), op=ALU.add)
            V(k, "reduce_max", out=t.mx[:, 0:1], in_=t.kvs[:], axis=AX.X)
            V(k, "reduce_sum", out=t.mx[:, 1:2], in_=t.nblc[:], axis=AX.X)
            V(k, "scalar_tensor_tensor", out=t.mfin[:], in0=t.mx[:, 1:2], scalar=-1.0, in1=t.mx[:, 0:1], op0=ALU.mult, op1=ALU.max)
            yield
            S.op("pe", "transpose", out=B[0][0:1, 40:44], in_=t.mfin[:], identity=k.cst[0:4, CONST_ORDER["ident"], 0:4])
            V(k, "tensor_copy", out=t.mrow[:], in_=B[0][0:1, 40:44])
            S.dma("pool", k.o_m[p, l, dr:dr + 1, :], t.mrow[:])
            V(k, "tensor_scalar", out=t.d4[:], in0=k.cst[0:4, CONST_ORDER["ident"], 0:4], scalar1=t.mfin[:, 0:1], scalar2=None, op0=ALU.mult)
            MM(k, B[0][0:64, 16:20], lhsT=k.cst[0:4, CONST_ORDER["ones"], 0:64], rhs=t.d4[:])
            yield
            V(k, "tensor_copy", out=t.emf[:], in_=B[0][0:64, 16:20])
            A(k, out=t.emf[:], in_=t.emf[:], func=AF.Exp, scale=-1.0)
            V(k, "tensor_tensor", out=t.so[:], in0=state[:, :, 0:64], in1=bc3(t.emf[:], 64), op=ALU.mult)
            S.dma("pool", k.o_C[p, l, dr, :, :, :].rearrange("h a b -> a h b"), t.so[:])
            V(k, "tensor_tensor", out=t.ncol[:], in0=state[:, :, 64], in1=t.emf[:], op=ALU.mult)
            S.op("pe", "transpose", out=B[0][0:4, 64:128], in_=t.ncol[:], identity=I64)
            yield
            V(k, "tensor_copy", out=t.nrow[:], in_=B[0][0:4, 64:128])
            S.dma("pool", k.o_n[p, l, dr, :, :], t.nrow[:])

    with S.scope():
        fbb = S.sb("fbb", [64, 8])
        S.dma("sp", fbb[:], k.f_b[l:l + 1, :].partition_broadcast(64))
        mgb = S.sb("mgb", [64, 64])
        S.dma("sp", mgb[:], k.mn_g[l:l + 1, :].partition_broadcast(64))
        tiles = [alloc("f"), alloc("b")]
        for sq_ in SEQS:
            run_streams([stream(sq_, dr, tiles[dr], k.ps[dr * 4:dr * 4 + 4], fbb, mgb) for dr in range(2)])


def phase_delta(k, l):
    phase_delta_prep(k, l)
    phase_delta_scan(k, l)


def phase_delta_prep(k, l):
    S = k.S
    with S.scope():
        cw = S.sb("cw", [128, DEPTH, 6, 5])
        S.dma("sp", cw[:], k.conv_w[:])
        xin = S.sb("dxin", [128, 6, 516])
        acc = S.sb("dacc", [128, 6, 512])
        sact = S.sb("dsact", [128, 6, 512])
        sqb = S.sb("dsq", [128, 512])
        rin = S.sb("drin", [128, 512])
        tok = S.sb("dtok", [128, 512])
        for sq_ in SEQS:
            tok0, L = sq_["tok0"], sq_["L"]
            W = min(512, L)
            for ti in range(L // W):
                t0 = tok0 + ti * W
                lo, hi = max(tok0, t0 - 2), min(tok0 + L, t0 + W + 2)
                k.S.op("dve", "memset", xin[:], 0.0)
                S.dma("sp", xin[:, :, lo - (t0 - 2):hi - (t0 - 2)], k.BT[:, :, lo:hi].rearrange("c p n -> p c n"))
                for c in range(6):
                    eng = "dve"
                    S.op(eng, "tensor_scalar", out=acc[:, c, 0:W], in0=xin[:, c, 0:W], scalar1=cw[:, l, c, 0:1], scalar2=None, op0=ALU.mult)
                    for j in range(1, 5):
                        S.op(eng, "scalar_tensor_tensor", out=acc[:, c, 0:W], in0=xin[:, c, j:j + W], scalar=cw[:, l, c, j:j + 1],
                             in1=acc[:, c, 0:W], op0=ALU.mult, op1=ALU.add)
                for c in range(6):
                    A(k, out=sact[:, c, 0:W], in_=acc[:, c, 0:W], func=AF.Silu)
                for c in range(4):
                    G(k, "tensor_tensor", out=sqb[:, 0:W], in0=sact[:, c, 0:W], in1=sact[:, c, 0:W], op=ALU.mult)
                    p = k.ps[c % 2]
                    MM(k, p[:, 0:W], lhsT=k.C("bd"), rhs=sqb[:, 0:W])
                    A(k, out=rin[:, 0:W], in_=p[:, 0:W], func=AF.Sqrt, bias=EPS, scale=1.0)
                    V(k, "reciprocal", out=rin[:, 0:W], in_=rin[:, 0:W])
                    V(k, "scalar_tensor_tensor", out=sact[:, c, 0:W], in0=sact[:, c, 0:W], scalar=(0.125 if c < 2 else 1.0), in1=rin[:, 0:W],
                      op0=ALU.mult, op1=ALU.mult)
                    h0 = (c // 2) * 4 + (c % 2) * 2
                    S.dma("pool", k.DQK[h0:h0 + 2, :, t0:t0 + W].rearrange("h p n -> (h p) n"), sact[:, c, 0:W])
                for b in range(W // 128):
                    p = k.ps[2 + b % 2]
                    for j, c in enumerate((2, 3, 4, 5)):
                        S.op("pe", "transpose", out=p[:, j * 128:(j + 1) * 128], in_=sact[:, c, b * 128:(b + 1) * 128], identity=k.C("ident"))
                    V(k, "tensor_copy", out=tok[:], in_=p[:])
                    S.dma("pool", k.DKV[t0 + b * 128:t0 + (b + 1) * 128, :], tok[:])


def phase_delta_scan(k, l):
    S = k.S
    I64 = k.cst[0:64, CONST_ORDER["ident"], 0:64]
    O64 = k.cst[0:64, CONST_ORDER["ones"], 0:64]

    def bc3(ap2, n):
        return ap2.unsqueeze(2).to_broadcast([ap2.shape[0], ap2.shape[1], n])

    def bcm(ap2, n):
        return ap2.unsqueeze(1).to_broadcast([ap2.shape[0], n, ap2.shape[1]])

    def r3(ap):
        return ap.rearrange("p (a n) -> p a n", n=64)

    def alloc(tag):
        t = NS_()
        for nm in ("diag", "X", "N", "TT", "u", "ebq"):
            setattr(t, nm, S.sb("d%s%s" % (nm, tag), [64, 8, 64]))
        for nm in ("P0", "P1", "PT0", "PT1", "TTb", "aT", "vb", "kbg", "kdec", "wT", "qd"):
            setattr(t, nm, S.sb("d%s%s" % (nm, tag), [64, 8, 64], DT_REC))
        t.qkb = S.sb("dqkb" + tag, [64, 8, 128], DT_REC)
        t.S4b = S.sb("dS4b" + tag, [64, 4, 64], DT_REC)
        for nm in ("beta", "nbeta", "ng", "tg", "ngc", "egc", "bgs", "kds", "gl"):
            setattr(t, nm, S.sb("d%s%s" % (nm, tag), [64, 8]))
        t.gt2 = S.sb("dgt2" + tag, [64, 2, 32])
        t.dkv = S.sb("ddkv" + tag, [64, 2, 512])
        t.qk = S.sb("dqk" + tag, [64, 8, 128])
        t.vnew = S.sb("dvnew" + tag, [64, 4, 64], DT_REC)
        t.S4 = S.sb("dS4" + tag, [64, 4, 64])
        t.oc = S.sb("doc" + tag, [64, 256])
        t.of = S.sb("dof" + tag, [64, 256])
        t.og = S.sb("dog" + tag, [64, 512])
        t.ss4 = S.sb("dss4" + tag, [64, 4])
        t.junk = S.sb("djunk" + tag, [64, 64])
        t.tt = S.sb("dtt" + tag, [64, 256])
        t.sig = S.sb("dsig" + tag, [64, 256])
        if DT_REC == F32:
            t.qkb, t.P0, t.TTb, t.S4b = t.qk, t.N, t.TT, t.S4
        return t

    def stream(sq_, dr, t, B, alb, dtb, dgb):
        tok0, L = sq_["tok0"], sq_["L"]
        nblk = L // 128
        ci = CONST_ORDER["m_le"] if dr == 0 else CONST_ORDER["m_ge"]
        si = CONST_ORDER["m_ig"] if dr == 0 else CONST_ORDER["m_il"]
        cum64 = k.cst[0:64, ci, 0:64]
        str64 = k.cst[0:64, si, 0:64]
        OWN, OTH = (k.OD, k.OD2) if dr == 0 else (k.OD2, k.OD)
        if sq_["sample"]:
            S.dma("sp", t.S4[:], k.st_d[l, dr, :, :, :].rearrange("h a b -> a h b"))
        else:
            S.op("dve", "memset", t.S4[:], 0.0)
        if DT_REC != F32:
            V(k, "tensor_copy", out=t.S4b[:], in_=t.S4[:])
        blocks = list(range(nblk)) if dr == 0 else list(range(nblk - 1, -1, -1))
        corder = (0, 1) if dr == 0 else (1, 0)
        for step, bi in enumerate(blocks):
            r0 = tok0 + bi * 128
            for c in range(2):
                S.dma("sp", t.gt2[:, c, :], k.TMS[r0 + c * 64:r0 + (c + 1) * 64, TM_OFF["gt"]:TM_OFF["gt"] + 32])
                S.dma("sp", t.dkv[:, c, :], k.DKV[r0 + c * 64:r0 + (c + 1) * 64, :])
            S.dma("sp", t.qk[:], k.DQK[:, :, r0:r0 + 128].rearrange("h p n -> p h n"))
            yield
            if DT_REC != F32:
                G(k, "tensor_copy", out=t.qkb[:], in_=t.qk[:])
            b3 = t.beta[:].rearrange("p (c h) -> p c h", c=2)
            A(k, out=b3, in_=t.gt2[:, :, 8 + dr * 4:12 + dr * 4], func=AF.Sigmoid)
            V(k, "tensor_scalar", out=t.nbeta[:], in0=t.beta[:], scalar1=-1.0, scalar2=None, op0=ALU.mult)
            t3 = t.tg[:].rearrange("p (c h) -> p c h", c=2)
            V(k, "tensor_tensor", out=t3, in0=t.gt2[:, :, dr * 4:dr * 4 + 4], in1=bcm(dtb[:, dr * 4:dr * 4 + 4], 2), op=ALU.add)
            A(k, out=t.tg[:], in_=t.tg[:], func=AF.Exp)
            A(k, out=t.tg[:], in_=t.tg[:], func=AF.Ln, bias=1.0, scale=1.0)
            V(k, "tensor_tensor", out=t.ng[:].rearrange("p (c h) -> p c h", c=2), in0=t3, in1=bcm(alb[:, dr * 4:dr * 4 + 4], 2), op=ALU.mult)
            yield
            pa = B[0]
            for c in range(2):
                MM(k, pa[0:64, c * 4:(c + 1) * 4], lhsT=cum64, rhs=t.ng[:, c * 4:(c + 1) * 4])
                MM(k, pa[0:64, 8 + c * 4:12 + c * 4], lhsT=O64, rhs=t.ng[:, c * 4:(c + 1) * 4])
            yield
            V(k, "tensor_copy", out=t.ngc[:], in_=pa[0:64, 0:8])
            A(k, out=t.egc[:], in_=t.ngc[:], func=AF.Exp, scale=-1.0)
            V(k, "tensor_tensor", out=t.bgs[:], in0=t.beta[:], in1=t.egc[:], op=ALU.mult)
            V(k, "tensor_tensor", out=t.kds[:], in0=t.ngc[:], in1=pa[0:64, 8:16], op=ALU.subtract)
            A(k, out=t.kds[:], in_=t.kds[:], func=AF.Exp)
            V(k, "tensor_copy", out=t.gl[:], in_=pa[0:64, 8:16])
            A(k, out=t.gl[:], in_=t.gl[:], func=AF.Exp, scale=-1.0)
            V(k, "tensor_tensor", out=t.diag[:], in0=bcm(I64, 8), in1=bc3(t.ngc[:], 64), op=ALU.mult)
            yield
            pb, pc, pd = B[1], B[2], B[3]
            for pi in range(8):
                MM(k, pb[0:64, pi * 64:(pi + 1) * 64], lhsT=O64, rhs=t.diag[:, pi, :])
            for c in range(2):
                for h in range(4):
                    pi = c * 4 + h
                    kTc = t.qkb[:, 4 + h, c * 64:(c + 1) * 64]
                    qTc = t.qkb[:, h, c * 64:(c + 1) * 64]
                    MM(k, pc[0:64, pi * 64:(pi + 1) * 64], lhsT=kTc, rhs=kTc)
                    MM(k, pd[0:64, pi * 64:(pi + 1) * 64], lhsT=kTc, rhs=qTc)
            yield
            pb3 = r3(pb[0:64, :])
            V(k, "tensor_tensor", out=t.X[:], in0=pb3, in1=bc3(t.ngc[:], 64), op=ALU.subtract)
            A(k, out=t.ebq[:], in_=pb3, func=AF.Exp, scale=-1.0)
            V(k, "tensor_scalar", out=t.diag[:], in0=t.X[:], scalar1=0.0, scalar2=None, op0=ALU.min)
            G(k, "tensor_scalar", out=t.X[:], in0=t.X[:], scalar1=0.0, scalar2=None, op0=ALU.max)
            yield
            A(k, out=t.diag[:], in_=t.diag[:], func=AF.Exp)
            A(k, out=t.X[:], in_=t.X[:], func=AF.Exp, scale=-1.0)
            G(k, "tensor_tensor", out=t.diag[:], in0=t.diag[:], in1=bcm(str64, 8), op=ALU.mult)
            G(k, "tensor_tensor", out=t.X[:], in0=t.X[:], in1=bcm(cum64, 8), op=ALU.mult)
            yield
            V(k, "tensor_tensor", out=t.N[:], in0=r3(pc[0:64, :]), in1=bc3(t.nbeta[:], 64), op=ALU.mult)
            V(k, "tensor_tensor", out=t.N[:], in0=t.N[:], in1=t.diag[:], op=ALU.mult)
            if DT_REC != F32:
                A(k, out=t.P0[:], in_=t.N[:], func=AF.Copy)
            V(k, "tensor_tensor", out=t.aT[:], in0=r3(pd[0:64, :]), in1=t.X[:], op=ALU.mult)
            yield
            pt = B[1]
            for pi in range(8):
                S.op("pe", "transpose", out=pt[0:64, pi * 64:(pi + 1) * 64], in_=t.N[:, pi, :], identity=I64)
            yield
            A(k, out=t.PT0[:], in_=r3(pt[0:64, :]), func=AF.Copy)
            V(k, "tensor_tensor", out=t.TT[:], in0=r3(pt[0:64, :]), in1=bcm(I64, 8), op=ALU.add)
            if DT_REC != F32:
                if DT_REC != F32:
                    A(k, out=t.TTb[:], in_=t.TT[:], func=AF.Copy)
            yield
            Pb, PTb = (t.P0, t.P1), (t.PT0, t.PT1)
            for stp in range(5):
                P, PT = Pb[stp % 2], PTb[stp % 2]
                Pn, PTn = Pb[(stp + 1) % 2], PTb[(stp + 1) % 2]
                p1, p2, p3 = B[2], B[3], B[1]
                for pi in range(8):
                    MM(k, p1[0:64, pi * 64:(pi + 1) * 64], lhsT=PT[:, pi, :], rhs=P[:, pi, :])
                if stp < 4:
                    for pi in range(8):
                        MM(k, p2[0:64, pi * 64:(pi + 1) * 64], lhsT=P[:, pi, :], rhs=PT[:, pi, :])
                yield
                A(k, out=Pn[:], in_=r3(p1[0:64, :]), func=AF.Copy)
                if stp < 4:
                    V(k, "tensor_copy", out=PTn[:], in_=r3(p2[0:64, :]))
                yield
                for pi in range(8):
                    MM(k, p3[0:64, pi * 64:(pi + 1) * 64], lhsT=Pn[:, pi, :], rhs=t.TTb[:, pi, :])
                yield
                V(k, "tensor_tensor", out=t.TT[:], in0=t.TT[:], in1=r3(p3[0:64, :]), op=ALU.add)
                if DT_REC != F32:
                    A(k, out=t.TTb[:], in_=t.TT[:], func=AF.Copy)
                yield
            kv4 = t.dkv[:].rearrange("p c (x h e) -> p c x h e", x=2, e=64)
            V(k, "tensor_tensor", out=t.vb[:].rearrange("p (c h) e -> p c h e", c=2), in0=kv4[:, :, 1, :, :],
              in1=bc3(t.beta[:], 64).rearrange("p (c h) e -> p c h e", c=2), op=ALU.mult)
            V(k, "tensor_tensor", out=t.kbg[:].rearrange("p (c h) e -> p c h e", c=2), in0=kv4[:, :, 0, :, :],
              in1=bc3(t.bgs[:], 64).rearrange("p (c h) e -> p c h e", c=2), op=ALU.mult)
            G(k, "tensor_tensor", out=t.kdec[:].rearrange("p (c h) e -> p c h e", c=2), in0=kv4[:, :, 0, :, :],
              in1=bc3(t.kds[:], 64).rearrange("p (c h) e -> p c h e", c=2), op=ALU.mult)
            G(k, "tensor_tensor", out=t.qd[:].rearrange("p (c h) n -> p c h n", c=2),
              in0=t.qk[:, 0:4, :].rearrange("p h (c n) -> p c h n", c=2), in1=t.ebq[:].rearrange("p (c h) n -> p c h n", c=2), op=ALU.mult)
            yield
            p1, p2 = B[2], B[3]
            for pi in range(8):
                MM(k, p1[0:64, pi * 64:(pi + 1) * 64], lhsT=t.TTb[:, pi, :], rhs=t.vb[:, pi, :])
                MM(k, p2[0:64, pi * 64:(pi + 1) * 64], lhsT=t.kbg[:, pi, :], rhs=t.TTb[:, pi, :])
            yield
            A(k, out=t.u[:], in_=r3(p1[0:64, :]), func=AF.Copy)
            V(k, "tensor_copy", out=t.wT[:], in_=r3(p2[0:64, :]))
            yield
            for c in corder:
                px, py, pz = B[1], B[2], B[3]
                for h in range(4):
                    MM(k, px[0:64, h * 64:(h + 1) * 64], lhsT=t.wT[:, c * 4 + h, :], rhs=t.S4b[:, h, :])
                yield
                V(k, "tensor_tensor", out=t.vnew[:], in0=t.u[:, c * 4:(c + 1) * 4, :], in1=r3(px[0:64, 0:256]), op=ALU.subtract)
                yield
                for h in range(4):
                    MM(k, py[0:64, h * 64:(h + 1) * 64], lhsT=t.qd[:, c * 4 + h, :], rhs=t.S4b[:, h, :], start=True, stop=False)
                    MM(k, py[0:64, h * 64:(h + 1) * 64], lhsT=t.aT[:, c * 4 + h, :], rhs=t.vnew[:, h, :], start=False, stop=True)
                    MM(k, pz[0:64, h * 64:(h + 1) * 64], lhsT=t.kdec[:, c * 4 + h, :], rhs=t.vnew[:, h, :])
                yield
                V(k, "tensor_tensor", out=t.S4[:], in0=t.S4[:], in1=bc3(t.gl[:, c * 4:(c + 1) * 4], 64), op=ALU.mult)
                V(k, "tensor_tensor", out=t.S4[:], in0=t.S4[:], in1=r3(pz[0:64, 0:256]), op=ALU.add)
                if DT_REC != F32:
                    G(k, "tensor_copy", out=t.S4b[:], in_=t.S4[:])
                rc = r0 + c * 64
                A(k, out=t.oc[:], in_=py[0:64, 0:256], func=AF.Copy)
                if step < nblk // 2:
                    S.dma("pool", OWN[rc:rc + 64, :], t.oc[:])
                    yield
                else:
                    S.dma("sp", t.of[:], OTH[rc:rc + 64, :])
                    S.dma("sp", t.og[:], k.TMS[rc:rc + 64, TM_OFF["og"]:TM_OFF["og"] + 512])
                    yield
                    V(k, "tensor_tensor", out=t.oc[:], in0=t.oc[:], in1=t.of[:], op=ALU.add)
                    S.op("dve", "memset", t.ss4[:], 0.0)
                    for h in range(4):
                        A(k, out=t.junk[:], in_=t.oc[:, h * 64:(h + 1) * 64], func=AF.Square, accum_out=t.ss4[:, h:h + 1])
                    A(k, out=t.ss4[:], in_=t.ss4[:], func=AF.Sqrt, bias=EPS, scale=1.0 / 64)
                    V(k, "reciprocal", out=t.ss4[:], in_=t.ss4[:])
                    yield
                    for h in range(4):
                        V(k, "scalar_tensor_tensor", out=t.tt[:, h * 64:(h + 1) * 64], in0=t.oc[:, h * 64:(h + 1) * 64],
                          scalar=t.ss4[:, h:h + 1], in1=dgb[:], op0=ALU.mult, op1=ALU.mult)
                    A(k, out=t.sig[:], in_=t.og[:, 0:256], func=AF.Silu)
                    V(k, "tensor_tensor", out=t.tt[:], in0=t.tt[:], in1=t.sig[:], op=ALU.mult)
                    yield
                    for j in range(2):
                        S.op("pe", "transpose", out=B[0][:, 128 + j * 64:128 + (j + 1) * 64], in_=t.tt[:, j * 128:(j + 1) * 128], identity=I64)
                    yield
                    V(k, "tensor_copy", out=k.BIG[:, 4:6, rc:rc + 64], in_=B[0][:, 128:256].rearrange("p (j n) -> p j n", n=64))
        if not sq_["sample"]:
            S.dma("pool", k.o_S[sq_["p"], l, dr, :, :, :].rearrange("h a b -> a h b"), t.S4[:])

    with S.scope():
        alb = S.sb("alb", [64, 8])
        S.dma("sp", alb[:], k.a_log[l:l + 1, :].partition_broadcast(64))
        A(k, out=alb[:], in_=alb[:], func=AF.Exp)
        dtb = S.sb("dtb", [64, 8])
        S.dma("sp", dtb[:], k.dt_b[l:l + 1, :].partition_broadcast(64))
        dgb = S.sb("dgb", [64, 64])
        S.dma("sp", dgb[:], k.dn_g[l:l + 1, :].partition_broadcast(64))
        tiles = [alloc("f"), alloc("b")]
        for sq_ in SEQS:
            run_streams([stream(sq_, dr, tiles[dr], k.ps[dr * 4:dr * 4 + 4], alb, dtb, dgb) for dr in range(2)])


def swap_pairs(n):
    idx = np.arange(n)
    return idx ^ 1


def prep_shared(inp):
    f = np.float32
    sh = {}
    w_in = inp["w_in"]
    b_in = inp["b_in"]
    sw = swap_pairs(512)
    w_ext = np.concatenate([w_in, w_in[:, :, O_AQ:O_AQ + 512][:, :, sw], w_in[:, :, O_AK:O_AK + 512][:, :, sw]], axis=2)
    b_ext = np.concatenate([b_in, b_in[:, O_AQ:O_AQ + 512][:, sw], b_in[:, O_AK:O_AK + 512][:, sw]], axis=1)
    sh["w_in"] = np.ascontiguousarray(w_ext, dtype=f)
    sh["w_mod"] = inp["w_mod"]
    sh["w_out"] = inp["w_out"]
    sh["w_gu"] = inp["w_gate_up"]
    sh["w_dn"] = inp["w_down"]
    cm, ct, st = host_consts()
    sh["consts"], sh["rope_c"], sh["rope_s"] = cm, ct, st

    def fm(v):
        d, n = v.shape
        return np.ascontiguousarray(v.reshape(d, n // 128, 128).transpose(2, 0, 1), dtype=f)

    sh["n1g"] = fm(inp["norm1_g"])
    sh["n2g"] = fm(inp["norm2_g"])
    sh["fng"] = np.ascontiguousarray(inp["final_norm_g"].reshape(1, D), dtype=f)
    sh["b_mod"] = fm(inp["b_mod"])
    sh["b_fm"] = np.ascontiguousarray(
        np.stack([b_ext[:, c:c + 128] for c in FM_CHUNKS], axis=1).transpose(2, 0, 1), dtype=f)
    tm_cols = np.concatenate([np.arange(c0, c0 + w) for _, cols in TM_GROUPS for (c0, w) in cols])
    sh["b_tm"] = np.ascontiguousarray(b_ext[:, tm_cols].reshape(1, DEPTH, TM_W), dtype=f)
    cw = inp["delta_conv_w"]
    sh["conv_w"] = np.ascontiguousarray(cw.reshape(DEPTH, 5, 6, 128).transpose(3, 0, 2, 1), dtype=f)
    sh["lam_qk"] = np.ascontiguousarray(inp["lambda_qk"].reshape(DEPTH, 256), dtype=f)
    sh["subln_g"] = np.ascontiguousarray(inp["attn_subln_g"].T, dtype=f)
    sh["dn_g"] = np.ascontiguousarray(inp["delta_norm_g"], dtype=f)
    sh["mn_g"] = np.ascontiguousarray(inp["mlstm_norm_g"], dtype=f)
    sh["a_log"] = np.ascontiguousarray(inp["delta_A_log"].reshape(DEPTH, 8), dtype=f)
    sh["dt_b"] = np.ascontiguousarray(inp["delta_dt_bias"].reshape(DEPTH, 8), dtype=f)
    sh["f_b"] = np.ascontiguousarray(inp["mlstm_f_bias"].reshape(DEPTH, 8), dtype=f)
    return sh


def prep_core(inp, sh, i):
    f = np.float32
    b = i % 4
    m = dict(sh)
    xp = inp["x_prompt"][2 * i:2 * i + 2].reshape(NPR * LP, D)
    m["x_all"] = np.ascontiguousarray(np.concatenate([inp["x_sample"][b], xp], axis=0), dtype=f)
    m["cache_k"] = np.ascontiguousarray(inp["cache_attn_k"][b].reshape(DEPTH, PAST, 512), dtype=f)
    m["cache_v"] = np.ascontiguousarray(inp["cache_attn_v"][b].reshape(DEPTH, PAST, 512), dtype=f)
    m["st_d"] = np.ascontiguousarray(inp["state_delta"][b], dtype=f)
    m["st_c"] = np.ascontiguousarray(inp["state_mlstm_C"][b], dtype=f)
    m["st_n"] = np.ascontiguousarray(inp["state_mlstm_n"][b], dtype=f)
    m["st_m"] = np.ascontiguousarray(inp["state_mlstm_m"][b], dtype=f)
    cc = np.stack([inp["c"][b], inp["c_ctx"]], axis=-1)
    m["cT"] = np.ascontiguousarray(cc.reshape(KC, 128, 2).transpose(1, 0, 2), dtype=f)
    return m


def build_program(stop_after=None, debug_outs=()):
    k = build(debug_outs)
    DBG["stop"] = stop_after
    try:
        phase_setup(k)
        chk("setup")
        for l in range(DEPTH):
            phase_norm(k, l, 1)
            chk("norm%d" % l)
            phase_inproj(k, l)
            chk("inproj%d" % l)
            phase_attn(k, l)
            chk("attn%d" % l)
            phase_mlstm(k, l)
            chk("mlstm%d" % l)
            phase_delta_prep(k, l)
            chk("dprep%d" % l)
            phase_delta_scan(k, l)
            chk("delta%d" % l)
            phase_outproj(k, l)
            chk("outproj%d" % l)
            phase_norm(k, l, 2)
            phase_ffn(k, l)
            chk("ffn%d" % l)
        phase_final(k)
    except StopBuild:
        pass
    st = k.S.finalize()
    return k, st


_CACHE = {}


def kernel(**inputs):
    inp = {n: np.asarray(v) for n, v in inputs.items()}
    if "prog" not in _CACHE:
        _CACHE["prog"] = build_program()
    k, st = _CACHE["prog"]
    sh = prep_shared(inp)
    in_maps = [prep_core(inp, sh, i) for i in range(8)]
    res = run_bass_kernel_spmd(k.nc, in_maps, core_ids=list(range(8)))
    R = res.results
    B = 16
    y_prompt = np.concatenate([R[i]["y_p"].reshape(NPR, LP, D) for i in range(8)], axis=0)
    y_sample = np.stack([R[b]["y_s"] for b in range(4)], axis=0)
    nk = np.concatenate([R[i]["o_k"] for i in range(8)], axis=0).reshape(B, DEPTH, LP, 4, 2, 64)
    nv = np.concatenate([R[i]["o_v"] for i in range(8)], axis=0).reshape(B, DEPTH, LP, 4, 128)
    nS = np.concatenate([R[i]["o_S"] for i in range(8)], axis=0)
    nC = np.concatenate([R[i]["o_C"] for i in range(8)], axis=0)
    nn = np.concatenate([R[i]["o_n"] for i in range(8)], axis=0)
    nm = np.concatenate([R[i]["o_m"] for i in range(8)], axis=0)
    return (y_prompt.astype(np.float32), y_sample.astype(np.float32), nk.astype(np.float32), nv.astype(np.float32),
            nS.astype(np.float32), nC.astype(np.float32), nn.astype(np.float32), nm.astype(np.float32))
```

```python
import contextlib
import math
import numpy as np
import concourse.bass as bass
import concourse.mybir as mybir
from concourse.bass_utils import run_bass_kernel_spmd

F32 = mybir.dt.float32
BF16 = mybir.dt.bfloat16
AF = mybir.ActivationFunctionType
ALU = mybir.AluOpType
AX = mybir.AxisListType


class SemSlot:
    __slots__ = ("sem", "v")

    def __init__(self):
        self.sem = None
        self.v = 0


class Trk:
    __slots__ = ("name", "lw", "rd", "ldma", "slot", "psum")

    def __init__(self, name):
        self.name = name
        self.psum = False
        self.lw = None
        self.rd = []
        self.ldma = None
        self.slot = {}


class Op:
    __slots__ = ("eng", "meth", "args", "kw", "deps", "isdma", "dtrk", "needinc", "ev")

    def __init__(self, eng, meth, args, kw, isdma=False, dtrk=None):
        self.eng, self.meth, self.args, self.kw = eng, meth, args, kw
        self.deps = []
        self.isdma = isdma
        self.dtrk = dtrk
        self.needinc = isdma
        self.ev = None


WRITE_KEYS = ("out", "accum_out")


class Sched:
    def __init__(self, nc):
        self.nc = nc
        self.ops = []
        self.trk = {}
        self.stack = contextlib.ExitStack()
        self.engs = {"pe": nc.tensor, "dve": nc.vector, "act": nc.scalar,
                     "pool": nc.gpsimd, "sp": nc.sync}
        self.sb_bytes = 0
        self.sb_peak = 0
        self.uid = 0
        self.all_trks = []
        self.scope_trks = [[]]
        self.free_slots = {"hw": [], "sw": []}
        self.bar = []
        self.bar_pending = {e: False for e in self.engs}
        self.last_eng_op = {e: None for e in self.engs}

    def _newtrk(self, tname, name):
        t = Trk(name)
        self.trk[tname] = t
        self.all_trks.append(t)
        self.scope_trks[-1].append(t)
        return t

    def sb(self, name, shape, dtype=F32):
        self.uid += 1
        t = self.stack.enter_context(self.nc.sbuf_tensor("%s_%d" % (name, self.uid), list(shape), dtype))
        self._newtrk(t.name, name)
        n = 1
        for s in shape[1:]:
            n *= s
        self.sb_bytes += n * (2 if dtype == BF16 else 4)
        self.sb_peak = max(self.sb_peak, self.sb_bytes)
        return t

    def ps(self, name, shape, dtype=F32):
        t = self.stack.enter_context(self.nc.psum_tensor(name, list(shape), dtype))
        self._newtrk(t.name, name).psum = True
        return t

    @contextlib.contextmanager
    def scope(self):
        old = self.stack
        self.stack = contextlib.ExitStack()
        self.scope_trks.append([])
        b0 = self.sb_bytes
        try:
            yield
        finally:
            self.stack.close()
            self.stack = old
            self.sb_bytes = b0
            self.barrier()
            for t in self.scope_trks.pop():
                for cls, sl in t.slot.items():
                    self.free_slots[cls].append(sl)

    def barrier(self):
        bar = [o for o in self.last_eng_op.values() if o is not None]
        bar += [t.ldma for t in self.all_trks if t.ldma is not None]
        self.bar = sorted(set(bar))
        for e in self.bar_pending:
            self.bar_pending[e] = True

    def dram(self, name, shape, dtype=F32, kind="Internal", track=True):
        t = self.nc.dram_tensor(name, list(shape), dtype, kind=kind)
        if track:
            self.trk[t.name] = Trk(name)
        return t

    def _tr(self, ap):
        try:
            return self.trk.get(ap.tensor.name)
        except AttributeError:
            return None

    def _record(self, op, reads, writes):
        oid = len(self.ops)
        deps = set()
        for t in reads:
            if t.lw is not None:
                deps.add((t.lw, "raw"))
            if t.psum:
                for r in t.rd:
                    deps.add((r, "rar"))
        for t in writes:
            if t.lw is not None:
                deps.add((t.lw, "waw"))
            for r in t.rd:
                deps.add((r, "war"))
        if op.isdma and op.dtrk.ldma is not None:
            deps.add((op.dtrk.ldma, "raw"))
        if self.bar_pending[op.eng]:
            self.bar_pending[op.eng] = False
            for b in self.bar:
                deps.add((b, "bar"))
        final = {}
        for d, kind in deps:
            dop = self.ops[d]
            if not dop.isdma and not op.isdma and dop.eng == op.eng:
                if op.eng == "pe":
                    continue
            final[d] = True
        latest = {}
        for d in list(final):
            dop = self.ops[d]
            if not dop.isdma and dop.eng in ("pe", "act", "dve"):
                if dop.eng in latest:
                    lo = min(latest[dop.eng], d)
                    latest[dop.eng] = max(latest[dop.eng], d)
                    del final[lo]
                else:
                    latest[dop.eng] = d
        op.deps = sorted(final)
        for d in op.deps:
            self.ops[d].needinc = True
        self.ops.append(op)
        for t in reads:
            t.rd.append(oid)
        for t in writes:
            t.lw = oid
            t.rd = []
        if op.isdma:
            op.dtrk.ldma = oid
            cls = "sw" if op.eng == "pool" else "hw"
            if cls not in op.dtrk.slot:
                op.dtrk.slot[cls] = self.free_slots[cls].pop() if self.free_slots[cls] else SemSlot()
            sl = op.dtrk.slot[cls]
            if sl.v >= 30000:
                sl = op.dtrk.slot[cls] = SemSlot()
            sl.v += 16
            op.ev = (sl, sl.v)
        else:
            self.last_eng_op[op.eng] = oid
        return oid

    def op(self, eng, meth, *args, **kw):
        reads, writes = [], []
        names = list(kw.items())
        for i, a in enumerate(args):
            names.append(("out" if i == 0 else "in", a))
        for k, v in names:
            t = self._tr(v) if hasattr(v, "tensor") else None
            if t is None:
                continue
            if k in WRITE_KEYS:
                if t not in writes:
                    writes.append(t)
            elif t not in reads:
                reads.append(t)
        return self._record(Op(eng, meth, args, kw), reads, writes)

    def dma(self, q, out, in_, **kw):
        to, ti = self._tr(out), self._tr(in_)
        dtrk = None
        for ap, t in ((out, to), (in_, ti)):
            if t is not None and not type(ap.tensor).__name__.startswith("DRam"):
                dtrk = t
        if dtrk is None:
            dtrk = to if to is not None else ti
        assert dtrk is not None, "dma with no tracked side"
        o = Op(q, "dma_start", (), dict(out=out, in_=in_, **kw), isdma=True, dtrk=dtrk)
        return self._record(o, [ti] if ti is not None else [], [to] if to is not None else [])

    def finalize(self, final_wait_eng="sp"):
        nc = self.nc
        esem = {e: nc.alloc_semaphore("es_" + e) for e in self.engs}
        ecnt = {e: 0 for e in self.engs}
        known = {e: {} for e in self.engs}
        nwait = 0
        nroll = 0
        for op in self.ops:
            eng = self.engs[op.eng]
            kn = known[op.eng]
            need = {}
            for d in op.deps:
                sem, val = self.ops[d].ev
                if isinstance(sem, SemSlot):
                    if sem.sem is None:
                        sem.sem = nc.alloc_semaphore("ds%d" % id(sem))
                    sem = sem.sem
                k = id(sem)
                if kn.get(k, 0) >= val:
                    continue
                if k not in need or need[k][1] < val:
                    need[k] = (sem, val)
            for k, (sem, val) in need.items():
                eng.wait_ge(sem, val)
                kn[k] = val
                nwait += 1
            ins = getattr(eng, op.meth)(*op.args, **op.kw)
            if op.isdma:
                slot = op.ev[0]
                if slot.sem is None:
                    slot.sem = nc.alloc_semaphore("ds%d" % id(slot))
                ins.then_inc(slot.sem, 16)
            elif op.needinc:
                if ecnt[op.eng] >= 30000:
                    nroll += 1
                    esem[op.eng] = nc.alloc_semaphore("es_%s_%d" % (op.eng, nroll))
                    ecnt[op.eng] = 0
                ecnt[op.eng] += 1
                ins.then_inc(esem[op.eng], 1)
                op.ev = (esem[op.eng], ecnt[op.eng])
        eng = self.engs[final_wait_eng]
        seen = set()
        for t in self.all_trks:
            for sl in t.slot.values():
                if sl.sem is not None and id(sl) not in seen:
                    seen.add(id(sl))
                    eng.wait_ge(sl.sem, sl.v)
        for e in self.engs:
            if ecnt[e] and e != final_wait_eng:
                eng.wait_ge(esem[e], ecnt[e])
        self.stats = dict(n_ops=len(self.ops), n_wait=nwait, ecnt=dict(ecnt), sb_peak=self.sb_peak, nsem=len(seen) + 5)
        return self.stats


D = 1024
KC = 8
LS = 4096
LP = 256
NPR = 2
T = LS + NPR * LP
NT = T // 512
NB = T // 128
PAST = 512
DEPTH = 2
FH = 2816
FC = FH // 128
EPS = 1e-6
O_AQ, O_AK, O_AV = 0, 512, 1024
O_BQ, O_BK, O_BV, O_BG, O_BA, O_BB = 1536, 1792, 2048, 2304, 2560, 2568
O_CQ, O_CK, O_CV, O_CO, O_CI, O_CF = 2576, 2832, 3088, 3344, 3600, 3608
O_AQS, O_AKS = 3616, 4128
WIN = 4640
FM_CHUNKS = ([O_AQ + 128 * i for i in range(4)] + [O_AK + 128 * i for i in range(4)]
             + [O_BQ + 128 * i for i in range(6)] + [O_CQ + 128 * i for i in range(4)]
             + [O_AQS + 128 * i for i in range(4)] + [O_AKS + 128 * i for i in range(4)])
TM_GROUPS = [
    ("av", [(O_AV, 512)]),
    ("ckv", [(O_CK, 256), (O_CV, 256)]),
    ("og", [(O_BG, 256), (O_CO, 256)]),
    ("gt", [(O_BA, 16), (O_CI, 16)]),
    ("ak", [(O_AK, 512)]),
]
TM_OFF = {}
_o = 0
for _n, _cols in TM_GROUPS:
    TM_OFF[_n] = _o
    _o += sum(w for _, w in _cols)
TM_W = _o


class StopBuild(Exception):
    pass


DBG = {}


def chk(name):
    if DBG.get("stop") == name:
        raise StopBuild(name)


def lam_init_of(l):
    return 0.8 - 0.6 * math.exp(-0.3 * l)


def host_consts():
    i = np.arange(128)
    same = (i[:, None] // 64) == (i[None, :] // 64)
    c = {}
    c["ident"] = np.eye(128, dtype=np.float32)
    c["ones"] = np.ones((128, 128), np.float32)
    c["bd"] = same.astype(np.float32)
    c["m_ig"] = (same & (i[:, None] > i[None, :])).astype(np.float32)
    c["m_il"] = (same & (i[:, None] < i[None, :])).astype(np.float32)
    c["m_le"] = (same & (i[:, None] <= i[None, :])).astype(np.float32)
    c["m_ge"] = (same & (i[:, None] >= i[None, :])).astype(np.float32)
    order = ["ident", "ones", "bd", "m_ig", "m_il", "m_le", "m_ge"]
    cm = np.stack([c[k] for k in order], axis=1)
    t = np.arange(LS)
    rows = (t // 64).astype(np.float64)
    cols = (t % 64).astype(np.float64)
    nf = 16
    inv = 10000.0 ** (-np.arange(nf, dtype=np.float64) / nf)
    ang = np.concatenate([rows[:, None] * inv, cols[:, None] * inv], axis=-1)
    ang = ang.astype(np.float32).astype(np.float64)
    cos = np.cos(ang).astype(np.float32)
    sin = np.sin(ang).astype(np.float32)
    ct = np.zeros((128, LS), np.float32)
    st = np.zeros((128, LS), np.float32)
    for m in range(2):
        for d in range(64):
            ct[m * 64 + d] = cos[:, d // 2]
            st[m * 64 + d] = sin[:, d // 2] * (-1.0 if d % 2 == 0 else 1.0)
    return np.ascontiguousarray(cm), ct, st


CONST_ORDER = {"ident": 0, "ones": 1, "bd": 2, "m_ig": 3, "m_il": 4, "m_le": 5, "m_ge": 6}


class K:
    pass


def build(debug_outs=()):
    nc = bass.Bass("TRN2", target_bir_lowering=False)
    S = Sched(nc)
    k = K()
    k.nc, k.S = nc, S

    def din(name, shape):
        return nc.dram_tensor(name, list(shape), F32, kind="ExternalInput")

    def dout(name, shape):
        return S.dram(name, shape, F32, kind="ExternalOutput")

    k.x_all = din("x_all", [T, D])
    k.cache_k = din("cache_k", [DEPTH, PAST, 512])
    k.cache_v = din("cache_v", [DEPTH, PAST, 512])
    k.st_d = din("st_d", [DEPTH, 2, 4, 64, 64])
    k.st_c = din("st_c", [DEPTH, 2, 4, 64, 64])
    k.st_n = din("st_n", [DEPTH, 2, 4, 64])
    k.st_m = din("st_m", [DEPTH, 2, 4])
    k.cT = din("cT", [128, KC, 2])
    k.w_mod = din("w_mod", [DEPTH, D, 6 * D])
    k.w_in = din("w_in", [DEPTH, D, WIN])
    k.w_out = din("w_out", [DEPTH, D, D])
    k.w_gu = din("w_gu", [DEPTH, D, 2 * FH])
    k.w_dn = din("w_dn", [DEPTH, FH, D])
    k.consts = din("consts", [128, 7, 128])
    k.rope_c = din("rope_c", [128, LS])
    k.rope_s = din("rope_s", [128, LS])
    k.n1g = din("n1g", [128, DEPTH, KC])
    k.n2g = din("n2g", [128, DEPTH, KC])
    k.fng = din("fng", [1, D])
    k.b_mod = din("b_mod", [128, DEPTH, 48])
    k.b_fm = din("b_fm", [128, DEPTH, len(FM_CHUNKS)])
    k.b_tm = din("b_tm", [1, DEPTH, TM_W])
    k.conv_w = din("conv_w", [128, DEPTH, 6, 5])
    k.lam_qk = din("lam_qk", [DEPTH, 256])
    k.subln_g = din("subln_g", [128, DEPTH])
    k.dn_g = din("dn_g", [DEPTH, 64])
    k.mn_g = din("mn_g", [DEPTH, 64])
    k.a_log = din("a_log", [DEPTH, 8])
    k.dt_b = din("dt_b", [DEPTH, 8])
    k.f_b = din("f_b", [DEPTH, 8])
    k.y_s = dout("y_s", [LS, D])
    k.y_p = dout("y_p", [NPR * LP, D])
    k.o_k = dout("o_k", [NPR, DEPTH, LP, 512])
    k.o_v = dout("o_v", [NPR, DEPTH, LP, 512])
    k.o_S = dout("o_S", [NPR, DEPTH, 2, 4, 64, 64])
    k.o_C = dout("o_C", [NPR, DEPTH, 2, 4, 64, 64])
    k.o_n = dout("o_n", [NPR, DEPTH, 2, 4, 64])
    k.o_m = dout("o_m", [NPR, DEPTH, 2, 4])
    k.XT = S.dram("XT", [KC, 128, T])
    k.QT = S.dram("QT", [4, 128, T], BF16)
    k.KT = S.dram("KT", [4, 128, T + PAST], BF16)
    k.VV = S.dram("VV", [T + PAST, 512], BF16)
    k.BT = S.dram("BT", [6, 128, T])
    k.CQK = S.dram("CQK", [4, 128, T])
    k.TMS = S.dram("TMS", [T, TM_W])
    k.DQK = S.dram("DQK", [8, 64, T])
    k.DKV = S.dram("DKV", [T, 512])
    k.OD = S.dram("OD", [T, 256])
    k.OD2 = S.dram("OD2", [T, 256])
    k.OM2 = S.dram("OM2", [T, 256])
    k.OM = S.dram("OM", [T, 256])
    k.HT = S.dram("HT", [FC, 128, T], BF16)
    k.dbg = {}
    for name, shape in debug_outs:
        k.dbg[name] = dout("dbg_" + name, shape)

    k.cst = S.sb("cst", [128, 7, 128])
    k.cstb = S.sb("cstb", [128, 7, 128], BF16)
    S.dma("sp", k.cst[:], k.consts[:])
    S.op("dve", "tensor_copy", out=k.cstb[:], in_=k.cst[:])
    k.C = lambda name: k.cst[:, CONST_ORDER[name], :]
    k.Cb = lambda name: k.cstb[:, CONST_ORDER[name], :]
    k.BIG = S.sb("BIG", [128, KC, T], BF16)
    k.ps = [S.ps("ps%d" % i, [128, 512]) for i in range(8)]
    k.mod = S.sb("mod", [128, DEPTH, 48, 2])
    k.g1 = S.sb("g1", [128, DEPTH, KC, 2])
    k.g2 = S.sb("g2", [128, DEPTH, KC, 2])
    k.bfm = S.sb("bfm", [128, DEPTH, len(FM_CHUNKS)])
    S.dma("sp", k.bfm[:], k.b_fm[:])
    k.btm = S.sb("btm", [1, DEPTH, TM_W], BF16)
    k.ones1 = S.sb("ones1", [1, 128], BF16)
    S.op("dve", "memset", k.ones1[:], 1.0)
    return k


def V(k, meth, **kw):
    return k.S.op("dve", meth, **kw)


def A(k, **kw):
    return k.S.op("act", "activation", **kw)


def G(k, meth, *a, **kw):
    return k.S.op("pool", meth, *a, **kw)


def MM(k, out, lhsT, rhs, start=True, stop=True):
    return k.S.op("pe", "matmul", out, lhsT=lhsT, rhs=rhs, start=start, stop=stop)


def phase_setup(k):
    with k.S.scope():
        _phase_setup(k)


def _phase_setup(k):
    S = k.S
    csil = S.sb("csil", [128, KC, 2])
    ctmp = S.sb("ctmp", [128, KC, 2])
    S.dma("sp", ctmp[:], k.cT[:])
    A(k, out=csil[:], in_=ctmp[:], func=AF.Silu)
    bm = S.sb("bm", [128, DEPTH, 48])
    S.dma("sp", bm[:], k.b_mod[:])
    n1 = S.sb("n1", [128, DEPTH, KC])
    n2 = S.sb("n2", [128, DEPTH, KC])
    S.dma("sp", n1[:], k.n1g[:])
    S.dma("sp", n2[:], k.n2g[:])
    btmf = S.sb("btmf", [1, DEPTH, TM_W])
    S.dma("sp", btmf[:], k.b_tm[:])
    V(k, "tensor_copy", out=k.btm[:], in_=btmf[:])
    wst = [S.sb("wmst%d" % i, [128, KC, 768]) for i in range(2)]
    pm = k.ps[0]
    n = 0
    for l in range(DEPTH):
        for g in range(8):
            w = wst[n % 2]
            n += 1
            S.dma("sp", w[:], k.w_mod[l, :, g * 768:(g + 1) * 768].rearrange("(c p) n -> p c n", p=128))
            for j in range(6):
                mc = g * 6 + j
                for kc in range(KC):
                    MM(k, pm[:, mc * 2:mc * 2 + 2], lhsT=w[:, kc, j * 128:(j + 1) * 128], rhs=csil[:, kc, :],
                       start=(kc == 0), stop=(kc == KC - 1))
        for r in range(2):
            V(k, "tensor_tensor", out=k.mod[:, l, :, r], in0=pm[:, 0:96].rearrange("p (c r) -> p c r", r=2)[:, :, r],
              in1=bm[:, l, :], op=ALU.add)
        for r in range(2):
            V(k, "scalar_tensor_tensor", out=k.g1[:, l, :, r], in0=k.mod[:, l, 8:16, r], scalar=1.0, in1=n1[:, l, :],
              op0=ALU.add, op1=ALU.mult)
            V(k, "scalar_tensor_tensor", out=k.g2[:, l, :, r], in0=k.mod[:, l, 32:40, r], scalar=1.0, in1=n2[:, l, :],
              op0=ALU.add, op1=ALU.mult)
    xin = [S.sb("xin%d" % i, [128, D]) for i in range(2)]
    xto = [S.sb("xto%d" % i, [128, KC, 128]) for i in range(2)]
    for b in range(NB):
        xi = xin[b % 2]
        xo = xto[b % 2]
        S.dma("sp", xi[:], k.x_all[b * 128:(b + 1) * 128, :])
        for half in range(2):
            p = k.ps[1 + (b % 2) * 2 + half]
            for j in range(4):
                kc = half * 4 + j
                S.op("pe", "transpose", out=p[:, j * 128:(j + 1) * 128], in_=xi[:, kc * 128:(kc + 1) * 128],
                     identity=k.C("ident"))
            if half == 0:
                V(k, "tensor_copy", out=xo[:, 0:4, :], in_=p[:].rearrange("p (c n) -> p c n", n=128))
            else:
                A(k, out=xo[:, 4:8, :], in_=p[:].rearrange("p (c n) -> p c n", n=128), func=AF.Copy)
        S.dma("pool", k.XT[:, :, b * 128:(b + 1) * 128].rearrange("c p n -> p c n"), xo[:])


def seq_r(tile):
    return 0 if tile < LS // 512 else 1


def phase_norm(k, l, which):
    with k.S.scope():
        S = k.S
        k.xt_buf = [S.sb("xt%d" % i, [128, KC, 512]) for i in range(2)]
        k.sq_buf = S.sb("sq", [128, 2, 512])
        k.rstd_buf = S.sb("rstd", [128, 512])
        k.tmp_buf = [S.sb("tmp%d" % i, [128, 512]) for i in range(2)]
        _phase_norm(k, l, which)


def _phase_norm(k, l, which):
    S = k.S
    gg = k.g1 if which == 1 else k.g2
    sh0 = 0 if which == 1 else 24
    for t in range(NT):
        r = seq_r(t)
        xt = k.xt_buf[t % 2]
        S.dma("sp", xt[:], k.XT[:, :, t * 512:(t + 1) * 512].rearrange("c p n -> p c n"))
        sq = k.sq_buf
        pss = k.ps[t % 2]
        for kc in range(KC):
            A(k, out=sq[:, kc % 2, :], in_=xt[:, kc, :], func=AF.Square)
            MM(k, pss[:], lhsT=k.C("ones"), rhs=sq[:, kc % 2, :], start=(kc == 0), stop=(kc == KC - 1))
        rstd = k.rstd_buf
        A(k, out=k.tmp_buf[0][:], in_=pss[:], func=AF.Sqrt, bias=EPS, scale=1.0 / D)
        V(k, "reciprocal", out=rstd[:], in_=k.tmp_buf[0][:])
        for kc in range(KC):
            tmp = k.tmp_buf[kc % 2]
            V(k, "scalar_tensor_tensor", out=tmp[:], in0=xt[:, kc, :], scalar=gg[:, l, kc, r:r + 1], in1=rstd[:],
              op0=ALU.mult, op1=ALU.mult)
            A(k, out=k.BIG[:, kc, t * 512:(t + 1) * 512], in_=tmp[:], func=AF.Identity,
              bias=k.mod[:, l, sh0 + kc, r:r + 1], scale=1.0)


def load_w_bf16(k, dst, src_ap, stage, eng_i):
    S = k.S
    n = src_ap.shape[-1]
    S.dma("sp", stage[:, :, 0:n], src_ap.rearrange("(c p) n -> p c n", p=128))
    if eng_i % 2 == 0:
        V(k, "tensor_copy", out=dst, in_=stage[:, :, 0:n])
    else:
        G(k, "tensor_copy", out=dst, in_=stage[:, :, 0:n])


def phase_inproj(k, l):
    with k.S.scope():
        S = k.S
        k.tmp_buf = [S.sb("tmp%d" % i, [128, 512]) for i in range(2)]
        k.ob_buf = [S.sb("ob%d" % i, [128, 512], BF16) for i in range(2)]
        k.obf_buf = [S.sb("obf%d" % i, [128, 512]) for i in range(2)]
        k.wfm = [S.sb("wfm%d" % i, [128, KC, 128], BF16) for i in range(2)]
        k.wstage = [S.sb("wstage%d" % i, [128, KC, 256]) for i in range(2)]
        k.wtm = S.sb("wtm", [128, KC, TM_W], BF16)
        k.vb_buf = [S.sb("vb%d" % i, [128, 512], BF16) for i in range(2)]
        k.tmf_buf = [S.sb("tmf%d" % i, [128, 512]) for i in range(2)]
        k.ropec = S.sb("ropec", [128, LS])
        k.ropes = S.sb("ropes", [128, LS])
        S.dma("sp", k.ropec[:], k.rope_c[:])
        S.dma("sp", k.ropes[:], k.rope_s[:])
        _phase_inproj(k, l)


def _phase_inproj(k, l):
    S = k.S
    nfm = len(FM_CHUNKS)
    wfm = k.wfm
    stage = k.wstage
    fm_index = {c: i for i, c in enumerate(FM_CHUNKS)}

    def fm_matmul(col, t, ps):
        for kc in range(KC):
            MM(k, ps[:], lhsT=wcur[:, kc, :], rhs=k.BIG[:, kc, t * 512:(t + 1) * 512], start=(kc == 0), stop=(kc == KC - 1))

    cnt = 0
    for which, o_main, o_sw, dst in (("q", O_AQ, O_AQS, k.QT), ("k", O_AK, O_AKS, k.KT)):
        for h in range(4):
            wm = wfm[0]
            ws = wfm[1]
            load_w_bf16(k, wm[:], k.w_in[l, :, o_main + h * 128:o_main + (h + 1) * 128], stage[0], 0)
            load_w_bf16(k, ws[:], k.w_in[l, :, o_sw + h * 128:o_sw + (h + 1) * 128], stage[1], 1)
            bm = k.bfm[:, l, fm_index[o_main + h * 128]:fm_index[o_main + h * 128] + 1]
            bs = k.bfm[:, l, fm_index[o_sw + h * 128]:fm_index[o_sw + h * 128] + 1]
            for t in range(NT):
                p1 = k.ps[(cnt % 2) * 2]
                p2 = k.ps[(cnt % 2) * 2 + 1]
                ob = k.ob_buf[cnt % 2]
                cnt += 1
                for kc in range(KC):
                    MM(k, p1[:], lhsT=wm[:, kc, :], rhs=k.BIG[:, kc, t * 512:(t + 1) * 512], start=(kc == 0), stop=(kc == KC - 1))
                if seq_r(t) == 0:
                    for kc in range(KC):
                        MM(k, p2[:], lhsT=ws[:, kc, :], rhs=k.BIG[:, kc, t * 512:(t + 1) * 512], start=(kc == 0), stop=(kc == KC - 1))
                    t1 = k.tmp_buf[0]
                    t2 = k.tmp_buf[1]
                    V(k, "scalar_tensor_tensor", out=t1[:], in0=p1[:], scalar=bm, in1=k.ropec[:, t * 512:(t + 1) * 512],
                      op0=ALU.add, op1=ALU.mult)
                    V(k, "scalar_tensor_tensor", out=t2[:], in0=p2[:], scalar=bs, in1=k.ropes[:, t * 512:(t + 1) * 512],
                      op0=ALU.add, op1=ALU.mult)
                    G(k, "tensor_tensor", out=ob[:], in0=t1[:], in1=t2[:], op=ALU.add)
                else:
                    A(k, out=ob[:], in_=p1[:], func=AF.Identity, bias=bm, scale=1.0)
                S.dma("pool", dst[h, :, t * 512:(t + 1) * 512], ob[:])
    for o_main, nch, dst in ((O_BQ, 6, k.BT), (O_CQ, 4, k.CQK)):
        for c in range(nch):
            wm = wfm[cnt % 2]
            load_w_bf16(k, wm[:], k.w_in[l, :, o_main + c * 128:o_main + (c + 1) * 128], stage[cnt % 2], cnt)
            bm = k.bfm[:, l, fm_index[o_main + c * 128]:fm_index[o_main + c * 128] + 1]
            for t in range(NT):
                p1 = k.ps[(cnt % 2) * 2]
                ob = k.obf_buf[cnt % 2]
                cnt += 1
                for kc in range(KC):
                    MM(k, p1[:], lhsT=wm[:, kc, :], rhs=k.BIG[:, kc, t * 512:(t + 1) * 512], start=(kc == 0), stop=(kc == KC - 1))
                A(k, out=ob[:], in_=p1[:], func=AF.Identity, bias=bm, scale=1.0)
                S.dma("pool", dst[c, :, t * 512:(t + 1) * 512], ob[:])
    wtm = k.wtm
    for name, cols in TM_GROUPS:
        o = TM_OFF[name]
        for (c0, w) in cols:
            done = 0
            while done < w:
                ww = min(256, w - done)
                st = stage[cnt % 2]
                cnt += 1
                S.dma("sp", st[:, :, 0:ww], k.w_in[l, :, c0 + done:c0 + done + ww].rearrange("(c p) n -> p c n", p=128))
                V(k, "tensor_copy", out=wtm[:, :, o + done:o + done + ww], in_=st[:, :, 0:ww])
                done += ww
            o += w
    for b in range(NB):
        isprompt = b >= LS // 128
        for gi, (name, cols) in enumerate(TM_GROUPS):
            if name == "ak" and not isprompt:
                continue
            o = TM_OFF[name]
            w = sum(x for _, x in cols)
            p = k.ps[4 + (cnt % 2)]
            cnt += 1
            for kc in range(KC):
                MM(k, p[:, 0:w], lhsT=k.BIG[:, kc, b * 128:(b + 1) * 128], rhs=wtm[:, kc, o:o + w], start=(kc == 0), stop=False)
            MM(k, p[:, 0:w], lhsT=k.ones1[:, :], rhs=k.btm[:, l, o:o + w], start=False, stop=True)
            if name == "av":
                vb = k.vb_buf[b % 2]
                V(k, "tensor_copy", out=vb[:], in_=p[:])
                S.dma("pool", k.VV[b * 128:(b + 1) * 128, :], vb[:])
                if isprompt:
                    vf = k.tmf_buf[cnt % 2]
                    A(k, out=vf[:], in_=p[:], func=AF.Copy)
                    pb = b - LS // 128
                    S.dma("pool", k.o_v[pb // 2, l, (pb % 2) * 128:(pb % 2 + 1) * 128, :], vf[:])
            elif name == "ak":
                vf = k.tmf_buf[cnt % 2]
                A(k, out=vf[:], in_=p[:], func=AF.Copy)
                pb = b - LS // 128
                S.dma("pool", k.o_k[pb // 2, l, (pb % 2) * 128:(pb % 2 + 1) * 128, :], vf[:])
            else:
                vf = k.tmf_buf[cnt % 2]
                A(k, out=vf[:, 0:w], in_=p[:, 0:w], func=AF.Copy)
                S.dma("pool", k.TMS[b * 128:(b + 1) * 128, o:o + w], vf[:, 0:w])


def phase_outproj(k, l):
    S = k.S
    with S.scope():
        wo = S.sb("wo", [128, KC, D], BF16)
        stage = [S.sb("ostage%d" % i, [128, KC, 256]) for i in range(2)]
        for j in range(4):
            load_w_bf16(k, wo[:, :, j * 256:(j + 1) * 256], k.w_out[l, :, j * 256:(j + 1) * 256], stage[j % 2], j)
        xt = [S.sb("oxt%d" % i, [128, KC, 512]) for i in range(2)]
        for t in range(NT):
            r = seq_r(t)
            x = xt[t % 2]
            S.dma("sp", x[:], k.XT[:, :, t * 512:(t + 1) * 512].rearrange("c p n -> p c n"))
            for mc in range(KC):
                p = k.ps[mc % 2]
                for kc in range(KC):
                    MM(k, p[:], lhsT=wo[:, kc, mc * 128:(mc + 1) * 128], rhs=k.BIG[:, kc, t * 512:(t + 1) * 512],
                       start=(kc == 0), stop=(kc == KC - 1))
                V(k, "scalar_tensor_tensor", out=x[:, mc, :], in0=p[:], scalar=k.mod[:, l, 16 + mc, r:r + 1], in1=x[:, mc, :],
                  op0=ALU.mult, op1=ALU.add)
            S.dma("pool", k.XT[:, :, t * 512:(t + 1) * 512].rearrange("c p n -> p c n"), x[:])


def phase_ffn(k, l):
    S = k.S
    with S.scope():
        wg = [S.sb("wg%d" % i, [128, KC, 128], BF16) for i in range(2)]
        wu = [S.sb("wu%d" % i, [128, KC, 128], BF16) for i in range(2)]
        stage = [S.sb("fstage%d" % i, [128, KC, 256]) for i in range(2)]
        sil = [S.sb("sil%d" % i, [128, 512]) for i in range(2)]
        hb = [S.sb("hb%d" % i, [128, 512], BF16) for i in range(2)]
        cnt = 0
        for j in range(FC):
            g, u = wg[j % 2], wu[j % 2]
            load_w_bf16(k, g[:], k.w_gu[l, :, j * 128:(j + 1) * 128], stage[0], 0)
            load_w_bf16(k, u[:], k.w_gu[l, :, FH + j * 128:FH + (j + 1) * 128], stage[1], 1)
            for t in range(NT):
                pg = k.ps[(cnt % 2) * 2]
                pu = k.ps[(cnt % 2) * 2 + 1]
                for kc in range(KC):
                    MM(k, pg[:], lhsT=g[:, kc, :], rhs=k.BIG[:, kc, t * 512:(t + 1) * 512], start=(kc == 0), stop=(kc == KC - 1))
                for kc in range(KC):
                    MM(k, pu[:], lhsT=u[:, kc, :], rhs=k.BIG[:, kc, t * 512:(t + 1) * 512], start=(kc == 0), stop=(kc == KC - 1))
                A(k, out=sil[cnt % 2][:], in_=pg[:], func=AF.Silu)
                V(k, "tensor_tensor", out=hb[cnt % 2][:], in0=sil[cnt % 2][:], in1=pu[:], op=ALU.mult)
                S.dma("pool", k.HT[j, :, t * 512:(t + 1) * 512], hb[cnt % 2][:])
                cnt += 1
    with S.scope():
        wd = S.sb("wd", [128, FC, D], BF16)
        stage = S.sb("dstage", [128, KC, 256])
        n = 0
        for cp in range(4):
            for kr in (0, 8, 16):
                nn = min(8, FC - kr)
                S.dma("sp", stage[:, 0:nn, :], k.w_dn[l, kr * 128:(kr + nn) * 128, cp * 256:(cp + 1) * 256].rearrange("(c p) n -> p c n", p=128))
                if n % 2 == 0:
                    V(k, "tensor_copy", out=wd[:, kr:kr + nn, cp * 256:(cp + 1) * 256], in_=stage[:, 0:nn, :])
                else:
                    G(k, "tensor_copy", out=wd[:, kr:kr + nn, cp * 256:(cp + 1) * 256], in_=stage[:, 0:nn, :])
                n += 1
        hts = [S.sb("ht%d" % i, [128, FC, 512], BF16) for i in range(2)]
        x = S.sb("fxt", [128, KC, 512])
        for t in range(NT):
            r = seq_r(t)
            ht = hts[t % 2]
            for c4 in range(0, FC, 6):
                c5 = min(FC, c4 + 6)
                S.dma("sp", ht[:, c4:c5, :], k.HT[c4:c5, :, t * 512:(t + 1) * 512].rearrange("c p n -> p c n"))
            S.dma("sp", x[:], k.XT[:, :, t * 512:(t + 1) * 512].rearrange("c p n -> p c n"))
            for mc in range(KC):
                p = k.ps[mc % 2]
                for kc in range(FC):
                    MM(k, p[:], lhsT=wd[:, kc, mc * 128:(mc + 1) * 128], rhs=ht[:, kc, :], start=(kc == 0), stop=(kc == FC - 1))
                V(k, "scalar_tensor_tensor", out=x[:, mc, :], in0=p[:], scalar=k.mod[:, l, 40 + mc, r:r + 1], in1=x[:, mc, :],
                  op0=ALU.mult, op1=ALU.add)
            S.dma("pool", k.XT[:, :, t * 512:(t + 1) * 512].rearrange("c p n -> p c n"), x[:])


def phase_final(k):
    S = k.S
    with S.scope():
        fr = S.sb("fr", [1, D])
        S.dma("sp", fr[:], k.fng[:])
        fb = S.sb("fb", [128, D])
        for j in range(2):
            MM(k, k.ps[j][:], lhsT=k.cst[0:1, 1, :], rhs=fr[:, j * 512:(j + 1) * 512])
            V(k, "tensor_copy", out=fb[:, j * 512:(j + 1) * 512], in_=k.ps[j][:])
        xb = [S.sb("yx%d" % i, [128, KC, 128]) for i in range(2)]
        xk = [S.sb("yk%d" % i, [128, D]) for i in range(2)]
        junk = S.sb("yjunk", [128, D])
        ss = S.sb("yss", [128, 2])
        yo = [S.sb("yo%d" % i, [128, D]) for i in range(2)]
        for b in range(NB):
            x = xb[b % 2]
            xt = xk[b % 2]
            S.dma("sp", x[:], k.XT[:, :, b * 128:(b + 1) * 128].rearrange("c p n -> p c n"))
            for half in range(2):
                p = k.ps[2 + (b % 2) * 2 + half]
                for j in range(4):
                    S.op("pe", "transpose", out=p[:, j * 128:(j + 1) * 128], in_=x[:, half * 4 + j, :], identity=k.C("ident"))
                if half == 0:
                    V(k, "tensor_copy", out=xt[:, 0:512], in_=p[:])
                else:
                    A(k, out=xt[:, 512:1024], in_=p[:], func=AF.Copy)
            k.S.op("dve", "memset", ss[:, 0:1], 0.0)
            A(k, out=junk[:], in_=xt[:], func=AF.Square, accum_out=ss[:, 0:1])
            A(k, out=ss[:, 1:2], in_=ss[:, 0:1], func=AF.Sqrt, bias=EPS, scale=1.0 / D)
            V(k, "reciprocal", out=ss[:, 1:2], in_=ss[:, 1:2])
            y = yo[b % 2]
            V(k, "scalar_tensor_tensor", out=y[:], in0=xt[:], scalar=ss[:, 1:2], in1=fb[:], op0=ALU.mult, op1=ALU.mult)
            if b < LS // 128:
                S.dma("pool", k.y_s[b * 128:(b + 1) * 128, :], y[:])
            else:
                pb = b - LS // 128
                S.dma("pool", k.y_p[pb * 128:(pb + 1) * 128, :], y[:])


SEQS = [dict(tok0=0, L=LS, sample=True, p=-1)] + [dict(tok0=LS + i * LP, L=LP, sample=False, p=i) for i in range(NPR)]


def phase_attn(k, l):
    S = k.S
    with S.scope():
        lt = S.sb("lamt", [128, 256])
        S.dma("sp", lt[:], k.lam_qk[l:l + 1, :].partition_broadcast(128))
        lp = S.sb("lamp", [128, 256])
        ls = S.sb("lams", [128, 4])
        V(k, "tensor_tensor", out=lp[:, 0:64], in0=lt[:, 0:64], in1=lt[:, 64:128], op=ALU.mult)
        V(k, "tensor_tensor", out=lp[:, 64:128], in0=lt[:, 128:192], in1=lt[:, 192:256], op=ALU.mult)
        V(k, "reduce_sum", out=ls[:, 0:1], in_=lp[:, 0:64], axis=AX.X)
        V(k, "reduce_sum", out=ls[:, 1:2], in_=lp[:, 64:128], axis=AX.X)
        A(k, out=ls[:, 0:2], in_=ls[:, 0:2], func=AF.Exp)
        V(k, "tensor_tensor", out=ls[:, 2:3], in0=ls[:, 1:2], in1=ls[:, 0:1], op=ALU.subtract)
        V(k, "tensor_scalar", out=ls[:, 3:4], in0=ls[:, 2:3], scalar1=-lam_init_of(l), scalar2=None, op0=ALU.add)
        nlam = ls[:, 3:4]
        sg = S.sb("sublg", [128, DEPTH])
        S.dma("sp", sg[:], k.subln_g[:])
        sgl = S.sb("sublgl", [128, 1])
        V(k, "tensor_scalar", out=sgl[:], in0=sg[:, l:l + 1], scalar1=1.0 - lam_init_of(l), scalar2=None, op0=ALU.mult)
        ckf = S.sb("ckf", [128, 512])
        ckb = S.sb("ckb", [128, 4, 128], BF16)
        cvf = S.sb("cvf", [128, 512])
        cvb = S.sb("cvb", [128, 512], BF16)
        for b in range(PAST // 128):
            S.dma("sp", ckf[:], k.cache_k[l, b * 128:(b + 1) * 128, :])
            for h in range(4):
                S.op("pe", "transpose", out=k.ps[7][:, h * 128:(h + 1) * 128], in_=ckf[:, h * 128:(h + 1) * 128], identity=k.C("ident"))
            V(k, "tensor_copy", out=ckb[:], in_=k.ps[7][:].rearrange("p (h n) -> p h n", n=128))
            S.dma("pool", k.KT[:, :, T + b * 128:T + (b + 1) * 128].rearrange("h p n -> p h n"), ckb[:])
            S.dma("sp", cvf[:], k.cache_v[l, b * 128:(b + 1) * 128, :])
            V(k, "tensor_copy", out=cvb[:], in_=cvf[:])
            S.dma("pool", k.VV[T + b * 128:T + (b + 1) * 128, :], cvb[:])
        ktb = S.sb("ktb", [128, LS + PAST], BF16)
        vsb = S.sb("vsb", [128, (LS + PAST) // 128, 128], BF16)
        qsb = [S.sb("qsb%d" % i, [128, LS], BF16) for i in range(2)]
        for i in range(2):
            S.op("dve", "memset", qsb[i][:], 0.0)
        ptb = [S.sb("ptb%d" % i, [128, 512], BF16) for i in range(6)]
        zacc = [S.sb("zacc%d" % i, [128, 512]) for i in range(4)]
        sbank = [k.ps[0], k.ps[1], k.ps[6], k.ps[7]]
        rz = S.sb("rz", [128, 512])
        a0 = S.sb("a0", [128, 512])
        a1 = S.sb("a1", [128, 512])
        sq = S.sb("asq", [128, 512])
        for sq_ in SEQS:
            tok0, L = sq_["tok0"], sq_["L"]
            QW = min(512, L)
            nkt_own = L // 128
            nkt = nkt_own + (PAST // 128 if sq_["sample"] else 0)
            for h in range(4):
                S.dma("sp", ktb[:, 0:L], k.KT[h, :, tok0:tok0 + L])
                for t4 in range(0, nkt_own, 4):
                    t5 = min(nkt_own, t4 + 4)
                    S.dma("sp", vsb[:, t4:t5, :], k.VV[tok0 + t4 * 128:tok0 + t5 * 128, h * 128:(h + 1) * 128].rearrange("(t p) e -> p t e", p=128))
                if sq_["sample"]:
                    S.dma("sp", ktb[:, L:L + PAST], k.KT[h, :, T:T + PAST])
                    S.dma("sp", vsb[:, nkt_own:nkt, :], k.VV[T:T + PAST, h * 128:(h + 1) * 128].rearrange("(t p) e -> p t e", p=128))
                for m in range(2):
                    S.dma("sp", qsb[m][m * 64:(m + 1) * 64, 0:L], k.QT[h, m * 64:(m + 1) * 64, tok0:tok0 + L])
                for qt in range(L // QW):
                    units = [(m, kt) for m in range(2) for kt in range(nkt)]
                    NS = len(sbank)
                    NU = len(units)

                    def qk_mm(i):
                        m, kt = units[i]
                        MM(k, sbank[i % NS][:, 0:QW], lhsT=ktb[:, kt * 128:(kt + 1) * 128], rhs=qsb[m][:, qt * QW:(qt + 1) * QW])

                    for i in range(min(NS, NU)):
                        qk_mm(i)
                    first_pv = [True, True]
                    zused = set()
                    npv = [0, 0]
                    for i0 in range(0, NU, 2):
                        grp = [i for i in (i0, i0 + 1) if i < NU]
                        for i in grp:
                            m, kt = units[i]
                            pt = ptb[i % len(ptb)]
                            A(k, out=pt[:, 0:QW], in_=sbank[i % NS][:, 0:QW], func=AF.Exp, scale=0.125)
                            par = kt % 3
                            if par == 0:
                                MM(k, k.ps[4 + m][:, 0:QW], lhsT=k.Cb("ones"), rhs=pt[:, 0:QW], start=(kt == 0), stop=False)
                            else:
                                eng = "dve" if par == 1 else "pool"
                                za = zacc[m * 2 + par - 1]
                                if kt < 3:
                                    S.op(eng, "tensor_copy", out=za[:, 0:QW], in_=pt[:, 0:QW])
                                    zused.add(m * 2 + par - 1)
                                else:
                                    S.op(eng, "tensor_tensor", out=za[:, 0:QW], in0=za[:, 0:QW], in1=pt[:, 0:QW], op=ALU.add)
                        for i in reversed(grp):
                            m, kt = units[i]
                            pt = ptb[i % len(ptb)]
                            npv[m] += 1
                            MM(k, k.ps[2 + m][:, 0:QW], lhsT=vsb[:, kt, :], rhs=pt[:, 0:QW], start=first_pv[m], stop=(npv[m] == nkt))
                            first_pv[m] = False
                        for i in grp:
                            if i + NS < NU:
                                qk_mm(i + NS)
                    for m in range(2):
                        zl = [z for z in (m * 2, m * 2 + 1) if z in zused]
                        for zi, z in enumerate(zl):
                            MM(k, k.ps[4 + m][:, 0:QW], lhsT=k.C("ones"), rhs=zacc[z][:, 0:QW], start=False, stop=(zi == len(zl) - 1))
                    V(k, "reciprocal", out=rz[:, 0:QW], in_=k.ps[4][:, 0:QW])
                    V(k, "tensor_tensor", out=a0[:, 0:QW], in0=k.ps[2][:, 0:QW], in1=rz[:, 0:QW], op=ALU.mult)
                    V(k, "reciprocal", out=rz[:, 0:QW], in_=k.ps[5][:, 0:QW])
                    V(k, "tensor_tensor", out=a1[:, 0:QW], in0=k.ps[3][:, 0:QW], in1=rz[:, 0:QW], op=ALU.mult)
                    V(k, "scalar_tensor_tensor", out=a0[:, 0:QW], in0=a1[:, 0:QW], scalar=nlam, in1=a0[:, 0:QW], op0=ALU.mult, op1=ALU.add)
                    G(k, "tensor_tensor", out=sq[:, 0:QW], in0=a0[:, 0:QW], in1=a0[:, 0:QW], op=ALU.mult)
                    MM(k, k.ps[6][:, 0:QW], lhsT=k.C("ones"), rhs=sq[:, 0:QW])
                    A(k, out=sq[:, 0:QW], in_=k.ps[6][:, 0:QW], func=AF.Sqrt, bias=EPS, scale=1.0 / 128)
                    V(k, "reciprocal", out=rz[:, 0:QW], in_=sq[:, 0:QW])
                    V(k, "scalar_tensor_tensor", out=k.BIG[:, h, tok0 + qt * QW:tok0 + (qt + 1) * QW], in0=a0[:, 0:QW], scalar=sgl[:, 0:1],
                      in1=rz[:, 0:QW], op0=ALU.mult, op1=ALU.mult)


def run_streams(gens):
    gens = list(gens)
    while gens:
        for g in list(gens):
            try:
                next(g)
            except StopIteration:
                gens.remove(g)


class NS_:
    pass


DT_REC = F32


def phase_mlstm(k, l):
    S = k.S
    LN8 = math.log(0.125)
    I64 = k.cst[0:64, CONST_ORDER["ident"], 0:64]
    O64 = k.cst[0:64, CONST_ORDER["ones"], 0:64]

    def bc3(ap2, n):
        return ap2.unsqueeze(2).to_broadcast([ap2.shape[0], ap2.shape[1], n])

    def bcm(ap2, n):
        return ap2.unsqueeze(1).to_broadcast([ap2.shape[0], n, ap2.shape[1]])

    def r3(ap):
        return ap.rearrange("p (a n) -> p a n", n=64)

    def alloc(tag):
        t = NS_()
        for nm in ("diag", "X", "ET", "ebq", "qb", "sT", "kw"):
            setattr(t, nm, S.sb("m%s%s" % (nm, tag), [64, 8, 64]))
        for nm in ("nlf", "tg", "nbs", "t4", "t4b", "colE", "wk", "dec"):
            setattr(t, nm, S.sb("m%s%s" % (nm, tag), [64, 8]))
        t.gt2 = S.sb("mgt2" + tag, [64, 2, 32])
        t.ckv = S.sb("mckv" + tag, [64, 2, 512])
        t.og = S.sb("mog" + tag, [64, 512])
        t.qk = S.sb("mcqk" + tag, [64, 4, 2, 128])
        t.v1 = S.sb("mv1" + tag, [64, 8, 128])
        S.op("dve", "memset", t.v1[:], 0.0)
        S.op("dve", "memset", t.v1[:, :, 64:65], 1.0)
        t.state = S.sb("mstate" + tag, [64, 4, 128])
        for nm in ("hm", "omf", "tt", "sig"):
            setattr(t, nm, S.sb("m%s%s" % (nm, tag), [64, 256]))
        t.den = S.sb("mden" + tag, [64, 4])
        t.ss4 = S.sb("mss4" + tag, [64, 4])
        t.junk = S.sb("mjunk" + tag, [64, 64])
        t.em0 = S.sb("mem0" + tag, [64, 4])
        t.n0r = S.sb("mn0r" + tag, [4, 64])
        t.kvs = S.sb("mkvs" + tag, [4, 256])
        t.nls = S.sb("mnls" + tag, [4, 256])
        t.nblc = S.sb("mnblc" + tag, [4, 4])
        t.off = S.sb("moff" + tag, [4, 4])
        t.mx = S.sb("mmx" + tag, [4, 2])
        t.mfin = S.sb("mmfin" + tag, [4, 1])
        t.mrow = S.sb("mmrow" + tag, [1, 4])
        t.d4 = S.sb("md4" + tag, [4, 4])
        t.emf = S.sb("memf" + tag, [64, 4])
        t.so = S.sb("mso" + tag, [64, 4, 64])
        t.ncol = S.sb("mncol" + tag, [64, 4])
        t.nrow = S.sb("mnrow" + tag, [4, 64])
        return t

    def stream(sq_, dr, t, B, fbb, mgb):
        tok0, L = sq_["tok0"], sq_["L"]
        nblk = L // 128
        ci = CONST_ORDER["m_le"] if dr == 0 else CONST_ORDER["m_ge"]
        cum64 = k.cst[0:64, ci, 0:64]
        OWN, OTH = (k.OM, k.OM2) if dr == 0 else (k.OM2, k.OM)
        state = t.state
        S.op("dve", "memset", state[:], 0.0)
        if sq_["sample"]:
            S.dma("sp", t.em0[:], k.st_m[l, dr:dr + 1, :].partition_broadcast(64))
            A(k, out=t.em0[:], in_=t.em0[:], func=AF.Exp)
            S.dma("sp", state[:, :, 0:64], k.st_c[l, dr, :, :, :].rearrange("h a b -> a h b"))
            S.dma("sp", t.n0r[:], k.st_n[l, dr, :, :])
            S.op("pe", "transpose", out=B[0][0:64, 32:36], in_=t.n0r[:], identity=k.cst[0:4, CONST_ORDER["ident"], 0:4])
            V(k, "tensor_copy", out=state[:, :, 64], in_=B[0][0:64, 32:36])
            V(k, "tensor_tensor", out=state[:], in0=state[:], in1=bc3(t.em0[:], 128), op=ALU.mult)
        blocks = list(range(nblk)) if dr == 0 else list(range(nblk - 1, -1, -1))
        corder = (0, 1) if dr == 0 else (1, 0)
        for step, bi in enumerate(blocks):
            r0 = tok0 + bi * 128
            for c in range(2):
                S.dma("sp", t.gt2[:, c, :], k.TMS[r0 + c * 64:r0 + (c + 1) * 64, TM_OFF["gt"]:TM_OFF["gt"] + 32])
                S.dma("sp", t.ckv[:, c, :], k.TMS[r0 + c * 64:r0 + (c + 1) * 64, TM_OFF["ckv"]:TM_OFF["ckv"] + 512])
            S.dma("sp", t.qk[:], k.CQK[:, :, r0:r0 + 128].rearrange("c (hh p) n -> p c hh n", p=64))
            yield
            kv4 = t.ckv[:].rearrange("p c (x h e) -> p c x h e", x=2, e=64)
            G(k, "tensor_copy", out=t.v1[:, :, 0:64].rearrange("p (c h) e -> p c h e", c=2), in_=kv4[:, :, 1, :, :])
            t3 = t.tg[:].rearrange("p (c h) -> p c h", c=2)
            V(k, "tensor_tensor", out=t3, in0=t.gt2[:, :, 24 + dr * 4:28 + dr * 4], in1=bcm(fbb[:, dr * 4:dr * 4 + 4], 2), op=ALU.add)
            A(k, out=t.tg[:], in_=t.tg[:], func=AF.Exp, scale=-1.0)
            A(k, out=t.nlf[:], in_=t.tg[:], func=AF.Ln, bias=1.0, scale=1.0)
            yield
            pa = B[0]
            for c in range(2):
                MM(k, pa[0:64, c * 4:(c + 1) * 4], lhsT=cum64, rhs=t.nlf[:, c * 4:(c + 1) * 4])
                MM(k, pa[0:64, 8 + c * 4:12 + c * 4], lhsT=O64, rhs=t.nlf[:, c * 4:(c + 1) * 4])
            yield
            V(k, "tensor_copy", out=t.nbs[:], in_=pa[0:64, 0:8])
            V(k, "tensor_tensor", out=t.t4[:], in0=t.nbs[:], in1=pa[0:64, 8:16], op=ALU.subtract)
            ig3 = t.gt2[:, :, 16 + dr * 4:20 + dr * 4]
            V(k, "tensor_tensor", out=t.t4b[:].rearrange("p (c h) -> p c h", c=2), in0=t.t4[:].rearrange("p (c h) -> p c h", c=2), in1=ig3, op=ALU.add)
            A(k, out=t.wk[:], in_=t.t4b[:], func=AF.Exp, bias=LN8, scale=1.0)
            V(k, "tensor_tensor", out=t.colE[:].rearrange("p (c h) -> p c h", c=2), in0=t.nbs[:].rearrange("p (c h) -> p c h", c=2), in1=ig3, op=ALU.add)
            V(k, "tensor_copy", out=t.dec[:], in_=pa[0:64, 8:16])
            A(k, out=t.dec[:], in_=t.dec[:], func=AF.Exp, scale=-1.0)
            V(k, "tensor_tensor", out=t.diag[:], in0=bcm(I64, 8), in1=bc3(t.nbs[:], 64), op=ALU.mult)
            yield
            if not sq_["sample"]:
                for c in range(2):
                    cc = bi * 2 + c
                    S.op("pe", "transpose", out=B[0][0:4, 256:320], in_=t.t4b[:, c * 4:(c + 1) * 4], identity=I64)
                    S.op("pe", "transpose", out=B[0][0:4, 384:448], in_=t.nlf[:, c * 4:(c + 1) * 4], identity=I64)
                    V(k, "tensor_copy", out=t.kvs[:, cc * 64:(cc + 1) * 64], in_=B[0][0:4, 256:320])
                    V(k, "tensor_copy", out=t.nls[:, cc * 64:(cc + 1) * 64], in_=B[0][0:4, 384:448])
            pb, pc = B[1], B[2]
            for pi in range(8):
                MM(k, pb[0:64, pi * 64:(pi + 1) * 64], lhsT=O64, rhs=t.diag[:, pi, :])
            for c in range(2):
                for h in range(4):
                    pi = c * 4 + h
                    MM(k, pc[0:64, pi * 64:(pi + 1) * 64], lhsT=t.qk[:, 2 + h // 2, h % 2, c * 64:(c + 1) * 64],
                       rhs=t.qk[:, h // 2, h % 2, c * 64:(c + 1) * 64])
            yield
            pb3 = r3(pb[0:64, :])
            V(k, "tensor_tensor", out=t.X[:], in0=bc3(t.colE[:], 64), in1=pb3, op=ALU.subtract)
            A(k, out=t.ebq[:], in_=pb3, func=AF.Exp, scale=-1.0)
            yield
            A(k, out=t.ET[:], in_=t.X[:], func=AF.Exp)
            q4 = t.qk[:, 0:2, :, :].rearrange("p a hh (c n) -> p c (a hh) n", c=2)
            G(k, "tensor_tensor", out=t.qb[:].rearrange("p (c h) n -> p c h n", c=2), in0=q4,
              in1=t.ebq[:].rearrange("p (c h) n -> p c h n", c=2), op=ALU.mult)
            V(k, "tensor_tensor", out=t.kw[:].rearrange("p (c h) e -> p c h e", c=2), in0=kv4[:, :, 0, :, :],
              in1=bc3(t.wk[:], 64).rearrange("p (c h) e -> p c h e", c=2), op=ALU.mult)
            yield
            G(k, "tensor_tensor", out=t.ET[:], in0=t.ET[:], in1=bcm(cum64, 8), op=ALU.mult)
            yield
            V(k, "scalar_tensor_tensor", out=t.sT[:], in0=r3(pc[0:64, :]), scalar=0.125, in1=t.ET[:], op0=ALU.mult, op1=ALU.mult)
            yield
            for c in corder:
                po, pst = B[3], B[1]
                for h in range(4):
                    pi = c * 4 + h
                    for (a0_, a1_) in ((0, 64), (64, 128)):
                        MM(k, po[0:64, h * 128 + a0_:h * 128 + a1_], lhsT=t.qb[:, pi, :], rhs=state[:, h, a0_:a1_], start=True, stop=False)
                        MM(k, po[0:64, h * 128 + a0_:h * 128 + a1_], lhsT=t.sT[:, pi, :], rhs=t.v1[:, pi, a0_:a1_], start=False, stop=True)
                    MM(k, pst[0:64, h * 128:(h + 1) * 128], lhsT=t.kw[:, pi, :], rhs=t.v1[:, pi, :])
                yield
                V(k, "tensor_tensor", out=state[:], in0=state[:], in1=bc3(t.dec[:, c * 4:(c + 1) * 4], 128), op=ALU.mult)
                V(k, "tensor_tensor", out=state[:], in0=state[:], in1=pst[0:64, :].rearrange("p (h e) -> p h e", e=128), op=ALU.add)
                rc = r0 + c * 64
                po3 = po[0:64, :].rearrange("p (h e) -> p h e", e=128)
                A(k, out=t.den[:], in_=po3[:, :, 64], func=AF.Abs)
                yield
                V(k, "tensor_scalar", out=t.den[:], in0=t.den[:], scalar1=1.0, scalar2=None, op0=ALU.max)
                V(k, "reciprocal", out=t.den[:], in_=t.den[:])
                V(k, "tensor_tensor", out=t.hm[:].rearrange("p (h e) -> p h e", e=64), in0=po3[:, :, 0:64], in1=bc3(t.den[:], 64), op=ALU.mult)
                if step < nblk // 2:
                    S.dma("pool", OWN[rc:rc + 64, :], t.hm[:])
                    yield
                else:
                    S.dma("sp", t.omf[:], OTH[rc:rc + 64, :])
                    S.dma("sp", t.og[:], k.TMS[rc:rc + 64, TM_OFF["og"]:TM_OFF["og"] + 512])
                    yield
                    V(k, "tensor_tensor", out=t.hm[:], in0=t.hm[:], in1=t.omf[:], op=ALU.add)
                    S.op("dve", "memset", t.ss4[:], 0.0)
                    for h in range(4):
                        A(k, out=t.junk[:], in_=t.hm[:, h * 64:(h + 1) * 64], func=AF.Square, accum_out=t.ss4[:, h:h + 1])
                    A(k, out=t.ss4[:], in_=t.ss4[:], func=AF.Sqrt, bias=EPS, scale=1.0 / 64)
                    V(k, "reciprocal", out=t.ss4[:], in_=t.ss4[:])
                    yield
                    for h in range(4):
                        V(k, "scalar_tensor_tensor", out=t.tt[:, h * 64:(h + 1) * 64], in0=t.hm[:, h * 64:(h + 1) * 64],
                          scalar=t.ss4[:, h:h + 1], in1=mgb[:], op0=ALU.mult, op1=ALU.mult)
                    A(k, out=t.sig[:], in_=t.og[:, 256:512], func=AF.Sigmoid)
                    V(k, "tensor_tensor", out=t.tt[:], in0=t.tt[:], in1=t.sig[:], op=ALU.mult)
                    yield
                    for j in range(2):
                        S.op("pe", "transpose", out=B[2][:, j * 64:(j + 1) * 64], in_=t.tt[:, j * 128:(j + 1) * 128], identity=I64)
                    yield
                    V(k, "tensor_copy", out=k.BIG[:, 6:8, rc:rc + 64], in_=B[2][:, 0:128].rearrange("p (j n) -> p j n", n=64))
        if not sq_["sample"]:
            p = sq_["p"]
            V(k, "reduce_sum", out=t.nblc[:], in_=t.nls[:].rearrange("p (c s) -> p c s", s=64), axis=AX.X)
            nch = L // 64
            S.op("dve", "memset", t.off[:], 0.0)
            if dr == 0:
                for c in range(nch - 2, -1, -1):
                    V(k, "tensor_tensor", out=t.off[:, c:c + 1], in0=t.off[:, c + 1:c + 2], in1=t.nblc[:, c + 1:c + 2], op=ALU.subtract)
            else:
                for c in range(1, nch):
                    V(k, "tensor_tensor", out=t.off[:, c:c + 1], in0=t.off[:, c - 1:c], in1=t.nblc[:, c - 1:c], op=ALU.subtract)
            V(k, "tensor_tensor", out=t.kvs[:].rearrange("p (c s) -> p c s", s=64), in0=t.kvs[:].rearrange("p (c s) -> p c s", s=64),
              in1=bc3(t.off[:], 64), op=ALU.add)
            V(k, "reduce_max", out=t.mx[:, 0:1], in_=t.kvs[:], axis=AX.X)
            V(k, "reduce_sum", out=t.mx[:, 1:2], in_=t.nblc[:], axis=AX.X)
            V(k, "scalar_tensor_tensor", out=t.mfin[:], in0=t.mx[:, 1:2], scalar=-1.0, in1=t.mx[:, 0:1], op0=ALU.mult, op1=ALU.max)
            yield
            S.op("pe", "transpose", out=B[0][0:1, 40:44], in_=t.mfin[:], identity=k.cst[0:4, CONST_ORDER["ident"], 0:4])
            V(k, "tensor_copy", out=t.mrow[:], in_=B[0][0:1, 40:44])
            S.dma("pool", k.o_m[p, l, dr:dr + 1, :], t.mrow[:])
            V(k, "tensor_scalar", out=t.d4[:], in0=k.cst[0:4, CONST_ORDER["ident"], 0:4], scalar1=t.mfin[:, 0:1], scalar2=None, op0=ALU.mult)
            MM(k, B[0][0:64, 16:20], lhsT=k.cst[0:4, CONST_ORDER["ones"], 0:64], rhs=t.d4[:])
            yield
            V(k, "tensor_copy", out=t.emf[:], in_=B[0][0:64, 16:20])
            A(k, out=t.emf[:], in_=t.emf[:], func=AF.Exp, scale=-1.0)
            V(k, "tensor_tensor", out=t.so[:], in0=state[:, :, 0:64], in1=bc3(t.emf[:], 64), op=ALU.mult)
            S.dma("pool", k.o_C[p, l, dr, :, :, :].rearrange("h a b -> a h b"), t.so[:])
            V(k, "tensor_tensor", out=t.ncol[:], in0=state[:, :, 64], in1=t.emf[:], op=ALU.mult)
            S.op("pe", "transpose", out=B[0][0:4, 64:128], in_=t.ncol[:], identity=I64)
            yield
            V(k, "tensor_copy", out=t.nrow[:], in_=B[0][0:4, 64:128])
            S.dma("pool", k.o_n[p, l, dr, :, :], t.nrow[:])

    with S.scope():
        fbb = S.sb("fbb", [64, 8])
        S.dma("sp", fbb[:], k.f_b[l:l + 1, :].partition_broadcast(64))
        mgb = S.sb("mgb", [64, 64])
        S.dma("sp", mgb[:], k.mn_g[l:l + 1, :].partition_broadcast(64))
        tiles = [alloc("f"), alloc("b")]
        for sq_ in SEQS:
            run_streams([stream(sq_, dr, tiles[dr], k.ps[dr * 4:dr * 4 + 4], fbb, mgb) for dr in range(2)])


def phase_delta(k, l):
    phase_delta_prep(k, l)
    phase_delta_scan(k, l)


def phase_delta_prep(k, l):
    S = k.S
    with S.scope():
        cw = S.sb("cw", [128, DEPTH, 6, 5])
        S.dma("sp", cw[:], k.conv_w[:])
        xin = S.sb("dxin", [128, 6, 516])
        acc = S.sb("dacc", [128, 6, 512])
        sact = S.sb("dsact", [128, 6, 512])
        sqb = S.sb("dsq", [128, 512])
        rin = S.sb("drin", [128, 512])
        tok = S.sb("dtok", [128, 512])
        for sq_ in SEQS:
            tok0, L = sq_["tok0"], sq_["L"]
            W = min(512, L)
            for ti in range(L // W):
                t0 = tok0 + ti * W
                lo, hi = max(tok0, t0 - 2), min(tok0 + L, t0 + W + 2)
                k.S.op("dve", "memset", xin[:], 0.0)
                S.dma("sp", xin[:, :, lo - (t0 - 2):hi - (t0 - 2)], k.BT[:, :, lo:hi].rearrange("c p n -> p c n"))
                for c in range(6):
                    eng = "dve"
                    S.op(eng, "tensor_scalar", out=acc[:, c, 0:W], in0=xin[:, c, 0:W], scalar1=cw[:, l, c, 0:1], scalar2=None, op0=ALU.mult)
                    for j in range(1, 5):
                        S.op(eng, "scalar_tensor_tensor", out=acc[:, c, 0:W], in0=xin[:, c, j:j + W], scalar=cw[:, l, c, j:j + 1],
                             in1=acc[:, c, 0:W], op0=ALU.mult, op1=ALU.add)
                for c in range(6):
                    A(k, out=sact[:, c, 0:W], in_=acc[:, c, 0:W], func=AF.Silu)
                for c in range(4):
                    G(k, "tensor_tensor", out=sqb[:, 0:W], in0=sact[:, c, 0:W], in1=sact[:, c, 0:W], op=ALU.mult)
                    p = k.ps[c % 2]
                    MM(k, p[:, 0:W], lhsT=k.C("bd"), rhs=sqb[:, 0:W])
                    A(k, out=rin[:, 0:W], in_=p[:, 0:W], func=AF.Sqrt, bias=EPS, scale=1.0)
                    V(k, "reciprocal", out=rin[:, 0:W], in_=rin[:, 0:W])
                    V(k, "scalar_tensor_tensor", out=sact[:, c, 0:W], in0=sact[:, c, 0:W], scalar=(0.125 if c < 2 else 1.0), in1=rin[:, 0:W],
                      op0=ALU.mult, op1=ALU.mult)
                    h0 = (c // 2) * 4 + (c % 2) * 2
                    S.dma("pool", k.DQK[h0:h0 + 2, :, t0:t0 + W].rearrange("h p n -> (h p) n"), sact[:, c, 0:W])
                for b in range(W // 128):
                    p = k.ps[2 + b % 2]
                    for j, c in enumerate((2, 3, 4, 5)):
                        S.op("pe", "transpose", out=p[:, j * 128:(j + 1) * 128], in_=sact[:, c, b * 128:(b + 1) * 128], identity=k.C("ident"))
                    V(k, "tensor_copy", out=tok[:], in_=p[:])
                    S.dma("pool", k.DKV[t0 + b * 128:t0 + (b + 1) * 128, :], tok[:])


def phase_delta_scan(k, l):
    S = k.S
    I64 = k.cst[0:64, CONST_ORDER["ident"], 0:64]
    O64 = k.cst[0:64, CONST_ORDER["ones"], 0:64]

    def bc3(ap2, n):
        return ap2.unsqueeze(2).to_broadcast([ap2.shape[0], ap2.shape[1], n])

    def bcm(ap2, n):
        return ap2.unsqueeze(1).to_broadcast([ap2.shape[0], n, ap2.shape[1]])

    def r3(ap):
        return ap.rearrange("p (a n) -> p a n", n=64)

    def alloc(tag):
        t = NS_()
        for nm in ("diag", "X", "N", "TT", "u", "ebq"):
            setattr(t, nm, S.sb("d%s%s" % (nm, tag), [64, 8, 64]))
        for nm in ("P0", "P1", "PT0", "PT1", "TTb", "aT", "vb", "kbg", "kdec", "wT", "qd"):
            setattr(t, nm, S.sb("d%s%s" % (nm, tag), [64, 8, 64], DT_REC))
        t.qkb = S.sb("dqkb" + tag, [64, 8, 128], DT_REC)
        t.S4b = S.sb("dS4b" + tag, [64, 4, 64], DT_REC)
        for nm in ("beta", "nbeta", "ng", "tg", "ngc", "egc", "bgs", "kds", "gl"):
            setattr(t, nm, S.sb("d%s%s" % (nm, tag), [64, 8]))
        t.gt2 = S.sb("dgt2" + tag, [64, 2, 32])
        t.dkv = S.sb("ddkv" + tag, [64, 2, 512])
        t.qk = S.sb("dqk" + tag, [64, 8, 128])
        t.vnew = S.sb("dvnew" + tag, [64, 4, 64], DT_REC)
        t.S4 = S.sb("dS4" + tag, [64, 4, 64])
        t.oc = S.sb("doc" + tag, [64, 256])
        t.of = S.sb("dof" + tag, [64, 256])
        t.og = S.sb("dog" + tag, [64, 512])
        t.ss4 = S.sb("dss4" + tag, [64, 4])
        t.junk = S.sb("djunk" + tag, [64, 64])
        t.tt = S.sb("dtt" + tag, [64, 256])
        t.sig = S.sb("dsig" + tag, [64, 256])
        if DT_REC == F32:
            t.qkb, t.P0, t.TTb, t.S4b = t.qk, t.N, t.TT, t.S4
        return t

    def stream(sq_, dr, t, B, alb, dtb, dgb):
        tok0, L = sq_["tok0"], sq_["L"]
        nblk = L // 128
        ci = CONST_ORDER["m_le"] if dr == 0 else CONST_ORDER["m_ge"]
        si = CONST_ORDER["m_ig"] if dr == 0 else CONST_ORDER["m_il"]
        cum64 = k.cst[0:64, ci, 0:64]
        str64 = k.cst[0:64, si, 0:64]
        OWN, OTH = (k.OD, k.OD2) if dr == 0 else (k.OD2, k.OD)
        if sq_["sample"]:
            S.dma("sp", t.S4[:], k.st_d[l, dr, :, :, :].rearrange("h a b -> a h b"))
        else:
            S.op("dve", "memset", t.S4[:], 0.0)
        if DT_REC != F32:
            V(k, "tensor_copy", out=t.S4b[:], in_=t.S4[:])
        blocks = list(range(nblk)) if dr == 0 else list(range(nblk - 1, -1, -1))
        corder = (0, 1) if dr == 0 else (1, 0)
        for step, bi in enumerate(blocks):
            r0 = tok0 + bi * 128
            for c in range(2):
                S.dma("sp", t.gt2[:, c, :], k.TMS[r0 + c * 64:r0 + (c + 1) * 64, TM_OFF["gt"]:TM_OFF["gt"] + 32])
                S.dma("sp", t.dkv[:, c, :], k.DKV[r0 + c * 64:r0 + (c + 1) * 64, :])
            S.dma("sp", t.qk[:], k.DQK[:, :, r0:r0 + 128].rearrange("h p n -> p h n"))
            yield
            if DT_REC != F32:
                G(k, "tensor_copy", out=t.qkb[:], in_=t.qk[:])
            b3 = t.beta[:].rearrange("p (c h) -> p c h", c=2)
            A(k, out=b3, in_=t.gt2[:, :, 8 + dr * 4:12 + dr * 4], func=AF.Sigmoid)
            V(k, "tensor_scalar", out=t.nbeta[:], in0=t.beta[:], scalar1=-1.0, scalar2=None, op0=ALU.mult)
            t3 = t.tg[:].rearrange("p (c h) -> p c h", c=2)
            V(k, "tensor_tensor", out=t3, in0=t.gt2[:, :, dr * 4:dr * 4 + 4], in1=bcm(dtb[:, dr * 4:dr * 4 + 4], 2), op=ALU.add)
            A(k, out=t.tg[:], in_=t.tg[:], func=AF.Exp)
            A(k, out=t.tg[:], in_=t.tg[:], func=AF.Ln, bias=1.0, scale=1.0)
            V(k, "tensor_tensor", out=t.ng[:].rearrange("p (c h) -> p c h", c=2), in0=t3, in1=bcm(alb[:, dr * 4:dr * 4 + 4], 2), op=ALU.mult)
            yield
            pa = B[0]
            for c in range(2):
                MM(k, pa[0:64, c * 4:(c + 1) * 4], lhsT=cum64, rhs=t.ng[:, c * 4:(c + 1) * 4])
                MM(k, pa[0:64, 8 + c * 4:12 + c * 4], lhsT=O64, rhs=t.ng[:, c * 4:(c + 1) * 4])
            yield
            V(k, "tensor_copy", out=t.ngc[:], in_=pa[0:64, 0:8])
            A(k, out=t.egc[:], in_=t.ngc[:], func=AF.Exp, scale=-1.0)
            V(k, "tensor_tensor", out=t.bgs[:], in0=t.beta[:], in1=t.egc[:], op=ALU.mult)
            V(k, "tensor_tensor", out=t.kds[:], in0=t.ngc[:], in1=pa[0:64, 8:16], op=ALU.subtract)
            A(k, out=t.kds[:], in_=t.kds[:], func=AF.Exp)
            V(k, "tensor_copy", out=t.gl[:], in_=pa[0:64, 8:16])
            A(k, out=t.gl[:], in_=t.gl[:], func=AF.Exp, scale=-1.0)
            V(k, "tensor_tensor", out=t.diag[:], in0=bcm(I64, 8), in1=bc3(t.ngc[:], 64), op=ALU.mult)
            yield
            pb, pc, pd = B[1], B[2], B[3]
            for pi in range(8):
                MM(k, pb[0:64, pi * 64:(pi + 1) * 64], lhsT=O64, rhs=t.diag[:, pi, :])
            for c in range(2):
                for h in range(4):
                    pi = c * 4 + h
                    kTc = t.qkb[:, 4 + h, c * 64:(c + 1) * 64]
                    qTc = t.qkb[:, h, c * 64:(c + 1) * 64]
                    MM(k, pc[0:64, pi * 64:(pi + 1) * 64], lhsT=kTc, rhs=kTc)
                    MM(k, pd[0:64, pi * 64:(pi + 1) * 64], lhsT=kTc, rhs=qTc)
            yield
            pb3 = r3(pb[0:64, :])
            V(k, "tensor_tensor", out=t.X[:], in0=pb3, in1=bc3(t.ngc[:], 64), op=ALU.subtract)
            A(k, out=t.ebq[:], in_=pb3, func=AF.Exp, scale=-1.0)
            V(k, "tensor_scalar", out=t.diag[:], in0=t.X[:], scalar1=0.0, scalar2=None, op0=ALU.min)
            G(k, "tensor_scalar", out=t.X[:], in0=t.X[:], scalar1=0.0, scalar2=None, op0=ALU.max)
            yield
            A(k, out=t.diag[:], in_=t.diag[:], func=AF.Exp)
            A(k, out=t.X[:], in_=t.X[:], func=AF.Exp, scale=-1.0)
            G(k, "tensor_tensor", out=t.diag[:], in0=t.diag[:], in1=bcm(str64, 8), op=ALU.mult)
            G(k, "tensor_tensor", out=t.X[:], in0=t.X[:], in1=bcm(cum64, 8), op=ALU.mult)
            yield
            V(k, "tensor_tensor", out=t.N[:], in0=r3(pc[0:64, :]), in1=bc3(t.nbeta[:], 64), op=ALU.mult)
            V(k, "tensor_tensor", out=t.N[:], in0=t.N[:], in1=t.diag[:], op=ALU.mult)
            if DT_REC != F32:
                A(k, out=t.P0[:], in_=t.N[:], func=AF.Copy)
            V(k, "tensor_tensor", out=t.aT[:], in0=r3(pd[0:64, :]), in1=t.X[:], op=ALU.mult)
            yield
            pt = B[1]
            for pi in range(8):
                S.op("pe", "transpose", out=pt[0:64, pi * 64:(pi + 1) * 64], in_=t.N[:, pi, :], identity=I64)
            yield
            A(k, out=t.PT0[:], in_=r3(pt[0:64, :]), func=AF.Copy)
            V(k, "tensor_tensor", out=t.TT[:], in0=r3(pt[0:64, :]), in1=bcm(I64, 8), op=ALU.add)
            if DT_REC != F32:
                if DT_REC != F32:
                    A(k, out=t.TTb[:], in_=t.TT[:], func=AF.Copy)
            yield
            Pb, PTb = (t.P0, t.P1), (t.PT0, t.PT1)
            for stp in range(5):
                P, PT = Pb[stp % 2], PTb[stp % 2]
                Pn, PTn = Pb[(stp + 1) % 2], PTb[(stp + 1) % 2]
                p1, p2, p3 = B[2], B[3], B[1]
                for pi in range(8):
                    MM(k, p1[0:64, pi * 64:(pi + 1) * 64], lhsT=PT[:, pi, :], rhs=P[:, pi, :])
                if stp < 4:
                    for pi in range(8):
                        MM(k, p2[0:64, pi * 64:(pi + 1) * 64], lhsT=P[:, pi, :], rhs=PT[:, pi, :])
                yield
                A(k, out=Pn[:], in_=r3(p1[0:64, :]), func=AF.Copy)
                if stp < 4:
                    V(k, "tensor_copy", out=PTn[:], in_=r3(p2[0:64, :]))
                yield
                for pi in range(8):
                    MM(k, p3[0:64, pi * 64:(pi + 1) * 64], lhsT=Pn[:, pi, :], rhs=t.TTb[:, pi, :])
                yield
                V(k, "tensor_tensor", out=t.TT[:], in0=t.TT[:], in1=r3(p3[0:64, :]), op=ALU.add)
                if DT_REC != F32:
                    A(k, out=t.TTb[:], in_=t.TT[:], func=AF.Copy)
                yield
            kv4 = t.dkv[:].rearrange("p c (x h e) -> p c x h e", x=2, e=64)
            V(k, "tensor_tensor", out=t.vb[:].rearrange("p (c h) e -> p c h e", c=2), in0=kv4[:, :, 1, :, :],
              in1=bc3(t.beta[:], 64).rearrange("p (c h) e -> p c h e", c=2), op=ALU.mult)
            V(k, "tensor_tensor", out=t.kbg[:].rearrange("p (c h) e -> p c h e", c=2), in0=kv4[:, :, 0, :, :],
              in1=bc3(t.bgs[:], 64).rearrange("p (c h) e -> p c h e", c=2), op=ALU.mult)
            G(k, "tensor_tensor", out=t.kdec[:].rearrange("p (c h) e -> p c h e", c=2), in0=kv4[:, :, 0, :, :],
              in1=bc3(t.kds[:], 64).rearrange("p (c h) e -> p c h e", c=2), op=ALU.mult)
            G(k, "tensor_tensor", out=t.qd[:].rearrange("p (c h) n -> p c h n", c=2),
              in0=t.qk[:, 0:4, :].rearrange("p h (c n) -> p c h n", c=2), in1=t.ebq[:].rearrange("p (c h) n -> p c h n", c=2), op=ALU.mult)
            yield
            p1, p2 = B[2], B[3]
            for pi in range(8):
                MM(k, p1[0:64, pi * 64:(pi + 1) * 64], lhsT=t.TTb[:, pi, :], rhs=t.vb[:, pi, :])
                MM(k, p2[0:64, pi * 64:(pi + 1) * 64], lhsT=t.kbg[:, pi, :], rhs=t.TTb[:, pi, :])
            yield
            A(k, out=t.u[:], in_=r3(p1[0:64, :]), func=AF.Copy)
            V(k, "tensor_copy", out=t.wT[:], in_=r3(p2[0:64, :]))
            yield
            for c in corder:
                px, py, pz = B[1], B[2], B[3]
                for h in range(4):
                    MM(k, px[0:64, h * 64:(h + 1) * 64], lhsT=t.wT[:, c * 4 + h, :], rhs=t.S4b[:, h, :])
                yield
                V(k, "tensor_tensor", out=t.vnew[:], in0=t.u[:, c * 4:(c + 1) * 4, :], in1=r3(px[0:64, 0:256]), op=ALU.subtract)
                yield
                for h in range(4):
                    MM(k, py[0:64, h * 64:(h + 1) * 64], lhsT=t.qd[:, c * 4 + h, :], rhs=t.S4b[:, h, :], start=True, stop=False)
                    MM(k, py[0:64, h * 64:(h + 1) * 64], lhsT=t.aT[:, c * 4 + h, :], rhs=t.vnew[:, h, :], start=False, stop=True)
                    MM(k, pz[0:64, h * 64:(h + 1) * 64], lhsT=t.kdec[:, c * 4 + h, :], rhs=t.vnew[:, h, :])
                yield
                V(k, "tensor_tensor", out=t.S4[:], in0=t.S4[:], in1=bc3(t.gl[:, c * 4:(c + 1) * 4], 64), op=ALU.mult)
                V(k, "tensor_tensor", out=t.S4[:], in0=t.S4[:], in1=r3(pz[0:64, 0:256]), op=ALU.add)
                if DT_REC != F32:
                    G(k, "tensor_copy", out=t.S4b[:], in_=t.S4[:])
                rc = r0 + c * 64
                A(k, out=t.oc[:], in_=py[0:64, 0:256], func=AF.Copy)
                if step < nblk // 2:
                    S.dma("pool", OWN[rc:rc + 64, :], t.oc[:])
                    yield
                else:
                    S.dma("sp", t.of[:], OTH[rc:rc + 64, :])
                    S.dma("sp", t.og[:], k.TMS[rc:rc + 64, TM_OFF["og"]:TM_OFF["og"] + 512])
                    yield
                    V(k, "tensor_tensor", out=t.oc[:], in0=t.oc[:], in1=t.of[:], op=ALU.add)
                    S.op("dve", "memset", t.ss4[:], 0.0)
                    for h in range(4):
                        A(k, out=t.junk[:], in_=t.oc[:, h * 64:(h + 1) * 64], func=AF.Square, accum_out=t.ss4[:, h:h + 1])
                    A(k, out=t.ss4[:], in_=t.ss4[:], func=AF.Sqrt, bias=EPS, scale=1.0 / 64)
                    V(k, "reciprocal", out=t.ss4[:], in_=t.ss4[:])
                    yield
                    for h in range(4):
                        V(k, "scalar_tensor_tensor", out=t.tt[:, h * 64:(h + 1) * 64], in0=t.oc[:, h * 64:(h + 1) * 64],
                          scalar=t.ss4[:, h:h + 1], in1=dgb[:], op0=ALU.mult, op1=ALU.mult)
                    A(k, out=t.sig[:], in_=t.og[:, 0:256], func=AF.Silu)
                    V(k, "tensor_tensor", out=t.tt[:], in0=t.tt[:], in1=t.sig[:], op=ALU.mult)
                    yield
                    for j in range(2):
                        S.op("pe", "transpose", out=B[0][:, 128 + j * 64:128 + (j + 1) * 64], in_=t.tt[:, j * 128:(j + 1) * 128], identity=I64)
                    yield
                    V(k, "tensor_copy", out=k.BIG[:, 4:6, rc:rc + 64], in_=B[0][:, 128:256].rearrange("p (j n) -> p j n", n=64))
        if not sq_["sample"]:
            S.dma("pool", k.o_S[sq_["p"], l, dr, :, :, :].rearrange("h a b -> a h b"), t.S4[:])

    with S.scope():
        alb = S.sb("alb", [64, 8])
        S.dma("sp", alb[:], k.a_log[l:l + 1, :].partition_broadcast(64))
        A(k, out=alb[:], in_=alb[:], func=AF.Exp)
        dtb = S.sb("dtb", [64, 8])
        S.dma("sp", dtb[:], k.dt_b[l:l + 1, :].partition_broadcast(64))
        dgb = S.sb("dgb", [64, 64])
        S.dma("sp", dgb[:], k.dn_g[l:l + 1, :].partition_broadcast(64))
        tiles = [alloc("f"), alloc("b")]
        for sq_ in SEQS:
            run_streams([stream(sq_, dr, tiles[dr], k.ps[dr * 4:dr * 4 + 4], alb, dtb, dgb) for dr in range(2)])


def swap_pairs(n):
    idx = np.arange(n)
    return idx ^ 1


def prep_shared(inp):
    f = np.float32
    sh = {}
    w_in = inp["w_in"]
    b_in = inp["b_in"]
    sw = swap_pairs(512)
    w_ext = np.concatenate([w_in, w_in[:, :, O_AQ:O_AQ + 512][:, :, sw], w_in[:, :, O_AK:O_AK + 512][:, :, sw]], axis=2)
    b_ext = np.concatenate([b_in, b_in[:, O_AQ:O_AQ + 512][:, sw], b_in[:, O_AK:O_AK + 512][:, sw]], axis=1)
    sh["w_in"] = np.ascontiguousarray(w_ext, dtype=f)
    sh["w_mod"] = inp["w_mod"]
    sh["w_out"] = inp["w_out"]
    sh["w_gu"] = inp["w_gate_up"]
    sh["w_dn"] = inp["w_down"]
    cm, ct, st = host_consts()
    sh["consts"], sh["rope_c"], sh["rope_s"] = cm, ct, st

    def fm(v):
        d, n = v.shape
        return np.ascontiguousarray(v.reshape(d, n // 128, 128).transpose(2, 0, 1), dtype=f)

    sh["n1g"] = fm(inp["norm1_g"])
    sh["n2g"] = fm(inp["norm2_g"])
    sh["fng"] = np.ascontiguousarray(inp["final_norm_g"].reshape(1, D), dtype=f)
    sh["b_mod"] = fm(inp["b_mod"])
    sh["b_fm"] = np.ascontiguousarray(
        np.stack([b_ext[:, c:c + 128] for c in FM_CHUNKS], axis=1).transpose(2, 0, 1), dtype=f)
    tm_cols = np.concatenate([np.arange(c0, c0 + w) for _, cols in TM_GROUPS for (c0, w) in cols])
    sh["b_tm"] = np.ascontiguousarray(b_ext[:, tm_cols].reshape(1, DEPTH, TM_W), dtype=f)
    cw = inp["delta_conv_w"]
    sh["conv_w"] = np.ascontiguousarray(cw.reshape(DEPTH, 5, 6, 128).transpose(3, 0, 2, 1), dtype=f)
    sh["lam_qk"] = np.ascontiguousarray(inp["lambda_qk"].reshape(DEPTH, 256), dtype=f)
    sh["subln_g"] = np.ascontiguousarray(inp["attn_subln_g"].T, dtype=f)
    sh["dn_g"] = np.ascontiguousarray(inp["delta_norm_g"], dtype=f)
    sh["mn_g"] = np.ascontiguousarray(inp["mlstm_norm_g"], dtype=f)
    sh["a_log"] = np.ascontiguousarray(inp["delta_A_log"].reshape(DEPTH, 8), dtype=f)
    sh["dt_b"] = np.ascontiguousarray(inp["delta_dt_bias"].reshape(DEPTH, 8), dtype=f)
    sh["f_b"] = np.ascontiguousarray(inp["mlstm_f_bias"].reshape(DEPTH, 8), dtype=f)
    return sh


def prep_core(inp, sh, i):
    f = np.float32
    b = i % 4
    m = dict(sh)
    xp = inp["x_prompt"][2 * i:2 * i + 2].reshape(NPR * LP, D)
    m["x_all"] = np.ascontiguousarray(np.concatenate([inp["x_sample"][b], xp], axis=0), dtype=f)
    m["cache_k"] = np.ascontiguousarray(inp["cache_attn_k"][b].reshape(DEPTH, PAST, 512), dtype=f)
    m["cache_v"] = np.ascontiguousarray(inp["cache_attn_v"][b].reshape(DEPTH, PAST, 512), dtype=f)
    m["st_d"] = np.ascontiguousarray(inp["state_delta"][b], dtype=f)
    m["st_c"] = np.ascontiguousarray(inp["state_mlstm_C"][b], dtype=f)
    m["st_n"] = np.ascontiguousarray(inp["state_mlstm_n"][b], dtype=f)
    m["st_m"] = np.ascontiguousarray(inp["state_mlstm_m"][b], dtype=f)
    cc = np.stack([inp["c"][b], inp["c_ctx"]], axis=-1)
    m["cT"] = np.ascontiguousarray(cc.reshape(KC, 128, 2).transpose(1, 0, 2), dtype=f)
    return m


def build_program(stop_after=None, debug_outs=()):
    k = build(debug_outs)
    DBG["stop"] = stop_after
    try:
        phase_setup(k)
        chk("setup")
        for l in range(DEPTH):
            phase_norm(k, l, 1)
            chk("norm%d" % l)
            phase_inproj(k, l)
            chk("inproj%d" % l)
            phase_attn(k, l)
            chk("attn%d" % l)
            phase_mlstm(k, l)
            chk("mlstm%d" % l)
            phase_delta_prep(k, l)
            chk("dprep%d" % l)
            phase_delta_scan(k, l)
            chk("delta%d" % l)
            phase_outproj(k, l)
            chk("outproj%d" % l)
            phase_norm(k, l, 2)
            phase_ffn(k, l)
            chk("ffn%d" % l)
        phase_final(k)
    except StopBuild:
        pass
    st = k.S.finalize()
    return k, st


_CACHE = {}


def kernel(**inputs):
    inp = {n: np.asarray(v) for n, v in inputs.items()}
    if "prog" not in _CACHE:
        _CACHE["prog"] = build_program()
    k, st = _CACHE["prog"]
    sh = prep_shared(inp)
    in_maps = [prep_core(inp, sh, i) for i in range(8)]
    res = run_bass_kernel_spmd(k.nc, in_maps, core_ids=list(range(8)))
    R = res.results
    B = 16
    y_prompt = np.concatenate([R[i]["y_p"].reshape(NPR, LP, D) for i in range(8)], axis=0)
    y_sample = np.stack([R[b]["y_s"] for b in range(4)], axis=0)
    nk = np.concatenate([R[i]["o_k"] for i in range(8)], axis=0).reshape(B, DEPTH, LP, 4, 2, 64)
    nv = np.concatenate([R[i]["o_v"] for i in range(8)], axis=0).reshape(B, DEPTH, LP, 4, 128)
    nS = np.concatenate([R[i]["o_S"] for i in range(8)], axis=0)
    nC = np.concatenate([R[i]["o_C"] for i in range(8)], axis=0)
    nn = np.concatenate([R[i]["o_n"] for i in range(8)], axis=0)
    nm = np.concatenate([R[i]["o_m"] for i in range(8)], axis=0)
    return (y_prompt.astype(np.float32), y_sample.astype(np.float32), nk.astype(np.float32), nv.astype(np.float32),
            nS.astype(np.float32), nC.astype(np.float32), nn.astype(np.float32), nm.astype(np.float32))
```

```python
import contextlib
import math
import numpy as np
import concourse.bass as bass
import concourse.mybir as mybir
from concourse.bass_utils import run_bass_kernel_spmd

F32 = mybir.dt.float32
BF16 = mybir.dt.bfloat16
AF = mybir.ActivationFunctionType
ALU = mybir.AluOpType
AX = mybir.AxisListType


class SemSlot:
    __slots__ = ("sem", "v")

    def __init__(self):
        self.sem = None
        self.v = 0


class Trk:
    __slots__ = ("name", "lw", "rd", "ldma", "slot", "psum")

    def __init__(self, name):
        self.name = name
        self.psum = False
        self.lw = None
        self.rd = []
        self.ldma = None
        self.slot = {}


class Op:
    __slots__ = ("eng", "meth", "args", "kw", "deps", "isdma", "dtrk", "needinc", "ev")

    def __init__(self, eng, meth, args, kw, isdma=False, dtrk=None):
        self.eng, self.meth, self.args, self.kw = eng, meth, args, kw
        self.deps = []
        self.isdma = isdma
        self.dtrk = dtrk
        self.needinc = isdma
        self.ev = None


WRITE_KEYS = ("out", "accum_out")


class Sched:
    def __init__(self, nc):
        self.nc = nc
        self.ops = []
        self.trk = {}
        self.stack = contextlib.ExitStack()
        self.engs = {"pe": nc.tensor, "dve": nc.vector, "act": nc.scalar,
                     "pool": nc.gpsimd, "sp": nc.sync}
        self.sb_bytes = 0
        self.sb_peak = 0
        self.uid = 0
        self.all_trks = []
        self.scope_trks = [[]]
        self.free_slots = {"hw": [], "sw": []}
        self.bar = []
        self.bar_pending = {e: False for e in self.engs}
        self.last_eng_op = {e: None for e in self.engs}

    def _newtrk(self, tname, name):
        t = Trk(name)
        self.trk[tname] = t
        self.all_trks.append(t)
        self.scope_trks[-1].append(t)
        return t

    def sb(self, name, shape, dtype=F32):
        self.uid += 1
        t = self.stack.enter_context(self.nc.sbuf_tensor("%s_%d" % (name, self.uid), list(shape), dtype))
        self._newtrk(t.name, name)
        n = 1
        for s in shape[1:]:
            n *= s
        self.sb_bytes += n * (2 if dtype == BF16 else 4)
        self.sb_peak = max(self.sb_peak, self.sb_bytes)
        return t

    def ps(self, name, shape, dtype=F32):
        t = self.stack.enter_context(self.nc.psum_tensor(name, list(shape), dtype))
        self._newtrk(t.name, name).psum = True
        return t

    @contextlib.contextmanager
    def scope(self):
        old = self.stack
        self.stack = contextlib.ExitStack()
        self.scope_trks.append([])
        b0 = self.sb_bytes
        try:
            yield
        finally:
            self.stack.close()
            self.stack = old
            self.sb_bytes = b0
            self.barrier()
            for t in self.scope_trks.pop():
                for cls, sl in t.slot.items():
                    self.free_slots[cls].append(sl)

    def barrier(self):
        bar = [o for o in self.last_eng_op.values() if o is not None]
        bar += [t.ldma for t in self.all_trks if t.ldma is not None]
        self.bar = sorted(set(bar))
        for e in self.bar_pending:
            self.bar_pending[e] = True

    def dram(self, name, shape, dtype=F32, kind="Internal", track=True):
        t = self.nc.dram_tensor(name, list(shape), dtype, kind=kind)
        if track:
            self.trk[t.name] = Trk(name)
        return t

    def _tr(self, ap):
        try:
            return self.trk.get(ap.tensor.name)
        except AttributeError:
            return None

    def _record(self, op, reads, writes):
        oid = len(self.ops)
        deps = set()
        for t in reads:
            if t.lw is not None:
                deps.add((t.lw, "raw"))
            if t.psum:
                for r in t.rd:
                    deps.add((r, "rar"))
        for t in writes:
            if t.lw is not None:
                deps.add((t.lw, "waw"))
            for r in t.rd:
                deps.add((r, "war"))
        if op.isdma and op.dtrk.ldma is not None:
            deps.add((op.dtrk.ldma, "raw"))
        if self.bar_pending[op.eng]:
            self.bar_pending[op.eng] = False
            for b in self.bar:
                deps.add((b, "bar"))
        final = {}
        for d, kind in deps:
            dop = self.ops[d]
            if not dop.isdma and not op.isdma and dop.eng == op.eng:
                if op.eng == "pe":
                    continue
            final[d] = True
        latest = {}
        for d in list(final):
            dop = self.ops[d]
            if not dop.isdma and dop.eng in ("pe", "act", "dve"):
                if dop.eng in latest:
                    lo = min(latest[dop.eng], d)
                    latest[dop.eng] = max(latest[dop.eng], d)
                    del final[lo]
                else:
                    latest[dop.eng] = d
        op.deps = sorted(final)
        for d in op.deps:
            self.ops[d].needinc = True
        self.ops.append(op)
        for t in reads:
            t.rd.append(oid)
        for t in writes:
            t.lw = oid
            t.rd = []
        if op.isdma:
            op.dtrk.ldma = oid
            cls = "sw" if op.eng == "pool" else "hw"
            if cls not in op.dtrk.slot:
                op.dtrk.slot[cls] = self.free_slots[cls].pop() if self.free_slots[cls] else SemSlot()
            sl = op.dtrk.slot[cls]
            if sl.v >= 30000:
                sl = op.dtrk.slot[cls] = SemSlot()
            sl.v += 16
            op.ev = (sl, sl.v)
        else:
            self.last_eng_op[op.eng] = oid
        return oid

    def op(self, eng, meth, *args, **kw):
        reads, writes = [], []
        names = list(kw.items())
        for i, a in enumerate(args):
            names.append(("out" if i == 0 else "in", a))
        for k, v in names:
            t = self._tr(v) if hasattr(v, "tensor") else None
            if t is None:
                continue
            if k in WRITE_KEYS:
                if t not in writes:
                    writes.append(t)
            elif t not in reads:
                reads.append(t)
        return self._record(Op(eng, meth, args, kw), reads, writes)

    def dma(self, q, out, in_, **kw):
        to, ti = self._tr(out), self._tr(in_)
        dtrk = None
        for ap, t in ((out, to), (in_, ti)):
            if t is not None and not type(ap.tensor).__name__.startswith("DRam"):
                dtrk = t
        if dtrk is None:
            dtrk = to if to is not None else ti
        assert dtrk is not None, "dma with no tracked side"
        o = Op(q, "dma_start", (), dict(out=out, in_=in_, **kw), isdma=True, dtrk=dtrk)
        return self._record(o, [ti] if ti is not None else [], [to] if to is not None else [])

    def finalize(self, final_wait_eng="sp"):
        nc = self.nc
        esem = {e: nc.alloc_semaphore("es_" + e) for e in self.engs}
        ecnt = {e: 0 for e in self.engs}
        known = {e: {} for e in self.engs}
        nwait = 0
        nroll = 0
        for op in self.ops:
            eng = self.engs[op.eng]
            kn = known[op.eng]
            need = {}
            for d in op.deps:
                sem, val = self.ops[d].ev
                if isinstance(sem, SemSlot):
                    if sem.sem is None:
                        sem.sem = nc.alloc_semaphore("ds%d" % id(sem))
                    sem = sem.sem
                k = id(sem)
                if kn.get(k, 0) >= val:
                    continue
                if k not in need or need[k][1] < val:
                    need[k] = (sem, val)
            for k, (sem, val) in need.items():
                eng.wait_ge(sem, val)
                kn[k] = val
                nwait += 1
            ins = getattr(eng, op.meth)(*op.args, **op.kw)
            if op.isdma:
                slot = op.ev[0]
                if slot.sem is None:
                    slot.sem = nc.alloc_semaphore("ds%d" % id(slot))
                ins.then_inc(slot.sem, 16)
            elif op.needinc:
                if ecnt[op.eng] >= 30000:
                    nroll += 1
                    esem[op.eng] = nc.alloc_semaphore("es_%s_%d" % (op.eng, nroll))
                    ecnt[op.eng] = 0
                ecnt[op.eng] += 1
                ins.then_inc(esem[op.eng], 1)
                op.ev = (esem[op.eng], ecnt[op.eng])
        eng = self.engs[final_wait_eng]
        seen = set()
        for t in self.all_trks:
            for sl in t.slot.values():
                if sl.sem is not None and id(sl) not in seen:
                    seen.add(id(sl))
                    eng.wait_ge(sl.sem, sl.v)
        for e in self.engs:
            if ecnt[e] and e != final_wait_eng:
                eng.wait_ge(esem[e], ecnt[e])
        self.stats = dict(n_ops=len(self.ops), n_wait=nwait, ecnt=dict(ecnt), sb_peak=self.sb_peak, nsem=len(seen) + 5)
        return self.stats


D = 1024
KC = 8
LS = 4096
LP = 256
NPR = 2
T = LS + NPR * LP
NT = T // 512
NB = T // 128
PAST = 512
DEPTH = 2
FH = 2816
FC = FH // 128
EPS = 1e-6
O_AQ, O_AK, O_AV = 0, 512, 1024
O_BQ, O_BK, O_BV, O_BG, O_BA, O_BB = 1536, 1792, 2048, 2304, 2560, 2568
O_CQ, O_CK, O_CV, O_CO, O_CI, O_CF = 2576, 2832, 3088, 3344, 3600, 3608
O_AQS, O_AKS = 3616, 4128
WIN = 4640
FM_CHUNKS = ([O_AQ + 128 * i for i in range(4)] + [O_AK + 128 * i for i in range(4)]
             + [O_BQ + 128 * i for i in range(6)] + [O_CQ + 128 * i for i in range(4)]
             + [O_AQS + 128 * i for i in range(4)] + [O_AKS + 128 * i for i in range(4)])
TM_GROUPS = [
    ("av", [(O_AV, 512)]),
    ("ckv", [(O_CK, 256), (O_CV, 256)]),
    ("og", [(O_BG, 256), (O_CO, 256)]),
    ("gt", [(O_BA, 16), (O_CI, 16)]),
    ("ak", [(O_AK, 512)]),
]
TM_OFF = {}
_o = 0
for _n, _cols in TM_GROUPS:
    TM_OFF[_n] = _o
    _o += sum(w for _, w in _cols)
TM_W = _o


class StopBuild(Exception):
    pass


DBG = {}


def chk(name):
    if DBG.get("stop") == name:
        raise StopBuild(name)


def lam_init_of(l):
    return 0.8 - 0.6 * math.exp(-0.3 * l)


def host_consts():
    i = np.arange(128)
    same = (i[:, None] // 64) == (i[None, :] // 64)
    c = {}
    c["ident"] = np.eye(128, dtype=np.float32)
    c["ones"] = np.ones((128, 128), np.float32)
    c["bd"] = same.astype(np.float32)
    c["m_ig"] = (same & (i[:, None] > i[None, :])).astype(np.float32)
    c["m_il"] = (same & (i[:, None] < i[None, :])).astype(np.float32)
    c["m_le"] = (same & (i[:, None] <= i[None, :])).astype(np.float32)
    c["m_ge"] = (same & (i[:, None] >= i[None, :])).astype(np.float32)
    order = ["ident", "ones", "bd", "m_ig", "m_il", "m_le", "m_ge"]
    cm = np.stack([c[k] for k in order], axis=1)
    t = np.arange(LS)
    rows = (t // 64).astype(np.float64)
    cols = (t % 64).astype(np.float64)
    nf = 16
    inv = 10000.0 ** (-np.arange(nf, dtype=np.float64) / nf)
    ang = np.concatenate([rows[:, None] * inv, cols[:, None] * inv], axis=-1)
    ang = ang.astype(np.float32).astype(np.float64)
    cos = np.cos(ang).astype(np.float32)
    sin = np.sin(ang).astype(np.float32)
    ct = np.zeros((128, LS), np.float32)
    st = np.zeros((128, LS), np.float32)
    for m in range(2):
        for d in range(64):
            ct[m * 64 + d] = cos[:, d // 2]
            st[m * 64 + d] = sin[:, d // 2] * (-1.0 if d % 2 == 0 else 1.0)
    return np.ascontiguousarray(cm), ct, st


CONST_ORDER = {"ident": 0, "ones": 1, "bd": 2, "m_ig": 3, "m_il": 4, "m_le": 5, "m_ge": 6}


class K:
    pass


def build(debug_outs=()):
    nc = bass.Bass("TRN2", target_bir_lowering=False)
    S = Sched(nc)
    k = K()
    k.nc, k.S = nc, S

    def din(name, shape):
        return nc.dram_tensor(name, list(shape), F32, kind="ExternalInput")

    def dout(name, shape):
        return S.dram(name, shape, F32, kind="ExternalOutput")

    k.x_all = din("x_all", [T, D])
    k.cache_k = din("cache_k", [DEPTH, PAST, 512])
    k.cache_v = din("cache_v", [DEPTH, PAST, 512])
    k.st_d = din("st_d", [DEPTH, 2, 4, 64, 64])
    k.st_c = din("st_c", [DEPTH, 2, 4, 64, 64])
    k.st_n = din("st_n", [DEPTH, 2, 4, 64])
    k.st_m = din("st_m", [DEPTH, 2, 4])
    k.cT = din("cT", [128, KC, 2])
    k.w_mod = din("w_mod", [DEPTH, D, 6 * D])
    k.w_in = din("w_in", [DEPTH, D, WIN])
    k.w_out = din("w_out", [DEPTH, D, D])
    k.w_gu = din("w_gu", [DEPTH, D, 2 * FH])
    k.w_dn = din("w_dn", [DEPTH, FH, D])
    k.consts = din("consts", [128, 7, 128])
    k.rope_c = din("rope_c", [128, LS])
    k.rope_s = din("rope_s", [128, LS])
    k.n1g = din("n1g", [128, DEPTH, KC])
    k.n2g = din("n2g", [128, DEPTH, KC])
    k.fng = din("fng", [1, D])
    k.b_mod = din("b_mod", [128, DEPTH, 48])
    k.b_fm = din("b_fm", [128, DEPTH, len(FM_CHUNKS)])
    k.b_tm = din("b_tm", [1, DEPTH, TM_W])
    k.conv_w = din("conv_w", [128, DEPTH, 6, 5])
    k.lam_qk = din("lam_qk", [DEPTH, 256])
    k.subln_g = din("subln_g", [128, DEPTH])
    k.dn_g = din("dn_g", [DEPTH, 64])
    k.mn_g = din("mn_g", [DEPTH, 64])
    k.a_log = din("a_log", [DEPTH, 8])
    k.dt_b = din("dt_b", [DEPTH, 8])
    k.f_b = din("f_b", [DEPTH, 8])
    k.y_s = dout("y_s", [LS, D])
    k.y_p = dout("y_p", [NPR * LP, D])
    k.o_k = dout("o_k", [NPR, DEPTH, LP, 512])
    k.o_v = dout("o_v", [NPR, DEPTH, LP, 512])
    k.o_S = dout("o_S", [NPR, DEPTH, 2, 4, 64, 64])
    k.o_C = dout("o_C", [NPR, DEPTH, 2, 4, 64, 64])
    k.o_n = dout("o_n", [NPR, DEPTH, 2, 4, 64])
    k.o_m = dout("o_m", [NPR, DEPTH, 2, 4])
    k.XT = S.dram("XT", [KC, 128, T])
    k.QT = S.dram("QT", [4, 128, T], BF16)
    k.KT = S.dram("KT", [4, 128, T + PAST], BF16)
    k.VV = S.dram("VV", [T + PAST, 512], BF16)
    k.BT = S.dram("BT", [6, 128, T])
    k.CQK = S.dram("CQK", [4, 128, T])
    k.TMS = S.dram("TMS", [T, TM_W])
    k.DQK = S.dram("DQK", [8, 64, T])
    k.DKV = S.dram("DKV", [T, 512])
    k.OD = S.dram("OD", [T, 256])
    k.OD2 = S.dram("OD2", [T, 256])
    k.OM2 = S.dram("OM2", [T, 256])
    k.OM = S.dram("OM", [T, 256])
    k.HT = S.dram("HT", [FC, 128, T], BF16)
    k.dbg = {}
    for name, shape in debug_outs:
        k.dbg[name] = dout("dbg_" + name, shape)

    k.cst = S.sb("cst", [128, 7, 128])
    k.cstb = S.sb("cstb", [128, 7, 128], BF16)
    S.dma("sp", k.cst[:], k.consts[:])
    S.op("dve", "tensor_copy", out=k.cstb[:], in_=k.cst[:])
    k.C = lambda name: k.cst[:, CONST_ORDER[name], :]
    k.Cb = lambda name: k.cstb[:, CONST_ORDER[name], :]
    k.BIG = S.sb("BIG", [128, KC, T], BF16)
    k.ps = [S.ps("ps%d" % i, [128, 512]) for i in range(8)]
    k.mod = S.sb("mod", [128, DEPTH, 48, 2])
    k.g1 = S.sb("g1", [128, DEPTH, KC, 2])
    k.g2 = S.sb("g2", [128, DEPTH, KC, 2])
    k.bfm = S.sb("bfm", [128, DEPTH, len(FM_CHUNKS)])
    S.dma("sp", k.bfm[:], k.b_fm[:])
    k.btm = S.sb("btm", [1, DEPTH, TM_W], BF16)
    k.ones1 = S.sb("ones1", [1, 128], BF16)
    S.op("dve", "memset", k.ones1[:], 1.0)
    return k


def V(k, meth, **kw):
    return k.S.op("dve", meth, **kw)


def A(k, **kw):
    return k.S.op("act", "activation", **kw)


def G(k, meth, *a, **kw):
    return k.S.op("pool", meth, *a, **kw)


def MM(k, out, lhsT, rhs, start=True, stop=True):
    return k.S.op("pe", "matmul", out, lhsT=lhsT, rhs=rhs, start=start, stop=stop)


def phase_setup(k):
    with k.S.scope():
        _phase_setup(k)


def _phase_setup(k):
    S = k.S
    csil = S.sb("csil", [128, KC, 2])
    ctmp = S.sb("ctmp", [128, KC, 2])
    S.dma("sp", ctmp[:], k.cT[:])
    A(k, out=csil[:], in_=ctmp[:], func=AF.Silu)
    bm = S.sb("bm", [128, DEPTH, 48])
    S.dma("sp", bm[:], k.b_mod[:])
    n1 = S.sb("n1", [128, DEPTH, KC])
    n2 = S.sb("n2", [128, DEPTH, KC])
    S.dma("sp", n1[:], k.n1g[:])
    S.dma("sp", n2[:], k.n2g[:])
    btmf = S.sb("btmf", [1, DEPTH, TM_W])
    S.dma("sp", btmf[:], k.b_tm[:])
    V(k, "tensor_copy", out=k.btm[:], in_=btmf[:])
    wst = [S.sb("wmst%d" % i, [128, KC, 768]) for i in range(2)]
    pm = k.ps[0]
    n = 0
    for l in range(DEPTH):
        for g in range(8):
            w = wst[n % 2]
            n += 1
            S.dma("sp", w[:], k.w_mod[l, :, g * 768:(g + 1) * 768].rearrange("(c p) n -> p c n", p=128))
            for j in range(6):
                mc = g * 6 + j
                for kc in range(KC):
                    MM(k, pm[:, mc * 2:mc * 2 + 2], lhsT=w[:, kc, j * 128:(j + 1) * 128], rhs=csil[:, kc, :],
                       start=(kc == 0), stop=(kc == KC - 1))
        for r in range(2):
            V(k, "tensor_tensor", out=k.mod[:, l, :, r], in0=pm[:, 0:96].rearrange("p (c r) -> p c r", r=2)[:, :, r],
              in1=bm[:, l, :], op=ALU.add)
        for r in range(2):
            V(k, "scalar_tensor_tensor", out=k.g1[:, l, :, r], in0=k.mod[:, l, 8:16, r], scalar=1.0, in1=n1[:, l, :],
              op0=ALU.add, op1=ALU.mult)
            V(k, "scalar_tensor_tensor", out=k.g2[:, l, :, r], in0=k.mod[:, l, 32:40, r], scalar=1.0, in1=n2[:, l, :],
              op0=ALU.add, op1=ALU.mult)
    xin = [S.sb("xin%d" % i, [128, D]) for i in range(2)]
    xto = [S.sb("xto%d" % i, [128, KC, 128]) for i in range(2)]
    for b in range(NB):
        xi = xin[b % 2]
        xo = xto[b % 2]
        S.dma("sp", xi[:], k.x_all[b * 128:(b + 1) * 128, :])
        for half in range(2):
            p = k.ps[1 + (b % 2) * 2 + half]
            for j in range(4):
                kc = half * 4 + j
                S.op("pe", "transpose", out=p[:, j * 128:(j + 1) * 128], in_=xi[:, kc * 128:(kc + 1) * 128],
                     identity=k.C("ident"))
            if half == 0:
                V(k, "tensor_copy", out=xo[:, 0:4, :], in_=p[:].rearrange("p (c n) -> p c n", n=128))
            else:
                A(k, out=xo[:, 4:8, :], in_=p[:].rearrange("p (c n) -> p c n", n=128), func=AF.Copy)
        S.dma("pool", k.XT[:, :, b * 128:(b + 1) * 128].rearrange("c p n -> p c n"), xo[:])


def seq_r(tile):
    return 0 if tile < LS // 512 else 1


def phase_norm(k, l, which):
    with k.S.scope():
        S = k.S
        k.xt_buf = [S.sb("xt%d" % i, [128, KC, 512]) for i in range(2)]
        k.sq_buf = S.sb("sq", [128, 2, 512])
        k.rstd_buf = S.sb("rstd", [128, 512])
        k.tmp_buf = [S.sb("tmp%d" % i, [128, 512]) for i in range(2)]
        _phase_norm(k, l, which)


def _phase_norm(k, l, which):
    S = k.S
    gg = k.g1 if which == 1 else k.g2
    sh0 = 0 if which == 1 else 24
    for t in range(NT):
        r = seq_r(t)
        xt = k.xt_buf[t % 2]
        S.dma("sp", xt[:], k.XT[:, :, t * 512:(t + 1) * 512].rearrange("c p n -> p c n"))
        sq = k.sq_buf
        pss = k.ps[t % 2]
        for kc in range(KC):
            A(k, out=sq[:, kc % 2, :], in_=xt[:, kc, :], func=AF.Square)
            MM(k, pss[:], lhsT=k.C("ones"), rhs=sq[:, kc % 2, :], start=(kc == 0), stop=(kc == KC - 1))
        rstd = k.rstd_buf
        A(k, out=k.tmp_buf[0][:], in_=pss[:], func=AF.Sqrt, bias=EPS, scale=1.0 / D)
        V(k, "reciprocal", out=rstd[:], in_=k.tmp_buf[0][:])
        for kc in range(KC):
            tmp = k.tmp_buf[kc % 2]
            V(k, "scalar_tensor_tensor", out=tmp[:], in0=xt[:, kc, :], scalar=gg[:, l, kc, r:r + 1], in1=rstd[:],
              op0=ALU.mult, op1=ALU.mult)
            A(k, out=k.BIG[:, kc, t * 512:(t + 1) * 512], in_=tmp[:], func=AF.Identity,
              bias=k.mod[:, l, sh0 + kc, r:r + 1], scale=1.0)


def load_w_bf16(k, dst, src_ap, stage, eng_i):
    S = k.S
    n = src_ap.shape[-1]
    S.dma("sp", stage[:, :, 0:n], src_ap.rearrange("(c p) n -> p c n", p=128))
    if eng_i % 2 == 0:
        V(k, "tensor_copy", out=dst, in_=stage[:, :, 0:n])
    else:
        G(k, "tensor_copy", out=dst, in_=stage[:, :, 0:n])


def phase_inproj(k, l):
    with k.S.scope():
        S = k.S
        k.tmp_buf = [S.sb("tmp%d" % i, [128, 512]) for i in range(2)]
        k.ob_buf = [S.sb("ob%d" % i, [128, 512], BF16) for i in range(2)]
        k.obf_buf = [S.sb("obf%d" % i, [128, 512]) for i in range(2)]
        k.wfm = [S.sb("wfm%d" % i, [128, KC, 128], BF16) for i in range(2)]
        k.wstage = [S.sb("wstage%d" % i, [128, KC, 256]) for i in range(2)]
        k.wtm = S.sb("wtm", [128, KC, TM_W], BF16)
        k.vb_buf = [S.sb("vb%d" % i, [128, 512], BF16) for i in range(2)]
        k.tmf_buf = [S.sb("tmf%d" % i, [128, 512]) for i in range(2)]
        k.ropec = S.sb("ropec", [128, LS])
        k.ropes = S.sb("ropes", [128, LS])
        S.dma("sp", k.ropec[:], k.rope_c[:])
        S.dma("sp", k.ropes[:], k.rope_s[:])
        _phase_inproj(k, l)


def _phase_inproj(k, l):
    S = k.S
    nfm = len(FM_CHUNKS)
    wfm = k.wfm
    stage = k.wstage
    fm_index = {c: i for i, c in enumerate(FM_CHUNKS)}

    def fm_matmul(col, t, ps):
        for kc in range(KC):
            MM(k, ps[:], lhsT=wcur[:, kc, :], rhs=k.BIG[:, kc, t * 512:(t + 1) * 512], start=(kc == 0), stop=(kc == KC - 1))

    cnt = 0
    for which, o_main, o_sw, dst in (("q", O_AQ, O_AQS, k.QT), ("k", O_AK, O_AKS, k.KT)):
        for h in range(4):
            wm = wfm[0]
            ws = wfm[1]
            load_w_bf16(k, wm[:], k.w_in[l, :, o_main + h * 128:o_main + (h + 1) * 128], stage[0], 0)
            load_w_bf16(k, ws[:], k.w_in[l, :, o_sw + h * 128:o_sw + (h + 1) * 128], stage[1], 1)
            bm = k.bfm[:, l, fm_index[o_main + h * 128]:fm_index[o_main + h * 128] + 1]
            bs = k.bfm[:, l, fm_index[o_sw + h * 128]:fm_index[o_sw + h * 128] + 1]
            for t in range(NT):
                p1 = k.ps[(cnt % 2) * 2]
                p2 = k.ps[(cnt % 2) * 2 + 1]
                ob = k.ob_buf[cnt % 2]
                cnt += 1
                for kc in range(KC):
                    MM(k, p1[:], lhsT=wm[:, kc, :], rhs=k.BIG[:, kc, t * 512:(t + 1) * 512], start=(kc == 0), stop=(kc == KC - 1))
                if seq_r(t) == 0:
                    for kc in range(KC):
                        MM(k, p2[:], lhsT=ws[:, kc, :], rhs=k.BIG[:, kc, t * 512:(t + 1) * 512], start=(kc == 0), stop=(kc == KC - 1))
                    t1 = k.tmp_buf[0]
                    t2 = k.tmp_buf[1]
                    V(k, "scalar_tensor_tensor", out=t1[:], in0=p1[:], scalar=bm, in1=k.ropec[:, t * 512:(t + 1) * 512],
                      op0=ALU.add, op1=ALU.mult)
                    V(k, "scalar_tensor_tensor", out=t2[:], in0=p2[:], scalar=bs, in1=k.ropes[:, t * 512:(t + 1) * 512],
                      op0=ALU.add, op1=ALU.mult)
                    G(k, "tensor_tensor", out=ob[:], in0=t1[:], in1=t2[:], op=ALU.add)
                else:
                    A(k, out=ob[:], in_=p1[:], func=AF.Identity, bias=bm, scale=1.0)
                S.dma("pool", dst[h, :, t * 512:(t + 1) * 512], ob[:])
    for o_main, nch, dst in ((O_BQ, 6, k.BT), (O_CQ, 4, k.CQK)):
        for c in range(nch):
            wm = wfm[cnt % 2]
            load_w_bf16(k, wm[:], k.w_in[l, :, o_main + c * 128:o_main + (c + 1) * 128], stage[cnt % 2], cnt)
            bm = k.bfm[:, l, fm_index[o_main + c * 128]:fm_index[o_main + c * 128] + 1]
            for t in range(NT):
                p1 = k.ps[(cnt % 2) * 2]
                ob = k.obf_buf[cnt % 2]
                cnt += 1
                for kc in range(KC):
                    MM(k, p1[:], lhsT=wm[:, kc, :], rhs=k.BIG[:, kc, t * 512:(t + 1) * 512], start=(kc == 0), stop=(kc == KC - 1))
                A(k, out=ob[:], in_=p1[:], func=AF.Identity, bias=bm, scale=1.0)
                S.dma("pool", dst[c, :, t * 512:(t + 1) * 512], ob[:])
    wtm = k.wtm
    for name, cols in TM_GROUPS:
        o = TM_OFF[name]
        for (c0, w) in cols:
            done = 0
            while done < w:
                ww = min(256, w - done)
                st = stage[cnt % 2]
                cnt += 1
                S.dma("sp", st[:, :, 0:ww], k.w_in[l, :, c0 + done:c0 + done + ww].rearrange("(c p) n -> p c n", p=128))
                V(k, "tensor_copy", out=wtm[:, :, o + done:o + done + ww], in_=st[:, :, 0:ww])
                done += ww
            o += w
    for b in range(NB):
        isprompt = b >= LS // 128
        for gi, (name, cols) in enumerate(TM_GROUPS):
            if name == "ak" and not isprompt:
                continue
            o = TM_OFF[name]
            w = sum(x for _, x in cols)
            p = k.ps[4 + (cnt % 2)]
            cnt += 1
            for kc in range(KC):
                MM(k, p[:, 0:w], lhsT=k.BIG[:, kc, b * 128:(b + 1) * 128], rhs=wtm[:, kc, o:o + w], start=(kc == 0), stop=False)
            MM(k, p[:, 0:w], lhsT=k.ones1[:, :], rhs=k.btm[:, l, o:o + w], start=False, stop=True)
            if name == "av":
                vb = k.vb_buf[b % 2]
                V(k, "tensor_copy", out=vb[:], in_=p[:])
                S.dma("pool", k.VV[b * 128:(b + 1) * 128, :], vb[:])
                if isprompt:
                    vf = k.tmf_buf[cnt % 2]
                    A(k, out=vf[:], in_=p[:], func=AF.Copy)
                    pb = b - LS // 128
                    S.dma("pool", k.o_v[pb // 2, l, (pb % 2) * 128:(pb % 2 + 1) * 128, :], vf[:])
            elif name == "ak":
                vf = k.tmf_buf[cnt % 2]
                A(k, out=vf[:], in_=p[:], func=AF.Copy)
                pb = b - LS // 128
                S.dma("pool", k.o_k[pb // 2, l, (pb % 2) * 128:(pb % 2 + 1) * 128, :], vf[:])
            else:
                vf = k.tmf_buf[cnt % 2]
                A(k, out=vf[:, 0:w], in_=p[:, 0:w], func=AF.Copy)
                S.dma("pool", k.TMS[b * 128:(b + 1) * 128, o:o + w], vf[:, 0:w])


def phase_outproj(k, l):
    S = k.S
    with S.scope():
        wo = S.sb("wo", [128, KC, D], BF16)
        stage = [S.sb("ostage%d" % i, [128, KC, 256]) for i in range(2)]
        for j in range(4):
            load_w_bf16(k, wo[:, :, j * 256:(j + 1) * 256], k.w_out[l, :, j * 256:(j + 1) * 256], stage[j % 2], j)
        xt = [S.sb("oxt%d" % i, [128, KC, 512]) for i in range(2)]
        for t in range(NT):
            r = seq_r(t)
            x = xt[t % 2]
            S.dma("sp", x[:], k.XT[:, :, t * 512:(t + 1) * 512].rearrange("c p n -> p c n"))
            for mc in range(KC):
                p = k.ps[mc % 2]
                for kc in range(KC):
                    MM(k, p[:], lhsT=wo[:, kc, mc * 128:(mc + 1) * 128], rhs=k.BIG[:, kc, t * 512:(t + 1) * 512],
                       start=(kc == 0), stop=(kc == KC - 1))
                V(k, "scalar_tensor_tensor", out=x[:, mc, :], in0=p[:], scalar=k.mod[:, l, 16 + mc, r:r + 1], in1=x[:, mc, :],
                  op0=ALU.mult, op1=ALU.add)
            S.dma("pool", k.XT[:, :, t * 512:(t + 1) * 512].rearrange("c p n -> p c n"), x[:])


def phase_ffn(k, l):
    S = k.S
    with S.scope():
        wg = [S.sb("wg%d" % i, [128, KC, 128], BF16) for i in range(2)]
        wu = [S.sb("wu%d" % i, [128, KC, 128], BF16) for i in range(2)]
        stage = [S.sb("fstage%d" % i, [128, KC, 256]) for i in range(2)]
        sil = [S.sb("sil%d" % i, [128, 512]) for i in range(2)]
        hb = [S.sb("hb%d" % i, [128, 512], BF16) for i in range(2)]
        cnt = 0
        for j in range(FC):
            g, u = wg[j % 2], wu[j % 2]
            load_w_bf16(k, g[:], k.w_gu[l, :, j * 128:(j + 1) * 128], stage[0], 0)
            load_w_bf16(k, u[:], k.w_gu[l, :, FH + j * 128:FH + (j + 1) * 128], stage[1], 1)
            for t in range(NT):
                pg = k.ps[(cnt % 2) * 2]
                pu = k.ps[(cnt % 2) * 2 + 1]
                for kc in range(KC):
                    MM(k, pg[:], lhsT=g[:, kc, :], rhs=k.BIG[:, kc, t * 512:(t + 1) * 512], start=(kc == 0), stop=(kc == KC - 1))
                for kc in range(KC):
                    MM(k, pu[:], lhsT=u[:, kc, :], rhs=k.BIG[:, kc, t * 512:(t + 1) * 512], start=(kc == 0), stop=(kc == KC - 1))
                A(k, out=sil[cnt % 2][:], in_=pg[:], func=AF.Silu)
                V(k, "tensor_tensor", out=hb[cnt % 2][:], in0=sil[cnt % 2][:], in1=pu[:], op=ALU.mult)
                S.dma("pool", k.HT[j, :, t * 512:(t + 1) * 512], hb[cnt % 2][:])
                cnt += 1
    with S.scope():
        wd = S.sb("wd", [128, FC, D], BF16)
        stage = S.sb("dstage", [128, KC, 256])
        n = 0
        for cp in range(4):
            for kr in (0, 8, 16):
                nn = min(8, FC - kr)
                S.dma("sp", stage[:, 0:nn, :], k.w_dn[l, kr * 128:(kr + nn) * 128, cp * 256:(cp + 1) * 256].rearrange("(c p) n -> p c n", p=128))
                if n % 2 == 0:
                    V(k, "tensor_copy", out=wd[:, kr:kr + nn, cp * 256:(cp + 1) * 256], in_=stage[:, 0:nn, :])
                else:
                    G(k, "tensor_copy", out=wd[:, kr:kr + nn, cp * 256:(cp + 1) * 256], in_=stage[:, 0:nn, :])
                n += 1
        hts = [S.sb("ht%d" % i, [128, FC, 512], BF16) for i in range(2)]
        x = S.sb("fxt", [128, KC, 512])
        for t in range(NT):
            r = seq_r(t)
            ht = hts[t % 2]
            for c4 in range(0, FC, 6):
                c5 = min(FC, c4 + 6)
                S.dma("sp", ht[:, c4:c5, :], k.HT[c4:c5, :, t * 512:(t + 1) * 512].rearrange("c p n -> p c n"))
            S.dma("sp", x[:], k.XT[:, :, t * 512:(t + 1) * 512].rearrange("c p n -> p c n"))
            for mc in range(KC):
                p = k.ps[mc % 2]
                for kc in range(FC):
                    MM(k, p[:], lhsT=wd[:, kc, mc * 128:(mc + 1) * 128], rhs=ht[:, kc, :], start=(kc == 0), stop=(kc == FC - 1))
                V(k, "scalar_tensor_tensor", out=x[:, mc, :], in0=p[:], scalar=k.mod[:, l, 40 + mc, r:r + 1], in1=x[:, mc, :],
                  op0=ALU.mult, op1=ALU.add)
            S.dma("pool", k.XT[:, :, t * 512:(t + 1) * 512].rearrange("c p n -> p c n"), x[:])


def phase_final(k):
    S = k.S
    with S.scope():
        fr = S.sb("fr", [1, D])
        S.dma("sp", fr[:], k.fng[:])
        fb = S.sb("fb", [128, D])
        for j in range(2):
            MM(k, k.ps[j][:], lhsT=k.cst[0:1, 1, :], rhs=fr[:, j * 512:(j + 1) * 512])
            V(k, "tensor_copy", out=fb[:, j * 512:(j + 1) * 512], in_=k.ps[j][:])
        xb = [S.sb("yx%d" % i, [128, KC, 128]) for i in range(2)]
        xk = [S.sb("yk%d" % i, [128, D]) for i in range(2)]
        junk = S.sb("yjunk", [128, D])
        ss = S.sb("yss", [128, 2])
        yo = [S.sb("yo%d" % i, [128, D]) for i in range(2)]
        for b in range(NB):
            x = xb[b % 2]
            xt = xk[b % 2]
            S.dma("sp", x[:], k.XT[:, :, b * 128:(b + 1) * 128].rearrange("c p n -> p c n"))
            for half in range(2):
                p = k.ps[2 + (b % 2) * 2 + half]
                for j in range(4):
                    S.op("pe", "transpose", out=p[:, j * 128:(j + 1) * 128], in_=x[:, half * 4 + j, :], identity=k.C("ident"))
                if half == 0:
                    V(k, "tensor_copy", out=xt[:, 0:512], in_=p[:])
                else:
                    A(k, out=xt[:, 512:1024], in_=p[:], func=AF.Copy)
            k.S.op("dve", "memset", ss[:, 0:1], 0.0)
            A(k, out=junk[:], in_=xt[:], func=AF.Square, accum_out=ss[:, 0:1])
            A(k, out=ss[:, 1:2], in_=ss[:, 0:1], func=AF.Sqrt, bias=EPS, scale=1.0 / D)
            V(k, "reciprocal", out=ss[:, 1:2], in_=ss[:, 1:2])
            y = yo[b % 2]
            V(k, "scalar_tensor_tensor", out=y[:], in0=xt[:], scalar=ss[:, 1:2], in1=fb[:], op0=ALU.mult, op1=ALU.mult)
            if b < LS // 128:
                S.dma("pool", k.y_s[b * 128:(b + 1) * 128, :], y[:])
            else:
                pb = b - LS // 128
                S.dma("pool", k.y_p[pb * 128:(pb + 1) * 128, :], y[:])


SEQS = [dict(tok0=0, L=LS, sample=True, p=-1)] + [dict(tok0=LS + i * LP, L=LP, sample=False, p=i) for i in range(NPR)]


def phase_attn(k, l):
    S = k.S
    with S.scope():
        lt = S.sb("lamt", [128, 256])
        S.dma("sp", lt[:], k.lam_qk[l:l + 1, :].partition_broadcast(128))
        lp = S.sb("lamp", [128, 256])
        ls = S.sb("lams", [128, 4])
        V(k, "tensor_tensor", out=lp[:, 0:64], in0=lt[:, 0:64], in1=lt[:, 64:128], op=ALU.mult)
        V(k, "tensor_tensor", out=lp[:, 64:128], in0=lt[:, 128:192], in1=lt[:, 192:256], op=ALU.mult)
        V(k, "reduce_sum", out=ls[:, 0:1], in_=lp[:, 0:64], axis=AX.X)
        V(k, "reduce_sum", out=ls[:, 1:2], in_=lp[:, 64:128], axis=AX.X)
        A(k, out=ls[:, 0:2], in_=ls[:, 0:2], func=AF.Exp)
        V(k, "tensor_tensor", out=ls[:, 2:3], in0=ls[:, 1:2], in1=ls[:, 0:1], op=ALU.subtract)
        V(k, "tensor_scalar", out=ls[:, 3:4], in0=ls[:, 2:3], scalar1=-lam_init_of(l), scalar2=None, op0=ALU.add)
        nlam = ls[:, 3:4]
        sg = S.sb("sublg", [128, DEPTH])
        S.dma("sp", sg[:], k.subln_g[:])
        sgl = S.sb("sublgl", [128, 1])
        V(k, "tensor_scalar", out=sgl[:], in0=sg[:, l:l + 1], scalar1=1.0 - lam_init_of(l), scalar2=None, op0=ALU.mult)
        ckf = S.sb("ckf", [128, 512])
        ckb = S.sb("ckb", [128, 4, 128], BF16)
        cvf = S.sb("cvf", [128, 512])
        cvb = S.sb("cvb", [128, 512], BF16)
        for b in range(PAST // 128):
            S.dma("sp", ckf[:], k.cache_k[l, b * 128:(b + 1) * 128, :])
            for h in range(4):
                S.op("pe", "transpose", out=k.ps[7][:, h * 128:(h + 1) * 128], in_=ckf[:, h * 128:(h + 1) * 128], identity=k.C("ident"))
            V(k, "tensor_copy", out=ckb[:], in_=k.ps[7][:].rearrange("p (h n) -> p h n", n=128))
            S.dma("pool", k.KT[:, :, T + b * 128:T + (b + 1) * 128].rearrange("h p n -> p h n"), ckb[:])
            S.dma("sp", cvf[:], k.cache_v[l, b * 128:(b + 1) * 128, :])
            V(k, "tensor_copy", out=cvb[:], in_=cvf[:])
            S.dma("pool", k.VV[T + b * 128:T + (b + 1) * 128, :], cvb[:])
        ktb = S.sb("ktb", [128, LS + PAST], BF16)
        vsb = S.sb("vsb", [128, (LS + PAST) // 128, 128], BF16)
        qsb = [S.sb("qsb%d" % i, [128, LS], BF16) for i in range(2)]
        for i in range(2):
            S.op("dve", "memset", qsb[i][:], 0.0)
        ptb = [S.sb("ptb%d" % i, [128, 512], BF16) for i in range(6)]
        zacc = [S.sb("zacc%d" % i, [128, 512]) for i in range(4)]
        sbank = [k.ps[0], k.ps[1], k.ps[6], k.ps[7]]
        rz = S.sb("rz", [128, 512])
        a0 = S.sb("a0", [128, 512])
        a1 = S.sb("a1", [128, 512])
        sq = S.sb("asq", [128, 512])
        for sq_ in SEQS:
            tok0, L = sq_["tok0"], sq_["L"]
            QW = min(512, L)
            nkt_own = L // 128
            nkt = nkt_own + (PAST // 128 if sq_["sample"] else 0)
            for h in range(4):
                S.dma("sp", ktb[:, 0:L], k.KT[h, :, tok0:tok0 + L])
                for t4 in range(0, nkt_own, 4):
                    t5 = min(nkt_own, t4 + 4)
                    S.dma("sp", vsb[:, t4:t5, :], k.VV[tok0 + t4 * 128:tok0 + t5 * 128, h * 128:(h + 1) * 128].rearrange("(t p) e -> p t e", p=128))
                if sq_["sample"]:
                    S.dma("sp", ktb[:, L:L + PAST], k.KT[h, :, T:T + PAST])
                    S.dma("sp", vsb[:, nkt_own:nkt, :], k.VV[T:T + PAST, h * 128:(h + 1) * 128].rearrange("(t p) e -> p t e", p=128))
                for m in range(2):
                    S.dma("sp", qsb[m][m * 64:(m + 1) * 64, 0:L], k.QT[h, m * 64:(m + 1) * 64, tok0:tok0 + L])
                for qt in range(L // QW):
                    units = [(m, kt) for m in range(2) for kt in range(nkt)]
                    NS = len(sbank)
                    NU = len(units)

                    def qk_mm(i):
                        m, kt = units[i]
                        MM(k, sbank[i % NS][:, 0:QW], lhsT=ktb[:, kt * 128:(kt + 1) * 128], rhs=qsb[m][:, qt * QW:(qt + 1) * QW])

                    for i in range(min(NS, NU)):
                        qk_mm(i)
                    first_pv = [True, True]
                    zused = set()
                    npv = [0, 0]
                    for i0 in range(0, NU, 2):
                        grp = [i for i in (i0, i0 + 1) if i < NU]
                        for i in grp:
                            m, kt = units[i]
                            pt = ptb[i % len(ptb)]
                            A(k, out=pt[:, 0:QW], in_=sbank[i % NS][:, 0:QW], func=AF.Exp, scale=0.125)
                            par = kt % 3
                            if par == 0:
                                MM(k, k.ps[4 + m][:, 0:QW], lhsT=k.Cb("ones"), rhs=pt[:, 0:QW], start=(kt == 0), stop=False)
                            else:
                                eng = "dve" if par == 1 else "pool"
                                za = zacc[m * 2 + par - 1]
                                if kt < 3:
                                    S.op(eng, "tensor_copy", out=za[:, 0:QW], in_=pt[:, 0:QW])
                                    zused.add(m * 2 + par - 1)
                                else:
                                    S.op(eng, "tensor_tensor", out=za[:, 0:QW], in0=za[:, 0:QW], in1=pt[:, 0:QW], op=ALU.add)
                        for i in reversed(grp):
                            m, kt = units[i]
                            pt = ptb[i % len(ptb)]
                            npv[m] += 1
                            MM(k, k.ps[2 + m][:, 0:QW], lhsT=vsb[:, kt, :], rhs=pt[:, 0:QW], start=first_pv[m], stop=(npv[m] == nkt))
                            first_pv[m] = False
                        for i in grp:
                            if i + NS < NU:
                                qk_mm(i + NS)
                    for m in range(2):
                        zl = [z for z in (m * 2, m * 2 + 1) if z in zused]
                        for zi, z in enumerate(zl):
                            MM(k, k.ps[4 + m][:, 0:QW], lhsT=k.C("ones"), rhs=zacc[z][:, 0:QW], start=False, stop=(zi == len(zl) - 1))
                    V(k, "reciprocal", out=rz[:, 0:QW], in_=k.ps[4][:, 0:QW])
                    V(k, "tensor_tensor", out=a0[:, 0:QW], in0=k.ps[2][:, 0:QW], in1=rz[:, 0:QW], op=ALU.mult)
                    V(k, "reciprocal", out=rz[:, 0:QW], in_=k.ps[5][:, 0:QW])
                    V(k, "tensor_tensor", out=a1[:, 0:QW], in0=k.ps[3][:, 0:QW], in1=rz[:, 0:QW], op=ALU.mult)
                    V(k, "scalar_tensor_tensor", out=a0[:, 0:QW], in0=a1[:, 0:QW], scalar=nlam, in1=a0[:, 0:QW], op0=ALU.mult, op1=ALU.add)
                    G(k, "tensor_tensor", out=sq[:, 0:QW], in0=a0[:, 0:QW], in1=a0[:, 0:QW], op=ALU.mult)
                    MM(k, k.ps[6][:, 0:QW], lhsT=k.C("ones"), rhs=sq[:, 0:QW])
                    A(k, out=sq[:, 0:QW], in_=k.ps[6][:, 0:QW], func=AF.Sqrt, bias=EPS, scale=1.0 / 128)
                    V(k, "reciprocal", out=rz[:, 0:QW], in_=sq[:, 0:QW])
                    V(k, "scalar_tensor_tensor", out=k.BIG[:, h, tok0 + qt * QW:tok0 + (qt + 1) * QW], in0=a0[:, 0:QW], scalar=sgl[:, 0:1],
                      in1=rz[:, 0:QW], op0=ALU.mult, op1=ALU.mult)


def run_streams(gens):
    gens = list(gens)
    while gens:
        for g in list(gens):
            try:
                next(g)
            except StopIteration:
                gens.remove(g)


class NS_:
    pass


DT_REC = F32


DT_ML = BF16


def phase_mlstm(k, l):
    S = k.S
    LN8 = math.log(0.125)
    I64 = k.cst[0:64, CONST_ORDER["ident"], 0:64]
    O64 = k.cst[0:64, CONST_ORDER["ones"], 0:64]

    def bc3(ap2, n):
        return ap2.unsqueeze(2).to_broadcast([ap2.shape[0], ap2.shape[1], n])

    def bcm(ap2, n):
        return ap2.unsqueeze(1).to_broadcast([ap2.shape[0], n, ap2.shape[1]])

    def r3(ap):
        return ap.rearrange("p (a n) -> p a n", n=64)

    def alloc(tag):
        t = NS_()
        for nm in ("diag", "X", "ET", "ebq"):
            setattr(t, nm, S.sb("m%s%s" % (nm, tag), [64, 8, 64]))
        for nm in ("qb", "sT", "kw"):
            setattr(t, nm, S.sb("m%s%s" % (nm, tag), [64, 8, 64], DT_ML))
        t.stb = S.sb("mstb" + tag, [64, 4, 128], DT_ML)
        for nm in ("nlf", "tg", "nbs", "t4", "t4b", "colE", "wk", "dec"):
            setattr(t, nm, S.sb("m%s%s" % (nm, tag), [64, 8]))
        t.gt2 = S.sb("mgt2" + tag, [64, 2, 32])
        t.ckv = S.sb("mckv" + tag, [64, 2, 512])
        t.og = S.sb("mog" + tag, [64, 512])
        t.qk = S.sb("mcqk" + tag, [64, 4, 2, 128])
        t.v1 = S.sb("mv1" + tag, [64, 8, 128], DT_ML)
        S.op("dve", "memset", t.v1[:], 0.0)
        S.op("dve", "memset", t.v1[:, :, 64:65], 1.0)
        t.state = S.sb("mstate" + tag, [64, 4, 128])
        for nm in ("hm", "omf", "tt", "sig"):
            setattr(t, nm, S.sb("m%s%s" % (nm, tag), [64, 256]))
        t.den = S.sb("mden" + tag, [64, 4])
        t.ss4 = S.sb("mss4" + tag, [64, 4])
        t.junk = S.sb("mjunk" + tag, [64, 64])
        t.em0 = S.sb("mem0" + tag, [64, 4])
        t.n0r = S.sb("mn0r" + tag, [4, 64])
        t.kvs = S.sb("mkvs" + tag, [4, 256])
        t.nls = S.sb("mnls" + tag, [4, 256])
        t.nblc = S.sb("mnblc" + tag, [4, 4])
        t.off = S.sb("moff" + tag, [4, 4])
        t.mx = S.sb("mmx" + tag, [4, 2])
        t.mfin = S.sb("mmfin" + tag, [4, 1])
        t.mrow = S.sb("mmrow" + tag, [1, 4])
        t.d4 = S.sb("md4" + tag, [4, 4])
        t.emf = S.sb("memf" + tag, [64, 4])
        t.so = S.sb("mso" + tag, [64, 4, 64])
        t.ncol = S.sb("mncol" + tag, [64, 4])
        t.nrow = S.sb("mnrow" + tag, [4, 64])
        return t

    def stream(sq_, dr, t, B, fbb, mgb):
        tok0, L = sq_["tok0"], sq_["L"]
        nblk = L // 128
        ci = CONST_ORDER["m_le"] if dr == 0 else CONST_ORDER["m_ge"]
        cum64 = k.cst[0:64, ci, 0:64]
        OWN, OTH = (k.OM, k.OM2) if dr == 0 else (k.OM2, k.OM)
        state = t.state
        S.op("dve", "memset", state[:], 0.0)
        if sq_["sample"]:
            S.dma("sp", t.em0[:], k.st_m[l, dr:dr + 1, :].partition_broadcast(64))
            A(k, out=t.em0[:], in_=t.em0[:], func=AF.Exp)
            S.dma("sp", state[:, :, 0:64], k.st_c[l, dr, :, :, :].rearrange("h a b -> a h b"))
            S.dma("sp", t.n0r[:], k.st_n[l, dr, :, :])
            S.op("pe", "transpose", out=B[0][0:64, 32:36], in_=t.n0r[:], identity=k.cst[0:4, CONST_ORDER["ident"], 0:4])
            V(k, "tensor_copy", out=state[:, :, 64], in_=B[0][0:64, 32:36])
            V(k, "tensor_tensor", out=state[:], in0=state[:], in1=bc3(t.em0[:], 128), op=ALU.mult)
        A(k, out=t.stb[:], in_=state[:], func=AF.Copy)
        blocks = list(range(nblk)) if dr == 0 else list(range(nblk - 1, -1, -1))
        corder = (0, 1) if dr == 0 else (1, 0)
        for step, bi in enumerate(blocks):
            r0 = tok0 + bi * 128
            for c in range(2):
                S.dma("sp", t.gt2[:, c, :], k.TMS[r0 + c * 64:r0 + (c + 1) * 64, TM_OFF["gt"]:TM_OFF["gt"] + 32])
                S.dma("sp", t.ckv[:, c, :], k.TMS[r0 + c * 64:r0 + (c + 1) * 64, TM_OFF["ckv"]:TM_OFF["ckv"] + 512])
            S.dma("sp", t.qk[:], k.CQK[:, :, r0:r0 + 128].rearrange("c (hh p) n -> p c hh n", p=64))
            yield
            kv4 = t.ckv[:].rearrange("p c (x h e) -> p c x h e", x=2, e=64)
            G(k, "tensor_copy", out=t.v1[:, :, 0:64].rearrange("p (c h) e -> p c h e", c=2), in_=kv4[:, :, 1, :, :])
            t3 = t.tg[:].rearrange("p (c h) -> p c h", c=2)
            V(k, "tensor_tensor", out=t3, in0=t.gt2[:, :, 24 + dr * 4:28 + dr * 4], in1=bcm(fbb[:, dr * 4:dr * 4 + 4], 2), op=ALU.add)
            A(k, out=t.tg[:], in_=t.tg[:], func=AF.Exp, scale=-1.0)
            A(k, out=t.nlf[:], in_=t.tg[:], func=AF.Ln, bias=1.0, scale=1.0)
            yield
            pa = B[0]
            for c in range(2):
                MM(k, pa[0:64, c * 4:(c + 1) * 4], lhsT=cum64, rhs=t.nlf[:, c * 4:(c + 1) * 4])
                MM(k, pa[0:64, 8 + c * 4:12 + c * 4], lhsT=O64, rhs=t.nlf[:, c * 4:(c + 1) * 4])
            yield
            V(k, "tensor_copy", out=t.nbs[:], in_=pa[0:64, 0:8])
            V(k, "tensor_tensor", out=t.t4[:], in0=t.nbs[:], in1=pa[0:64, 8:16], op=ALU.subtract)
            ig3 = t.gt2[:, :, 16 + dr * 4:20 + dr * 4]
            V(k, "tensor_tensor", out=t.t4b[:].rearrange("p (c h) -> p c h", c=2), in0=t.t4[:].rearrange("p (c h) -> p c h", c=2), in1=ig3, op=ALU.add)
            A(k, out=t.wk[:], in_=t.t4b[:], func=AF.Exp, bias=LN8, scale=1.0)
            V(k, "tensor_tensor", out=t.colE[:].rearrange("p (c h) -> p c h", c=2), in0=t.nbs[:].rearrange("p (c h) -> p c h", c=2), in1=ig3, op=ALU.add)
            V(k, "tensor_copy", out=t.dec[:], in_=pa[0:64, 8:16])
            A(k, out=t.dec[:], in_=t.dec[:], func=AF.Exp, scale=-1.0)
            V(k, "tensor_tensor", out=t.diag[:], in0=bcm(I64, 8), in1=bc3(t.nbs[:], 64), op=ALU.mult)
            yield
            if not sq_["sample"]:
                for c in range(2):
                    cc = bi * 2 + c
                    S.op("pe", "transpose", out=B[0][0:4, 256:320], in_=t.t4b[:, c * 4:(c + 1) * 4], identity=I64)
                    S.op("pe", "transpose", out=B[0][0:4, 384:448], in_=t.nlf[:, c * 4:(c + 1) * 4], identity=I64)
                    V(k, "tensor_copy", out=t.kvs[:, cc * 64:(cc + 1) * 64], in_=B[0][0:4, 256:320])
                    V(k, "tensor_copy", out=t.nls[:, cc * 64:(cc + 1) * 64], in_=B[0][0:4, 384:448])
            pb, pc = B[1], B[2]
            for pi in range(8):
                MM(k, pb[0:64, pi * 64:(pi + 1) * 64], lhsT=O64, rhs=t.diag[:, pi, :])
            for c in range(2):
                for h in range(4):
                    pi = c * 4 + h
                    MM(k, pc[0:64, pi * 64:(pi + 1) * 64], lhsT=t.qk[:, 2 + h // 2, h % 2, c * 64:(c + 1) * 64],
                       rhs=t.qk[:, h // 2, h % 2, c * 64:(c + 1) * 64])
            yield
            pb3 = r3(pb[0:64, :])
            V(k, "tensor_tensor", out=t.X[:], in0=bc3(t.colE[:], 64), in1=pb3, op=ALU.subtract)
            A(k, out=t.ebq[:], in_=pb3, func=AF.Exp, scale=-1.0)
            yield
            A(k, out=t.ET[:], in_=t.X[:], func=AF.Exp)
            q4 = t.qk[:, 0:2, :, :].rearrange("p a hh (c n) -> p c (a hh) n", c=2)
            G(k, "tensor_tensor", out=t.qb[:].rearrange("p (c h) n -> p c h n", c=2), in0=q4,
              in1=t.ebq[:].rearrange("p (c h) n -> p c h n", c=2), op=ALU.mult)
            V(k, "tensor_tensor", out=t.kw[:].rearrange("p (c h) e -> p c h e", c=2), in0=kv4[:, :, 0, :, :],
              in1=bc3(t.wk[:], 64).rearrange("p (c h) e -> p c h e", c=2), op=ALU.mult)
            yield
            G(k, "tensor_tensor", out=t.ET[:], in0=t.ET[:], in1=bcm(cum64, 8), op=ALU.mult)
            yield
            V(k, "scalar_tensor_tensor", out=t.sT[:], in0=r3(pc[0:64, :]), scalar=0.125, in1=t.ET[:], op0=ALU.mult, op1=ALU.mult)
            yield
            for c in corder:
                po, pst = B[3], B[1]
                for h in range(4):
                    pi = c * 4 + h
                    for (a0_, a1_) in ((0, 64), (64, 128)):
                        MM(k, po[0:64, h * 128 + a0_:h * 128 + a1_], lhsT=t.qb[:, pi, :], rhs=t.stb[:, h, a0_:a1_], start=True, stop=False)
                        MM(k, po[0:64, h * 128 + a0_:h * 128 + a1_], lhsT=t.sT[:, pi, :], rhs=t.v1[:, pi, a0_:a1_], start=False, stop=True)
                    MM(k, pst[0:64, h * 128:(h + 1) * 128], lhsT=t.kw[:, pi, :], rhs=t.v1[:, pi, :])
                yield
                V(k, "tensor_tensor", out=state[:], in0=state[:], in1=bc3(t.dec[:, c * 4:(c + 1) * 4], 128), op=ALU.mult)
                V(k, "tensor_tensor", out=state[:], in0=state[:], in1=pst[0:64, :].rearrange("p (h e) -> p h e", e=128), op=ALU.add)
                A(k, out=t.stb[:], in_=state[:], func=AF.Copy)
                rc = r0 + c * 64
                po3 = po[0:64, :].rearrange("p (h e) -> p h e", e=128)
                A(k, out=t.den[:], in_=po3[:, :, 64], func=AF.Abs)
                yield
                V(k, "tensor_scalar", out=t.den[:], in0=t.den[:], scalar1=1.0, scalar2=None, op0=ALU.max)
                V(k, "reciprocal", out=t.den[:], in_=t.den[:])
                V(k, "tensor_tensor", out=t.hm[:].rearrange("p (h e) -> p h e", e=64), in0=po3[:, :, 0:64], in1=bc3(t.den[:], 64), op=ALU.mult)
                if step < nblk // 2:
                    S.dma("pool", OWN[rc:rc + 64, :], t.hm[:])
                    yield
                else:
                    S.dma("sp", t.omf[:], OTH[rc:rc + 64, :])
                    S.dma("sp", t.og[:], k.TMS[rc:rc + 64, TM_OFF["og"]:TM_OFF["og"] + 512])
                    yield
                    V(k, "tensor_tensor", out=t.hm[:], in0=t.hm[:], in1=t.omf[:], op=ALU.add)
                    S.op("dve", "memset", t.ss4[:], 0.0)
                    for h in range(4):
                        A(k, out=t.junk[:], in_=t.hm[:, h * 64:(h + 1) * 64], func=AF.Square, accum_out=t.ss4[:, h:h + 1])
                    A(k, out=t.ss4[:], in_=t.ss4[:], func=AF.Sqrt, bias=EPS, scale=1.0 / 64)
                    V(k, "reciprocal", out=t.ss4[:], in_=t.ss4[:])
                    yield
                    for h in range(4):
                        V(k, "scalar_tensor_tensor", out=t.tt[:, h * 64:(h + 1) * 64], in0=t.hm[:, h * 64:(h + 1) * 64],
                          scalar=t.ss4[:, h:h + 1], in1=mgb[:], op0=ALU.mult, op1=ALU.mult)
                    A(k, out=t.sig[:], in_=t.og[:, 256:512], func=AF.Sigmoid)
                    V(k, "tensor_tensor", out=t.tt[:], in0=t.tt[:], in1=t.sig[:], op=ALU.mult)
                    yield
                    for j in range(2):
                        S.op("pe", "transpose", out=B[2][:, j * 64:(j + 1) * 64], in_=t.tt[:, j * 128:(j + 1) * 128], identity=I64)
                    yield
                    V(k, "tensor_copy", out=k.BIG[:, 6:8, rc:rc + 64], in_=B[2][:, 0:128].rearrange("p (j n) -> p j n", n=64))
        if not sq_["sample"]:
            p = sq_["p"]
            V(k, "reduce_sum", out=t.nblc[:], in_=t.nls[:].rearrange("p (c s) -> p c s", s=64), axis=AX.X)
            nch = L // 64
            S.op("dve", "memset", t.off[:], 0.0)
            if dr == 0:
                for c in range(nch - 2, -1, -1):
                    V(k, "tensor_tensor", out=t.off[:, c:c + 1], in0=t.off[:, c + 1:c + 2], in1=t.nblc[:, c + 1:c + 2], op=ALU.subtract)
            else:
                for c in range(1, nch):
                    V(k, "tensor_tensor", out=t.off[:, c:c + 1], in0=t.off[:, c - 1:c], in1=t.nblc[:, c - 1:c], op=ALU.subtract)
            V(k, "tensor_tensor", out=t.kvs[:].rearrange("p (c s) -> p c s", s=64), in0=t.kvs[:].rearrange("p (c s) -> p c s", s=64),
              in1=bc3(t.off[:], 64), op=ALU.add)
            V(k, "reduce_max", out=t.mx[:, 0:1], in_=t.kvs[:], axis=AX.X)
            V(k, "reduce_sum", out=t.mx[:, 1:2], in_=t.nblc[:], axis=AX.X)
            V(k, "scalar_tensor_tensor", out=t.mfin[:], in0=t.mx[:, 1:2], scalar=-1.0, in1=t.mx[:, 0:1], op0=ALU.mult, op1=ALU.max)
            yield
            S.op("pe", "transpose", out=B[0][0:1, 40:44], in_=t.mfin[:], identity=k.cst[0:4, CONST_ORDER["ident"], 0:4])
            V(k, "tensor_copy", out=t.mrow[:], in_=B[0][0:1, 40:44])
            S.dma("pool", k.o_m[p, l, dr:dr + 1, :], t.mrow[:])
            V(k, "tensor_scalar", out=t.d4[:], in0=k.cst[0:4, CONST_ORDER["ident"], 0:4], scalar1=t.mfin[:, 0:1], scalar2=None, op0=ALU.mult)
            MM(k, B[0][0:64, 16:20], lhsT=k.cst[0:4, CONST_ORDER["ones"], 0:64], rhs=t.d4[:])
            yield
            V(k, "tensor_copy", out=t.emf[:], in_=B[0][0:64, 16:20])
            A(k, out=t.emf[:], in_=t.emf[:], func=AF.Exp, scale=-1.0)
            V(k, "tensor_tensor", out=t.so[:], in0=state[:, :, 0:64], in1=bc3(t.emf[:], 64), op=ALU.mult)
            S.dma("pool", k.o_C[p, l, dr, :, :, :].rearrange("h a b -> a h b"), t.so[:])
            V(k, "tensor_tensor", out=t.ncol[:], in0=state[:, :, 64], in1=t.emf[:], op=ALU.mult)
            S.op("pe", "transpose", out=B[0][0:4, 64:128], in_=t.ncol[:], identity=I64)
            yield
            V(k, "tensor_copy", out=t.nrow[:], in_=B[0][0:4, 64:128])
            S.dma("pool", k.o_n[p, l, dr, :, :], t.nrow[:])

    with S.scope():
        fbb = S.sb("fbb", [64, 8])
        S.dma("sp", fbb[:], k.f_b[l:l + 1, :].partition_broadcast(64))
        mgb = S.sb("mgb", [64, 64])
        S.dma("sp", mgb[:], k.mn_g[l:l + 1, :].partition_broadcast(64))
        tiles = [alloc("f"), alloc("b")]
        for sq_ in SEQS:
            run_streams([stream(sq_, dr, tiles[dr], k.ps[dr * 4:dr * 4 + 4], fbb, mgb) for dr in range(2)])


def phase_delta(k, l):
    phase_delta_prep(k, l)
    phase_delta_scan(k, l)


def phase_delta_prep(k, l):
    S = k.S
    with S.scope():
        cw = S.sb("cw", [128, DEPTH, 6, 5])
        S.dma("sp", cw[:], k.conv_w[:])
        xin = S.sb("dxin", [128, 6, 516])
        acc = S.sb("dacc", [128, 6, 512])
        sact = S.sb("dsact", [128, 6, 512])
        sqb = S.sb("dsq", [128, 512])
        rin = S.sb("drin", [128, 512])
        tok = S.sb("dtok", [128, 512])
        for sq_ in SEQS:
            tok0, L = sq_["tok0"], sq_["L"]
            W = min(512, L)
            for ti in range(L // W):
                t0 = tok0 + ti * W
                lo, hi = max(tok0, t0 - 2), min(tok0 + L, t0 + W + 2)
                k.S.op("dve", "memset", xin[:], 0.0)
                S.dma("sp", xin[:, :, lo - (t0 - 2):hi - (t0 - 2)], k.BT[:, :, lo:hi].rearrange("c p n -> p c n"))
                for c in range(6):
                    eng = "dve"
                    S.op(eng, "tensor_scalar", out=acc[:, c, 0:W], in0=xin[:, c, 0:W], scalar1=cw[:, l, c, 0:1], scalar2=None, op0=ALU.mult)
                    for j in range(1, 5):
                        S.op(eng, "scalar_tensor_tensor", out=acc[:, c, 0:W], in0=xin[:, c, j:j + W], scalar=cw[:, l, c, j:j + 1],
                             in1=acc[:, c, 0:W], op0=ALU.mult, op1=ALU.add)
                for c in range(6):
                    A(k, out=sact[:, c, 0:W], in_=acc[:, c, 0:W], func=AF.Silu)
                for c in range(4):
                    G(k, "tensor_tensor", out=sqb[:, 0:W], in0=sact[:, c, 0:W], in1=sact[:, c, 0:W], op=ALU.mult)
                    p = k.ps[c % 2]
                    MM(k, p[:, 0:W], lhsT=k.C("bd"), rhs=sqb[:, 0:W])
                    A(k, out=rin[:, 0:W], in_=p[:, 0:W], func=AF.Sqrt, bias=EPS, scale=1.0)
                    V(k, "reciprocal", out=rin[:, 0:W], in_=rin[:, 0:W])
                    V(k, "scalar_tensor_tensor", out=sact[:, c, 0:W], in0=sact[:, c, 0:W], scalar=(0.125 if c < 2 else 1.0), in1=rin[:, 0:W],
                      op0=ALU.mult, op1=ALU.mult)
                    h0 = (c // 2) * 4 + (c % 2) * 2
                    S.dma("pool", k.DQK[h0:h0 + 2, :, t0:t0 + W].rearrange("h p n -> (h p) n"), sact[:, c, 0:W])
                for b in range(W // 128):
                    p = k.ps[2 + b % 2]
                    for j, c in enumerate((2, 3, 4, 5)):
                        S.op("pe", "transpose", out=p[:, j * 128:(j + 1) * 128], in_=sact[:, c, b * 128:(b + 1) * 128], identity=k.C("ident"))
                    V(k, "tensor_copy", out=tok[:], in_=p[:])
                    S.dma("pool", k.DKV[t0 + b * 128:t0 + (b + 1) * 128, :], tok[:])


def phase_delta_scan(k, l):
    S = k.S
    I64 = k.cst[0:64, CONST_ORDER["ident"], 0:64]
    O64 = k.cst[0:64, CONST_ORDER["ones"], 0:64]

    def bc3(ap2, n):
        return ap2.unsqueeze(2).to_broadcast([ap2.shape[0], ap2.shape[1], n])

    def bcm(ap2, n):
        return ap2.unsqueeze(1).to_broadcast([ap2.shape[0], n, ap2.shape[1]])

    def r3(ap):
        return ap.rearrange("p (a n) -> p a n", n=64)

    def alloc(tag):
        t = NS_()
        for nm in ("diag", "X", "N", "TT", "u", "ebq"):
            setattr(t, nm, S.sb("d%s%s" % (nm, tag), [64, 8, 64]))
        for nm in ("P0", "P1", "PT0", "PT1", "TTb", "aT", "vb", "kbg", "kdec", "wT", "qd"):
            setattr(t, nm, S.sb("d%s%s" % (nm, tag), [64, 8, 64], DT_REC))
        t.qkb = S.sb("dqkb" + tag, [64, 8, 128], DT_REC)
        t.S4b = S.sb("dS4b" + tag, [64, 4, 64], DT_REC)
        for nm in ("beta", "nbeta", "ng", "tg", "ngc", "egc", "bgs", "kds", "gl"):
            setattr(t, nm, S.sb("d%s%s" % (nm, tag), [64, 8]))
        t.gt2 = S.sb("dgt2" + tag, [64, 2, 32])
        t.dkv = S.sb("ddkv" + tag, [64, 2, 512])
        t.qk = S.sb("dqk" + tag, [64, 8, 128])
        t.vnew = S.sb("dvnew" + tag, [64, 4, 64], DT_REC)
        t.S4 = S.sb("dS4" + tag, [64, 4, 64])
        t.oc = S.sb("doc" + tag, [64, 256])
        t.of = S.sb("dof" + tag, [64, 256])
        t.og = S.sb("dog" + tag, [64, 512])
        t.ss4 = S.sb("dss4" + tag, [64, 4])
        t.junk = S.sb("djunk" + tag, [64, 64])
        t.tt = S.sb("dtt" + tag, [64, 256])
        t.sig = S.sb("dsig" + tag, [64, 256])
        if DT_REC == F32:
            t.qkb, t.P0, t.TTb, t.S4b = t.qk, t.N, t.TT, t.S4
        return t

    def stream(sq_, dr, t, B, alb, dtb, dgb):
        tok0, L = sq_["tok0"], sq_["L"]
        nblk = L // 128
        ci = CONST_ORDER["m_le"] if dr == 0 else CONST_ORDER["m_ge"]
        si = CONST_ORDER["m_ig"] if dr == 0 else CONST_ORDER["m_il"]
        cum64 = k.cst[0:64, ci, 0:64]
        str64 = k.cst[0:64, si, 0:64]
        OWN, OTH = (k.OD, k.OD2) if dr == 0 else (k.OD2, k.OD)
        if sq_["sample"]:
            S.dma("sp", t.S4[:], k.st_d[l, dr, :, :, :].rearrange("h a b -> a h b"))
        else:
            S.op("dve", "memset", t.S4[:], 0.0)
        if DT_REC != F32:
            V(k, "tensor_copy", out=t.S4b[:], in_=t.S4[:])
        blocks = list(range(nblk)) if dr == 0 else list(range(nblk - 1, -1, -1))
        corder = (0, 1) if dr == 0 else (1, 0)
        for step, bi in enumerate(blocks):
            r0 = tok0 + bi * 128
            for c in range(2):
                S.dma("sp", t.gt2[:, c, :], k.TMS[r0 + c * 64:r0 + (c + 1) * 64, TM_OFF["gt"]:TM_OFF["gt"] + 32])
                S.dma("sp", t.dkv[:, c, :], k.DKV[r0 + c * 64:r0 + (c + 1) * 64, :])
            S.dma("sp", t.qk[:], k.DQK[:, :, r0:r0 + 128].rearrange("h p n -> p h n"))
            yield
            if DT_REC != F32:
                G(k, "tensor_copy", out=t.qkb[:], in_=t.qk[:])
            b3 = t.beta[:].rearrange("p (c h) -> p c h", c=2)
            A(k, out=b3, in_=t.gt2[:, :, 8 + dr * 4:12 + dr * 4], func=AF.Sigmoid)
            V(k, "tensor_scalar", out=t.nbeta[:], in0=t.beta[:], scalar1=-1.0, scalar2=None, op0=ALU.mult)
            t3 = t.tg[:].rearrange("p (c h) -> p c h", c=2)
            V(k, "tensor_tensor", out=t3, in0=t.gt2[:, :, dr * 4:dr * 4 + 4], in1=bcm(dtb[:, dr * 4:dr * 4 + 4], 2), op=ALU.add)
            A(k, out=t.tg[:], in_=t.tg[:], func=AF.Exp)
            A(k, out=t.tg[:], in_=t.tg[:], func=AF.Ln, bias=1.0, scale=1.0)
            V(k, "tensor_tensor", out=t.ng[:].rearrange("p (c h) -> p c h", c=2), in0=t3, in1=bcm(alb[:, dr * 4:dr * 4 + 4], 2), op=ALU.mult)
            yield
            pa = B[0]
            for c in range(2):
                MM(k, pa[0:64, c * 4:(c + 1) * 4], lhsT=cum64, rhs=t.ng[:, c * 4:(c + 1) * 4])
                MM(k, pa[0:64, 8 + c * 4:12 + c * 4], lhsT=O64, rhs=t.ng[:, c * 4:(c + 1) * 4])
            yield
            V(k, "tensor_copy", out=t.ngc[:], in_=pa[0:64, 0:8])
            A(k, out=t.egc[:], in_=t.ngc[:], func=AF.Exp, scale=-1.0)
            V(k, "tensor_tensor", out=t.bgs[:], in0=t.beta[:], in1=t.egc[:], op=ALU.mult)
            V(k, "tensor_tensor", out=t.kds[:], in0=t.ngc[:], in1=pa[0:64, 8:16], op=ALU.subtract)
            A(k, out=t.kds[:], in_=t.kds[:], func=AF.Exp)
            V(k, "tensor_copy", out=t.gl[:], in_=pa[0:64, 8:16])
            A(k, out=t.gl[:], in_=t.gl[:], func=AF.Exp, scale=-1.0)
            V(k, "tensor_tensor", out=t.diag[:], in0=bcm(I64, 8), in1=bc3(t.ngc[:], 64), op=ALU.mult)
            yield
            pb, pc, pd = B[1], B[2], B[3]
            for pi in range(8):
                MM(k, pb[0:64, pi * 64:(pi + 1) * 64], lhsT=O64, rhs=t.diag[:, pi, :])
            for c in range(2):
                for h in range(4):
                    pi = c * 4 + h
                    kTc = t.qkb[:, 4 + h, c * 64:(c + 1) * 64]
                    qTc = t.qkb[:, h, c * 64:(c + 1) * 64]
                    MM(k, pc[0:64, pi * 64:(pi + 1) * 64], lhsT=kTc, rhs=kTc)
                    MM(k, pd[0:64, pi * 64:(pi + 1) * 64], lhsT=kTc, rhs=qTc)
            yield
            pb3 = r3(pb[0:64, :])
            V(k, "tensor_tensor", out=t.X[:], in0=pb3, in1=bc3(t.ngc[:], 64), op=ALU.subtract)
            A(k, out=t.ebq[:], in_=pb3, func=AF.Exp, scale=-1.0)
            V(k, "tensor_scalar", out=t.diag[:], in0=t.X[:], scalar1=0.0, scalar2=None, op0=ALU.min)
            G(k, "tensor_scalar", out=t.X[:], in0=t.X[:], scalar1=0.0, scalar2=None, op0=ALU.max)
            yield
            A(k, out=t.diag[:], in_=t.diag[:], func=AF.Exp)
            A(k, out=t.X[:], in_=t.X[:], func=AF.Exp, scale=-1.0)
            G(k, "tensor_tensor", out=t.diag[:], in0=t.diag[:], in1=bcm(str64, 8), op=ALU.mult)
            G(k, "tensor_tensor", out=t.X[:], in0=t.X[:], in1=bcm(cum64, 8), op=ALU.mult)
            yield
            V(k, "tensor_tensor", out=t.N[:], in0=r3(pc[0:64, :]), in1=bc3(t.nbeta[:], 64), op=ALU.mult)
            V(k, "tensor_tensor", out=t.N[:], in0=t.N[:], in1=t.diag[:], op=ALU.mult)
            if DT_REC != F32:
                A(k, out=t.P0[:], in_=t.N[:], func=AF.Copy)
            V(k, "tensor_tensor", out=t.aT[:], in0=r3(pd[0:64, :]), in1=t.X[:], op=ALU.mult)
            yield
            pt = B[1]
            for pi in range(8):
                S.op("pe", "transpose", out=pt[0:64, pi * 64:(pi + 1) * 64], in_=t.N[:, pi, :], identity=I64)
            yield
            A(k, out=t.PT0[:], in_=r3(pt[0:64, :]), func=AF.Copy)
            V(k, "tensor_tensor", out=t.TT[:], in0=r3(pt[0:64, :]), in1=bcm(I64, 8), op=ALU.add)
            if DT_REC != F32:
                if DT_REC != F32:
                    A(k, out=t.TTb[:], in_=t.TT[:], func=AF.Copy)
            yield
            Pb, PTb = (t.P0, t.P1), (t.PT0, t.PT1)
            for stp in range(5):
                P, PT = Pb[stp % 2], PTb[stp % 2]
                Pn, PTn = Pb[(stp + 1) % 2], PTb[(stp + 1) % 2]
                p1, p2, p3 = B[2], B[3], B[1]
                for pi in range(8):
                    MM(k, p1[0:64, pi * 64:(pi + 1) * 64], lhsT=PT[:, pi, :], rhs=P[:, pi, :])
                if stp < 4:
                    for pi in range(8):
                        MM(k, p2[0:64, pi * 64:(pi + 1) * 64], lhsT=P[:, pi, :], rhs=PT[:, pi, :])
                yield
                A(k, out=Pn[:], in_=r3(p1[0:64, :]), func=AF.Copy)
                if stp < 4:
                    V(k, "tensor_copy", out=PTn[:], in_=r3(p2[0:64, :]))
                yield
                for pi in range(8):
                    MM(k, p3[0:64, pi * 64:(pi + 1) * 64], lhsT=Pn[:, pi, :], rhs=t.TTb[:, pi, :])
                yield
                V(k, "tensor_tensor", out=t.TT[:], in0=t.TT[:], in1=r3(p3[0:64, :]), op=ALU.add)
                if DT_REC != F32:
                    A(k, out=t.TTb[:], in_=t.TT[:], func=AF.Copy)
                yield
            kv4 = t.dkv[:].rearrange("p c (x h e) -> p c x h e", x=2, e=64)
            V(k, "tensor_tensor", out=t.vb[:].rearrange("p (c h) e -> p c h e", c=2), in0=kv4[:, :, 1, :, :],
              in1=bc3(t.beta[:], 64).rearrange("p (c h) e -> p c h e", c=2), op=ALU.mult)
            V(k, "tensor_tensor", out=t.kbg[:].rearrange("p (c h) e -> p c h e", c=2), in0=kv4[:, :, 0, :, :],
              in1=bc3(t.bgs[:], 64).rearrange("p (c h) e -> p c h e", c=2), op=ALU.mult)
            G(k, "tensor_tensor", out=t.kdec[:].rearrange("p (c h) e -> p c h e", c=2), in0=kv4[:, :, 0, :, :],
              in1=bc3(t.kds[:], 64).rearrange("p (c h) e -> p c h e", c=2), op=ALU.mult)
            G(k, "tensor_tensor", out=t.qd[:].rearrange("p (c h) n -> p c h n", c=2),
              in0=t.qk[:, 0:4, :].rearrange("p h (c n) -> p c h n", c=2), in1=t.ebq[:].rearrange("p (c h) n -> p c h n", c=2), op=ALU.mult)
            yield
            p1, p2 = B[2], B[3]
            for pi in range(8):
                MM(k, p1[0:64, pi * 64:(pi + 1) * 64], lhsT=t.TTb[:, pi, :], rhs=t.vb[:, pi, :])
                MM(k, p2[0:64, pi * 64:(pi + 1) * 64], lhsT=t.kbg[:, pi, :], rhs=t.TTb[:, pi, :])
            yield
            A(k, out=t.u[:], in_=r3(p1[0:64, :]), func=AF.Copy)
            V(k, "tensor_copy", out=t.wT[:], in_=r3(p2[0:64, :]))
            yield
            for c in corder:
                px, py, pz = B[1], B[2], B[3]
                for h in range(4):
                    MM(k, px[0:64, h * 64:(h + 1) * 64], lhsT=t.wT[:, c * 4 + h, :], rhs=t.S4b[:, h, :])
                yield
                V(k, "tensor_tensor", out=t.vnew[:], in0=t.u[:, c * 4:(c + 1) * 4, :], in1=r3(px[0:64, 0:256]), op=ALU.subtract)
                yield
                for h in range(4):
                    MM(k, py[0:64, h * 64:(h + 1) * 64], lhsT=t.qd[:, c * 4 + h, :], rhs=t.S4b[:, h, :], start=True, stop=False)
                    MM(k, py[0:64, h * 64:(h + 1) * 64], lhsT=t.aT[:, c * 4 + h, :], rhs=t.vnew[:, h, :], start=False, stop=True)
                    MM(k, pz[0:64, h * 64:(h + 1) * 64], lhsT=t.kdec[:, c * 4 + h, :], rhs=t.vnew[:, h, :])
                yield
                V(k, "tensor_tensor", out=t.S4[:], in0=t.S4[:], in1=bc3(t.gl[:, c * 4:(c + 1) * 4], 64), op=ALU.mult)
                V(k, "tensor_tensor", out=t.S4[:], in0=t.S4[:], in1=r3(pz[0:64, 0:256]), op=ALU.add)
                if DT_REC != F32:
                    G(k, "tensor_copy", out=t.S4b[:], in_=t.S4[:])
                rc = r0 + c * 64
                A(k, out=t.oc[:], in_=py[0:64, 0:256], func=AF.Copy)
                if step < nblk // 2:
                    S.dma("pool", OWN[rc:rc + 64, :], t.oc[:])
                    yield
                else:
                    S.dma("sp", t.of[:], OTH[rc:rc + 64, :])
                    S.dma("sp", t.og[:], k.TMS[rc:rc + 64, TM_OFF["og"]:TM_OFF["og"] + 512])
                    yield
                    V(k, "tensor_tensor", out=t.oc[:], in0=t.oc[:], in1=t.of[:], op=ALU.add)
                    S.op("dve", "memset", t.ss4[:], 0.0)
                    for h in range(4):
                        A(k, out=t.junk[:], in_=t.oc[:, h * 64:(h + 1) * 64], func=AF.Square, accum_out=t.ss4[:, h:h + 1])
                    A(k, out=t.ss4[:], in_=t.ss4[:], func=AF.Sqrt, bias=EPS, scale=1.0 / 64)
                    V(k, "reciprocal", out=t.ss4[:], in_=t.ss4[:])
                    yield
                    for h in range(4):
                        V(k, "scalar_tensor_tensor", out=t.tt[:, h * 64:(h + 1) * 64], in0=t.oc[:, h * 64:(h + 1) * 64],
                          scalar=t.ss4[:, h:h + 1], in1=dgb[:], op0=ALU.mult, op1=ALU.mult)
                    A(k, out=t.sig[:], in_=t.og[:, 0:256], func=AF.Silu)
                    V(k, "tensor_tensor", out=t.tt[:], in0=t.tt[:], in1=t.sig[:], op=ALU.mult)
                    yield
                    for j in range(2):
                        S.op("pe", "transpose", out=B[0][:, 128 + j * 64:128 + (j + 1) * 64], in_=t.tt[:, j * 128:(j + 1) * 128], identity=I64)
                    yield
                    V(k, "tensor_copy", out=k.BIG[:, 4:6, rc:rc + 64], in_=B[0][:, 128:256].rearrange("p (j n) -> p j n", n=64))
        if not sq_["sample"]:
            S.dma("pool", k.o_S[sq_["p"], l, dr, :, :, :].rearrange("h a b -> a h b"), t.S4[:])

    with S.scope():
        alb = S.sb("alb", [64, 8])
        S.dma("sp", alb[:], k.a_log[l:l + 1, :].partition_broadcast(64))
        A(k, out=alb[:], in_=alb[:], func=AF.Exp)
        dtb = S.sb("dtb", [64, 8])
        S.dma("sp", dtb[:], k.dt_b[l:l + 1, :].partition_broadcast(64))
        dgb = S.sb("dgb", [64, 64])
        S.dma("sp", dgb[:], k.dn_g[l:l + 1, :].partition_broadcast(64))
        tiles = [alloc("f"), alloc("b")]
        for sq_ in SEQS:
            run_streams([stream(sq_, dr, tiles[dr], k.ps[dr * 4:dr * 4 + 4], alb, dtb, dgb) for dr in range(2)])


def swap_pairs(n):
    idx = np.arange(n)
    return idx ^ 1


def prep_shared(inp):
    f = np.float32
    sh = {}
    w_in = inp["w_in"]
    b_in = inp["b_in"]
    sw = swap_pairs(512)
    w_ext = np.concatenate([w_in, w_in[:, :, O_AQ:O_AQ + 512][:, :, sw], w_in[:, :, O_AK:O_AK + 512][:, :, sw]], axis=2)
    b_ext = np.concatenate([b_in, b_in[:, O_AQ:O_AQ + 512][:, sw], b_in[:, O_AK:O_AK + 512][:, sw]], axis=1)
    sh["w_in"] = np.ascontiguousarray(w_ext, dtype=f)
    sh["w_mod"] = inp["w_mod"]
    sh["w_out"] = inp["w_out"]
    sh["w_gu"] = inp["w_gate_up"]
    sh["w_dn"] = inp["w_down"]
    cm, ct, st = host_consts()
    sh["consts"], sh["rope_c"], sh["rope_s"] = cm, ct, st

    def fm(v):
        d, n = v.shape
        return np.ascontiguousarray(v.reshape(d, n // 128, 128).transpose(2, 0, 1), dtype=f)

    sh["n1g"] = fm(inp["norm1_g"])
    sh["n2g"] = fm(inp["norm2_g"])
    sh["fng"] = np.ascontiguousarray(inp["final_norm_g"].reshape(1, D), dtype=f)
    sh["b_mod"] = fm(inp["b_mod"])
    sh["b_fm"] = np.ascontiguousarray(
        np.stack([b_ext[:, c:c + 128] for c in FM_CHUNKS], axis=1).transpose(2, 0, 1), dtype=f)
    tm_cols = np.concatenate([np.arange(c0, c0 + w) for _, cols in TM_GROUPS for (c0, w) in cols])
    sh["b_tm"] = np.ascontiguousarray(b_ext[:, tm_cols].reshape(1, DEPTH, TM_W), dtype=f)
    cw = inp["delta_conv_w"]
    sh["conv_w"] = np.ascontiguousarray(cw.reshape(DEPTH, 5, 6, 128).transpose(3, 0, 2, 1), dtype=f)
    sh["lam_qk"] = np.ascontiguousarray(inp["lambda_qk"].reshape(DEPTH, 256), dtype=f)
    sh["subln_g"] = np.ascontiguousarray(inp["attn_subln_g"].T, dtype=f)
    sh["dn_g"] = np.ascontiguousarray(inp["delta_norm_g"], dtype=f)
    sh["mn_g"] = np.ascontiguousarray(inp["mlstm_norm_g"], dtype=f)
    sh["a_log"] = np.ascontiguousarray(inp["delta_A_log"].reshape(DEPTH, 8), dtype=f)
    sh["dt_b"] = np.ascontiguousarray(inp["delta_dt_bias"].reshape(DEPTH, 8), dtype=f)
    sh["f_b"] = np.ascontiguousarray(inp["mlstm_f_bias"].reshape(DEPTH, 8), dtype=f)
    return sh


def prep_core(inp, sh, i):
    f = np.float32
    b = i % 4
    m = dict(sh)
    xp = inp["x_prompt"][2 * i:2 * i + 2].reshape(NPR * LP, D)
    m["x_all"] = np.ascontiguousarray(np.concatenate([inp["x_sample"][b], xp], axis=0), dtype=f)
    m["cache_k"] = np.ascontiguousarray(inp["cache_attn_k"][b].reshape(DEPTH, PAST, 512), dtype=f)
    m["cache_v"] = np.ascontiguousarray(inp["cache_attn_v"][b].reshape(DEPTH, PAST, 512), dtype=f)
    m["st_d"] = np.ascontiguousarray(inp["state_delta"][b], dtype=f)
    m["st_c"] = np.ascontiguousarray(inp["state_mlstm_C"][b], dtype=f)
    m["st_n"] = np.ascontiguousarray(inp["state_mlstm_n"][b], dtype=f)
    m["st_m"] = np.ascontiguousarray(inp["state_mlstm_m"][b], dtype=f)
    cc = np.stack([inp["c"][b], inp["c_ctx"]], axis=-1)
    m["cT"] = np.ascontiguousarray(cc.reshape(KC, 128, 2).transpose(1, 0, 2), dtype=f)
    return m


def build_program(stop_after=None, debug_outs=()):
    k = build(debug_outs)
    DBG["stop"] = stop_after
    try:
        phase_setup(k)
        chk("setup")
        for l in range(DEPTH):
            phase_norm(k, l, 1)
            chk("norm%d" % l)
            phase_inproj(k, l)
            chk("inproj%d" % l)
            phase_attn(k, l)
            chk("attn%d" % l)
            phase_mlstm(k, l)
            chk("mlstm%d" % l)
            phase_delta_prep(k, l)
            chk("dprep%d" % l)
            phase_delta_scan(k, l)
            chk("delta%d" % l)
            phase_outproj(k, l)
            chk("outproj%d" % l)
            phase_norm(k, l, 2)
            phase_ffn(k, l)
            chk("ffn%d" % l)
        phase_final(k)
    except StopBuild:
        pass
    st = k.S.finalize()
    return k, st


_CACHE = {}


def kernel(**inputs):
    inp = {n: np.asarray(v) for n, v in inputs.items()}
    if "prog" not in _CACHE:
        _CACHE["prog"] = build_program()
    k, st = _CACHE["prog"]
    sh = prep_shared(inp)
    in_maps = [prep_core(inp, sh, i) for i in range(8)]
    res = run_bass_kernel_spmd(k.nc, in_maps, core_ids=list(range(8)))
    R = res.results
    B = 16
    y_prompt = np.concatenate([R[i]["y_p"].reshape(NPR, LP, D) for i in range(8)], axis=0)
    y_sample = np.stack([R[b]["y_s"] for b in range(4)], axis=0)
    nk = np.concatenate([R[i]["o_k"] for i in range(8)], axis=0).reshape(B, DEPTH, LP, 4, 2, 64)
    nv = np.concatenate([R[i]["o_v"] for i in range(8)], axis=0).reshape(B, DEPTH, LP, 4, 128)
    nS = np.concatenate([R[i]["o_S"] for i in range(8)], axis=0)
    nC = np.concatenate([R[i]["o_C"] for i in range(8)], axis=0)
    nn = np.concatenate([R[i]["o_n"] for i in range(8)], axis=0)
    nm = np.concatenate([R[i]["o_m"] for i in range(8)], axis=0)
    return (y_prompt.astype(np.float32), y_sample.astype(np.float32), nk.astype(np.float32), nv.astype(np.float32),
            nS.astype(np.float32), nC.astype(np.float32), nn.astype(np.float32), nm.astype(np.float32))
```

```python
import contextlib
import math
import numpy as np
import concourse.bass as bass
import concourse.mybir as mybir
from concourse.bass_utils import run_bass_kernel_spmd

F32 = mybir.dt.float32
BF16 = mybir.dt.bfloat16
AF = mybir.ActivationFunctionType
ALU = mybir.AluOpType
AX = mybir.AxisListType


class SemSlot:
    __slots__ = ("sem", "v")

    def __init__(self):
        self.sem = None
        self.v = 0


class Trk:
    __slots__ = ("name", "lw", "rd", "ldma", "slot", "psum")

    def __init__(self, name):
        self.name = name
        self.psum = False
        self.lw = None
        self.rd = []
        self.ldma = None
        self.slot = {}


class Op:
    __slots__ = ("eng", "meth", "args", "kw", "deps", "isdma", "dtrk", "needinc", "ev")

    def __init__(self, eng, meth, args, kw, isdma=False, dtrk=None):
        self.eng, self.meth, self.args, self.kw = eng, meth, args, kw
        self.deps = []
        self.isdma = isdma
        self.dtrk = dtrk
        self.needinc = isdma
        self.ev = None


WRITE_KEYS = ("out", "accum_out")


class Sched:
    def __init__(self, nc):
        self.nc = nc
        self.ops = []
        self.trk = {}
        self.stack = contextlib.ExitStack()
        self.engs = {"pe": nc.tensor, "dve": nc.vector, "act": nc.scalar,
                     "pool": nc.gpsimd, "sp": nc.sync}
        self.sb_bytes = 0
        self.sb_peak = 0
        self.uid = 0
        self.all_trks = []
        self.scope_trks = [[]]
        self.free_slots = {"hw": [], "sw": []}
        self.bar = []
        self.bar_pending = {e: False for e in self.engs}
        self.last_eng_op = {e: None for e in self.engs}

    def _newtrk(self, tname, name):
        t = Trk(name)
        self.trk[tname] = t
        self.all_trks.append(t)
        self.scope_trks[-1].append(t)
        return t

    def sb(self, name, shape, dtype=F32):
        self.uid += 1
        t = self.stack.enter_context(self.nc.sbuf_tensor("%s_%d" % (name, self.uid), list(shape), dtype))
        self._newtrk(t.name, name)
        n = 1
        for s in shape[1:]:
            n *= s
        self.sb_bytes += n * (2 if dtype == BF16 else 4)
        self.sb_peak = max(self.sb_peak, self.sb_bytes)
        return t

    def ps(self, name, shape, dtype=F32):
        t = self.stack.enter_context(self.nc.psum_tensor(name, list(shape), dtype))
        self._newtrk(t.name, name).psum = True
        return t

    @contextlib.contextmanager
    def scope(self):
        old = self.stack
        self.stack = contextlib.ExitStack()
        self.scope_trks.append([])
        b0 = self.sb_bytes
        try:
            yield
        finally:
            self.stack.close()
            self.stack = old
            self.sb_bytes = b0
            self.barrier()
            for t in self.scope_trks.pop():
                for cls, sl in t.slot.items():
                    self.free_slots[cls].append(sl)

    def barrier(self):
        bar = [o for o in self.last_eng_op.values() if o is not None]
        bar += [t.ldma for t in self.all_trks if t.ldma is not None]
        self.bar = sorted(set(bar))
        for e in self.bar_pending:
            self.bar_pending[e] = True

    def dram(self, name, shape, dtype=F32, kind="Internal", track=True):
        t = self.nc.dram_tensor(name, list(shape), dtype, kind=kind)
        if track:
            self.trk[t.name] = Trk(name)
        return t

    def _tr(self, ap):
        try:
            return self.trk.get(ap.tensor.name)
        except AttributeError:
            return None

    def _record(self, op, reads, writes):
        oid = len(self.ops)
        deps = set()
        for t in reads:
            if t.lw is not None:
                deps.add((t.lw, "raw"))
            if t.psum:
                for r in t.rd:
                    deps.add((r, "rar"))
        for t in writes:
            if t.lw is not None:
                deps.add((t.lw, "waw"))
            for r in t.rd:
                deps.add((r, "war"))
        if op.isdma and op.dtrk.ldma is not None:
            deps.add((op.dtrk.ldma, "raw"))
        if self.bar_pending[op.eng]:
            self.bar_pending[op.eng] = False
            for b in self.bar:
                deps.add((b, "bar"))
        final = {}
        for d, kind in deps:
            dop = self.ops[d]
            if not dop.isdma and not op.isdma and dop.eng == op.eng:
                if op.eng == "pe":
                    continue
            final[d] = True
        latest = {}
        for d in list(final):
            dop = self.ops[d]
            if not dop.isdma and dop.eng in ("pe", "act", "dve"):
                if dop.eng in latest:
                    lo = min(latest[dop.eng], d)
                    latest[dop.eng] = max(latest[dop.eng], d)
                    del final[lo]
                else:
                    latest[dop.eng] = d
        op.deps = sorted(final)
        for d in op.deps:
            self.ops[d].needinc = True
        self.ops.append(op)
        for t in reads:
            t.rd.append(oid)
        for t in writes:
            t.lw = oid
            t.rd = []
        if op.isdma:
            op.dtrk.ldma = oid
            cls = "sw" if op.eng == "pool" else "hw"
            if cls not in op.dtrk.slot:
                op.dtrk.slot[cls] = self.free_slots[cls].pop() if self.free_slots[cls] else SemSlot()
            sl = op.dtrk.slot[cls]
            if sl.v >= 30000:
                sl = op.dtrk.slot[cls] = SemSlot()
            sl.v += 16
            op.ev = (sl, sl.v)
        else:
            self.last_eng_op[op.eng] = oid
        return oid

    def op(self, eng, meth, *args, **kw):
        reads, writes = [], []
        names = list(kw.items())
        for i, a in enumerate(args):
            names.append(("out" if i == 0 else "in", a))
        for k, v in names:
            t = self._tr(v) if hasattr(v, "tensor") else None
            if t is None:
                continue
            if k in WRITE_KEYS:
                if t not in writes:
                    writes.append(t)
            elif t not in reads:
                reads.append(t)
        return self._record(Op(eng, meth, args, kw), reads, writes)

    def dma(self, q, out, in_, **kw):
        to, ti = self._tr(out), self._tr(in_)
        dtrk = None
        for ap, t in ((out, to), (in_, ti)):
            if t is not None and not type(ap.tensor).__name__.startswith("DRam"):
                dtrk = t
        if dtrk is None:
            dtrk = to if to is not None else ti
        assert dtrk is not None, "dma with no tracked side"
        o = Op(q, "dma_start", (), dict(out=out, in_=in_, **kw), isdma=True, dtrk=dtrk)
        return self._record(o, [ti] if ti is not None else [], [to] if to is not None else [])

    def finalize(self, final_wait_eng="sp"):
        nc = self.nc
        esem = {e: nc.alloc_semaphore("es_" + e) for e in self.engs}
        ecnt = {e: 0 for e in self.engs}
        known = {e: {} for e in self.engs}
        nwait = 0
        nroll = 0
        for op in self.ops:
            eng = self.engs[op.eng]
            kn = known[op.eng]
            need = {}
            for d in op.deps:
                sem, val = self.ops[d].ev
                if isinstance(sem, SemSlot):
                    if sem.sem is None:
                        sem.sem = nc.alloc_semaphore("ds%d" % id(sem))
                    sem = sem.sem
                k = id(sem)
                if kn.get(k, 0) >= val:
                    continue
                if k not in need or need[k][1] < val:
                    need[k] = (sem, val)
            for k, (sem, val) in need.items():
                eng.wait_ge(sem, val)
                kn[k] = val
                nwait += 1
            ins = getattr(eng, op.meth)(*op.args, **op.kw)
            if op.isdma:
                slot = op.ev[0]
                if slot.sem is None:
                    slot.sem = nc.alloc_semaphore("ds%d" % id(slot))
                ins.then_inc(slot.sem, 16)
            elif op.needinc:
                if ecnt[op.eng] >= 30000:
                    nroll += 1
                    esem[op.eng] = nc.alloc_semaphore("es_%s_%d" % (op.eng, nroll))
                    ecnt[op.eng] = 0
                ecnt[op.eng] += 1
                ins.then_inc(esem[op.eng], 1)
                op.ev = (esem[op.eng], ecnt[op.eng])
        eng = self.engs[final_wait_eng]
        seen = set()
        for t in self.all_trks:
            for sl in t.slot.values():
                if sl.sem is not None and id(sl) not in seen:
                    seen.add(id(sl))
                    eng.wait_ge(sl.sem, sl.v)
        for e in self.engs:
            if ecnt[e] and e != final_wait_eng:
                eng.wait_ge(esem[e], ecnt[e])
        self.stats = dict(n_ops=len(self.ops), n_wait=nwait, ecnt=dict(ecnt), sb_peak=self.sb_peak, nsem=len(seen) + 5)
        return self.stats


D = 1024
KC = 8
LS = 4096
LP = 256
NPR = 2
T = LS + NPR * LP
NT = T // 512
NB = T // 128
PAST = 512
DEPTH = 2
FH = 2816
FC = FH // 128
EPS = 1e-6
O_AQ, O_AK, O_AV = 0, 512, 1024
O_BQ, O_BK, O_BV, O_BG, O_BA, O_BB = 1536, 1792, 2048, 2304, 2560, 2568
O_CQ, O_CK, O_CV, O_CO, O_CI, O_CF = 2576, 2832, 3088, 3344, 3600, 3608
O_AQS, O_AKS = 3616, 4128
WIN = 4640
FM_CHUNKS = ([O_AQ + 128 * i for i in range(4)] + [O_AK + 128 * i for i in range(4)]
             + [O_BQ + 128 * i for i in range(6)] + [O_CQ + 128 * i for i in range(4)]
             + [O_AQS + 128 * i for i in range(4)] + [O_AKS + 128 * i for i in range(4)])
TM_GROUPS = [
    ("av", [(O_AV, 512)]),
    ("ckv", [(O_CK, 256), (O_CV, 256)]),
    ("og", [(O_BG, 256), (O_CO, 256)]),
    ("gt", [(O_BA, 16), (O_CI, 16)]),
    ("ak", [(O_AK, 512)]),
]
TM_OFF = {}
_o = 0
for _n, _cols in TM_GROUPS:
    TM_OFF[_n] = _o
    _o += sum(w for _, w in _cols)
TM_W = _o


class StopBuild(Exception):
    pass


DBG = {}


def chk(name):
    if DBG.get("stop") == name:
        raise StopBuild(name)


def lam_init_of(l):
    return 0.8 - 0.6 * math.exp(-0.3 * l)


def host_consts():
    i = np.arange(128)
    same = (i[:, None] // 64) == (i[None, :] // 64)
    c = {}
    c["ident"] = np.eye(128, dtype=np.float32)
    c["ones"] = np.ones((128, 128), np.float32)
    c["bd"] = same.astype(np.float32)
    c["m_ig"] = (same & (i[:, None] > i[None, :])).astype(np.float32)
    c["m_il"] = (same & (i[:, None] < i[None, :])).astype(np.float32)
    c["m_le"] = (same & (i[:, None] <= i[None, :])).astype(np.float32)
    c["m_ge"] = (same & (i[:, None] >= i[None, :])).astype(np.float32)
    order = ["ident", "ones", "bd", "m_ig", "m_il", "m_le", "m_ge"]
    cm = np.stack([c[k] for k in order], axis=1)
    t = np.arange(LS)
    rows = (t // 64).astype(np.float64)
    cols = (t % 64).astype(np.float64)
    nf = 16
    inv = 10000.0 ** (-np.arange(nf, dtype=np.float64) / nf)
    ang = np.concatenate([rows[:, None] * inv, cols[:, None] * inv], axis=-1)
    ang = ang.astype(np.float32).astype(np.float64)
    cos = np.cos(ang).astype(np.float32)
    sin = np.sin(ang).astype(np.float32)
    ct = np.zeros((128, LS), np.float32)
    st = np.zeros((128, LS), np.float32)
    for m in range(2):
        for d in range(64):
            ct[m * 64 + d] = cos[:, d // 2]
            st[m * 64 + d] = sin[:, d // 2] * (-1.0 if d % 2 == 0 else 1.0)
    return np.ascontiguousarray(cm), ct, st


CONST_ORDER = {"ident": 0, "ones": 1, "bd": 2, "m_ig": 3, "m_il": 4, "m_le": 5, "m_ge": 6}


class K:
    pass


def build(debug_outs=()):
    nc = bass.Bass("TRN2", target_bir_lowering=False)
    S = Sched(nc)
    k = K()
    k.nc, k.S = nc, S

    def din(name, shape):
        return nc.dram_tensor(name, list(shape), F32, kind="ExternalInput")

    def dout(name, shape):
        return S.dram(name, shape, F32, kind="ExternalOutput")

    k.x_all = din("x_all", [T, D])
    k.cache_k = din("cache_k", [DEPTH, PAST, 512])
    k.cache_v = din("cache_v", [DEPTH, PAST, 512])
    k.st_d = din("st_d", [DEPTH, 2, 4, 64, 64])
    k.st_c = din("st_c", [DEPTH, 2, 4, 64, 64])
    k.st_n = din("st_n", [DEPTH, 2, 4, 64])
    k.st_m = din("st_m", [DEPTH, 2, 4])
    k.cT = din("cT", [128, KC, 2])
    k.w_mod = din("w_mod", [DEPTH, D, 6 * D])
    k.w_in = din("w_in", [DEPTH, D, WIN])
    k.w_out = din("w_out", [DEPTH, D, D])
    k.w_gu = din("w_gu", [DEPTH, D, 2 * FH])
    k.w_dn = din("w_dn", [DEPTH, FH, D])
    k.consts = din("consts", [128, 7, 128])
    k.rope_c = din("rope_c", [128, LS])
    k.rope_s = din("rope_s", [128, LS])
    k.n1g = din("n1g", [128, DEPTH, KC])
    k.n2g = din("n2g", [128, DEPTH, KC])
    k.fng = din("fng", [1, D])
    k.b_mod = din("b_mod", [128, DEPTH, 48])
    k.b_fm = din("b_fm", [128, DEPTH, len(FM_CHUNKS)])
    k.b_tm = din("b_tm", [1, DEPTH, TM_W])
    k.conv_w = din("conv_w", [128, DEPTH, 6, 5])
    k.lam_qk = din("lam_qk", [DEPTH, 256])
    k.subln_g = din("subln_g", [128, DEPTH])
    k.dn_g = din("dn_g", [DEPTH, 64])
    k.mn_g = din("mn_g", [DEPTH, 64])
    k.a_log = din("a_log", [DEPTH, 8])
    k.dt_b = din("dt_b", [DEPTH, 8])
    k.f_b = din("f_b", [DEPTH, 8])
    k.y_s = dout("y_s", [LS, D])
    k.y_p = dout("y_p", [NPR * LP, D])
    k.o_k = dout("o_k", [NPR, DEPTH, LP, 512])
    k.o_v = dout("o_v", [NPR, DEPTH, LP, 512])
    k.o_S = dout("o_S", [NPR, DEPTH, 2, 4, 64, 64])
    k.o_C = dout("o_C", [NPR, DEPTH, 2, 4, 64, 64])
    k.o_n = dout("o_n", [NPR, DEPTH, 2, 4, 64])
    k.o_m = dout("o_m", [NPR, DEPTH, 2, 4])
    k.XT = S.dram("XT", [KC, 128, T])
    k.QT = S.dram("QT", [4, 128, T], BF16)
    k.KT = S.dram("KT", [4, 128, T + PAST], BF16)
    k.VV = S.dram("VV", [T + PAST, 512], BF16)
    k.BT = S.dram("BT", [6, 128, T])
    k.CQK = S.dram("CQK", [4, 128, T])
    k.TMS = S.dram("TMS", [T, TM_W])
    k.DQK = S.dram("DQK", [8, 64, T])
    k.DKV = S.dram("DKV", [T, 512])
    k.OD = S.dram("OD", [T, 256])
    k.OD2 = S.dram("OD2", [T, 256])
    k.OM2 = S.dram("OM2", [T, 256])
    k.OM = S.dram("OM", [T, 256])
    k.HT = S.dram("HT", [FC, 128, T], BF16)
    k.dbg = {}
    for name, shape in debug_outs:
        k.dbg[name] = dout("dbg_" + name, shape)

    k.cst = S.sb("cst", [128, 7, 128])
    k.cstb = S.sb("cstb", [128, 7, 128], BF16)
    S.dma("sp", k.cst[:], k.consts[:])
    S.op("dve", "tensor_copy", out=k.cstb[:], in_=k.cst[:])
    k.C = lambda name: k.cst[:, CONST_ORDER[name], :]
    k.Cb = lambda name: k.cstb[:, CONST_ORDER[name], :]
    k.BIG = S.sb("BIG", [128, KC, T], BF16)
    k.ps = [S.ps("ps%d" % i, [128, 512]) for i in range(8)]
    k.mod = S.sb("mod", [128, DEPTH, 48, 2])
    k.g1 = S.sb("g1", [128, DEPTH, KC, 2])
    k.g2 = S.sb("g2", [128, DEPTH, KC, 2])
    k.bfm = S.sb("bfm", [128, DEPTH, len(FM_CHUNKS)])
    S.dma("sp", k.bfm[:], k.b_fm[:])
    k.btm = S.sb("btm", [1, DEPTH, TM_W], BF16)
    k.ones1 = S.sb("ones1", [1, 128], BF16)
    S.op("dve", "memset", k.ones1[:], 1.0)
    return k


def V(k, meth, **kw):
    return k.S.op("dve", meth, **kw)


def A(k, **kw):
    return k.S.op("act", "activation", **kw)


def G(k, meth, *a, **kw):
    return k.S.op("pool", meth, *a, **kw)


def MM(k, out, lhsT, rhs, start=True, stop=True):
    return k.S.op("pe", "matmul", out, lhsT=lhsT, rhs=rhs, start=start, stop=stop)


def phase_setup(k):
    with k.S.scope():
        _phase_setup(k)


def _phase_setup(k):
    S = k.S
    csil = S.sb("csil", [128, KC, 2])
    ctmp = S.sb("ctmp", [128, KC, 2])
    S.dma("sp", ctmp[:], k.cT[:])
    A(k, out=csil[:], in_=ctmp[:], func=AF.Silu)
    bm = S.sb("bm", [128, DEPTH, 48])
    S.dma("sp", bm[:], k.b_mod[:])
    n1 = S.sb("n1", [128, DEPTH, KC])
    n2 = S.sb("n2", [128, DEPTH, KC])
    S.dma("sp", n1[:], k.n1g[:])
    S.dma("sp", n2[:], k.n2g[:])
    btmf = S.sb("btmf", [1, DEPTH, TM_W])
    S.dma("sp", btmf[:], k.b_tm[:])
    V(k, "tensor_copy", out=k.btm[:], in_=btmf[:])
    wst = [S.sb("wmst%d" % i, [128, KC, 768]) for i in range(2)]
    pm = k.ps[0]
    n = 0
    for l in range(DEPTH):
        for g in range(8):
            w = wst[n % 2]
            n += 1
            S.dma("sp", w[:], k.w_mod[l, :, g * 768:(g + 1) * 768].rearrange("(c p) n -> p c n", p=128))
            for j in range(6):
                mc = g * 6 + j
                for kc in range(KC):
                    MM(k, pm[:, mc * 2:mc * 2 + 2], lhsT=w[:, kc, j * 128:(j + 1) * 128], rhs=csil[:, kc, :],
                       start=(kc == 0), stop=(kc == KC - 1))
        for r in range(2):
            V(k, "tensor_tensor", out=k.mod[:, l, :, r], in0=pm[:, 0:96].rearrange("p (c r) -> p c r", r=2)[:, :, r],
              in1=bm[:, l, :], op=ALU.add)
        for r in range(2):
            V(k, "scalar_tensor_tensor", out=k.g1[:, l, :, r], in0=k.mod[:, l, 8:16, r], scalar=1.0, in1=n1[:, l, :],
              op0=ALU.add, op1=ALU.mult)
            V(k, "scalar_tensor_tensor", out=k.g2[:, l, :, r], in0=k.mod[:, l, 32:40, r], scalar=1.0, in1=n2[:, l, :],
              op0=ALU.add, op1=ALU.mult)
    xin = [S.sb("xin%d" % i, [128, D]) for i in range(2)]
    xto = [S.sb("xto%d" % i, [128, KC, 128]) for i in range(2)]
    for b in range(NB):
        xi = xin[b % 2]
        xo = xto[b % 2]
        S.dma("sp", xi[:], k.x_all[b * 128:(b + 1) * 128, :])
        for half in range(2):
            p = k.ps[1 + (b % 2) * 2 + half]
            for j in range(4):
                kc = half * 4 + j
                S.op("pe", "transpose", out=p[:, j * 128:(j + 1) * 128], in_=xi[:, kc * 128:(kc + 1) * 128],
                     identity=k.C("ident"))
            if half == 0:
                V(k, "tensor_copy", out=xo[:, 0:4, :], in_=p[:].rearrange("p (c n) -> p c n", n=128))
            else:
                A(k, out=xo[:, 4:8, :], in_=p[:].rearrange("p (c n) -> p c n", n=128), func=AF.Copy)
        S.dma("pool", k.XT[:, :, b * 128:(b + 1) * 128].rearrange("c p n -> p c n"), xo[:])


def seq_r(tile):
    return 0 if tile < LS // 512 else 1


def phase_norm(k, l, which):
    with k.S.scope():
        S = k.S
        k.xt_buf = [S.sb("xt%d" % i, [128, KC, 512]) for i in range(2)]
        k.sq_buf = S.sb("sq", [128, 2, 512])
        k.rstd_buf = S.sb("rstd", [128, 512])
        k.tmp_buf = [S.sb("tmp%d" % i, [128, 512]) for i in range(2)]
        _phase_norm(k, l, which)


def _phase_norm(k, l, which):
    S = k.S
    gg = k.g1 if which == 1 else k.g2
    sh0 = 0 if which == 1 else 24
    for t in range(NT):
        r = seq_r(t)
        xt = k.xt_buf[t % 2]
        S.dma("sp", xt[:], k.XT[:, :, t * 512:(t + 1) * 512].rearrange("c p n -> p c n"))
        sq = k.sq_buf
        pss = k.ps[t % 2]
        for kc in range(KC):
            A(k, out=sq[:, kc % 2, :], in_=xt[:, kc, :], func=AF.Square)
            MM(k, pss[:], lhsT=k.C("ones"), rhs=sq[:, kc % 2, :], start=(kc == 0), stop=(kc == KC - 1))
        rstd = k.rstd_buf
        A(k, out=k.tmp_buf[0][:], in_=pss[:], func=AF.Sqrt, bias=EPS, scale=1.0 / D)
        V(k, "reciprocal", out=rstd[:], in_=k.tmp_buf[0][:])
        for kc in range(KC):
            tmp = k.tmp_buf[kc % 2]
            V(k, "scalar_tensor_tensor", out=tmp[:], in0=xt[:, kc, :], scalar=gg[:, l, kc, r:r + 1], in1=rstd[:],
              op0=ALU.mult, op1=ALU.mult)
            A(k, out=k.BIG[:, kc, t * 512:(t + 1) * 512], in_=tmp[:], func=AF.Identity,
              bias=k.mod[:, l, sh0 + kc, r:r + 1], scale=1.0)


def load_w_bf16(k, dst, src_ap, stage, eng_i):
    S = k.S
    n = src_ap.shape[-1]
    S.dma("sp", stage[:, :, 0:n], src_ap.rearrange("(c p) n -> p c n", p=128))
    if eng_i % 2 == 0:
        V(k, "tensor_copy", out=dst, in_=stage[:, :, 0:n])
    else:
        G(k, "tensor_copy", out=dst, in_=stage[:, :, 0:n])


def phase_inproj(k, l):
    with k.S.scope():
        S = k.S
        k.tmp_buf = [S.sb("tmp%d" % i, [128, 512]) for i in range(2)]
        k.ob_buf = [S.sb("ob%d" % i, [128, 512], BF16) for i in range(2)]
        k.obf_buf = [S.sb("obf%d" % i, [128, 512]) for i in range(2)]
        k.wfm = [S.sb("wfm%d" % i, [128, KC, 128], BF16) for i in range(2)]
        k.wstage = [S.sb("wstage%d" % i, [128, KC, 256]) for i in range(2)]
        k.wtm = S.sb("wtm", [128, KC, TM_W], BF16)
        k.vb_buf = [S.sb("vb%d" % i, [128, 512], BF16) for i in range(2)]
        k.tmf_buf = [S.sb("tmf%d" % i, [128, 512]) for i in range(2)]
        k.ropec = S.sb("ropec", [128, LS])
        k.ropes = S.sb("ropes", [128, LS])
        S.dma("sp", k.ropec[:], k.rope_c[:])
        S.dma("sp", k.ropes[:], k.rope_s[:])
        _phase_inproj(k, l)


def _phase_inproj(k, l):
    S = k.S
    nfm = len(FM_CHUNKS)
    wfm = k.wfm
    stage = k.wstage
    fm_index = {c: i for i, c in enumerate(FM_CHUNKS)}

    def fm_matmul(col, t, ps):
        for kc in range(KC):
            MM(k, ps[:], lhsT=wcur[:, kc, :], rhs=k.BIG[:, kc, t * 512:(t + 1) * 512], start=(kc == 0), stop=(kc == KC - 1))

    cnt = 0
    for which, o_main, o_sw, dst in (("q", O_AQ, O_AQS, k.QT), ("k", O_AK, O_AKS, k.KT)):
        for h in range(4):
            wm = wfm[0]
            ws = wfm[1]
            load_w_bf16(k, wm[:], k.w_in[l, :, o_main + h * 128:o_main + (h + 1) * 128], stage[0], 0)
            load_w_bf16(k, ws[:], k.w_in[l, :, o_sw + h * 128:o_sw + (h + 1) * 128], stage[1], 1)
            bm = k.bfm[:, l, fm_index[o_main + h * 128]:fm_index[o_main + h * 128] + 1]
            bs = k.bfm[:, l, fm_index[o_sw + h * 128]:fm_index[o_sw + h * 128] + 1]
            for t in range(NT):
                p1 = k.ps[(cnt % 2) * 2]
                p2 = k.ps[(cnt % 2) * 2 + 1]
                ob = k.ob_buf[cnt % 2]
                cnt += 1
                for kc in range(KC):
                    MM(k, p1[:], lhsT=wm[:, kc, :], rhs=k.BIG[:, kc, t * 512:(t + 1) * 512], start=(kc == 0), stop=(kc == KC - 1))
                if seq_r(t) == 0:
                    for kc in range(KC):
                        MM(k, p2[:], lhsT=ws[:, kc, :], rhs=k.BIG[:, kc, t * 512:(t + 1) * 512], start=(kc == 0), stop=(kc == KC - 1))
                    t1 = k.tmp_buf[0]
                    t2 = k.tmp_buf[1]
                    V(k, "scalar_tensor_tensor", out=t1[:], in0=p1[:], scalar=bm, in1=k.ropec[:, t * 512:(t + 1) * 512],
                      op0=ALU.add, op1=ALU.mult)
                    V(k, "scalar_tensor_tensor", out=t2[:], in0=p2[:], scalar=bs, in1=k.ropes[:, t * 512:(t + 1) * 512],
                      op0=ALU.add, op1=ALU.mult)
                    G(k, "tensor_tensor", out=ob[:], in0=t1[:], in1=t2[:], op=ALU.add)
                else:
                    A(k, out=ob[:], in_=p1[:], func=AF.Identity, bias=bm, scale=1.0)
                S.dma("pool", dst[h, :, t * 512:(t + 1) * 512], ob[:])
    for o_main, nch, dst in ((O_BQ, 6, k.BT), (O_CQ, 4, k.CQK)):
        for c in range(nch):
            wm = wfm[cnt % 2]
            load_w_bf16(k, wm[:], k.w_in[l, :, o_main + c * 128:o_main + (c + 1) * 128], stage[cnt % 2], cnt)
            bm = k.bfm[:, l, fm_index[o_main + c * 128]:fm_index[o_main + c * 128] + 1]
            for t in range(NT):
                p1 = k.ps[(cnt % 2) * 2]
                ob = k.obf_buf[cnt % 2]
                cnt += 1
                for kc in range(KC):
                    MM(k, p1[:], lhsT=wm[:, kc, :], rhs=k.BIG[:, kc, t * 512:(t + 1) * 512], start=(kc == 0), stop=(kc == KC - 1))
                A(k, out=ob[:], in_=p1[:], func=AF.Identity, bias=bm, scale=1.0)
                S.dma("pool", dst[c, :, t * 512:(t + 1) * 512], ob[:])
    wtm = k.wtm
    for name, cols in TM_GROUPS:
        o = TM_OFF[name]
        for (c0, w) in cols:
            done = 0
            while done < w:
                ww = min(256, w - done)
                st = stage[cnt % 2]
                cnt += 1
                S.dma("sp", st[:, :, 0:ww], k.w_in[l, :, c0 + done:c0 + done + ww].rearrange("(c p) n -> p c n", p=128))
                V(k, "tensor_copy", out=wtm[:, :, o + done:o + done + ww], in_=st[:, :, 0:ww])
                done += ww
            o += w
    for b in range(NB):
        isprompt = b >= LS // 128
        for gi, (name, cols) in enumerate(TM_GROUPS):
            if name == "ak" and not isprompt:
                continue
            o = TM_OFF[name]
            w = sum(x for _, x in cols)
            p = k.ps[4 + (cnt % 2)]
            cnt += 1
            for kc in range(KC):
                MM(k, p[:, 0:w], lhsT=k.BIG[:, kc, b * 128:(b + 1) * 128], rhs=wtm[:, kc, o:o + w], start=(kc == 0), stop=False)
            MM(k, p[:, 0:w], lhsT=k.ones1[:, :], rhs=k.btm[:, l, o:o + w], start=False, stop=True)
            if name == "av":
                vb = k.vb_buf[b % 2]
                V(k, "tensor_copy", out=vb[:], in_=p[:])
                S.dma("pool", k.VV[b * 128:(b + 1) * 128, :], vb[:])
                if isprompt:
                    vf = k.tmf_buf[cnt % 2]
                    A(k, out=vf[:], in_=p[:], func=AF.Copy)
                    pb = b - LS // 128
                    S.dma("pool", k.o_v[pb // 2, l, (pb % 2) * 128:(pb % 2 + 1) * 128, :], vf[:])
            elif name == "ak":
                vf = k.tmf_buf[cnt % 2]
                A(k, out=vf[:], in_=p[:], func=AF.Copy)
                pb = b - LS // 128
                S.dma("pool", k.o_k[pb // 2, l, (pb % 2) * 128:(pb % 2 + 1) * 128, :], vf[:])
            else:
                vf = k.tmf_buf[cnt % 2]
                A(k, out=vf[:, 0:w], in_=p[:, 0:w], func=AF.Copy)
                S.dma("pool", k.TMS[b * 128:(b + 1) * 128, o:o + w], vf[:, 0:w])


def phase_outproj(k, l):
    S = k.S
    with S.scope():
        wo = S.sb("wo", [128, KC, D], BF16)
        stage = [S.sb("ostage%d" % i, [128, KC, 256]) for i in range(2)]
        for j in range(4):
            load_w_bf16(k, wo[:, :, j * 256:(j + 1) * 256], k.w_out[l, :, j * 256:(j + 1) * 256], stage[j % 2], j)
        xt = [S.sb("oxt%d" % i, [128, KC, 512]) for i in range(2)]
        for t in range(NT):
            r = seq_r(t)
            x = xt[t % 2]
            S.dma("sp", x[:], k.XT[:, :, t * 512:(t + 1) * 512].rearrange("c p n -> p c n"))
            for mc in range(KC):
                p = k.ps[mc % 2]
                for kc in range(KC):
                    MM(k, p[:], lhsT=wo[:, kc, mc * 128:(mc + 1) * 128], rhs=k.BIG[:, kc, t * 512:(t + 1) * 512],
                       start=(kc == 0), stop=(kc == KC - 1))
                V(k, "scalar_tensor_tensor", out=x[:, mc, :], in0=p[:], scalar=k.mod[:, l, 16 + mc, r:r + 1], in1=x[:, mc, :],
                  op0=ALU.mult, op1=ALU.add)
            S.dma("pool", k.XT[:, :, t * 512:(t + 1) * 512].rearrange("c p n -> p c n"), x[:])


def phase_ffn(k, l):
    S = k.S
    with S.scope():
        wg = [S.sb("wg%d" % i, [128, KC, 128], BF16) for i in range(2)]
        wu = [S.sb("wu%d" % i, [128, KC, 128], BF16) for i in range(2)]
        stage = [S.sb("fstage%d" % i, [128, KC, 256]) for i in range(2)]
        sil = [S.sb("sil%d" % i, [128, 512]) for i in range(2)]
        hb = [S.sb("hb%d" % i, [128, 512], BF16) for i in range(2)]
        cnt = 0
        for j in range(FC):
            g, u = wg[j % 2], wu[j % 2]
            load_w_bf16(k, g[:], k.w_gu[l, :, j * 128:(j + 1) * 128], stage[0], 0)
            load_w_bf16(k, u[:], k.w_gu[l, :, FH + j * 128:FH + (j + 1) * 128], stage[1], 1)
            for t in range(NT):
                pg = k.ps[(cnt % 2) * 2]
                pu = k.ps[(cnt % 2) * 2 + 1]
                for kc in range(KC):
                    MM(k, pg[:], lhsT=g[:, kc, :], rhs=k.BIG[:, kc, t * 512:(t + 1) * 512], start=(kc == 0), stop=(kc == KC - 1))
                for kc in range(KC):
                    MM(k, pu[:], lhsT=u[:, kc, :], rhs=k.BIG[:, kc, t * 512:(t + 1) * 512], start=(kc == 0), stop=(kc == KC - 1))
                A(k, out=sil[cnt % 2][:], in_=pg[:], func=AF.Silu)
                V(k, "tensor_tensor", out=hb[cnt % 2][:], in0=sil[cnt % 2][:], in1=pu[:], op=ALU.mult)
                S.dma("pool", k.HT[j, :, t * 512:(t + 1) * 512], hb[cnt % 2][:])
                cnt += 1
    with S.scope():
        wd = S.sb("wd", [128, FC, D], BF16)
        stage = S.sb("dstage", [128, KC, 256])
        n = 0
        for cp in range(4):
            for kr in (0, 8, 16):
                nn = min(8, FC - kr)
                S.dma("sp", stage[:, 0:nn, :], k.w_dn[l, kr * 128:(kr + nn) * 128, cp * 256:(cp + 1) * 256].rearrange("(c p) n -> p c n", p=128))
                if n % 2 == 0:
                    V(k, "tensor_copy", out=wd[:, kr:kr + nn, cp * 256:(cp + 1) * 256], in_=stage[:, 0:nn, :])
                else:
                    G(k, "tensor_copy", out=wd[:, kr:kr + nn, cp * 256:(cp + 1) * 256], in_=stage[:, 0:nn, :])
                n += 1
        hts = [S.sb("ht%d" % i, [128, FC, 512], BF16) for i in range(2)]
        x = S.sb("fxt", [128, KC, 512])
        for t in range(NT):
            r = seq_r(t)
            ht = hts[t % 2]
            for c4 in range(0, FC, 6):
                c5 = min(FC, c4 + 6)
                S.dma("sp", ht[:, c4:c5, :], k.HT[c4:c5, :, t * 512:(t + 1) * 512].rearrange("c p n -> p c n"))
            S.dma("sp", x[:], k.XT[:, :, t * 512:(t + 1) * 512].rearrange("c p n -> p c n"))
            for mc in range(KC):
                p = k.ps[mc % 2]
                for kc in range(FC):
                    MM(k, p[:], lhsT=wd[:, kc, mc * 128:(mc + 1) * 128], rhs=ht[:, kc, :], start=(kc == 0), stop=(kc == FC - 1))
                V(k, "scalar_tensor_tensor", out=x[:, mc, :], in0=p[:], scalar=k.mod[:, l, 40 + mc, r:r + 1], in1=x[:, mc, :],
                  op0=ALU.mult, op1=ALU.add)
            S.dma("pool", k.XT[:, :, t * 512:(t + 1) * 512].rearrange("c p n -> p c n"), x[:])


def phase_final(k):
    S = k.S
    with S.scope():
        fr = S.sb("fr", [1, D])
        S.dma("sp", fr[:], k.fng[:])
        fb = S.sb("fb", [128, D])
        for j in range(2):
            MM(k, k.ps[j][:], lhsT=k.cst[0:1, 1, :], rhs=fr[:, j * 512:(j + 1) * 512])
            V(k, "tensor_copy", out=fb[:, j * 512:(j + 1) * 512], in_=k.ps[j][:])
        xb = [S.sb("yx%d" % i, [128, KC, 128]) for i in range(2)]
        xk = [S.sb("yk%d" % i, [128, D]) for i in range(2)]
        junk = S.sb("yjunk", [128, D])
        ss = S.sb("yss", [128, 2])
        yo = [S.sb("yo%d" % i, [128, D]) for i in range(2)]
        for b in range(NB):
            x = xb[b % 2]
            xt = xk[b % 2]
            S.dma("sp", x[:], k.XT[:, :, b * 128:(b + 1) * 128].rearrange("c p n -> p c n"))
            for half in range(2):
                p = k.ps[2 + (b % 2) * 2 + half]
                for j in range(4):
                    S.op("pe", "transpose", out=p[:, j * 128:(j + 1) * 128], in_=x[:, half * 4 + j, :], identity=k.C("ident"))
                if half == 0:
                    V(k, "tensor_copy", out=xt[:, 0:512], in_=p[:])
                else:
                    A(k, out=xt[:, 512:1024], in_=p[:], func=AF.Copy)
            k.S.op("dve", "memset", ss[:, 0:1], 0.0)
            A(k, out=junk[:], in_=xt[:], func=AF.Square, accum_out=ss[:, 0:1])
            A(k, out=ss[:, 1:2], in_=ss[:, 0:1], func=AF.Sqrt, bias=EPS, scale=1.0 / D)
            V(k, "reciprocal", out=ss[:, 1:2], in_=ss[:, 1:2])
            y = yo[b % 2]
            V(k, "scalar_tensor_tensor", out=y[:], in0=xt[:], scalar=ss[:, 1:2], in1=fb[:], op0=ALU.mult, op1=ALU.mult)
            if b < LS // 128:
                S.dma("pool", k.y_s[b * 128:(b + 1) * 128, :], y[:])
            else:
                pb = b - LS // 128
                S.dma("pool", k.y_p[pb * 128:(pb + 1) * 128, :], y[:])


SEQS = [dict(tok0=0, L=LS, sample=True, p=-1)] + [dict(tok0=LS + i * LP, L=LP, sample=False, p=i) for i in range(NPR)]


def phase_attn(k, l):
    S = k.S
    with S.scope():
        lt = S.sb("lamt", [128, 256])
        S.dma("sp", lt[:], k.lam_qk[l:l + 1, :].partition_broadcast(128))
        lp = S.sb("lamp", [128, 256])
        ls = S.sb("lams", [128, 4])
        V(k, "tensor_tensor", out=lp[:, 0:64], in0=lt[:, 0:64], in1=lt[:, 64:128], op=ALU.mult)
        V(k, "tensor_tensor", out=lp[:, 64:128], in0=lt[:, 128:192], in1=lt[:, 192:256], op=ALU.mult)
        V(k, "reduce_sum", out=ls[:, 0:1], in_=lp[:, 0:64], axis=AX.X)
        V(k, "reduce_sum", out=ls[:, 1:2], in_=lp[:, 64:128], axis=AX.X)
        A(k, out=ls[:, 0:2], in_=ls[:, 0:2], func=AF.Exp)
        V(k, "tensor_tensor", out=ls[:, 2:3], in0=ls[:, 1:2], in1=ls[:, 0:1], op=ALU.subtract)
        V(k, "tensor_scalar", out=ls[:, 3:4], in0=ls[:, 2:3], scalar1=-lam_init_of(l), scalar2=None, op0=ALU.add)
        nlam = ls[:, 3:4]
        sg = S.sb("sublg", [128, DEPTH])
        S.dma("sp", sg[:], k.subln_g[:])
        sgl = S.sb("sublgl", [128, 1])
        V(k, "tensor_scalar", out=sgl[:], in0=sg[:, l:l + 1], scalar1=1.0 - lam_init_of(l), scalar2=None, op0=ALU.mult)
        ckf = S.sb("ckf", [128, 512])
        ckb = S.sb("ckb", [128, 4, 128], BF16)
        cvf = S.sb("cvf", [128, 512])
        cvb = S.sb("cvb", [128, 512], BF16)
        for b in range(PAST // 128):
            S.dma("sp", ckf[:], k.cache_k[l, b * 128:(b + 1) * 128, :])
            for h in range(4):
                S.op("pe", "transpose", out=k.ps[7][:, h * 128:(h + 1) * 128], in_=ckf[:, h * 128:(h + 1) * 128], identity=k.C("ident"))
            V(k, "tensor_copy", out=ckb[:], in_=k.ps[7][:].rearrange("p (h n) -> p h n", n=128))
            S.dma("pool", k.KT[:, :, T + b * 128:T + (b + 1) * 128].rearrange("h p n -> p h n"), ckb[:])
            S.dma("sp", cvf[:], k.cache_v[l, b * 128:(b + 1) * 128, :])
            V(k, "tensor_copy", out=cvb[:], in_=cvf[:])
            S.dma("pool", k.VV[T + b * 128:T + (b + 1) * 128, :], cvb[:])
        ktb = S.sb("ktb", [128, LS + PAST], BF16)
        vsb = S.sb("vsb", [128, (LS + PAST) // 128, 128], BF16)
        qsb = [S.sb("qsb%d" % i, [128, LS], BF16) for i in range(2)]
        for i in range(2):
            S.op("dve", "memset", qsb[i][:], 0.0)
        ptb = [S.sb("ptb%d" % i, [128, 512], BF16) for i in range(6)]
        zacc = [S.sb("zacc%d" % i, [128, 512]) for i in range(4)]
        sbank = [k.ps[0], k.ps[1], k.ps[6], k.ps[7]]
        rz = S.sb("rz", [128, 512])
        a0 = S.sb("a0", [128, 512])
        a1 = S.sb("a1", [128, 512])
        sq = S.sb("asq", [128, 512])
        for sq_ in SEQS:
            tok0, L = sq_["tok0"], sq_["L"]
            QW = min(512, L)
            nkt_own = L // 128
            nkt = nkt_own + (PAST // 128 if sq_["sample"] else 0)
            for h in range(4):
                S.dma("sp", ktb[:, 0:L], k.KT[h, :, tok0:tok0 + L])
                for t4 in range(0, nkt_own, 4):
                    t5 = min(nkt_own, t4 + 4)
                    S.dma("sp", vsb[:, t4:t5, :], k.VV[tok0 + t4 * 128:tok0 + t5 * 128, h * 128:(h + 1) * 128].rearrange("(t p) e -> p t e", p=128))
                if sq_["sample"]:
                    S.dma("sp", ktb[:, L:L + PAST], k.KT[h, :, T:T + PAST])
                    S.dma("sp", vsb[:, nkt_own:nkt, :], k.VV[T:T + PAST, h * 128:(h + 1) * 128].rearrange("(t p) e -> p t e", p=128))
                for m in range(2):
                    S.dma("sp", qsb[m][m * 64:(m + 1) * 64, 0:L], k.QT[h, m * 64:(m + 1) * 64, tok0:tok0 + L])
                for qt in range(L // QW):
                    units = [(m, kt) for m in range(2) for kt in range(nkt)]
                    NS = len(sbank)
                    NU = len(units)

                    def qk_mm(i):
                        m, kt = units[i]
                        MM(k, sbank[i % NS][:, 0:QW], lhsT=ktb[:, kt * 128:(kt + 1) * 128], rhs=qsb[m][:, qt * QW:(qt + 1) * QW])

                    for i in range(min(NS, NU)):
                        qk_mm(i)
                    first_pv = [True, True]
                    zused = set()
                    npv = [0, 0]
                    for i0 in range(0, NU, 2):
                        grp = [i for i in (i0, i0 + 1) if i < NU]
                        for i in grp:
                            m, kt = units[i]
                            pt = ptb[i % len(ptb)]
                            A(k, out=pt[:, 0:QW], in_=sbank[i % NS][:, 0:QW], func=AF.Exp, scale=0.125)
                            par = kt % 3
                            if par == 0:
                                MM(k, k.ps[4 + m][:, 0:QW], lhsT=k.Cb("ones"), rhs=pt[:, 0:QW], start=(kt == 0), stop=False)
                            else:
                                eng = "dve" if par == 1 else "pool"
                                za = zacc[m * 2 + par - 1]
                                if kt < 3:
                                    S.op(eng, "tensor_copy", out=za[:, 0:QW], in_=pt[:, 0:QW])
                                    zused.add(m * 2 + par - 1)
                                else:
                                    S.op(eng, "tensor_tensor", out=za[:, 0:QW], in0=za[:, 0:QW], in1=pt[:, 0:QW], op=ALU.add)
                        for i in reversed(grp):
                            m, kt = units[i]
                            pt = ptb[i % len(ptb)]
                            npv[m] += 1
                            MM(k, k.ps[2 + m][:, 0:QW], lhsT=vsb[:, kt, :], rhs=pt[:, 0:QW], start=first_pv[m], stop=(npv[m] == nkt))
                            first_pv[m] = False
                        for i in grp:
                            if i + NS < NU:
                                qk_mm(i + NS)
                    for m in range(2):
                        zl = [z for z in (m * 2, m * 2 + 1) if z in zused]
                        for zi, z in enumerate(zl):
                            MM(k, k.ps[4 + m][:, 0:QW], lhsT=k.C("ones"), rhs=zacc[z][:, 0:QW], start=False, stop=(zi == len(zl) - 1))
                    V(k, "reciprocal", out=rz[:, 0:QW], in_=k.ps[4][:, 0:QW])
                    V(k, "tensor_tensor", out=a0[:, 0:QW], in0=k.ps[2][:, 0:QW], in1=rz[:, 0:QW], op=ALU.mult)
                    V(k, "reciprocal", out=rz[:, 0:QW], in_=k.ps[5][:, 0:QW])
                    V(k, "tensor_tensor", out=a1[:, 0:QW], in0=k.ps[3][:, 0:QW], in1=rz[:, 0:QW], op=ALU.mult)
                    V(k, "scalar_tensor_tensor", out=a0[:, 0:QW], in0=a1[:, 0:QW], scalar=nlam, in1=a0[:, 0:QW], op0=ALU.mult, op1=ALU.add)
                    G(k, "tensor_tensor", out=sq[:, 0:QW], in0=a0[:, 0:QW], in1=a0[:, 0:QW], op=ALU.mult)
                    MM(k, k.ps[6][:, 0:QW], lhsT=k.C("ones"), rhs=sq[:, 0:QW])
                    A(k, out=sq[:, 0:QW], in_=k.ps[6][:, 0:QW], func=AF.Sqrt, bias=EPS, scale=1.0 / 128)
                    V(k, "reciprocal", out=rz[:, 0:QW], in_=sq[:, 0:QW])
                    V(k, "scalar_tensor_tensor", out=k.BIG[:, h, tok0 + qt * QW:tok0 + (qt + 1) * QW], in0=a0[:, 0:QW], scalar=sgl[:, 0:1],
                      in1=rz[:, 0:QW], op0=ALU.mult, op1=ALU.mult)


def run_streams(gens):
    gens = list(gens)
    while gens:
        for g in list(gens):
            try:
                next(g)
            except StopIteration:
                gens.remove(g)


class NS_:
    pass


DT_REC = F32


DT_ML = BF16


def phase_mlstm(k, l):
    S = k.S
    LN8 = math.log(0.125)
    I64 = k.cst[0:64, CONST_ORDER["ident"], 0:64]
    O64 = k.cst[0:64, CONST_ORDER["ones"], 0:64]

    def bc3(ap2, n):
        return ap2.unsqueeze(2).to_broadcast([ap2.shape[0], ap2.shape[1], n])

    def bcm(ap2, n):
        return ap2.unsqueeze(1).to_broadcast([ap2.shape[0], n, ap2.shape[1]])

    def r3(ap):
        return ap.rearrange("p (a n) -> p a n", n=64)

    def alloc(tag):
        t = NS_()
        for nm in ("diag", "X", "ET", "ebq"):
            setattr(t, nm, S.sb("m%s%s" % (nm, tag), [64, 8, 64]))
        for nm in ("qb", "sT", "kw"):
            setattr(t, nm, S.sb("m%s%s" % (nm, tag), [64, 8, 64], DT_ML))
        t.stb = S.sb("mstb" + tag, [64, 4, 128], DT_ML)
        for nm in ("nlf", "tg", "nbs", "t4", "t4b", "colE", "wk", "dec"):
            setattr(t, nm, S.sb("m%s%s" % (nm, tag), [64, 8]))
        t.gt2 = S.sb("mgt2" + tag, [64, 2, 32])
        t.ckv = S.sb("mckv" + tag, [64, 2, 512])
        t.og = S.sb("mog" + tag, [64, 512])
        t.qk = S.sb("mcqk" + tag, [64, 4, 2, 128])
        t.v1 = S.sb("mv1" + tag, [64, 8, 128], DT_ML)
        S.op("dve", "memset", t.v1[:], 0.0)
        S.op("dve", "memset", t.v1[:, :, 64:65], 1.0)
        t.state = S.sb("mstate" + tag, [64, 4, 128])
        for nm in ("hm", "omf", "tt", "sig"):
            setattr(t, nm, S.sb("m%s%s" % (nm, tag), [64, 256]))
        t.den = S.sb("mden" + tag, [64, 4])
        t.ss4 = S.sb("mss4" + tag, [64, 4])
        t.junk = S.sb("mjunk" + tag, [64, 64])
        t.em0 = S.sb("mem0" + tag, [64, 4])
        t.n0r = S.sb("mn0r" + tag, [4, 64])
        t.kvs = S.sb("mkvs" + tag, [4, 256])
        t.nls = S.sb("mnls" + tag, [4, 256])
        t.nblc = S.sb("mnblc" + tag, [4, 4])
        t.off = S.sb("moff" + tag, [4, 4])
        t.mx = S.sb("mmx" + tag, [4, 2])
        t.mfin = S.sb("mmfin" + tag, [4, 1])
        t.mrow = S.sb("mmrow" + tag, [1, 4])
        t.d4 = S.sb("md4" + tag, [4, 4])
        t.emf = S.sb("memf" + tag, [64, 4])
        t.so = S.sb("mso" + tag, [64, 4, 64])
        t.ncol = S.sb("mncol" + tag, [64, 4])
        t.nrow = S.sb("mnrow" + tag, [4, 64])
        return t

    def stream(sq_, dr, t, B, fbb, mgb):
        tok0, L = sq_["tok0"], sq_["L"]
        nblk = L // 128
        ci = CONST_ORDER["m_le"] if dr == 0 else CONST_ORDER["m_ge"]
        cum64 = k.cst[0:64, ci, 0:64]
        OWN, OTH = (k.OM, k.OM2) if dr == 0 else (k.OM2, k.OM)
        state = t.state
        S.op("dve", "memset", state[:], 0.0)
        if sq_["sample"]:
            S.dma("sp", t.em0[:], k.st_m[l, dr:dr + 1, :].partition_broadcast(64))
            A(k, out=t.em0[:], in_=t.em0[:], func=AF.Exp)
            S.dma("sp", state[:, :, 0:64], k.st_c[l, dr, :, :, :].rearrange("h a b -> a h b"))
            S.dma("sp", t.n0r[:], k.st_n[l, dr, :, :])
            S.op("pe", "transpose", out=B[0][0:64, 32:36], in_=t.n0r[:], identity=k.cst[0:4, CONST_ORDER["ident"], 0:4])
            V(k, "tensor_copy", out=state[:, :, 64], in_=B[0][0:64, 32:36])
            V(k, "tensor_tensor", out=state[:], in0=state[:], in1=bc3(t.em0[:], 128), op=ALU.mult)
        A(k, out=t.stb[:], in_=state[:], func=AF.Copy)
        blocks = list(range(nblk)) if dr == 0 else list(range(nblk - 1, -1, -1))
        corder = (0, 1) if dr == 0 else (1, 0)
        for step, bi in enumerate(blocks):
            r0 = tok0 + bi * 128
            for c in range(2):
                S.dma("sp", t.gt2[:, c, :], k.TMS[r0 + c * 64:r0 + (c + 1) * 64, TM_OFF["gt"]:TM_OFF["gt"] + 32])
                S.dma("sp", t.ckv[:, c, :], k.TMS[r0 + c * 64:r0 + (c + 1) * 64, TM_OFF["ckv"]:TM_OFF["ckv"] + 512])
            S.dma("sp", t.qk[:], k.CQK[:, :, r0:r0 + 128].rearrange("c (hh p) n -> p c hh n", p=64))
            yield
            kv4 = t.ckv[:].rearrange("p c (x h e) -> p c x h e", x=2, e=64)
            G(k, "tensor_copy", out=t.v1[:, :, 0:64].rearrange("p (c h) e -> p c h e", c=2), in_=kv4[:, :, 1, :, :])
            t3 = t.tg[:].rearrange("p (c h) -> p c h", c=2)
            V(k, "tensor_tensor", out=t3, in0=t.gt2[:, :, 24 + dr * 4:28 + dr * 4], in1=bcm(fbb[:, dr * 4:dr * 4 + 4], 2), op=ALU.add)
            A(k, out=t.tg[:], in_=t.tg[:], func=AF.Exp, scale=-1.0)
            A(k, out=t.nlf[:], in_=t.tg[:], func=AF.Ln, bias=1.0, scale=1.0)
            yield
            pa = B[0]
            for c in range(2):
                MM(k, pa[0:64, c * 4:(c + 1) * 4], lhsT=cum64, rhs=t.nlf[:, c * 4:(c + 1) * 4])
                MM(k, pa[0:64, 8 + c * 4:12 + c * 4], lhsT=O64, rhs=t.nlf[:, c * 4:(c + 1) * 4])
            yield
            V(k, "tensor_copy", out=t.nbs[:], in_=pa[0:64, 0:8])
            V(k, "tensor_tensor", out=t.t4[:], in0=t.nbs[:], in1=pa[0:64, 8:16], op=ALU.subtract)
            ig3 = t.gt2[:, :, 16 + dr * 4:20 + dr * 4]
            V(k, "tensor_tensor", out=t.t4b[:].rearrange("p (c h) -> p c h", c=2), in0=t.t4[:].rearrange("p (c h) -> p c h", c=2), in1=ig3, op=ALU.add)
            A(k, out=t.wk[:], in_=t.t4b[:], func=AF.Exp, bias=LN8, scale=1.0)
            V(k, "tensor_tensor", out=t.colE[:].rearrange("p (c h) -> p c h", c=2), in0=t.nbs[:].rearrange("p (c h) -> p c h", c=2), in1=ig3, op=ALU.add)
            V(k, "tensor_copy", out=t.dec[:], in_=pa[0:64, 8:16])
            A(k, out=t.dec[:], in_=t.dec[:], func=AF.Exp, scale=-1.0)
            V(k, "tensor_tensor", out=t.diag[:], in0=bcm(I64, 8), in1=bc3(t.nbs[:], 64), op=ALU.mult)
            yield
            if not sq_["sample"]:
                for c in range(2):
                    cc = bi * 2 + c
                    S.op("pe", "transpose", out=B[0][0:4, 256:320], in_=t.t4b[:, c * 4:(c + 1) * 4], identity=I64)
                    S.op("pe", "transpose", out=B[0][0:4, 384:448], in_=t.nlf[:, c * 4:(c + 1) * 4], identity=I64)
                    V(k, "tensor_copy", out=t.kvs[:, cc * 64:(cc + 1) * 64], in_=B[0][0:4, 256:320])
                    V(k, "tensor_copy", out=t.nls[:, cc * 64:(cc + 1) * 64], in_=B[0][0:4, 384:448])
            pb, pc = B[1], B[2]
            for pi in range(8):
                MM(k, pb[0:64, pi * 64:(pi + 1) * 64], lhsT=O64, rhs=t.diag[:, pi, :])
            for c in range(2):
                for h in range(4):
                    pi = c * 4 + h
                    MM(k, pc[0:64, pi * 64:(pi + 1) * 64], lhsT=t.qk[:, 2 + h // 2, h % 2, c * 64:(c + 1) * 64],
                       rhs=t.qk[:, h // 2, h % 2, c * 64:(c + 1) * 64])
            yield
            pb3 = r3(pb[0:64, :])
            V(k, "tensor_tensor", out=t.X[:], in0=bc3(t.colE[:], 64), in1=pb3, op=ALU.subtract)
            A(k, out=t.ebq[:], in_=pb3, func=AF.Exp, scale=-1.0)
            yield
            A(k, out=t.ET[:], in_=t.X[:], func=AF.Exp)
            q4 = t.qk[:, 0:2, :, :].rearrange("p a hh (c n) -> p c (a hh) n", c=2)
            G(k, "tensor_tensor", out=t.qb[:].rearrange("p (c h) n -> p c h n", c=2), in0=q4,
              in1=t.ebq[:].rearrange("p (c h) n -> p c h n", c=2), op=ALU.mult)
            V(k, "tensor_tensor", out=t.kw[:].rearrange("p (c h) e -> p c h e", c=2), in0=kv4[:, :, 0, :, :],
              in1=bc3(t.wk[:], 64).rearrange("p (c h) e -> p c h e", c=2), op=ALU.mult)
            yield
            G(k, "tensor_tensor", out=t.ET[:], in0=t.ET[:], in1=bcm(cum64, 8), op=ALU.mult)
            yield
            V(k, "scalar_tensor_tensor", out=t.sT[:], in0=r3(pc[0:64, :]), scalar=0.125, in1=t.ET[:], op0=ALU.mult, op1=ALU.mult)
            yield
            for c in corder:
                po, pst = B[3], B[1]
                for h in range(4):
                    pi = c * 4 + h
                    for (a0_, a1_) in ((0, 64), (64, 128)):
                        MM(k, po[0:64, h * 128 + a0_:h * 128 + a1_], lhsT=t.qb[:, pi, :], rhs=t.stb[:, h, a0_:a1_], start=True, stop=False)
                        MM(k, po[0:64, h * 128 + a0_:h * 128 + a1_], lhsT=t.sT[:, pi, :], rhs=t.v1[:, pi, a0_:a1_], start=False, stop=True)
                    MM(k, pst[0:64, h * 128:(h + 1) * 128], lhsT=t.kw[:, pi, :], rhs=t.v1[:, pi, :])
                yield
                V(k, "tensor_tensor", out=state[:], in0=state[:], in1=bc3(t.dec[:, c * 4:(c + 1) * 4], 128), op=ALU.mult)
                V(k, "tensor_tensor", out=state[:], in0=state[:], in1=pst[0:64, :].rearrange("p (h e) -> p h e", e=128), op=ALU.add)
                A(k, out=t.stb[:], in_=state[:], func=AF.Copy)
                rc = r0 + c * 64
                po3 = po[0:64, :].rearrange("p (h e) -> p h e", e=128)
                A(k, out=t.den[:], in_=po3[:, :, 64], func=AF.Abs)
                yield
                V(k, "tensor_scalar", out=t.den[:], in0=t.den[:], scalar1=1.0, scalar2=None, op0=ALU.max)
                V(k, "reciprocal", out=t.den[:], in_=t.den[:])
                V(k, "tensor_tensor", out=t.hm[:].rearrange("p (h e) -> p h e", e=64), in0=po3[:, :, 0:64], in1=bc3(t.den[:], 64), op=ALU.mult)
                if step < nblk // 2:
                    S.dma("pool", OWN[rc:rc + 64, :], t.hm[:])
                    yield
                else:
                    S.dma("sp", t.omf[:], OTH[rc:rc + 64, :])
                    S.dma("sp", t.og[:], k.TMS[rc:rc + 64, TM_OFF["og"]:TM_OFF["og"] + 512])
                    yield
                    V(k, "tensor_tensor", out=t.hm[:], in0=t.hm[:], in1=t.omf[:], op=ALU.add)
                    S.op("dve", "memset", t.ss4[:], 0.0)
                    for h in range(4):
                        A(k, out=t.junk[:], in_=t.hm[:, h * 64:(h + 1) * 64], func=AF.Square, accum_out=t.ss4[:, h:h + 1])
                    A(k, out=t.ss4[:], in_=t.ss4[:], func=AF.Sqrt, bias=EPS, scale=1.0 / 64)
                    V(k, "reciprocal", out=t.ss4[:], in_=t.ss4[:])
                    yield
                    for h in range(4):
                        V(k, "scalar_tensor_tensor", out=t.tt[:, h * 64:(h + 1) * 64], in0=t.hm[:, h * 64:(h + 1) * 64],
                          scalar=t.ss4[:, h:h + 1], in1=mgb[:], op0=ALU.mult, op1=ALU.mult)
                    A(k, out=t.sig[:], in_=t.og[:, 256:512], func=AF.Sigmoid)
                    V(k, "tensor_tensor", out=t.tt[:], in0=t.tt[:], in1=t.sig[:], op=ALU.mult)
                    yield
                    for j in range(2):
                        S.op("pe", "transpose", out=B[2][:, j * 64:(j + 1) * 64], in_=t.tt[:, j * 128:(j + 1) * 128], identity=I64)
                    yield
                    V(k, "tensor_copy", out=k.BIG[:, 6:8, rc:rc + 64], in_=B[2][:, 0:128].rearrange("p (j n) -> p j n", n=64))
        if not sq_["sample"]:
            p = sq_["p"]
            V(k, "reduce_sum", out=t.nblc[:], in_=t.nls[:].rearrange("p (c s) -> p c s", s=64), axis=AX.X)
            nch = L // 64
            S.op("dve", "memset", t.off[:], 0.0)
            if dr == 0:
                for c in range(nch - 2, -1, -1):
                    V(k, "tensor_tensor", out=t.off[:, c:c + 1], in0=t.off[:, c + 1:c + 2], in1=t.nblc[:, c + 1:c + 2], op=ALU.subtract)
            else:
                for c in range(1, nch):
                    V(k, "tensor_tensor", out=t.off[:, c:c + 1], in0=t.off[:, c - 1:c], in1=t.nblc[:, c - 1:c], op=ALU.subtract)
            V(k, "tensor_tensor", out=t.kvs[:].rearrange("p (c s) -> p c s", s=64), in0=t.kvs[:].rearrange("p (c s) -> p c s", s=64),
              in1=bc3(t.off[:], 64), op=ALU.add)
            V(k, "reduce_max", out=t.mx[:, 0:1], in_=t.kvs[:], axis=AX.X)
            V(k, "reduce_sum", out=t.mx[:, 1:2], in_=t.nblc[:], axis=AX.X)
            V(k, "scalar_tensor_tensor", out=t.mfin[:], in0=t.mx[:, 1:2], scalar=-1.0, in1=t.mx[:, 0:1], op0=ALU.mult, op1=ALU.max)
            yield
            S.op("pe", "transpose", out=B[0][0:1, 40:44], in_=t.mfin[:], identity=k.cst[0:4, CONST_ORDER["ident"], 0:4])
            V(k, "tensor_copy", out=t.mrow[:], in_=B[0][0:1, 40:44])
            S.dma("pool", k.o_m[p, l, dr:dr + 1, :], t.mrow[:])
            V(k, "tensor_scalar", out=t.d4[:], in0=k.cst[0:4, CONST_ORDER["ident"], 0:4], scalar1=t.mfin[:, 0:1], scalar2=None, op0=ALU.mult)
            MM(k, B[0][0:64, 16:20], lhsT=k.cst[0:4, CONST_ORDER["ones"], 0:64], rhs=t.d4[:])
            yield
            V(k, "tensor_copy", out=t.emf[:], in_=B[0][0:64, 16:20])
            A(k, out=t.emf[:], in_=t.emf[:], func=AF.Exp, scale=-1.0)
            V(k, "tensor_tensor", out=t.so[:], in0=state[:, :, 0:64], in1=bc3(t.emf[:], 64), op=ALU.mult)
            S.dma("pool", k.o_C[p, l, dr, :, :, :].rearrange("h a b -> a h b"), t.so[:])
            V(k, "tensor_tensor", out=t.ncol[:], in0=state[:, :, 64], in1=t.emf[:], op=ALU.mult)
            S.op("pe", "transpose", out=B[0][0:4, 64:128], in_=t.ncol[:], identity=I64)
            yield
            V(k, "tensor_copy", out=t.nrow[:], in_=B[0][0:4, 64:128])
            S.dma("pool", k.o_n[p, l, dr, :, :], t.nrow[:])

    with S.scope():
        fbb = S.sb("fbb", [64, 8])
        S.dma("sp", fbb[:], k.f_b[l:l + 1, :].partition_broadcast(64))
        mgb = S.sb("mgb", [64, 64])
        S.dma("sp", mgb[:], k.mn_g[l:l + 1, :].partition_broadcast(64))
        tiles = [alloc("f"), alloc("b")]
        for sq_ in SEQS:
            run_streams([stream(sq_, dr, tiles[dr], k.ps[dr * 4:dr * 4 + 4], fbb, mgb) for dr in range(2)])


def phase_delta(k, l):
    phase_delta_prep(k, l)
    phase_delta_scan(k, l)


def phase_delta_prep(k, l):
    S = k.S
    with S.scope():
        cw = S.sb("cw", [128, DEPTH, 6, 5])
        S.dma("sp", cw[:], k.conv_w[:])
        xin = S.sb("dxin", [128, 6, 516])
        acc = S.sb("dacc", [128, 6, 512])
        sact = S.sb("dsact", [128, 6, 512])
        sqb = S.sb("dsq", [128, 512])
        rin = S.sb("drin", [128, 512])
        tok = S.sb("dtok", [128, 512])
        for sq_ in SEQS:
            tok0, L = sq_["tok0"], sq_["L"]
            W = min(512, L)
            for ti in range(L // W):
                t0 = tok0 + ti * W
                lo, hi = max(tok0, t0 - 2), min(tok0 + L, t0 + W + 2)
                k.S.op("dve", "memset", xin[:], 0.0)
                S.dma("sp", xin[:, :, lo - (t0 - 2):hi - (t0 - 2)], k.BT[:, :, lo:hi].rearrange("c p n -> p c n"))
                for c in range(6):
                    eng = "dve"
                    S.op(eng, "tensor_scalar", out=acc[:, c, 0:W], in0=xin[:, c, 0:W], scalar1=cw[:, l, c, 0:1], scalar2=None, op0=ALU.mult)
                    for j in range(1, 5):
                        S.op(eng, "scalar_tensor_tensor", out=acc[:, c, 0:W], in0=xin[:, c, j:j + W], scalar=cw[:, l, c, j:j + 1],
                             in1=acc[:, c, 0:W], op0=ALU.mult, op1=ALU.add)
                for c in range(6):
                    A(k, out=sact[:, c, 0:W], in_=acc[:, c, 0:W], func=AF.Silu)
                for c in range(4):
                    G(k, "tensor_tensor", out=sqb[:, 0:W], in0=sact[:, c, 0:W], in1=sact[:, c, 0:W], op=ALU.mult)
                    p = k.ps[c % 2]
                    MM(k, p[:, 0:W], lhsT=k.C("bd"), rhs=sqb[:, 0:W])
                    A(k, out=rin[:, 0:W], in_=p[:, 0:W], func=AF.Sqrt, bias=EPS, scale=1.0)
                    V(k, "reciprocal", out=rin[:, 0:W], in_=rin[:, 0:W])
                    V(k, "scalar_tensor_tensor", out=sact[:, c, 0:W], in0=sact[:, c, 0:W], scalar=(0.125 if c < 2 else 1.0), in1=rin[:, 0:W],
                      op0=ALU.mult, op1=ALU.mult)
                    h0 = (c // 2) * 4 + (c % 2) * 2
                    S.dma("pool", k.DQK[h0:h0 + 2, :, t0:t0 + W].rearrange("h p n -> (h p) n"), sact[:, c, 0:W])
                for b in range(W // 128):
                    p = k.ps[2 + b % 2]
                    for j, c in enumerate((2, 3, 4, 5)):
                        S.op("pe", "transpose", out=p[:, j * 128:(j + 1) * 128], in_=sact[:, c, b * 128:(b + 1) * 128], identity=k.C("ident"))
                    V(k, "tensor_copy", out=tok[:], in_=p[:])
                    S.dma("pool", k.DKV[t0 + b * 128:t0 + (b + 1) * 128, :], tok[:])


def phase_delta_scan(k, l):
    S = k.S
    I64 = k.cst[0:64, CONST_ORDER["ident"], 0:64]
    O64 = k.cst[0:64, CONST_ORDER["ones"], 0:64]

    def bc3(ap2, n):
        return ap2.unsqueeze(2).to_broadcast([ap2.shape[0], ap2.shape[1], n])

    def bcm(ap2, n):
        return ap2.unsqueeze(1).to_broadcast([ap2.shape[0], n, ap2.shape[1]])

    def r3(ap):
        return ap.rearrange("p (a n) -> p a n", n=64)

    def alloc(tag):
        t = NS_()
        for nm in ("diag", "X", "N", "TT", "u", "ebq"):
            setattr(t, nm, S.sb("d%s%s" % (nm, tag), [64, 8, 64]))
        for nm in ("P0", "P1", "PT0", "PT1", "TTb", "vb", "kbg", "wT"):
            setattr(t, nm, S.sb("d%s%s" % (nm, tag), [64, 8, 64], DT_REC))
        t.qkb = S.sb("dqkb" + tag, [64, 8, 128], DT_REC)
        t.S4b = S.sb("dS4b" + tag, [64, 4, 64], DT_REC)
        for nm in ("beta", "nbeta", "ng", "tg", "ngc", "egc", "bgs", "kds", "gl"):
            setattr(t, nm, S.sb("d%s%s" % (nm, tag), [64, 8]))
        t.gt2 = S.sb("dgt2" + tag, [64, 2, 32])
        t.dkv = S.sb("ddkv" + tag, [64, 2, 512])
        t.qk = S.sb("dqk" + tag, [64, 8, 128])
        t.S4 = S.sb("dS4" + tag, [64, 4, 64])
        t.oc = S.sb("doc" + tag, [64, 256])
        t.of = S.sb("dof" + tag, [64, 256])
        t.og = S.sb("dog" + tag, [64, 512])
        t.ss4 = S.sb("dss4" + tag, [64, 4])
        t.junk = S.sb("djunk" + tag, [64, 64])
        t.tt = S.sb("dtt" + tag, [64, 256])
        t.sig = S.sb("dsig" + tag, [64, 256])
        if DT_REC == F32:
            t.qkb, t.P0, t.TTb, t.S4b = t.qk, t.N, t.TT, t.S4
        t.qkr = S.sb("dqkr" + tag, [64, 8, 128], BF16)
        t.S4r = S.sb("dS4r" + tag, [64, 4, 64], BF16)
        t.vnr = S.sb("dvnr" + tag, [64, 4, 64], BF16)
        for nm in ("aT", "kdec", "qd"):
            setattr(t, nm, S.sb("d%sr%s" % (nm, tag), [64, 8, 64], BF16))
        return t

    def stream(sq_, dr, t, B, alb, dtb, dgb):
        tok0, L = sq_["tok0"], sq_["L"]
        nblk = L // 128
        ci = CONST_ORDER["m_le"] if dr == 0 else CONST_ORDER["m_ge"]
        si = CONST_ORDER["m_ig"] if dr == 0 else CONST_ORDER["m_il"]
        cum64 = k.cst[0:64, ci, 0:64]
        str64 = k.cst[0:64, si, 0:64]
        OWN, OTH = (k.OD, k.OD2) if dr == 0 else (k.OD2, k.OD)
        if sq_["sample"]:
            S.dma("sp", t.S4[:], k.st_d[l, dr, :, :, :].rearrange("h a b -> a h b"))
        else:
            S.op("dve", "memset", t.S4[:], 0.0)
        A(k, out=t.S4r[:], in_=t.S4[:], func=AF.Copy)
        if DT_REC != F32:
            V(k, "tensor_copy", out=t.S4b[:], in_=t.S4[:])
        blocks = list(range(nblk)) if dr == 0 else list(range(nblk - 1, -1, -1))
        corder = (0, 1) if dr == 0 else (1, 0)
        for step, bi in enumerate(blocks):
            r0 = tok0 + bi * 128
            for c in range(2):
                S.dma("sp", t.gt2[:, c, :], k.TMS[r0 + c * 64:r0 + (c + 1) * 64, TM_OFF["gt"]:TM_OFF["gt"] + 32])
                S.dma("sp", t.dkv[:, c, :], k.DKV[r0 + c * 64:r0 + (c + 1) * 64, :])
            S.dma("sp", t.qk[:], k.DQK[:, :, r0:r0 + 128].rearrange("h p n -> p h n"))
            yield
            G(k, "tensor_copy", out=t.qkr[:], in_=t.qk[:])
            if DT_REC != F32:
                G(k, "tensor_copy", out=t.qkb[:], in_=t.qk[:])
            b3 = t.beta[:].rearrange("p (c h) -> p c h", c=2)
            A(k, out=b3, in_=t.gt2[:, :, 8 + dr * 4:12 + dr * 4], func=AF.Sigmoid)
            V(k, "tensor_scalar", out=t.nbeta[:], in0=t.beta[:], scalar1=-1.0, scalar2=None, op0=ALU.mult)
            t3 = t.tg[:].rearrange("p (c h) -> p c h", c=2)
            V(k, "tensor_tensor", out=t3, in0=t.gt2[:, :, dr * 4:dr * 4 + 4], in1=bcm(dtb[:, dr * 4:dr * 4 + 4], 2), op=ALU.add)
            A(k, out=t.tg[:], in_=t.tg[:], func=AF.Exp)
            A(k, out=t.tg[:], in_=t.tg[:], func=AF.Ln, bias=1.0, scale=1.0)
            V(k, "tensor_tensor", out=t.ng[:].rearrange("p (c h) -> p c h", c=2), in0=t3, in1=bcm(alb[:, dr * 4:dr * 4 + 4], 2), op=ALU.mult)
            yield
            pa = B[0]
            for c in range(2):
                MM(k, pa[0:64, c * 4:(c + 1) * 4], lhsT=cum64, rhs=t.ng[:, c * 4:(c + 1) * 4])
                MM(k, pa[0:64, 8 + c * 4:12 + c * 4], lhsT=O64, rhs=t.ng[:, c * 4:(c + 1) * 4])
            yield
            V(k, "tensor_copy", out=t.ngc[:], in_=pa[0:64, 0:8])
            A(k, out=t.egc[:], in_=t.ngc[:], func=AF.Exp, scale=-1.0)
            V(k, "tensor_tensor", out=t.bgs[:], in0=t.beta[:], in1=t.egc[:], op=ALU.mult)
            V(k, "tensor_tensor", out=t.kds[:], in0=t.ngc[:], in1=pa[0:64, 8:16], op=ALU.subtract)
            A(k, out=t.kds[:], in_=t.kds[:], func=AF.Exp)
            V(k, "tensor_copy", out=t.gl[:], in_=pa[0:64, 8:16])
            A(k, out=t.gl[:], in_=t.gl[:], func=AF.Exp, scale=-1.0)
            V(k, "tensor_tensor", out=t.diag[:], in0=bcm(I64, 8), in1=bc3(t.ngc[:], 64), op=ALU.mult)
            yield
            pb, pc, pd = B[1], B[2], B[3]
            for pi in range(8):
                MM(k, pb[0:64, pi * 64:(pi + 1) * 64], lhsT=O64, rhs=t.diag[:, pi, :])
            for c in range(2):
                for h in range(4):
                    pi = c * 4 + h
                    kTc = t.qkb[:, 4 + h, c * 64:(c + 1) * 64]
                    qTc = t.qkb[:, h, c * 64:(c + 1) * 64]
                    MM(k, pc[0:64, pi * 64:(pi + 1) * 64], lhsT=kTc, rhs=kTc)
                    MM(k, pd[0:64, pi * 64:(pi + 1) * 64], lhsT=t.qkr[:, 4 + h, c * 64:(c + 1) * 64], rhs=t.qkr[:, h, c * 64:(c + 1) * 64])
            yield
            pb3 = r3(pb[0:64, :])
            V(k, "tensor_tensor", out=t.X[:], in0=pb3, in1=bc3(t.ngc[:], 64), op=ALU.subtract)
            A(k, out=t.ebq[:], in_=pb3, func=AF.Exp, scale=-1.0)
            V(k, "tensor_scalar", out=t.diag[:], in0=t.X[:], scalar1=0.0, scalar2=None, op0=ALU.min)
            G(k, "tensor_scalar", out=t.X[:], in0=t.X[:], scalar1=0.0, scalar2=None, op0=ALU.max)
            yield
            A(k, out=t.diag[:], in_=t.diag[:], func=AF.Exp)
            A(k, out=t.X[:], in_=t.X[:], func=AF.Exp, scale=-1.0)
            G(k, "tensor_tensor", out=t.diag[:], in0=t.diag[:], in1=bcm(str64, 8), op=ALU.mult)
            G(k, "tensor_tensor", out=t.X[:], in0=t.X[:], in1=bcm(cum64, 8), op=ALU.mult)
            yield
            V(k, "tensor_tensor", out=t.N[:], in0=r3(pc[0:64, :]), in1=bc3(t.nbeta[:], 64), op=ALU.mult)
            V(k, "tensor_tensor", out=t.N[:], in0=t.N[:], in1=t.diag[:], op=ALU.mult)
            if DT_REC != F32:
                A(k, out=t.P0[:], in_=t.N[:], func=AF.Copy)
            V(k, "tensor_tensor", out=t.aT[:], in0=r3(pd[0:64, :]), in1=t.X[:], op=ALU.mult)
            yield
            pt = B[1]
            for pi in range(8):
                S.op("pe", "transpose", out=pt[0:64, pi * 64:(pi + 1) * 64], in_=t.N[:, pi, :], identity=I64)
            yield
            A(k, out=t.PT0[:], in_=r3(pt[0:64, :]), func=AF.Copy)
            V(k, "tensor_tensor", out=t.TT[:], in0=r3(pt[0:64, :]), in1=bcm(I64, 8), op=ALU.add)
            if DT_REC != F32:
                if DT_REC != F32:
                    A(k, out=t.TTb[:], in_=t.TT[:], func=AF.Copy)
            yield
            Pb, PTb = (t.P0, t.P1), (t.PT0, t.PT1)
            for stp in range(5):
                P, PT = Pb[stp % 2], PTb[stp % 2]
                Pn, PTn = Pb[(stp + 1) % 2], PTb[(stp + 1) % 2]
                p1, p2, p3 = B[2], B[3], B[1]
                for pi in range(8):
                    MM(k, p1[0:64, pi * 64:(pi + 1) * 64], lhsT=PT[:, pi, :], rhs=P[:, pi, :])
                if stp < 4:
                    for pi in range(8):
                        MM(k, p2[0:64, pi * 64:(pi + 1) * 64], lhsT=P[:, pi, :], rhs=PT[:, pi, :])
                yield
                A(k, out=Pn[:], in_=r3(p1[0:64, :]), func=AF.Copy)
                if stp < 4:
                    V(k, "tensor_copy", out=PTn[:], in_=r3(p2[0:64, :]))
                yield
                for pi in range(8):
                    MM(k, p3[0:64, pi * 64:(pi + 1) * 64], lhsT=Pn[:, pi, :], rhs=t.TTb[:, pi, :])
                yield
                V(k, "tensor_tensor", out=t.TT[:], in0=t.TT[:], in1=r3(p3[0:64, :]), op=ALU.add)
                if DT_REC != F32:
                    A(k, out=t.TTb[:], in_=t.TT[:], func=AF.Copy)
                yield
            kv4 = t.dkv[:].rearrange("p c (x h e) -> p c x h e", x=2, e=64)
            V(k, "tensor_tensor", out=t.vb[:].rearrange("p (c h) e -> p c h e", c=2), in0=kv4[:, :, 1, :, :],
              in1=bc3(t.beta[:], 64).rearrange("p (c h) e -> p c h e", c=2), op=ALU.mult)
            V(k, "tensor_tensor", out=t.kbg[:].rearrange("p (c h) e -> p c h e", c=2), in0=kv4[:, :, 0, :, :],
              in1=bc3(t.bgs[:], 64).rearrange("p (c h) e -> p c h e", c=2), op=ALU.mult)
            G(k, "tensor_tensor", out=t.kdec[:].rearrange("p (c h) e -> p c h e", c=2), in0=kv4[:, :, 0, :, :],
              in1=bc3(t.kds[:], 64).rearrange("p (c h) e -> p c h e", c=2), op=ALU.mult)
            G(k, "tensor_tensor", out=t.qd[:].rearrange("p (c h) n -> p c h n", c=2),
              in0=t.qk[:, 0:4, :].rearrange("p h (c n) -> p c h n", c=2), in1=t.ebq[:].rearrange("p (c h) n -> p c h n", c=2), op=ALU.mult)
            yield
            p1, p2 = B[2], B[3]
            for pi in range(8):
                MM(k, p1[0:64, pi * 64:(pi + 1) * 64], lhsT=t.TTb[:, pi, :], rhs=t.vb[:, pi, :])
                MM(k, p2[0:64, pi * 64:(pi + 1) * 64], lhsT=t.kbg[:, pi, :], rhs=t.TTb[:, pi, :])
            yield
            A(k, out=t.u[:], in_=r3(p1[0:64, :]), func=AF.Copy)
            V(k, "tensor_copy", out=t.wT[:], in_=r3(p2[0:64, :]))
            yield
            for c in corder:
                px, py, pz = B[1], B[2], B[3]
                for h in range(4):
                    MM(k, px[0:64, h * 64:(h + 1) * 64], lhsT=t.wT[:, c * 4 + h, :], rhs=t.S4b[:, h, :])
                yield
                V(k, "tensor_tensor", out=t.vnr[:], in0=t.u[:, c * 4:(c + 1) * 4, :], in1=r3(px[0:64, 0:256]), op=ALU.subtract)
                yield
                for h in range(4):
                    MM(k, py[0:64, h * 64:(h + 1) * 64], lhsT=t.qd[:, c * 4 + h, :], rhs=t.S4r[:, h, :], start=True, stop=False)
                    MM(k, py[0:64, h * 64:(h + 1) * 64], lhsT=t.aT[:, c * 4 + h, :], rhs=t.vnr[:, h, :], start=False, stop=True)
                    MM(k, pz[0:64, h * 64:(h + 1) * 64], lhsT=t.kdec[:, c * 4 + h, :], rhs=t.vnr[:, h, :])
                yield
                V(k, "tensor_tensor", out=t.S4[:], in0=t.S4[:], in1=bc3(t.gl[:, c * 4:(c + 1) * 4], 64), op=ALU.mult)
                V(k, "tensor_tensor", out=t.S4[:], in0=t.S4[:], in1=r3(pz[0:64, 0:256]), op=ALU.add)
                A(k, out=t.S4r[:], in_=t.S4[:], func=AF.Copy)
                if DT_REC != F32:
                    G(k, "tensor_copy", out=t.S4b[:], in_=t.S4[:])
                rc = r0 + c * 64
                A(k, out=t.oc[:], in_=py[0:64, 0:256], func=AF.Copy)
                if step < nblk // 2:
                    S.dma("pool", OWN[rc:rc + 64, :], t.oc[:])
                    yield
                else:
                    S.dma("sp", t.of[:], OTH[rc:rc + 64, :])
                    S.dma("sp", t.og[:], k.TMS[rc:rc + 64, TM_OFF["og"]:TM_OFF["og"] + 512])
                    yield
                    V(k, "tensor_tensor", out=t.oc[:], in0=t.oc[:], in1=t.of[:], op=ALU.add)
                    S.op("dve", "memset", t.ss4[:], 0.0)
                    for h in range(4):
                        A(k, out=t.junk[:], in_=t.oc[:, h * 64:(h + 1) * 64], func=AF.Square, accum_out=t.ss4[:, h:h + 1])
                    A(k, out=t.ss4[:], in_=t.ss4[:], func=AF.Sqrt, bias=EPS, scale=1.0 / 64)
                    V(k, "reciprocal", out=t.ss4[:], in_=t.ss4[:])
                    yield
                    for h in range(4):
                        V(k, "scalar_tensor_tensor", out=t.tt[:, h * 64:(h + 1) * 64], in0=t.oc[:, h * 64:(h + 1) * 64],
                          scalar=t.ss4[:, h:h + 1], in1=dgb[:], op0=ALU.mult, op1=ALU.mult)
                    A(k, out=t.sig[:], in_=t.og[:, 0:256], func=AF.Silu)
                    V(k, "tensor_tensor", out=t.tt[:], in0=t.tt[:], in1=t.sig[:], op=ALU.mult)
                    yield
                    for j in range(2):
                        S.op("pe", "transpose", out=B[0][:, 128 + j * 64:128 + (j + 1) * 64], in_=t.tt[:, j * 128:(j + 1) * 128], identity=I64)
                    yield
                    V(k, "tensor_copy", out=k.BIG[:, 4:6, rc:rc + 64], in_=B[0][:, 128:256].rearrange("p (j n) -> p j n", n=64))
        if not sq_["sample"]:
            S.dma("pool", k.o_S[sq_["p"], l, dr, :, :, :].rearrange("h a b -> a h b"), t.S4[:])

    with S.scope():
        alb = S.sb("alb", [64, 8])
        S.dma("sp", alb[:], k.a_log[l:l + 1, :].partition_broadcast(64))
        A(k, out=alb[:], in_=alb[:], func=AF.Exp)
        dtb = S.sb("dtb", [64, 8])
        S.dma("sp", dtb[:], k.dt_b[l:l + 1, :].partition_broadcast(64))
        dgb = S.sb("dgb", [64, 64])
        S.dma("sp", dgb[:], k.dn_g[l:l + 1, :].partition_broadcast(64))
        tiles = [alloc("f"), alloc("b")]
        for sq_ in SEQS:
            run_streams([stream(sq_, dr, tiles[dr], k.ps[dr * 4:dr * 4 + 4], alb, dtb, dgb) for dr in range(2)])


def swap_pairs(n):
    idx = np.arange(n)
    return idx ^ 1


def prep_shared(inp):
    f = np.float32
    sh = {}
    w_in = inp["w_in"]
    b_in = inp["b_in"]
    sw = swap_pairs(512)
    w_ext = np.concatenate([w_in, w_in[:, :, O_AQ:O_AQ + 512][:, :, sw], w_in[:, :, O_AK:O_AK + 512][:, :, sw]], axis=2)
    b_ext = np.concatenate([b_in, b_in[:, O_AQ:O_AQ + 512][:, sw], b_in[:, O_AK:O_AK + 512][:, sw]], axis=1)
    sh["w_in"] = np.ascontiguousarray(w_ext, dtype=f)
    sh["w_mod"] = inp["w_mod"]
    sh["w_out"] = inp["w_out"]
    sh["w_gu"] = inp["w_gate_up"]
    sh["w_dn"] = inp["w_down"]
    cm, ct, st = host_consts()
    sh["consts"], sh["rope_c"], sh["rope_s"] = cm, ct, st

    def fm(v):
        d, n = v.shape
        return np.ascontiguousarray(v.reshape(d, n // 128, 128).transpose(2, 0, 1), dtype=f)

    sh["n1g"] = fm(inp["norm1_g"])
    sh["n2g"] = fm(inp["norm2_g"])
    sh["fng"] = np.ascontiguousarray(inp["final_norm_g"].reshape(1, D), dtype=f)
    sh["b_mod"] = fm(inp["b_mod"])
    sh["b_fm"] = np.ascontiguousarray(
        np.stack([b_ext[:, c:c + 128] for c in FM_CHUNKS], axis=1).transpose(2, 0, 1), dtype=f)
    tm_cols = np.concatenate([np.arange(c0, c0 + w) for _, cols in TM_GROUPS for (c0, w) in cols])
    sh["b_tm"] = np.ascontiguousarray(b_ext[:, tm_cols].reshape(1, DEPTH, TM_W), dtype=f)
    cw = inp["delta_conv_w"]
    sh["conv_w"] = np.ascontiguousarray(cw.reshape(DEPTH, 5, 6, 128).transpose(3, 0, 2, 1), dtype=f)
    sh["lam_qk"] = np.ascontiguousarray(inp["lambda_qk"].reshape(DEPTH, 256), dtype=f)
    sh["subln_g"] = np.ascontiguousarray(inp["attn_subln_g"].T, dtype=f)
    sh["dn_g"] = np.ascontiguousarray(inp["delta_norm_g"], dtype=f)
    sh["mn_g"] = np.ascontiguousarray(inp["mlstm_norm_g"], dtype=f)
    sh["a_log"] = np.ascontiguousarray(inp["delta_A_log"].reshape(DEPTH, 8), dtype=f)
    sh["dt_b"] = np.ascontiguousarray(inp["delta_dt_bias"].reshape(DEPTH, 8), dtype=f)
    sh["f_b"] = np.ascontiguousarray(inp["mlstm_f_bias"].reshape(DEPTH, 8), dtype=f)
    return sh


def prep_core(inp, sh, i):
    f = np.float32
    b = i % 4
    m = dict(sh)
    xp = inp["x_prompt"][2 * i:2 * i + 2].reshape(NPR * LP, D)
    m["x_all"] = np.ascontiguousarray(np.concatenate([inp["x_sample"][b], xp], axis=0), dtype=f)
    m["cache_k"] = np.ascontiguousarray(inp["cache_attn_k"][b].reshape(DEPTH, PAST, 512), dtype=f)
    m["cache_v"] = np.ascontiguousarray(inp["cache_attn_v"][b].reshape(DEPTH, PAST, 512), dtype=f)
    m["st_d"] = np.ascontiguousarray(inp["state_delta"][b], dtype=f)
    m["st_c"] = np.ascontiguousarray(inp["state_mlstm_C"][b], dtype=f)
    m["st_n"] = np.ascontiguousarray(inp["state_mlstm_n"][b], dtype=f)
    m["st_m"] = np.ascontiguousarray(inp["state_mlstm_m"][b], dtype=f)
    cc = np.stack([inp["c"][b], inp["c_ctx"]], axis=-1)
    m["cT"] = np.ascontiguousarray(cc.reshape(KC, 128, 2).transpose(1, 0, 2), dtype=f)
    return m


def build_program(stop_after=None, debug_outs=()):
    k = build(debug_outs)
    DBG["stop"] = stop_after
    try:
        phase_setup(k)
        chk("setup")
        for l in range(DEPTH):
            phase_norm(k, l, 1)
            chk("norm%d" % l)
            phase_inproj(k, l)
            chk("inproj%d" % l)
            phase_attn(k, l)
            chk("attn%d" % l)
            phase_mlstm(k, l)
            chk("mlstm%d" % l)
            phase_delta_prep(k, l)
            chk("dprep%d" % l)
            phase_delta_scan(k, l)
            chk("delta%d" % l)
            phase_outproj(k, l)
            chk("outproj%d" % l)
            phase_norm(k, l, 2)
            phase_ffn(k, l)
            chk("ffn%d" % l)
        phase_final(k)
    except StopBuild:
        pass
    st = k.S.finalize()
    return k, st


_CACHE = {}


def kernel(**inputs):
    inp = {n: np.asarray(v) for n, v in inputs.items()}
    if "prog" not in _CACHE:
        _CACHE["prog"] = build_program()
    k, st = _CACHE["prog"]
    sh = prep_shared(inp)
    in_maps = [prep_core(inp, sh, i) for i in range(8)]
    res = run_bass_kernel_spmd(k.nc, in_maps, core_ids=list(range(8)))
    R = res.results
    B = 16
    y_prompt = np.concatenate([R[i]["y_p"].reshape(NPR, LP, D) for i in range(8)], axis=0)
    y_sample = np.stack([R[b]["y_s"] for b in range(4)], axis=0)
    nk = np.concatenate([R[i]["o_k"] for i in range(8)], axis=0).reshape(B, DEPTH, LP, 4, 2, 64)
    nv = np.concatenate([R[i]["o_v"] for i in range(8)], axis=0).reshape(B, DEPTH, LP, 4, 128)
    nS = np.concatenate([R[i]["o_S"] for i in range(8)], axis=0)
    nC = np.concatenate([R[i]["o_C"] for i in range(8)], axis=0)
    nn = np.concatenate([R[i]["o_n"] for i in range(8)], axis=0)
    nm = np.concatenate([R[i]["o_m"] for i in range(8)], axis=0)
    return (y_prompt.astype(np.float32), y_sample.astype(np.float32), nk.astype(np.float32), nv.astype(np.float32),
            nS.astype(np.float32), nC.astype(np.float32), nn.astype(np.float32), nm.astype(np.float32))
```

```python
import contextlib
import math
import numpy as np
import concourse.bass as bass
import concourse.mybir as mybir
from concourse.bass_utils import run_bass_kernel_spmd

F32 = mybir.dt.float32
BF16 = mybir.dt.bfloat16
AF = mybir.ActivationFunctionType
ALU = mybir.AluOpType
AX = mybir.AxisListType


class SemSlot:
    __slots__ = ("sem", "v")

    def __init__(self):
        self.sem = None
        self.v = 0


class Trk:
    __slots__ = ("name", "lw", "rd", "ldma", "slot", "psum")

    def __init__(self, name):
        self.name = name
        self.psum = False
        self.lw = None
        self.rd = []
        self.ldma = None
        self.slot = {}


class Op:
    __slots__ = ("eng", "meth", "args", "kw", "deps", "isdma", "dtrk", "needinc", "ev")

    def __init__(self, eng, meth, args, kw, isdma=False, dtrk=None):
        self.eng, self.meth, self.args, self.kw = eng, meth, args, kw
        self.deps = []
        self.isdma = isdma
        self.dtrk = dtrk
        self.needinc = isdma
        self.ev = None


WRITE_KEYS = ("out", "accum_out")


class Sched:
    def __init__(self, nc):
        self.nc = nc
        self.ops = []
        self.trk = {}
        self.stack = contextlib.ExitStack()
        self.engs = {"pe": nc.tensor, "dve": nc.vector, "act": nc.scalar,
                     "pool": nc.gpsimd, "sp": nc.sync}
        self.sb_bytes = 0
        self.sb_peak = 0
        self.uid = 0
        self.all_trks = []
        self.scope_trks = [[]]
        self.free_slots = {"hw": [], "sw": []}
        self.bar = []
        self.bar_pending = {e: False for e in self.engs}
        self.last_eng_op = {e: None for e in self.engs}

    def _newtrk(self, tname, name):
        t = Trk(name)
        self.trk[tname] = t
        self.all_trks.append(t)
        self.scope_trks[-1].append(t)
        return t

    def sb(self, name, shape, dtype=F32):
        self.uid += 1
        t = self.stack.enter_context(self.nc.sbuf_tensor("%s_%d" % (name, self.uid), list(shape), dtype))
        self._newtrk(t.name, name)
        n = 1
        for s in shape[1:]:
            n *= s
        self.sb_bytes += n * (2 if dtype == BF16 else 4)
        self.sb_peak = max(self.sb_peak, self.sb_bytes)
        return t

    def ps(self, name, shape, dtype=F32):
        t = self.stack.enter_context(self.nc.psum_tensor(name, list(shape), dtype))
        self._newtrk(t.name, name).psum = True
        return t

    @contextlib.contextmanager
    def scope(self):
        old = self.stack
        self.stack = contextlib.ExitStack()
        self.scope_trks.append([])
        b0 = self.sb_bytes
        try:
            yield
        finally:
            self.stack.close()
            self.stack = old
            self.sb_bytes = b0
            self.barrier()
            for t in self.scope_trks.pop():
                for cls, sl in t.slot.items():
                    self.free_slots[cls].append(sl)

    def barrier(self):
        bar = [o for o in self.last_eng_op.values() if o is not None]
        bar += [t.ldma for t in self.all_trks if t.ldma is not None]
        self.bar = sorted(set(bar))
        for e in self.bar_pending:
            self.bar_pending[e] = True

    def dram(self, name, shape, dtype=F32, kind="Internal", track=True):
        t = self.nc.dram_tensor(name, list(shape), dtype, kind=kind)
        if track:
            self.trk[t.name] = Trk(name)
        return t

    def _tr(self, ap):
        try:
            return self.trk.get(ap.tensor.name)
        except AttributeError:
            return None

    def _record(self, op, reads, writes):
        oid = len(self.ops)
        deps = set()
        for t in reads:
            if t.lw is not None:
                deps.add((t.lw, "raw"))
            if t.psum:
                for r in t.rd:
                    deps.add((r, "rar"))
        for t in writes:
            if t.lw is not None:
                deps.add((t.lw, "waw"))
            for r in t.rd:
                deps.add((r, "war"))
        if op.isdma and op.dtrk.ldma is not None:
            deps.add((op.dtrk.ldma, "raw"))
        if self.bar_pending[op.eng]:
            self.bar_pending[op.eng] = False
            for b in self.bar:
                deps.add((b, "bar"))
        final = {}
        for d, kind in deps:
            dop = self.ops[d]
            if not dop.isdma and not op.isdma and dop.eng == op.eng:
                if op.eng == "pe":
                    continue
            final[d] = True
        latest = {}
        for d in list(final):
            dop = self.ops[d]
            if not dop.isdma and dop.eng in ("pe", "act", "dve"):
                if dop.eng in latest:
                    lo = min(latest[dop.eng], d)
                    latest[dop.eng] = max(latest[dop.eng], d)
                    del final[lo]
                else:
                    latest[dop.eng] = d
        op.deps = sorted(final)
        for d in op.deps:
            self.ops[d].needinc = True
        self.ops.append(op)
        for t in reads:
            t.rd.append(oid)
        for t in writes:
            t.lw = oid
            t.rd = []
        if op.isdma:
            op.dtrk.ldma = oid
            cls = "sw" if op.eng == "pool" else "hw"
            if cls not in op.dtrk.slot:
                op.dtrk.slot[cls] = self.free_slots[cls].pop() if self.free_slots[cls] else SemSlot()
            sl = op.dtrk.slot[cls]
            if sl.v >= 30000:
                sl = op.dtrk.slot[cls] = SemSlot()
            sl.v += 16
            op.ev = (sl, sl.v)
        else:
            self.last_eng_op[op.eng] = oid
        return oid

    def op(self, eng, meth, *args, **kw):
        reads, writes = [], []
        names = list(kw.items())
        for i, a in enumerate(args):
            names.append(("out" if i == 0 else "in", a))
        for k, v in names:
            t = self._tr(v) if hasattr(v, "tensor") else None
            if t is None:
                continue
            if k in WRITE_KEYS:
                if t not in writes:
                    writes.append(t)
            elif t not in reads:
                reads.append(t)
        return self._record(Op(eng, meth, args, kw), reads, writes)

    def dma(self, q, out, in_, **kw):
        to, ti = self._tr(out), self._tr(in_)
        dtrk = None
        for ap, t in ((out, to), (in_, ti)):
            if t is not None and not type(ap.tensor).__name__.startswith("DRam"):
                dtrk = t
        if dtrk is None:
            dtrk = to if to is not None else ti
        assert dtrk is not None, "dma with no tracked side"
        o = Op(q, "dma_start", (), dict(out=out, in_=in_, **kw), isdma=True, dtrk=dtrk)
        return self._record(o, [ti] if ti is not None else [], [to] if to is not None else [])

    def finalize(self, final_wait_eng="sp"):
        nc = self.nc
        esem = {e: nc.alloc_semaphore("es_" + e) for e in self.engs}
        ecnt = {e: 0 for e in self.engs}
        known = {e: {} for e in self.engs}
        nwait = 0
        nroll = 0
        for op in self.ops:
            eng = self.engs[op.eng]
            kn = known[op.eng]
            need = {}
            for d in op.deps:
                sem, val = self.ops[d].ev
                if isinstance(sem, SemSlot):
                    if sem.sem is None:
                        sem.sem = nc.alloc_semaphore("ds%d" % id(sem))
                    sem = sem.sem
                k = id(sem)
                if kn.get(k, 0) >= val:
                    continue
                if k not in need or need[k][1] < val:
                    need[k] = (sem, val)
            for k, (sem, val) in need.items():
                eng.wait_ge(sem, val)
                kn[k] = val
                nwait += 1
            ins = getattr(eng, op.meth)(*op.args, **op.kw)
            if op.isdma:
                slot = op.ev[0]
                if slot.sem is None:
                    slot.sem = nc.alloc_semaphore("ds%d" % id(slot))
                ins.then_inc(slot.sem, 16)
            elif op.needinc:
                if ecnt[op.eng] >= 30000:
                    nroll += 1
                    esem[op.eng] = nc.alloc_semaphore("es_%s_%d" % (op.eng, nroll))
                    ecnt[op.eng] = 0
                ecnt[op.eng] += 1
                ins.then_inc(esem[op.eng], 1)
                op.ev = (esem[op.eng], ecnt[op.eng])
        eng = self.engs[final_wait_eng]
        seen = set()
        for t in self.all_trks:
            for sl in t.slot.values():
                if sl.sem is not None and id(sl) not in seen:
                    seen.add(id(sl))
                    eng.wait_ge(sl.sem, sl.v)
        for e in self.engs:
            if ecnt[e] and e != final_wait_eng:
                eng.wait_ge(esem[e], ecnt[e])
        self.stats = dict(n_ops=len(self.ops), n_wait=nwait, ecnt=dict(ecnt), sb_peak=self.sb_peak, nsem=len(seen) + 5)
        return self.stats


D = 1024
KC = 8
LS = 4096
LP = 256
NPR = 2
T = LS + NPR * LP
NT = T // 512
NB = T // 128
PAST = 512
DEPTH = 2
FH = 2816
FC = FH // 128
EPS = 1e-6
O_AQ, O_AK, O_AV = 0, 512, 1024
O_BQ, O_BK, O_BV, O_BG, O_BA, O_BB = 1536, 1792, 2048, 2304, 2560, 2568
O_CQ, O_CK, O_CV, O_CO, O_CI, O_CF = 2576, 2832, 3088, 3344, 3600, 3608
O_AQS, O_AKS = 3616, 4128
WIN = 4640
FM_CHUNKS = ([O_AQ + 128 * i for i in range(4)] + [O_AK + 128 * i for i in range(4)]
             + [O_BQ + 128 * i for i in range(6)] + [O_CQ + 128 * i for i in range(4)]
             + [O_AQS + 128 * i for i in range(4)] + [O_AKS + 128 * i for i in range(4)])
TM_GROUPS = [
    ("av", [(O_AV, 512)]),
    ("ckv", [(O_CK, 256), (O_CV, 256)]),
    ("og", [(O_BG, 256), (O_CO, 256)]),
    ("gt", [(O_BA, 16), (O_CI, 16)]),
    ("ak", [(O_AK, 512)]),
]
TM_OFF = {}
_o = 0
for _n, _cols in TM_GROUPS:
    TM_OFF[_n] = _o
    _o += sum(w for _, w in _cols)
TM_W = _o


class StopBuild(Exception):
    pass


DBG = {}


def chk(name):
    if DBG.get("stop") == name:
        raise StopBuild(name)


def lam_init_of(l):
    return 0.8 - 0.6 * math.exp(-0.3 * l)


def host_consts():
    i = np.arange(128)
    same = (i[:, None] // 64) == (i[None, :] // 64)
    c = {}
    c["ident"] = np.eye(128, dtype=np.float32)
    c["ones"] = np.ones((128, 128), np.float32)
    c["bd"] = same.astype(np.float32)
    c["m_ig"] = (same & (i[:, None] > i[None, :])).astype(np.float32)
    c["m_il"] = (same & (i[:, None] < i[None, :])).astype(np.float32)
    c["m_le"] = (same & (i[:, None] <= i[None, :])).astype(np.float32)
    c["m_ge"] = (same & (i[:, None] >= i[None, :])).astype(np.float32)
    order = ["ident", "ones", "bd", "m_ig", "m_il", "m_le", "m_ge"]
    cm = np.stack([c[k] for k in order], axis=1)
    t = np.arange(LS)
    rows = (t // 64).astype(np.float64)
    cols = (t % 64).astype(np.float64)
    nf = 16
    inv = 10000.0 ** (-np.arange(nf, dtype=np.float64) / nf)
    ang = np.concatenate([rows[:, None] * inv, cols[:, None] * inv], axis=-1)
    ang = ang.astype(np.float32).astype(np.float64)
    cos = np.cos(ang).astype(np.float32)
    sin = np.sin(ang).astype(np.float32)
    ct = np.zeros((128, LS), np.float32)
    st = np.zeros((128, LS), np.float32)
    for m in range(2):
        for d in range(64):
            ct[m * 64 + d] = cos[:, d // 2]
            st[m * 64 + d] = sin[:, d // 2] * (-1.0 if d % 2 == 0 else 1.0)
    return np.ascontiguousarray(cm), ct, st


CONST_ORDER = {"ident": 0, "ones": 1, "bd": 2, "m_ig": 3, "m_il": 4, "m_le": 5, "m_ge": 6}


class K:
    pass


def build(debug_outs=()):
    nc = bass.Bass("TRN2", target_bir_lowering=False)
    S = Sched(nc)
    k = K()
    k.nc, k.S = nc, S

    def din(name, shape):
        return nc.dram_tensor(name, list(shape), F32, kind="ExternalInput")

    def dout(name, shape):
        return S.dram(name, shape, F32, kind="ExternalOutput")

    k.x_all = din("x_all", [T, D])
    k.cache_k = din("cache_k", [DEPTH, PAST, 512])
    k.cache_v = din("cache_v", [DEPTH, PAST, 512])
    k.st_d = din("st_d", [DEPTH, 2, 4, 64, 64])
    k.st_c = din("st_c", [DEPTH, 2, 4, 64, 64])
    k.st_n = din("st_n", [DEPTH, 2, 4, 64])
    k.st_m = din("st_m", [DEPTH, 2, 4])
    k.cT = din("cT", [128, KC, 2])
    k.w_mod = din("w_mod", [DEPTH, D, 6 * D])
    k.w_in = din("w_in", [DEPTH, D, WIN])
    k.w_out = din("w_out", [DEPTH, D, D])
    k.w_gu = din("w_gu", [DEPTH, D, 2 * FH])
    k.w_dn = din("w_dn", [DEPTH, FH, D])
    k.consts = din("consts", [128, 7, 128])
    k.rope_c = din("rope_c", [128, LS])
    k.rope_s = din("rope_s", [128, LS])
    k.n1g = din("n1g", [128, DEPTH, KC])
    k.n2g = din("n2g", [128, DEPTH, KC])
    k.fng = din("fng", [1, D])
    k.b_mod = din("b_mod", [128, DEPTH, 48])
    k.b_fm = din("b_fm", [128, DEPTH, len(FM_CHUNKS)])
    k.b_tm = din("b_tm", [1, DEPTH, TM_W])
    k.conv_w = din("conv_w", [128, DEPTH, 6, 5])
    k.lam_qk = din("lam_qk", [DEPTH, 256])
    k.subln_g = din("subln_g", [128, DEPTH])
    k.dn_g = din("dn_g", [DEPTH, 64])
    k.mn_g = din("mn_g", [DEPTH, 64])
    k.a_log = din("a_log", [DEPTH, 8])
    k.dt_b = din("dt_b", [DEPTH, 8])
    k.f_b = din("f_b", [DEPTH, 8])
    k.y_s = dout("y_s", [LS, D])
    k.y_p = dout("y_p", [NPR * LP, D])
    k.o_k = dout("o_k", [NPR, DEPTH, LP, 512])
    k.o_v = dout("o_v", [NPR, DEPTH, LP, 512])
    k.o_S = dout("o_S", [NPR, DEPTH, 2, 4, 64, 64])
    k.o_C = dout("o_C", [NPR, DEPTH, 2, 4, 64, 64])
    k.o_n = dout("o_n", [NPR, DEPTH, 2, 4, 64])
    k.o_m = dout("o_m", [NPR, DEPTH, 2, 4])
    k.XT = S.dram("XT", [KC, 128, T])
    k.QT = S.dram("QT", [4, 128, T], BF16)
    k.KT = S.dram("KT", [4, 128, T + PAST], BF16)
    k.VV = S.dram("VV", [T + PAST, 512], BF16)
    k.BT = S.dram("BT", [6, 128, T])
    k.CQK = S.dram("CQK", [4, 128, T])
    k.TMS = S.dram("TMS", [T, TM_W])
    k.DQK = S.dram("DQK", [8, 64, T])
    k.DKV = S.dram("DKV", [T, 512])
    k.OD = S.dram("OD", [T, 256])
    k.OD2 = S.dram("OD2", [T, 256])
    k.OM2 = S.dram("OM2", [T, 256])
    k.OM = S.dram("OM", [T, 256])
    k.HT = S.dram("HT", [FC, 128, T], BF16)
    k.dbg = {}
    for name, shape in debug_outs:
        k.dbg[name] = dout("dbg_" + name, shape)

    k.cst = S.sb("cst", [128, 7, 128])
    k.cstb = S.sb("cstb", [128, 7, 128], BF16)
    S.dma("sp", k.cst[:], k.consts[:])
    S.op("dve", "tensor_copy", out=k.cstb[:], in_=k.cst[:])
    k.C = lambda name: k.cst[:, CONST_ORDER[name], :]
    k.Cb = lambda name: k.cstb[:, CONST_ORDER[name], :]
    k.BIG = S.sb("BIG", [128, KC, T], BF16)
    k.ps = [S.ps("ps%d" % i, [128, 512]) for i in range(8)]
    k.mod = S.sb("mod", [128, DEPTH, 48, 2])
    k.g1 = S.sb("g1", [128, DEPTH, KC, 2])
    k.g2 = S.sb("g2", [128, DEPTH, KC, 2])
    k.bfm = S.sb("bfm", [128, DEPTH, len(FM_CHUNKS)])
    S.dma("sp", k.bfm[:], k.b_fm[:])
    k.btm = S.sb("btm", [1, DEPTH, TM_W], BF16)
    k.ones1 = S.sb("ones1", [1, 128], BF16)
    S.op("dve", "memset", k.ones1[:], 1.0)
    return k


def V(k, meth, **kw):
    return k.S.op("dve", meth, **kw)


def A(k, **kw):
    return k.S.op("act", "activation", **kw)


def G(k, meth, *a, **kw):
    return k.S.op("pool", meth, *a, **kw)


def MM(k, out, lhsT, rhs, start=True, stop=True):
    return k.S.op("pe", "matmul", out, lhsT=lhsT, rhs=rhs, start=start, stop=stop)


def phase_setup(k):
    with k.S.scope():
        _phase_setup(k)


def _phase_setup(k):
    S = k.S
    csil = S.sb("csil", [128, KC, 2])
    ctmp = S.sb("ctmp", [128, KC, 2])
    S.dma("sp", ctmp[:], k.cT[:])
    A(k, out=csil[:], in_=ctmp[:], func=AF.Silu)
    bm = S.sb("bm", [128, DEPTH, 48])
    S.dma("sp", bm[:], k.b_mod[:])
    n1 = S.sb("n1", [128, DEPTH, KC])
    n2 = S.sb("n2", [128, DEPTH, KC])
    S.dma("sp", n1[:], k.n1g[:])
    S.dma("sp", n2[:], k.n2g[:])
    btmf = S.sb("btmf", [1, DEPTH, TM_W])
    S.dma("sp", btmf[:], k.b_tm[:])
    V(k, "tensor_copy", out=k.btm[:], in_=btmf[:])
    xin = [S.sb("xin%d" % i, [128, D]) for i in range(2)]
    xto = [S.sb("xto%d" % i, [128, KC, 128]) for i in range(2)]

    def xblock(b):
        xi = xin[b % 2]
        xo = xto[b % 2]
        S.dma("sp", xi[:], k.x_all[b * 128:(b + 1) * 128, :])
        for half in range(2):
            p = k.ps[1 + (b % 2) * 2 + half]
            for j in range(4):
                kc = half * 4 + j
                S.op("pe", "transpose", out=p[:, j * 128:(j + 1) * 128], in_=xi[:, kc * 128:(kc + 1) * 128],
                     identity=k.C("ident"))
            if half == 0:
                V(k, "tensor_copy", out=xo[:, 0:4, :], in_=p[:].rearrange("p (c n) -> p c n", n=128))
            else:
                A(k, out=xo[:, 4:8, :], in_=p[:].rearrange("p (c n) -> p c n", n=128), func=AF.Copy)
        S.dma("pool", k.XT[:, :, b * 128:(b + 1) * 128].rearrange("c p n -> p c n"), xo[:])

    wst = [S.sb("wmst%d" % i, [128, KC, 768]) for i in range(2)]
    pm = k.ps[0]
    n = 0
    NG = DEPTH * 8
    for l in range(DEPTH):
        for g in range(8):
            w = wst[n % 2]
            n += 1
            S.dma("sp", w[:], k.w_mod[l, :, g * 768:(g + 1) * 768].rearrange("(c p) n -> p c n", p=128))
            for b in range((n - 1) * NB // NG, n * NB // NG):
                xblock(b)
            for j in range(6):
                mc = g * 6 + j
                for kc in range(KC):
                    MM(k, pm[:, mc * 2:mc * 2 + 2], lhsT=w[:, kc, j * 128:(j + 1) * 128], rhs=csil[:, kc, :],
                       start=(kc == 0), stop=(kc == KC - 1))
        for r in range(2):
            V(k, "tensor_tensor", out=k.mod[:, l, :, r], in0=pm[:, 0:96].rearrange("p (c r) -> p c r", r=2)[:, :, r],
              in1=bm[:, l, :], op=ALU.add)
        for r in range(2):
            V(k, "scalar_tensor_tensor", out=k.g1[:, l, :, r], in0=k.mod[:, l, 8:16, r], scalar=1.0, in1=n1[:, l, :],
              op0=ALU.add, op1=ALU.mult)
            V(k, "scalar_tensor_tensor", out=k.g2[:, l, :, r], in0=k.mod[:, l, 32:40, r], scalar=1.0, in1=n2[:, l, :],
              op0=ALU.add, op1=ALU.mult)


def seq_r(tile):
    return 0 if tile < LS // 512 else 1


def phase_norm(k, l, which):
    with k.S.scope():
        S = k.S
        k.xt_buf = [S.sb("xt%d" % i, [128, KC, 512]) for i in range(2)]
        k.sq_buf = S.sb("sq", [128, 2, 512])
        k.rstd_buf = S.sb("rstd", [128, 512])
        k.tmp_buf = [S.sb("tmp%d" % i, [128, 512]) for i in range(2)]
        _phase_norm(k, l, which)


def _phase_norm(k, l, which):
    S = k.S
    gg = k.g1 if which == 1 else k.g2
    sh0 = 0 if which == 1 else 24
    for t in range(NT):
        r = seq_r(t)
        xt = k.xt_buf[t % 2]
        S.dma("sp", xt[:], k.XT[:, :, t * 512:(t + 1) * 512].rearrange("c p n -> p c n"))
        sq = k.sq_buf
        pss = k.ps[t % 2]
        for kc in range(KC):
            A(k, out=sq[:, kc % 2, :], in_=xt[:, kc, :], func=AF.Square)
            MM(k, pss[:], lhsT=k.C("ones"), rhs=sq[:, kc % 2, :], start=(kc == 0), stop=(kc == KC - 1))
        rstd = k.rstd_buf
        A(k, out=k.tmp_buf[0][:], in_=pss[:], func=AF.Sqrt, bias=EPS, scale=1.0 / D)
        V(k, "reciprocal", out=rstd[:], in_=k.tmp_buf[0][:])
        for kc in range(KC):
            tmp = k.tmp_buf[kc % 2]
            V(k, "scalar_tensor_tensor", out=tmp[:], in0=xt[:, kc, :], scalar=gg[:, l, kc, r:r + 1], in1=rstd[:],
              op0=ALU.mult, op1=ALU.mult)
            A(k, out=k.BIG[:, kc, t * 512:(t + 1) * 512], in_=tmp[:], func=AF.Identity,
              bias=k.mod[:, l, sh0 + kc, r:r + 1], scale=1.0)


def load_w_bf16(k, dst, src_ap, stage, eng_i):
    S = k.S
    n = src_ap.shape[-1]
    S.dma("sp", stage[:, :, 0:n], src_ap.rearrange("(c p) n -> p c n", p=128))
    if eng_i % 2 == 0:
        V(k, "tensor_copy", out=dst, in_=stage[:, :, 0:n])
    else:
        G(k, "tensor_copy", out=dst, in_=stage[:, :, 0:n])


def phase_inproj(k, l):
    with k.S.scope():
        S = k.S
        k.tmp_buf = [S.sb("tmp%d" % i, [128, 512]) for i in range(2)]
        k.ob_buf = [S.sb("ob%d" % i, [128, 512], BF16) for i in range(2)]
        k.obf_buf = [S.sb("obf%d" % i, [128, 512]) for i in range(2)]
        k.wfm = [S.sb("wfm%d" % i, [128, KC, 128], BF16) for i in range(2)]
        k.wstage = [S.sb("wstage%d" % i, [128, KC, 256]) for i in range(2)]
        k.wtm = S.sb("wtm", [128, KC, TM_W], BF16)
        k.vb_buf = [S.sb("vb%d" % i, [128, 512], BF16) for i in range(2)]
        k.tmf_buf = [S.sb("tmf%d" % i, [128, 512]) for i in range(2)]
        k.ropec = S.sb("ropec", [128, LS])
        k.ropes = S.sb("ropes", [128, LS])
        S.dma("sp", k.ropec[:], k.rope_c[:])
        S.dma("sp", k.ropes[:], k.rope_s[:])
        _phase_inproj(k, l)


def _phase_inproj(k, l):
    S = k.S
    nfm = len(FM_CHUNKS)
    wfm = k.wfm
    stage = k.wstage
    fm_index = {c: i for i, c in enumerate(FM_CHUNKS)}

    def fm_matmul(col, t, ps):
        for kc in range(KC):
            MM(k, ps[:], lhsT=wcur[:, kc, :], rhs=k.BIG[:, kc, t * 512:(t + 1) * 512], start=(kc == 0), stop=(kc == KC - 1))

    cnt = 0
    for which, o_main, o_sw, dst in (("q", O_AQ, O_AQS, k.QT), ("k", O_AK, O_AKS, k.KT)):
        for h in range(4):
            wm = wfm[0]
            ws = wfm[1]
            load_w_bf16(k, wm[:], k.w_in[l, :, o_main + h * 128:o_main + (h + 1) * 128], stage[0], 0)
            load_w_bf16(k, ws[:], k.w_in[l, :, o_sw + h * 128:o_sw + (h + 1) * 128], stage[1], 1)
            bm = k.bfm[:, l, fm_index[o_main + h * 128]:fm_index[o_main + h * 128] + 1]
            bs = k.bfm[:, l, fm_index[o_sw + h * 128]:fm_index[o_sw + h * 128] + 1]
            for t in range(NT):
                p1 = k.ps[(cnt % 2) * 2]
                p2 = k.ps[(cnt % 2) * 2 + 1]
                ob = k.ob_buf[cnt % 2]
                cnt += 1
                for kc in range(KC):
                    MM(k, p1[:], lhsT=wm[:, kc, :], rhs=k.BIG[:, kc, t * 512:(t + 1) * 512], start=(kc == 0), stop=(kc == KC - 1))
                if seq_r(t) == 0:
                    for kc in range(KC):
                        MM(k, p2[:], lhsT=ws[:, kc, :], rhs=k.BIG[:, kc, t * 512:(t + 1) * 512], start=(kc == 0), stop=(kc == KC - 1))
                    t1 = k.tmp_buf[0]
                    t2 = k.tmp_buf[1]
                    V(k, "scalar_tensor_tensor", out=t1[:], in0=p1[:], scalar=bm, in1=k.ropec[:, t * 512:(t + 1) * 512],
                      op0=ALU.add, op1=ALU.mult)
                    V(k, "scalar_tensor_tensor", out=t2[:], in0=p2[:], scalar=bs, in1=k.ropes[:, t * 512:(t + 1) * 512],
                      op0=ALU.add, op1=ALU.mult)
                    G(k, "tensor_tensor", out=ob[:], in0=t1[:], in1=t2[:], op=ALU.add)
                else:
                    A(k, out=ob[:], in_=p1[:], func=AF.Identity, bias=bm, scale=1.0)
                S.dma("pool", dst[h, :, t * 512:(t + 1) * 512], ob[:])
    for o_main, nch, dst in ((O_BQ, 6, k.BT), (O_CQ, 4, k.CQK)):
        for c in range(nch):
            wm = wfm[cnt % 2]
            load_w_bf16(k, wm[:], k.w_in[l, :, o_main + c * 128:o_main + (c + 1) * 128], stage[cnt % 2], cnt)
            bm = k.bfm[:, l, fm_index[o_main + c * 128]:fm_index[o_main + c * 128] + 1]
            for t in range(NT):
                p1 = k.ps[(cnt % 2) * 2]
                ob = k.obf_buf[cnt % 2]
                cnt += 1
                for kc in range(KC):
                    MM(k, p1[:], lhsT=wm[:, kc, :], rhs=k.BIG[:, kc, t * 512:(t + 1) * 512], start=(kc == 0), stop=(kc == KC - 1))
                A(k, out=ob[:], in_=p1[:], func=AF.Identity, bias=bm, scale=1.0)
                S.dma("pool", dst[c, :, t * 512:(t + 1) * 512], ob[:])
    wtm = k.wtm
    for name, cols in TM_GROUPS:
        o = TM_OFF[name]
        for (c0, w) in cols:
            done = 0
            while done < w:
                ww = min(256, w - done)
                st = stage[cnt % 2]
                cnt += 1
                S.dma("sp", st[:, :, 0:ww], k.w_in[l, :, c0 + done:c0 + done + ww].rearrange("(c p) n -> p c n", p=128))
                V(k, "tensor_copy", out=wtm[:, :, o + done:o + done + ww], in_=st[:, :, 0:ww])
                done += ww
            o += w
    for b in range(NB):
        isprompt = b >= LS // 128
        for gi, (name, cols) in enumerate(TM_GROUPS):
            if name == "ak" and not isprompt:
                continue
            o = TM_OFF[name]
            w = sum(x for _, x in cols)
            p = k.ps[4 + (cnt % 2)]
            cnt += 1
            for kc in range(KC):
                MM(k, p[:, 0:w], lhsT=k.BIG[:, kc, b * 128:(b + 1) * 128], rhs=wtm[:, kc, o:o + w], start=(kc == 0), stop=False)
            MM(k, p[:, 0:w], lhsT=k.ones1[:, :], rhs=k.btm[:, l, o:o + w], start=False, stop=True)
            if name == "av":
                vb = k.vb_buf[b % 2]
                V(k, "tensor_copy", out=vb[:], in_=p[:])
                S.dma("pool", k.VV[b * 128:(b + 1) * 128, :], vb[:])
                if isprompt:
                    vf = k.tmf_buf[cnt % 2]
                    A(k, out=vf[:], in_=p[:], func=AF.Copy)
                    pb = b - LS // 128
                    S.dma("pool", k.o_v[pb // 2, l, (pb % 2) * 128:(pb % 2 + 1) * 128, :], vf[:])
            elif name == "ak":
                vf = k.tmf_buf[cnt % 2]
                A(k, out=vf[:], in_=p[:], func=AF.Copy)
                pb = b - LS // 128
                S.dma("pool", k.o_k[pb // 2, l, (pb % 2) * 128:(pb % 2 + 1) * 128, :], vf[:])
            else:
                vf = k.tmf_buf[cnt % 2]
                A(k, out=vf[:, 0:w], in_=p[:, 0:w], func=AF.Copy)
                S.dma("pool", k.TMS[b * 128:(b + 1) * 128, o:o + w], vf[:, 0:w])


def phase_outproj(k, l):
    S = k.S
    with S.scope():
        wo = S.sb("wo", [128, KC, D], BF16)
        stage = [S.sb("ostage%d" % i, [128, KC, 256]) for i in range(2)]
        for j in range(4):
            load_w_bf16(k, wo[:, :, j * 256:(j + 1) * 256], k.w_out[l, :, j * 256:(j + 1) * 256], stage[j % 2], j)
        xt = [S.sb("oxt%d" % i, [128, KC, 512]) for i in range(2)]
        for t in range(NT):
            r = seq_r(t)
            x = xt[t % 2]
            S.dma("sp", x[:], k.XT[:, :, t * 512:(t + 1) * 512].rearrange("c p n -> p c n"))
            for mc in range(KC):
                p = k.ps[mc % 2]
                for kc in range(KC):
                    MM(k, p[:], lhsT=wo[:, kc, mc * 128:(mc + 1) * 128], rhs=k.BIG[:, kc, t * 512:(t + 1) * 512],
                       start=(kc == 0), stop=(kc == KC - 1))
                V(k, "scalar_tensor_tensor", out=x[:, mc, :], in0=p[:], scalar=k.mod[:, l, 16 + mc, r:r + 1], in1=x[:, mc, :],
                  op0=ALU.mult, op1=ALU.add)
            S.dma("pool", k.XT[:, :, t * 512:(t + 1) * 512].rearrange("c p n -> p c n"), x[:])


def phase_ffn(k, l):
    S = k.S
    with S.scope():
        wg = [S.sb("wg%d" % i, [128, KC, 128], BF16) for i in range(2)]
        wu = [S.sb("wu%d" % i, [128, KC, 128], BF16) for i in range(2)]
        stage = [S.sb("fstage%d" % i, [128, KC, 256]) for i in range(2)]
        sil = [S.sb("sil%d" % i, [128, 512]) for i in range(2)]
        hb = [S.sb("hb%d" % i, [128, 512], BF16) for i in range(2)]
        cnt = 0
        for j in range(FC):
            g, u = wg[j % 2], wu[j % 2]
            load_w_bf16(k, g[:], k.w_gu[l, :, j * 128:(j + 1) * 128], stage[0], 0)
            load_w_bf16(k, u[:], k.w_gu[l, :, FH + j * 128:FH + (j + 1) * 128], stage[1], 1)
            for t in range(NT):
                pg = k.ps[(cnt % 2) * 2]
                pu = k.ps[(cnt % 2) * 2 + 1]
                for kc in range(KC):
                    MM(k, pg[:], lhsT=g[:, kc, :], rhs=k.BIG[:, kc, t * 512:(t + 1) * 512], start=(kc == 0), stop=(kc == KC - 1))
                for kc in range(KC):
                    MM(k, pu[:], lhsT=u[:, kc, :], rhs=k.BIG[:, kc, t * 512:(t + 1) * 512], start=(kc == 0), stop=(kc == KC - 1))
                A(k, out=sil[cnt % 2][:], in_=pg[:], func=AF.Silu)
                V(k, "tensor_tensor", out=hb[cnt % 2][:], in0=sil[cnt % 2][:], in1=pu[:], op=ALU.mult)
                S.dma("pool", k.HT[j, :, t * 512:(t + 1) * 512], hb[cnt % 2][:])
                cnt += 1
    with S.scope():
        wd = S.sb("wd", [128, FC, D], BF16)
        stage = S.sb("dstage", [128, KC, 256])
        n = 0
        for cp in range(4):
            for kr in (0, 8, 16):
                nn = min(8, FC - kr)
                S.dma("sp", stage[:, 0:nn, :], k.w_dn[l, kr * 128:(kr + nn) * 128, cp * 256:(cp + 1) * 256].rearrange("(c p) n -> p c n", p=128))
                if n % 2 == 0:
                    V(k, "tensor_copy", out=wd[:, kr:kr + nn, cp * 256:(cp + 1) * 256], in_=stage[:, 0:nn, :])
                else:
                    G(k, "tensor_copy", out=wd[:, kr:kr + nn, cp * 256:(cp + 1) * 256], in_=stage[:, 0:nn, :])
                n += 1
        hts = [S.sb("ht%d" % i, [128, FC, 512], BF16) for i in range(2)]
        x = S.sb("fxt", [128, KC, 512])
        for t in range(NT):
            r = seq_r(t)
            ht = hts[t % 2]
            for c4 in range(0, FC, 6):
                c5 = min(FC, c4 + 6)
                S.dma("sp", ht[:, c4:c5, :], k.HT[c4:c5, :, t * 512:(t + 1) * 512].rearrange("c p n -> p c n"))
            S.dma("sp", x[:], k.XT[:, :, t * 512:(t + 1) * 512].rearrange("c p n -> p c n"))
            for mc in range(KC):
                p = k.ps[mc % 2]
                for kc in range(FC):
                    MM(k, p[:], lhsT=wd[:, kc, mc * 128:(mc + 1) * 128], rhs=ht[:, kc, :], start=(kc == 0), stop=(kc == FC - 1))
                V(k, "scalar_tensor_tensor", out=x[:, mc, :], in0=p[:], scalar=k.mod[:, l, 40 + mc, r:r + 1], in1=x[:, mc, :],
                  op0=ALU.mult, op1=ALU.add)
            S.dma("pool", k.XT[:, :, t * 512:(t + 1) * 512].rearrange("c p n -> p c n"), x[:])


def phase_final(k):
    S = k.S
    with S.scope():
        fr = S.sb("fr", [1, D])
        S.dma("sp", fr[:], k.fng[:])
        fb = S.sb("fb", [128, D])
        for j in range(2):
            MM(k, k.ps[j][:], lhsT=k.cst[0:1, 1, :], rhs=fr[:, j * 512:(j + 1) * 512])
            V(k, "tensor_copy", out=fb[:, j * 512:(j + 1) * 512], in_=k.ps[j][:])
        xb = [S.sb("yx%d" % i, [128, KC, 128]) for i in range(2)]
        xk = [S.sb("yk%d" % i, [128, D]) for i in range(2)]
        junk = S.sb("yjunk", [128, D])
        ss = S.sb("yss", [128, 2])
        yo = [S.sb("yo%d" % i, [128, D]) for i in range(2)]
        for b in range(NB):
            x = xb[b % 2]
            xt = xk[b % 2]
            S.dma("sp", x[:], k.XT[:, :, b * 128:(b + 1) * 128].rearrange("c p n -> p c n"))
            for half in range(2):
                p = k.ps[2 + (b % 2) * 2 + half]
                for j in range(4):
                    S.op("pe", "transpose", out=p[:, j * 128:(j + 1) * 128], in_=x[:, half * 4 + j, :], identity=k.C("ident"))
                if half == 0:
                    V(k, "tensor_copy", out=xt[:, 0:512], in_=p[:])
                else:
                    A(k, out=xt[:, 512:1024], in_=p[:], func=AF.Copy)
            k.S.op("dve", "memset", ss[:, 0:1], 0.0)
            A(k, out=junk[:], in_=xt[:], func=AF.Square, accum_out=ss[:, 0:1])
            A(k, out=ss[:, 1:2], in_=ss[:, 0:1], func=AF.Sqrt, bias=EPS, scale=1.0 / D)
            V(k, "reciprocal", out=ss[:, 1:2], in_=ss[:, 1:2])
            y = yo[b % 2]
            V(k, "scalar_tensor_tensor", out=y[:], in0=xt[:], scalar=ss[:, 1:2], in1=fb[:], op0=ALU.mult, op1=ALU.mult)
            if b < LS // 128:
                S.dma("pool", k.y_s[b * 128:(b + 1) * 128, :], y[:])
            else:
                pb = b - LS // 128
                S.dma("pool", k.y_p[pb * 128:(pb + 1) * 128, :], y[:])


SEQS = [dict(tok0=0, L=LS, sample=True, p=-1)] + [dict(tok0=LS + i * LP, L=LP, sample=False, p=i) for i in range(NPR)]


def phase_attn(k, l):
    S = k.S
    with S.scope():
        lt = S.sb("lamt", [128, 256])
        S.dma("sp", lt[:], k.lam_qk[l:l + 1, :].partition_broadcast(128))
        lp = S.sb("lamp", [128, 256])
        ls = S.sb("lams", [128, 4])
        V(k, "tensor_tensor", out=lp[:, 0:64], in0=lt[:, 0:64], in1=lt[:, 64:128], op=ALU.mult)
        V(k, "tensor_tensor", out=lp[:, 64:128], in0=lt[:, 128:192], in1=lt[:, 192:256], op=ALU.mult)
        V(k, "reduce_sum", out=ls[:, 0:1], in_=lp[:, 0:64], axis=AX.X)
        V(k, "reduce_sum", out=ls[:, 1:2], in_=lp[:, 64:128], axis=AX.X)
        A(k, out=ls[:, 0:2], in_=ls[:, 0:2], func=AF.Exp)
        V(k, "tensor_tensor", out=ls[:, 2:3], in0=ls[:, 1:2], in1=ls[:, 0:1], op=ALU.subtract)
        V(k, "tensor_scalar", out=ls[:, 3:4], in0=ls[:, 2:3], scalar1=-lam_init_of(l), scalar2=None, op0=ALU.add)
        nlam = ls[:, 3:4]
        sg = S.sb("sublg", [128, DEPTH])
        S.dma("sp", sg[:], k.subln_g[:])
        sgl = S.sb("sublgl", [128, 1])
        V(k, "tensor_scalar", out=sgl[:], in0=sg[:, l:l + 1], scalar1=1.0 - lam_init_of(l), scalar2=None, op0=ALU.mult)
        ckf = S.sb("ckf", [128, 512])
        ckb = S.sb("ckb", [128, 4, 128], BF16)
        cvf = S.sb("cvf", [128, 512])
        cvb = S.sb("cvb", [128, 512], BF16)
        for b in range(PAST // 128):
            S.dma("sp", ckf[:], k.cache_k[l, b * 128:(b + 1) * 128, :])
            for h in range(4):
                S.op("pe", "transpose", out=k.ps[7][:, h * 128:(h + 1) * 128], in_=ckf[:, h * 128:(h + 1) * 128], identity=k.C("ident"))
            V(k, "tensor_copy", out=ckb[:], in_=k.ps[7][:].rearrange("p (h n) -> p h n", n=128))
            S.dma("pool", k.KT[:, :, T + b * 128:T + (b + 1) * 128].rearrange("h p n -> p h n"), ckb[:])
            S.dma("sp", cvf[:], k.cache_v[l, b * 128:(b + 1) * 128, :])
            V(k, "tensor_copy", out=cvb[:], in_=cvf[:])
            S.dma("pool", k.VV[T + b * 128:T + (b + 1) * 128, :], cvb[:])
        ktb = S.sb("ktb", [128, LS + PAST], BF16)
        vsb = S.sb("vsb", [128, (LS + PAST) // 128, 128], BF16)
        qsb = [S.sb("qsb%d" % i, [128, LS], BF16) for i in range(2)]
        for i in range(2):
            S.op("dve", "memset", qsb[i][:], 0.0)
        ptb = [S.sb("ptb%d" % i, [128, 512], BF16) for i in range(6)]
        zacc = [S.sb("zacc%d" % i, [128, 512]) for i in range(4)]
        sbank = [k.ps[0], k.ps[1], k.ps[6], k.ps[7]]
        rz = S.sb("rz", [128, 512])
        a0 = S.sb("a0", [128, 512])
        a1 = S.sb("a1", [128, 512])
        sq = S.sb("asq", [128, 512])
        for sq_ in SEQS:
            tok0, L = sq_["tok0"], sq_["L"]
            QW = min(512, L)
            nkt_own = L // 128
            nkt = nkt_own + (PAST // 128 if sq_["sample"] else 0)
            for h in range(4):
                S.dma("sp", ktb[:, 0:L], k.KT[h, :, tok0:tok0 + L])
                for t4 in range(0, nkt_own, 4):
                    t5 = min(nkt_own, t4 + 4)
                    S.dma("sp", vsb[:, t4:t5, :], k.VV[tok0 + t4 * 128:tok0 + t5 * 128, h * 128:(h + 1) * 128].rearrange("(t p) e -> p t e", p=128))
                if sq_["sample"]:
                    S.dma("sp", ktb[:, L:L + PAST], k.KT[h, :, T:T + PAST])
                    S.dma("sp", vsb[:, nkt_own:nkt, :], k.VV[T:T + PAST, h * 128:(h + 1) * 128].rearrange("(t p) e -> p t e", p=128))
                for m in range(2):
                    S.dma("sp", qsb[m][m * 64:(m + 1) * 64, 0:L], k.QT[h, m * 64:(m + 1) * 64, tok0:tok0 + L])
                for qt in range(L // QW):
                    units = [(m, kt) for m in range(2) for kt in range(nkt)]
                    NS = len(sbank)
                    NU = len(units)

                    def qk_mm(i):
                        m, kt = units[i]
                        MM(k, sbank[i % NS][:, 0:QW], lhsT=ktb[:, kt * 128:(kt + 1) * 128], rhs=qsb[m][:, qt * QW:(qt + 1) * QW])

                    for i in range(min(NS, NU)):
                        qk_mm(i)
                    first_pv = [True, True]
                    zused = set()
                    npv = [0, 0]
                    for i0 in range(0, NU, 2):
                        grp = [i for i in (i0, i0 + 1) if i < NU]
                        for i in grp:
                            m, kt = units[i]
                            pt = ptb[i % len(ptb)]
                            A(k, out=pt[:, 0:QW], in_=sbank[i % NS][:, 0:QW], func=AF.Exp, scale=0.125)
                            par = kt % 3
                            if par == 0:
                                MM(k, k.ps[4 + m][:, 0:QW], lhsT=k.Cb("ones"), rhs=pt[:, 0:QW], start=(kt == 0), stop=False)
                            else:
                                eng = "dve" if par == 1 else "pool"
                                za = zacc[m * 2 + par - 1]
                                if kt < 3:
                                    S.op(eng, "tensor_copy", out=za[:, 0:QW], in_=pt[:, 0:QW])
                                    zused.add(m * 2 + par - 1)
                                else:
                                    S.op(eng, "tensor_tensor", out=za[:, 0:QW], in0=za[:, 0:QW], in1=pt[:, 0:QW], op=ALU.add)
                        for i in reversed(grp):
                            m, kt = units[i]
                            pt = ptb[i % len(ptb)]
                            npv[m] += 1
                            MM(k, k.ps[2 + m][:, 0:QW], lhsT=vsb[:, kt, :], rhs=pt[:, 0:QW], start=first_pv[m], stop=(npv[m] == nkt))
                            first_pv[m] = False
                        for i in grp:
                            if i + NS < NU:
                                qk_mm(i + NS)
                    for m in range(2):
                        zl = [z for z in (m * 2, m * 2 + 1) if z in zused]
                        for zi, z in enumerate(zl):
                            MM(k, k.ps[4 + m][:, 0:QW], lhsT=k.C("ones"), rhs=zacc[z][:, 0:QW], start=False, stop=(zi == len(zl) - 1))
                    V(k, "reciprocal", out=rz[:, 0:QW], in_=k.ps[4][:, 0:QW])
                    V(k, "tensor_tensor", out=a0[:, 0:QW], in0=k.ps[2][:, 0:QW], in1=rz[:, 0:QW], op=ALU.mult)
                    V(k, "reciprocal", out=rz[:, 0:QW], in_=k.ps[5][:, 0:QW])
                    V(k, "tensor_tensor", out=a1[:, 0:QW], in0=k.ps[3][:, 0:QW], in1=rz[:, 0:QW], op=ALU.mult)
                    V(k, "scalar_tensor_tensor", out=a0[:, 0:QW], in0=a1[:, 0:QW], scalar=nlam, in1=a0[:, 0:QW], op0=ALU.mult, op1=ALU.add)
                    G(k, "tensor_tensor", out=sq[:, 0:QW], in0=a0[:, 0:QW], in1=a0[:, 0:QW], op=ALU.mult)
                    MM(k, k.ps[6][:, 0:QW], lhsT=k.C("ones"), rhs=sq[:, 0:QW])
                    A(k, out=sq[:, 0:QW], in_=k.ps[6][:, 0:QW], func=AF.Sqrt, bias=EPS, scale=1.0 / 128)
                    V(k, "reciprocal", out=rz[:, 0:QW], in_=sq[:, 0:QW])
                    V(k, "scalar_tensor_tensor", out=k.BIG[:, h, tok0 + qt * QW:tok0 + (qt + 1) * QW], in0=a0[:, 0:QW], scalar=sgl[:, 0:1],
                      in1=rz[:, 0:QW], op0=ALU.mult, op1=ALU.mult)


def run_streams(gens):
    gens = list(gens)
    while gens:
        for g in list(gens):
            try:
                next(g)
            except StopIteration:
                gens.remove(g)


class NS_:
    pass


DT_REC = F32


DT_ML = BF16


def phase_mlstm(k, l):
    S = k.S
    LN8 = math.log(0.125)
    I64 = k.cst[0:64, CONST_ORDER["ident"], 0:64]
    O64 = k.cst[0:64, CONST_ORDER["ones"], 0:64]

    def bc3(ap2, n):
        return ap2.unsqueeze(2).to_broadcast([ap2.shape[0], ap2.shape[1], n])

    def bcm(ap2, n):
        return ap2.unsqueeze(1).to_broadcast([ap2.shape[0], n, ap2.shape[1]])

    def r3(ap):
        return ap.rearrange("p (a n) -> p a n", n=64)

    def alloc(tag):
        t = NS_()
        for nm in ("diag", "X", "ET", "ebq"):
            setattr(t, nm, S.sb("m%s%s" % (nm, tag), [64, 8, 64]))
        for nm in ("qb", "sT", "kw"):
            setattr(t, nm, S.sb("m%s%s" % (nm, tag), [64, 8, 64], DT_ML))
        t.stb = S.sb("mstb" + tag, [64, 4, 128], DT_ML)
        for nm in ("nlf", "tg", "nbs", "t4", "t4b", "colE", "wk", "dec"):
            setattr(t, nm, S.sb("m%s%s" % (nm, tag), [64, 8]))
        t.gt2 = S.sb("mgt2" + tag, [64, 2, 32])
        t.ckv = S.sb("mckv" + tag, [64, 2, 512])
        t.og = S.sb("mog" + tag, [64, 512])
        t.qk = S.sb("mcqk" + tag, [64, 4, 2, 128])
        t.v1 = S.sb("mv1" + tag, [64, 8, 128], DT_ML)
        S.op("dve", "memset", t.v1[:], 0.0)
        S.op("dve", "memset", t.v1[:, :, 64:65], 1.0)
        t.state = S.sb("mstate" + tag, [64, 4, 128])
        for nm in ("hm", "omf", "tt", "sig"):
            setattr(t, nm, S.sb("m%s%s" % (nm, tag), [64, 256]))
        t.den = S.sb("mden" + tag, [64, 4])
        t.ss4 = S.sb("mss4" + tag, [64, 4])
        t.junk = S.sb("mjunk" + tag, [64, 64])
        t.em0 = S.sb("mem0" + tag, [64, 4])
        t.n0r = S.sb("mn0r" + tag, [4, 64])
        t.kvs = S.sb("mkvs" + tag, [4, 256])
        t.nls = S.sb("mnls" + tag, [4, 256])
        t.nblc = S.sb("mnblc" + tag, [4, 4])
        t.off = S.sb("moff" + tag, [4, 4])
        t.mx = S.sb("mmx" + tag, [4, 2])
        t.mfin = S.sb("mmfin" + tag, [4, 1])
        t.mrow = S.sb("mmrow" + tag, [1, 4])
        t.d4 = S.sb("md4" + tag, [4, 4])
        t.emf = S.sb("memf" + tag, [64, 4])
        t.so = S.sb("mso" + tag, [64, 4, 64])
        t.ncol = S.sb("mncol" + tag, [64, 4])
        t.nrow = S.sb("mnrow" + tag, [4, 64])
        return t

    def stream(sq_, dr, t, B, fbb, mgb):
        tok0, L = sq_["tok0"], sq_["L"]
        nblk = L // 128
        ci = CONST_ORDER["m_le"] if dr == 0 else CONST_ORDER["m_ge"]
        cum64 = k.cst[0:64, ci, 0:64]
        OWN, OTH = (k.OM, k.OM2) if dr == 0 else (k.OM2, k.OM)
        state = t.state
        S.op("dve", "memset", state[:], 0.0)
        if sq_["sample"]:
            S.dma("sp", t.em0[:], k.st_m[l, dr:dr + 1, :].partition_broadcast(64))
            A(k, out=t.em0[:], in_=t.em0[:], func=AF.Exp)
            S.dma("sp", state[:, :, 0:64], k.st_c[l, dr, :, :, :].rearrange("h a b -> a h b"))
            S.dma("sp", t.n0r[:], k.st_n[l, dr, :, :])
            S.op("pe", "transpose", out=B[0][0:64, 32:36], in_=t.n0r[:], identity=k.cst[0:4, CONST_ORDER["ident"], 0:4])
            V(k, "tensor_copy", out=state[:, :, 64], in_=B[0][0:64, 32:36])
            V(k, "tensor_tensor", out=state[:], in0=state[:], in1=bc3(t.em0[:], 128), op=ALU.mult)
        A(k, out=t.stb[:], in_=state[:], func=AF.Copy)
        blocks = list(range(nblk)) if dr == 0 else list(range(nblk - 1, -1, -1))
        corder = (0, 1) if dr == 0 else (1, 0)
        for step, bi in enumerate(blocks):
            r0 = tok0 + bi * 128
            for c in range(2):
                S.dma("sp", t.gt2[:, c, :], k.TMS[r0 + c * 64:r0 + (c + 1) * 64, TM_OFF["gt"]:TM_OFF["gt"] + 32])
                S.dma("sp", t.ckv[:, c, :], k.TMS[r0 + c * 64:r0 + (c + 1) * 64, TM_OFF["ckv"]:TM_OFF["ckv"] + 512])
            S.dma("sp", t.qk[:], k.CQK[:, :, r0:r0 + 128].rearrange("c (hh p) n -> p c hh n", p=64))
            yield
            kv4 = t.ckv[:].rearrange("p c (x h e) -> p c x h e", x=2, e=64)
            G(k, "tensor_copy", out=t.v1[:, :, 0:64].rearrange("p (c h) e -> p c h e", c=2), in_=kv4[:, :, 1, :, :])
            t3 = t.tg[:].rearrange("p (c h) -> p c h", c=2)
            V(k, "tensor_tensor", out=t3, in0=t.gt2[:, :, 24 + dr * 4:28 + dr * 4], in1=bcm(fbb[:, dr * 4:dr * 4 + 4], 2), op=ALU.add)
            A(k, out=t.tg[:], in_=t.tg[:], func=AF.Exp, scale=-1.0)
            A(k, out=t.nlf[:], in_=t.tg[:], func=AF.Ln, bias=1.0, scale=1.0)
            yield
            pa = B[0]
            for c in range(2):
                MM(k, pa[0:64, c * 4:(c + 1) * 4], lhsT=cum64, rhs=t.nlf[:, c * 4:(c + 1) * 4])
                MM(k, pa[0:64, 8 + c * 4:12 + c * 4], lhsT=O64, rhs=t.nlf[:, c * 4:(c + 1) * 4])
            yield
            V(k, "tensor_copy", out=t.nbs[:], in_=pa[0:64, 0:8])
            V(k, "tensor_tensor", out=t.t4[:], in0=t.nbs[:], in1=pa[0:64, 8:16], op=ALU.subtract)
            ig3 = t.gt2[:, :, 16 + dr * 4:20 + dr * 4]
            V(k, "tensor_tensor", out=t.t4b[:].rearrange("p (c h) -> p c h", c=2), in0=t.t4[:].rearrange("p (c h) -> p c h", c=2), in1=ig3, op=ALU.add)
            A(k, out=t.wk[:], in_=t.t4b[:], func=AF.Exp, bias=LN8, scale=1.0)
            V(k, "tensor_tensor", out=t.colE[:].rearrange("p (c h) -> p c h", c=2), in0=t.nbs[:].rearrange("p (c h) -> p c h", c=2), in1=ig3, op=ALU.add)
            V(k, "tensor_copy", out=t.dec[:], in_=pa[0:64, 8:16])
            A(k, out=t.dec[:], in_=t.dec[:], func=AF.Exp, scale=-1.0)
            V(k, "tensor_tensor", out=t.diag[:], in0=bcm(I64, 8), in1=bc3(t.nbs[:], 64), op=ALU.mult)
            yield
            if not sq_["sample"]:
                for c in range(2):
                    cc = bi * 2 + c
                    S.op("pe", "transpose", out=B[0][0:4, 256:320], in_=t.t4b[:, c * 4:(c + 1) * 4], identity=I64)
                    S.op("pe", "transpose", out=B[0][0:4, 384:448], in_=t.nlf[:, c * 4:(c + 1) * 4], identity=I64)
                    V(k, "tensor_copy", out=t.kvs[:, cc * 64:(cc + 1) * 64], in_=B[0][0:4, 256:320])
                    V(k, "tensor_copy", out=t.nls[:, cc * 64:(cc + 1) * 64], in_=B[0][0:4, 384:448])
            pb, pc = B[1], B[2]
            for pi in range(8):
                MM(k, pb[0:64, pi * 64:(pi + 1) * 64], lhsT=O64, rhs=t.diag[:, pi, :])
            for c in range(2):
                for h in range(4):
                    pi = c * 4 + h
                    MM(k, pc[0:64, pi * 64:(pi + 1) * 64], lhsT=t.qk[:, 2 + h // 2, h % 2, c * 64:(c + 1) * 64],
                       rhs=t.qk[:, h // 2, h % 2, c * 64:(c + 1) * 64])
            yield
            pb3 = r3(pb[0:64, :])
            V(k, "tensor_tensor", out=t.X[:], in0=bc3(t.colE[:], 64), in1=pb3, op=ALU.subtract)
            A(k, out=t.ebq[:], in_=pb3, func=AF.Exp, scale=-1.0)
            yield
            A(k, out=t.ET[:], in_=t.X[:], func=AF.Exp)
            q4 = t.qk[:, 0:2, :, :].rearrange("p a hh (c n) -> p c (a hh) n", c=2)
            G(k, "tensor_tensor", out=t.qb[:].rearrange("p (c h) n -> p c h n", c=2), in0=q4,
              in1=t.ebq[:].rearrange("p (c h) n -> p c h n", c=2), op=ALU.mult)
            V(k, "tensor_tensor", out=t.kw[:].rearrange("p (c h) e -> p c h e", c=2), in0=kv4[:, :, 0, :, :],
              in1=bc3(t.wk[:], 64).rearrange("p (c h) e -> p c h e", c=2), op=ALU.mult)
            yield
            G(k, "tensor_tensor", out=t.ET[:], in0=t.ET[:], in1=bcm(cum64, 8), op=ALU.mult)
            yield
            V(k, "scalar_tensor_tensor", out=t.sT[:], in0=r3(pc[0:64, :]), scalar=0.125, in1=t.ET[:], op0=ALU.mult, op1=ALU.mult)
            yield
            for c in corder:
                po, pst = B[3], B[1]
                for h in range(4):
                    pi = c * 4 + h
                    for (a0_, a1_) in ((0, 64), (64, 128)):
                        MM(k, po[0:64, h * 128 + a0_:h * 128 + a1_], lhsT=t.qb[:, pi, :], rhs=t.stb[:, h, a0_:a1_], start=True, stop=False)
                        MM(k, po[0:64, h * 128 + a0_:h * 128 + a1_], lhsT=t.sT[:, pi, :], rhs=t.v1[:, pi, a0_:a1_], start=False, stop=True)
                    MM(k, pst[0:64, h * 128:(h + 1) * 128], lhsT=t.kw[:, pi, :], rhs=t.v1[:, pi, :])
                yield
                V(k, "tensor_tensor", out=state[:], in0=state[:], in1=bc3(t.dec[:, c * 4:(c + 1) * 4], 128), op=ALU.mult)
                V(k, "tensor_tensor", out=state[:], in0=state[:], in1=pst[0:64, :].rearrange("p (h e) -> p h e", e=128), op=ALU.add)
                A(k, out=t.stb[:], in_=state[:], func=AF.Copy)
                rc = r0 + c * 64
                po3 = po[0:64, :].rearrange("p (h e) -> p h e", e=128)
                A(k, out=t.den[:], in_=po3[:, :, 64], func=AF.Abs)
                yield
                V(k, "tensor_scalar", out=t.den[:], in0=t.den[:], scalar1=1.0, scalar2=None, op0=ALU.max)
                V(k, "reciprocal", out=t.den[:], in_=t.den[:])
                V(k, "tensor_tensor", out=t.hm[:].rearrange("p (h e) -> p h e", e=64), in0=po3[:, :, 0:64], in1=bc3(t.den[:], 64), op=ALU.mult)
                if step < nblk // 2:
                    S.dma("pool", OWN[rc:rc + 64, :], t.hm[:])
                    yield
                else:
                    S.dma("sp", t.omf[:], OTH[rc:rc + 64, :])
                    S.dma("sp", t.og[:], k.TMS[rc:rc + 64, TM_OFF["og"]:TM_OFF["og"] + 512])
                    yield
                    V(k, "tensor_tensor", out=t.hm[:], in0=t.hm[:], in1=t.omf[:], op=ALU.add)
                    S.op("dve", "memset", t.ss4[:], 0.0)
                    for h in range(4):
                        A(k, out=t.junk[:], in_=t.hm[:, h * 64:(h + 1) * 64], func=AF.Square, accum_out=t.ss4[:, h:h + 1])
                    A(k, out=t.ss4[:], in_=t.ss4[:], func=AF.Sqrt, bias=EPS, scale=1.0 / 64)
                    V(k, "reciprocal", out=t.ss4[:], in_=t.ss4[:])
                    yield
                    for h in range(4):
                        V(k, "scalar_tensor_tensor", out=t.tt[:, h * 64:(h + 1) * 64], in0=t.hm[:, h * 64:(h + 1) * 64],
                          scalar=t.ss4[:, h:h + 1], in1=mgb[:], op0=ALU.mult, op1=ALU.mult)
                    A(k, out=t.sig[:], in_=t.og[:, 256:512], func=AF.Sigmoid)
                    V(k, "tensor_tensor", out=t.tt[:], in0=t.tt[:], in1=t.sig[:], op=ALU.mult)
                    yield
                    for j in range(2):
                        S.op("pe", "transpose", out=B[2][:, j * 64:(j + 1) * 64], in_=t.tt[:, j * 128:(j + 1) * 128], identity=I64)
                    yield
                    V(k, "tensor_copy", out=k.BIG[:, 6:8, rc:rc + 64], in_=B[2][:, 0:128].rearrange("p (j n) -> p j n", n=64))
        if not sq_["sample"]:
            p = sq_["p"]
            V(k, "reduce_sum", out=t.nblc[:], in_=t.nls[:].rearrange("p (c s) -> p c s", s=64), axis=AX.X)
            nch = L // 64
            S.op("dve", "memset", t.off[:], 0.0)
            if dr == 0:
                for c in range(nch - 2, -1, -1):
                    V(k, "tensor_tensor", out=t.off[:, c:c + 1], in0=t.off[:, c + 1:c + 2], in1=t.nblc[:, c + 1:c + 2], op=ALU.subtract)
            else:
                for c in range(1, nch):
                    V(k, "tensor_tensor", out=t.off[:, c:c + 1], in0=t.off[:, c - 1:c], in1=t.nblc[:, c - 1:c], op=ALU.subtract)
            V(k, "tensor_tensor", out=t.kvs[:].rearrange("p (c s) -> p c s", s=64), in0=t.kvs[:].rearrange("p (c s) -> p c s", s=64),
              in1=bc3(t.off[:], 64), op=ALU.add)
            V(k, "reduce_max", out=t.mx[:, 0:1], in_=t.kvs[:], axis=AX.X)
            V(k, "reduce_sum", out=t.mx[:, 1:2], in_=t.nblc[:], axis=AX.X)
            V(k, "scalar_tensor_tensor", out=t.mfin[:], in0=t.mx[:, 1:2], scalar=-1.0, in1=t.mx[:, 0:1], op0=ALU.mult, op1=ALU.max)
            yield
            S.op("pe", "transpose", out=B[0][0:1, 40:44], in_=t.mfin[:], identity=k.cst[0:4, CONST_ORDER["ident"], 0:4])
            V(k, "tensor_copy", out=t.mrow[:], in_=B[0][0:1, 40:44])
            S.dma("pool", k.o_m[p, l, dr:dr + 1, :], t.mrow[:])
            V(k, "tensor_scalar", out=t.d4[:], in0=k.cst[0:4, CONST_ORDER["ident"], 0:4], scalar1=t.mfin[:, 0:1], scalar2=None, op0=ALU.mult)
            MM(k, B[0][0:64, 16:20], lhsT=k.cst[0:4, CONST_ORDER["ones"], 0:64], rhs=t.d4[:])
            yield
            V(k, "tensor_copy", out=t.emf[:], in_=B[0][0:64, 16:20])
            A(k, out=t.emf[:], in_=t.emf[:], func=AF.Exp, scale=-1.0)
            V(k, "tensor_tensor", out=t.so[:], in0=state[:, :, 0:64], in1=bc3(t.emf[:], 64), op=ALU.mult)
            S.dma("pool", k.o_C[p, l, dr, :, :, :].rearrange("h a b -> a h b"), t.so[:])
            V(k, "tensor_tensor", out=t.ncol[:], in0=state[:, :, 64], in1=t.emf[:], op=ALU.mult)
            S.op("pe", "transpose", out=B[0][0:4, 64:128], in_=t.ncol[:], identity=I64)
            yield
            V(k, "tensor_copy", out=t.nrow[:], in_=B[0][0:4, 64:128])
            S.dma("pool", k.o_n[p, l, dr, :, :], t.nrow[:])

    with S.scope():
        fbb = S.sb("fbb", [64, 8])
        S.dma("sp", fbb[:], k.f_b[l:l + 1, :].partition_broadcast(64))
        mgb = S.sb("mgb", [64, 64])
        S.dma("sp", mgb[:], k.mn_g[l:l + 1, :].partition_broadcast(64))
        tiles = [alloc("f"), alloc("b")]
        for sq_ in SEQS:
            run_streams([stream(sq_, dr, tiles[dr], k.ps[dr * 4:dr * 4 + 4], fbb, mgb) for dr in range(2)])


def phase_delta(k, l):
    phase_delta_prep(k, l)
    phase_delta_scan(k, l)


def phase_delta_prep(k, l):
    S = k.S
    with S.scope():
        cw = S.sb("cw", [128, DEPTH, 6, 5])
        S.dma("sp", cw[:], k.conv_w[:])
        xin = S.sb("dxin", [128, 6, 516])
        acc = S.sb("dacc", [128, 6, 512])
        sact = S.sb("dsact", [128, 6, 512])
        sqb = S.sb("dsq", [128, 512])
        rin = S.sb("drin", [128, 512])
        tok = S.sb("dtok", [128, 512])
        for sq_ in SEQS:
            tok0, L = sq_["tok0"], sq_["L"]
            W = min(512, L)
            for ti in range(L // W):
                t0 = tok0 + ti * W
                lo, hi = max(tok0, t0 - 2), min(tok0 + L, t0 + W + 2)
                k.S.op("dve", "memset", xin[:], 0.0)
                S.dma("sp", xin[:, :, lo - (t0 - 2):hi - (t0 - 2)], k.BT[:, :, lo:hi].rearrange("c p n -> p c n"))
                for c in range(6):
                    eng = "dve"
                    S.op(eng, "tensor_scalar", out=acc[:, c, 0:W], in0=xin[:, c, 0:W], scalar1=cw[:, l, c, 0:1], scalar2=None, op0=ALU.mult)
                    for j in range(1, 5):
                        S.op(eng, "scalar_tensor_tensor", out=acc[:, c, 0:W], in0=xin[:, c, j:j + W], scalar=cw[:, l, c, j:j + 1],
                             in1=acc[:, c, 0:W], op0=ALU.mult, op1=ALU.add)
                for c in range(6):
                    A(k, out=sact[:, c, 0:W], in_=acc[:, c, 0:W], func=AF.Silu)
                for c in range(4):
                    G(k, "tensor_tensor", out=sqb[:, 0:W], in0=sact[:, c, 0:W], in1=sact[:, c, 0:W], op=ALU.mult)
                    p = k.ps[c % 2]
                    MM(k, p[:, 0:W], lhsT=k.C("bd"), rhs=sqb[:, 0:W])
                    A(k, out=rin[:, 0:W], in_=p[:, 0:W], func=AF.Sqrt, bias=EPS, scale=1.0)
                    V(k, "reciprocal", out=rin[:, 0:W], in_=rin[:, 0:W])
                    V(k, "scalar_tensor_tensor", out=sact[:, c, 0:W], in0=sact[:, c, 0:W], scalar=(0.125 if c < 2 else 1.0), in1=rin[:, 0:W],
                      op0=ALU.mult, op1=ALU.mult)
                    h0 = (c // 2) * 4 + (c % 2) * 2
                    S.dma("pool", k.DQK[h0:h0 + 2, :, t0:t0 + W].rearrange("h p n -> (h p) n"), sact[:, c, 0:W])
                for b in range(W // 128):
                    p = k.ps[2 + b % 2]
                    for j, c in enumerate((2, 3, 4, 5)):
                        S.op("pe", "transpose", out=p[:, j * 128:(j + 1) * 128], in_=sact[:, c, b * 128:(b + 1) * 128], identity=k.C("ident"))
                    V(k, "tensor_copy", out=tok[:], in_=p[:])
                    S.dma("pool", k.DKV[t0 + b * 128:t0 + (b + 1) * 128, :], tok[:])


def phase_delta_scan(k, l):
    S = k.S
    I64 = k.cst[0:64, CONST_ORDER["ident"], 0:64]
    O64 = k.cst[0:64, CONST_ORDER["ones"], 0:64]

    def bc3(ap2, n):
        return ap2.unsqueeze(2).to_broadcast([ap2.shape[0], ap2.shape[1], n])

    def bcm(ap2, n):
        return ap2.unsqueeze(1).to_broadcast([ap2.shape[0], n, ap2.shape[1]])

    def r3(ap):
        return ap.rearrange("p (a n) -> p a n", n=64)

    def alloc(tag):
        t = NS_()
        for nm in ("diag", "X", "N", "TT", "u", "ebq"):
            setattr(t, nm, S.sb("d%s%s" % (nm, tag), [64, 8, 64]))
        for nm in ("P0", "P1", "PT0", "PT1", "TTb", "aT", "vb", "kbg", "kdec", "wT", "qd"):
            setattr(t, nm, S.sb("d%s%s" % (nm, tag), [64, 8, 64], DT_REC))
        t.qkb = S.sb("dqkb" + tag, [64, 8, 128], DT_REC)
        t.S4b = S.sb("dS4b" + tag, [64, 4, 64], DT_REC)
        for nm in ("beta", "nbeta", "ng", "tg", "ngc", "egc", "bgs", "kds", "gl"):
            setattr(t, nm, S.sb("d%s%s" % (nm, tag), [64, 8]))
        t.gt2 = S.sb("dgt2" + tag, [64, 2, 32])
        t.dkv = S.sb("ddkv" + tag, [64, 2, 512])
        t.qk = S.sb("dqk" + tag, [64, 8, 128])
        t.vnew = S.sb("dvnew" + tag, [64, 4, 64], DT_REC)
        t.S4 = S.sb("dS4" + tag, [64, 4, 64])
        t.oc = S.sb("doc" + tag, [64, 256])
        t.of = S.sb("dof" + tag, [64, 256])
        t.og = S.sb("dog" + tag, [64, 512])
        t.ss4 = S.sb("dss4" + tag, [64, 4])
        t.junk = S.sb("djunk" + tag, [64, 64])
        t.tt = S.sb("dtt" + tag, [64, 256])
        t.sig = S.sb("dsig" + tag, [64, 256])
        if DT_REC == F32:
            t.qkb, t.P0, t.TTb, t.S4b = t.qk, t.N, t.TT, t.S4
        return t

    def stream(sq_, dr, t, B, alb, dtb, dgb):
        tok0, L = sq_["tok0"], sq_["L"]
        nblk = L // 128
        ci = CONST_ORDER["m_le"] if dr == 0 else CONST_ORDER["m_ge"]
        si = CONST_ORDER["m_ig"] if dr == 0 else CONST_ORDER["m_il"]
        cum64 = k.cst[0:64, ci, 0:64]
        str64 = k.cst[0:64, si, 0:64]
        OWN, OTH = (k.OD, k.OD2) if dr == 0 else (k.OD2, k.OD)
        if sq_["sample"]:
            S.dma("sp", t.S4[:], k.st_d[l, dr, :, :, :].rearrange("h a b -> a h b"))
        else:
            S.op("dve", "memset", t.S4[:], 0.0)
        if DT_REC != F32:
            V(k, "tensor_copy", out=t.S4b[:], in_=t.S4[:])
        blocks = list(range(nblk)) if dr == 0 else list(range(nblk - 1, -1, -1))
        corder = (0, 1) if dr == 0 else (1, 0)
        for step, bi in enumerate(blocks):
            r0 = tok0 + bi * 128
            for c in range(2):
                S.dma("sp", t.gt2[:, c, :], k.TMS[r0 + c * 64:r0 + (c + 1) * 64, TM_OFF["gt"]:TM_OFF["gt"] + 32])
                S.dma("sp", t.dkv[:, c, :], k.DKV[r0 + c * 64:r0 + (c + 1) * 64, :])
            S.dma("sp", t.qk[:], k.DQK[:, :, r0:r0 + 128].rearrange("h p n -> p h n"))
            yield
            if DT_REC != F32:
                G(k, "tensor_copy", out=t.qkb[:], in_=t.qk[:])
            b3 = t.beta[:].rearrange("p (c h) -> p c h", c=2)
            A(k, out=b3, in_=t.gt2[:, :, 8 + dr * 4:12 + dr * 4], func=AF.Sigmoid)
            V(k, "tensor_scalar", out=t.nbeta[:], in0=t.beta[:], scalar1=-1.0, scalar2=None, op0=ALU.mult)
            t3 = t.tg[:].rearrange("p (c h) -> p c h", c=2)
            V(k, "tensor_tensor", out=t3, in0=t.gt2[:, :, dr * 4:dr * 4 + 4], in1=bcm(dtb[:, dr * 4:dr * 4 + 4], 2), op=ALU.add)
            A(k, out=t.tg[:], in_=t.tg[:], func=AF.Exp)
            A(k, out=t.tg[:], in_=t.tg[:], func=AF.Ln, bias=1.0, scale=1.0)
            V(k, "tensor_tensor", out=t.ng[:].rearrange("p (c h) -> p c h", c=2), in0=t3, in1=bcm(alb[:, dr * 4:dr * 4 + 4], 2), op=ALU.mult)
            yield
            pa = B[0]
            for c in range(2):
                MM(k, pa[0:64, c * 4:(c + 1) * 4], lhsT=cum64, rhs=t.ng[:, c * 4:(c + 1) * 4])
                MM(k, pa[0:64, 8 + c * 4:12 + c * 4], lhsT=O64, rhs=t.ng[:, c * 4:(c + 1) * 4])
            yield
            V(k, "tensor_copy", out=t.ngc[:], in_=pa[0:64, 0:8])
            A(k, out=t.egc[:], in_=t.ngc[:], func=AF.Exp, scale=-1.0)
            V(k, "tensor_tensor", out=t.bgs[:], in0=t.beta[:], in1=t.egc[:], op=ALU.mult)
            V(k, "tensor_tensor", out=t.kds[:], in0=t.ngc[:], in1=pa[0:64, 8:16], op=ALU.subtract)
            A(k, out=t.kds[:], in_=t.kds[:], func=AF.Exp)
            V(k, "tensor_copy", out=t.gl[:], in_=pa[0:64, 8:16])
            A(k, out=t.gl[:], in_=t.gl[:], func=AF.Exp, scale=-1.0)
            V(k, "tensor_tensor", out=t.diag[:], in0=bcm(I64, 8), in1=bc3(t.ngc[:], 64), op=ALU.mult)
            yield
            pb, pc, pd = B[1], B[2], B[3]
            for pi in range(8):
                MM(k, pb[0:64, pi * 64:(pi + 1) * 64], lhsT=O64, rhs=t.diag[:, pi, :])
            for c in range(2):
                for h in range(4):
                    pi = c * 4 + h
                    kTc = t.qkb[:, 4 + h, c * 64:(c + 1) * 64]
                    qTc = t.qkb[:, h, c * 64:(c + 1) * 64]
                    MM(k, pc[0:64, pi * 64:(pi + 1) * 64], lhsT=kTc, rhs=kTc)
                    MM(k, pd[0:64, pi * 64:(pi + 1) * 64], lhsT=kTc, rhs=qTc)
            yield
            pb3 = r3(pb[0:64, :])
            V(k, "tensor_tensor", out=t.X[:], in0=pb3, in1=bc3(t.ngc[:], 64), op=ALU.subtract)
            A(k, out=t.ebq[:], in_=pb3, func=AF.Exp, scale=-1.0)
            V(k, "tensor_scalar", out=t.diag[:], in0=t.X[:], scalar1=0.0, scalar2=None, op0=ALU.min)
            G(k, "tensor_scalar", out=t.X[:], in0=t.X[:], scalar1=0.0, scalar2=None, op0=ALU.max)
            yield
            A(k, out=t.diag[:], in_=t.diag[:], func=AF.Exp)
            A(k, out=t.X[:], in_=t.X[:], func=AF.Exp, scale=-1.0)
            G(k, "tensor_tensor", out=t.diag[:], in0=t.diag[:], in1=bcm(str64, 8), op=ALU.mult)
            G(k, "tensor_tensor", out=t.X[:], in0=t.X[:], in1=bcm(cum64, 8), op=ALU.mult)
            yield
            V(k, "tensor_tensor", out=t.N[:], in0=r3(pc[0:64, :]), in1=bc3(t.nbeta[:], 64), op=ALU.mult)
            V(k, "tensor_tensor", out=t.N[:], in0=t.N[:], in1=t.diag[:], op=ALU.mult)
            if DT_REC != F32:
                A(k, out=t.P0[:], in_=t.N[:], func=AF.Copy)
            V(k, "tensor_tensor", out=t.aT[:], in0=r3(pd[0:64, :]), in1=t.X[:], op=ALU.mult)
            yield
            pt = B[1]
            for pi in range(8):
                S.op("pe", "transpose", out=pt[0:64, pi * 64:(pi + 1) * 64], in_=t.N[:, pi, :], identity=I64)
            yield
            A(k, out=t.PT0[:], in_=r3(pt[0:64, :]), func=AF.Copy)
            V(k, "tensor_tensor", out=t.TT[:], in0=r3(pt[0:64, :]), in1=bcm(I64, 8), op=ALU.add)
            if DT_REC != F32:
                if DT_REC != F32:
                    A(k, out=t.TTb[:], in_=t.TT[:], func=AF.Copy)
            yield
            Pb, PTb = (t.P0, t.P1), (t.PT0, t.PT1)
            for stp in range(5):
                P, PT = Pb[stp % 2], PTb[stp % 2]
                Pn, PTn = Pb[(stp + 1) % 2], PTb[(stp + 1) % 2]
                p1, p2, p3 = B[2], B[3], B[1]
                for pi in range(8):
                    MM(k, p1[0:64, pi * 64:(pi + 1) * 64], lhsT=PT[:, pi, :], rhs=P[:, pi, :])
                if stp < 4:
                    for pi in range(8):
                        MM(k, p2[0:64, pi * 64:(pi + 1) * 64], lhsT=P[:, pi, :], rhs=PT[:, pi, :])
                yield
                A(k, out=Pn[:], in_=r3(p1[0:64, :]), func=AF.Copy)
                if stp < 4:
                    V(k, "tensor_copy", out=PTn[:], in_=r3(p2[0:64, :]))
                yield
                for pi in range(8):
                    MM(k, p3[0:64, pi * 64:(pi + 1) * 64], lhsT=Pn[:, pi, :], rhs=t.TTb[:, pi, :])
                yield
                V(k, "tensor_tensor", out=t.TT[:], in0=t.TT[:], in1=r3(p3[0:64, :]), op=ALU.add)
                if DT_REC != F32:
                    A(k, out=t.TTb[:], in_=t.TT[:], func=AF.Copy)
                yield
            kv4 = t.dkv[:].rearrange("p c (x h e) -> p c x h e", x=2, e=64)
            V(k, "tensor_tensor", out=t.vb[:].rearrange("p (c h) e -> p c h e", c=2), in0=kv4[:, :, 1, :, :],
              in1=bc3(t.beta[:], 64).rearrange("p (c h) e -> p c h e", c=2), op=ALU.mult)
            V(k, "tensor_tensor", out=t.kbg[:].rearrange("p (c h) e -> p c h e", c=2), in0=kv4[:, :, 0, :, :],
              in1=bc3(t.bgs[:], 64).rearrange("p (c h) e -> p c h e", c=2), op=ALU.mult)
            G(k, "tensor_tensor", out=t.kdec[:].rearrange("p (c h) e -> p c h e", c=2), in0=kv4[:, :, 0, :, :],
              in1=bc3(t.kds[:], 64).rearrange("p (c h) e -> p c h e", c=2), op=ALU.mult)
            G(k, "tensor_tensor", out=t.qd[:].rearrange("p (c h) n -> p c h n", c=2),
              in0=t.qk[:, 0:4, :].rearrange("p h (c n) -> p c h n", c=2), in1=t.ebq[:].rearrange("p (c h) n -> p c h n", c=2), op=ALU.mult)
            yield
            p1, p2 = B[2], B[3]
            for pi in range(8):
                MM(k, p1[0:64, pi * 64:(pi + 1) * 64], lhsT=t.TTb[:, pi, :], rhs=t.vb[:, pi, :])
                MM(k, p2[0:64, pi * 64:(pi + 1) * 64], lhsT=t.kbg[:, pi, :], rhs=t.TTb[:, pi, :])
            yield
            A(k, out=t.u[:], in_=r3(p1[0:64, :]), func=AF.Copy)
            V(k, "tensor_copy", out=t.wT[:], in_=r3(p2[0:64, :]))
            yield
            for c in corder:
                px, py, pz = B[1], B[2], B[3]
                for h in range(4):
                    MM(k, px[0:64, h * 64:(h + 1) * 64], lhsT=t.wT[:, c * 4 + h, :], rhs=t.S4b[:, h, :])
                yield
                V(k, "tensor_tensor", out=t.vnew[:], in0=t.u[:, c * 4:(c + 1) * 4, :], in1=r3(px[0:64, 0:256]), op=ALU.subtract)
                yield
                for h in range(4):
                    MM(k, py[0:64, h * 64:(h + 1) * 64], lhsT=t.qd[:, c * 4 + h, :], rhs=t.S4b[:, h, :], start=True, stop=False)
                    MM(k, py[0:64, h * 64:(h + 1) * 64], lhsT=t.aT[:, c * 4 + h, :], rhs=t.vnew[:, h, :], start=False, stop=True)
                    MM(k, pz[0:64, h * 64:(h + 1) * 64], lhsT=t.kdec[:, c * 4 + h, :], rhs=t.vnew[:, h, :])
                yield
                V(k, "tensor_tensor", out=t.S4[:], in0=t.S4[:], in1=bc3(t.gl[:, c * 4:(c + 1) * 4], 64), op=ALU.mult)
                V(k, "tensor_tensor", out=t.S4[:], in0=t.S4[:], in1=r3(pz[0:64, 0:256]), op=ALU.add)
                if DT_REC != F32:
                    G(k, "tensor_copy", out=t.S4b[:], in_=t.S4[:])
                rc = r0 + c * 64
                A(k, out=t.oc[:], in_=py[0:64, 0:256], func=AF.Copy)
                if step < nblk // 2:
                    S.dma("pool", OWN[rc:rc + 64, :], t.oc[:])
                    yield
                else:
                    S.dma("sp", t.of[:], OTH[rc:rc + 64, :])
                    S.dma("sp", t.og[:], k.TMS[rc:rc + 64, TM_OFF["og"]:TM_OFF["og"] + 512])
                    yield
                    V(k, "tensor_tensor", out=t.oc[:], in0=t.oc[:], in1=t.of[:], op=ALU.add)
                    S.op("dve", "memset", t.ss4[:], 0.0)
                    for h in range(4):
                        A(k, out=t.junk[:], in_=t.oc[:, h * 64:(h + 1) * 64], func=AF.Square, accum_out=t.ss4[:, h:h + 1])
                    A(k, out=t.ss4[:], in_=t.ss4[:], func=AF.Sqrt, bias=EPS, scale=1.0 / 64)
                    V(k, "reciprocal", out=t.ss4[:], in_=t.ss4[:])
                    yield
                    for h in range(4):
                        V(k, "scalar_tensor_tensor", out=t.tt[:, h * 64:(h + 1) * 64], in0=t.oc[:, h * 64:(h + 1) * 64],
                          scalar=t.ss4[:, h:h + 1], in1=dgb[:], op0=ALU.mult, op1=ALU.mult)
                    A(k, out=t.sig[:], in_=t.og[:, 0:256], func=AF.Silu)
                    V(k, "tensor_tensor", out=t.tt[:], in0=t.tt[:], in1=t.sig[:], op=ALU.mult)
                    yield
                    for j in range(2):
                        S.op("pe", "transpose", out=B[0][:, 128 + j * 64:128 + (j + 1) * 64], in_=t.tt[:, j * 128:(j + 1) * 128], identity=I64)
                    yield
                    V(k, "tensor_copy", out=k.BIG[:, 4:6, rc:rc + 64], in_=B[0][:, 128:256].rearrange("p (j n) -> p j n", n=64))
        if not sq_["sample"]:
            S.dma("pool", k.o_S[sq_["p"], l, dr, :, :, :].rearrange("h a b -> a h b"), t.S4[:])

    with S.scope():
        alb = S.sb("alb", [64, 8])
        S.dma("sp", alb[:], k.a_log[l:l + 1, :].partition_broadcast(64))
        A(k, out=alb[:], in_=alb[:], func=AF.Exp)
        dtb = S.sb("dtb", [64, 8])
        S.dma("sp", dtb[:], k.dt_b[l:l + 1, :].partition_broadcast(64))
        dgb = S.sb("dgb", [64, 64])
        S.dma("sp", dgb[:], k.dn_g[l:l + 1, :].partition_broadcast(64))
        tiles = [alloc("f"), alloc("b")]
        for sq_ in SEQS:
            run_streams([stream(sq_, dr, tiles[dr], k.ps[dr * 4:dr * 4 + 4], alb, dtb, dgb) for dr in range(2)])


def swap_pairs(n):
    idx = np.arange(n)
    return idx ^ 1


def prep_shared(inp):
    f = np.float32
    sh = {}
    w_in = inp["w_in"]
    b_in = inp["b_in"]
    sw = swap_pairs(512)
    w_ext = np.concatenate([w_in, w_in[:, :, O_AQ:O_AQ + 512][:, :, sw], w_in[:, :, O_AK:O_AK + 512][:, :, sw]], axis=2)
    b_ext = np.concatenate([b_in, b_in[:, O_AQ:O_AQ + 512][:, sw], b_in[:, O_AK:O_AK + 512][:, sw]], axis=1)
    sh["w_in"] = np.ascontiguousarray(w_ext, dtype=f)
    sh["w_mod"] = inp["w_mod"]
    sh["w_out"] = inp["w_out"]
    sh["w_gu"] = inp["w_gate_up"]
    sh["w_dn"] = inp["w_down"]
    cm, ct, st = host_consts()
    sh["consts"], sh["rope_c"], sh["rope_s"] = cm, ct, st

    def fm(v):
        d, n = v.shape
        return np.ascontiguousarray(v.reshape(d, n // 128, 128).transpose(2, 0, 1), dtype=f)

    sh["n1g"] = fm(inp["norm1_g"])
    sh["n2g"] = fm(inp["norm2_g"])
    sh["fng"] = np.ascontiguousarray(inp["final_norm_g"].reshape(1, D), dtype=f)
    sh["b_mod"] = fm(inp["b_mod"])
    sh["b_fm"] = np.ascontiguousarray(
        np.stack([b_ext[:, c:c + 128] for c in FM_CHUNKS], axis=1).transpose(2, 0, 1), dtype=f)
    tm_cols = np.concatenate([np.arange(c0, c0 + w) for _, cols in TM_GROUPS for (c0, w) in cols])
    sh["b_tm"] = np.ascontiguousarray(b_ext[:, tm_cols].reshape(1, DEPTH, TM_W), dtype=f)
    cw = inp["delta_conv_w"]
    sh["conv_w"] = np.ascontiguousarray(cw.reshape(DEPTH, 5, 6, 128).transpose(3, 0, 2, 1), dtype=f)
    sh["lam_qk"] = np.ascontiguousarray(inp["lambda_qk"].reshape(DEPTH, 256), dtype=f)
    sh["subln_g"] = np.ascontiguousarray(inp["attn_subln_g"].T, dtype=f)
    sh["dn_g"] = np.ascontiguousarray(inp["delta_norm_g"], dtype=f)
    sh["mn_g"] = np.ascontiguousarray(inp["mlstm_norm_g"], dtype=f)
    sh["a_log"] = np.ascontiguousarray(inp["delta_A_log"].reshape(DEPTH, 8), dtype=f)
    sh["dt_b"] = np.ascontiguousarray(inp["delta_dt_bias"].reshape(DEPTH, 8), dtype=f)
    sh["f_b"] = np.ascontiguousarray(inp["mlstm_f_bias"].reshape(DEPTH, 8), dtype=f)
    return sh


def prep_core(inp, sh, i):
    f = np.float32
    b = i % 4
    m = dict(sh)
    xp = inp["x_prompt"][2 * i:2 * i + 2].reshape(NPR * LP, D)
    m["x_all"] = np.ascontiguousarray(np.concatenate([inp["x_sample"][b], xp], axis=0), dtype=f)
    m["cache_k"] = np.ascontiguousarray(inp["cache_attn_k"][b].reshape(DEPTH, PAST, 512), dtype=f)
    m["cache_v"] = np.ascontiguousarray(inp["cache_attn_v"][b].reshape(DEPTH, PAST, 512), dtype=f)
    m["st_d"] = np.ascontiguousarray(inp["state_delta"][b], dtype=f)
    m["st_c"] = np.ascontiguousarray(inp["state_mlstm_C"][b], dtype=f)
    m["st_n"] = np.ascontiguousarray(inp["state_mlstm_n"][b], dtype=f)
    m["st_m"] = np.ascontiguousarray(inp["state_mlstm_m"][b], dtype=f)
    cc = np.stack([inp["c"][b], inp["c_ctx"]], axis=-1)
    m["cT"] = np.ascontiguousarray(cc.reshape(KC, 128, 2).transpose(1, 0, 2), dtype=f)
    return m


def build_program(stop_after=None, debug_outs=()):
    k = build(debug_outs)
    DBG["stop"] = stop_after
    try:
        phase_setup(k)
        chk("setup")
        for l in range(DEPTH):
            phase_norm(k, l, 1)
            chk("norm%d" % l)
            phase_inproj(k, l)
            chk("inproj%d" % l)
            phase_attn(k, l)
            chk("attn%d" % l)
            phase_mlstm(k, l)
            chk("mlstm%d" % l)
            phase_delta_prep(k, l)
            chk("dprep%d" % l)
            phase_delta_scan(k, l)
            chk("delta%d" % l)
            phase_outproj(k, l)
            chk("outproj%d" % l)
            phase_norm(k, l, 2)
            phase_ffn(k, l)
            chk("ffn%d" % l)
        phase_final(k)
    except StopBuild:
        pass
    st = k.S.finalize()
    return k, st


_CACHE = {}


def kernel(**inputs):
    inp = {n: np.asarray(v) for n, v in inputs.items()}
    if "prog" not in _CACHE:
        _CACHE["prog"] = build_program()
    k, st = _CACHE["prog"]
    sh = prep_shared(inp)
    in_maps = [prep_core(inp, sh, i) for i in range(8)]
    res = run_bass_kernel_spmd(k.nc, in_maps, core_ids=list(range(8)))
    R = res.results
    B = 16
    y_prompt = np.concatenate([R[i]["y_p"].reshape(NPR, LP, D) for i in range(8)], axis=0)
    y_sample = np.stack([R[b]["y_s"] for b in range(4)], axis=0)
    nk = np.concatenate([R[i]["o_k"] for i in range(8)], axis=0).reshape(B, DEPTH, LP, 4, 2, 64)
    nv = np.concatenate([R[i]["o_v"] for i in range(8)], axis=0).reshape(B, DEPTH, LP, 4, 128)
    nS = np.concatenate([R[i]["o_S"] for i in range(8)], axis=0)
    nC = np.concatenate([R[i]["o_C"] for i in range(8)], axis=0)
    nn = np.concatenate([R[i]["o_n"] for i in range(8)], axis=0)
    nm = np.concatenate([R[i]["o_m"] for i in range(8)], axis=0)
    return (y_prompt.astype(np.float32), y_sample.astype(np.float32), nk.astype(np.float32), nv.astype(np.float32),
            nS.astype(np.float32), nC.astype(np.float32), nn.astype(np.float32), nm.astype(np.float32))
```

```python
import contextlib
import math
import numpy as np
import concourse.bass as bass
import concourse.mybir as mybir
from concourse.bass_utils import run_bass_kernel_spmd

F32 = mybir.dt.float32
BF16 = mybir.dt.bfloat16
AF = mybir.ActivationFunctionType
ALU = mybir.AluOpType
AX = mybir.AxisListType


class SemSlot:
    __slots__ = ("sem", "v")

    def __init__(self):
        self.sem = None
        self.v = 0


class Trk:
    __slots__ = ("name", "lw", "rd", "ldma", "slot", "psum")

    def __init__(self, name):
        self.name = name
        self.psum = False
        self.lw = None
        self.rd = []
        self.ldma = None
        self.slot = {}


class Op:
    __slots__ = ("eng", "meth", "args", "kw", "deps", "isdma", "dtrk", "needinc", "ev")

    def __init__(self, eng, meth, args, kw, isdma=False, dtrk=None):
        self.eng, self.meth, self.args, self.kw = eng, meth, args, kw
        self.deps = []
        self.isdma = isdma
        self.dtrk = dtrk
        self.needinc = isdma
        self.ev = None


WRITE_KEYS = ("out", "accum_out")


class Sched:
    def __init__(self, nc):
        self.nc = nc
        self.ops = []
        self.trk = {}
        self.stack = contextlib.ExitStack()
        self.engs = {"pe": nc.tensor, "dve": nc.vector, "act": nc.scalar,
                     "pool": nc.gpsimd, "sp": nc.sync}
        self.sb_bytes = 0
        self.sb_peak = 0
        self.uid = 0
        self.all_trks = []
        self.scope_trks = [[]]
        self.free_slots = {"hw": [], "sw": []}
        self.bar = []
        self.bar_pending = {e: False for e in self.engs}
        self.last_eng_op = {e: None for e in self.engs}

    def _newtrk(self, tname, name):
        t = Trk(name)
        self.trk[tname] = t
        self.all_trks.append(t)
        self.scope_trks[-1].append(t)
        return t

    def sb(self, name, shape, dtype=F32):
        self.uid += 1
        t = self.stack.enter_context(self.nc.sbuf_tensor("%s_%d" % (name, self.uid), list(shape), dtype))
        self._newtrk(t.name, name)
        n = 1
        for s in shape[1:]:
            n *= s
        self.sb_bytes += n * (2 if dtype == BF16 else 4)
        self.sb_peak = max(self.sb_peak, self.sb_bytes)
        return t

    def ps(self, name, shape, dtype=F32):
        t = self.stack.enter_context(self.nc.psum_tensor(name, list(shape), dtype))
        self._newtrk(t.name, name).psum = True
        return t

    @contextlib.contextmanager
    def scope(self):
        old = self.stack
        self.stack = contextlib.ExitStack()
        self.scope_trks.append([])
        b0 = self.sb_bytes
        try:
            yield
        finally:
            self.stack.close()
            self.stack = old
            self.sb_bytes = b0
            self.barrier()
            for t in self.scope_trks.pop():
                for cls, sl in t.slot.items():
                    self.free_slots[cls].append(sl)

    def barrier(self):
        bar = [o for o in self.last_eng_op.values() if o is not None]
        bar += [t.ldma for t in self.all_trks if t.ldma is not None]
        self.bar = sorted(set(bar))
        for e in self.bar_pending:
            self.bar_pending[e] = True

    def dram(self, name, shape, dtype=F32, kind="Internal", track=True):
        t = self.nc.dram_tensor(name, list(shape), dtype, kind=kind)
        if track:
            self.trk[t.name] = Trk(name)
        return t

    def _tr(self, ap):
        try:
            return self.trk.get(ap.tensor.name)
        except AttributeError:
            return None

    def _record(self, op, reads, writes):
        oid = len(self.ops)
        deps = set()
        for t in reads:
            if t.lw is not None:
                deps.add((t.lw, "raw"))
            if t.psum:
                for r in t.rd:
                    deps.add((r, "rar"))
        for t in writes:
            if t.lw is not None:
                deps.add((t.lw, "waw"))
            for r in t.rd:
                deps.add((r, "war"))
        if op.isdma and op.dtrk.ldma is not None:
            deps.add((op.dtrk.ldma, "raw"))
        if self.bar_pending[op.eng]:
            self.bar_pending[op.eng] = False
            for b in self.bar:
                deps.add((b, "bar"))
        final = {}
        for d, kind in deps:
            dop = self.ops[d]
            if not dop.isdma and not op.isdma and dop.eng == op.eng:
                if op.eng == "pe":
                    continue
            final[d] = True
        latest = {}
        for d in list(final):
            dop = self.ops[d]
            if not dop.isdma and dop.eng in ("pe", "act", "dve"):
                if dop.eng in latest:
                    lo = min(latest[dop.eng], d)
                    latest[dop.eng] = max(latest[dop.eng], d)
                    del final[lo]
                else:
                    latest[dop.eng] = d
        op.deps = sorted(final)
        for d in op.deps:
            self.ops[d].needinc = True
        self.ops.append(op)
        for t in reads:
            t.rd.append(oid)
        for t in writes:
            t.lw = oid
            t.rd = []
        if op.isdma:
            op.dtrk.ldma = oid
            cls = "sw" if op.eng == "pool" else "hw"
            if cls not in op.dtrk.slot:
                op.dtrk.slot[cls] = self.free_slots[cls].pop() if self.free_slots[cls] else SemSlot()
            sl = op.dtrk.slot[cls]
            if sl.v >= 30000:
                sl = op.dtrk.slot[cls] = SemSlot()
            sl.v += 16
            op.ev = (sl, sl.v)
        else:
            self.last_eng_op[op.eng] = oid
        return oid

    def op(self, eng, meth, *args, **kw):
        reads, writes = [], []
        names = list(kw.items())
        for i, a in enumerate(args):
            names.append(("out" if i == 0 else "in", a))
        for k, v in names:
            t = self._tr(v) if hasattr(v, "tensor") else None
            if t is None:
                continue
            if k in WRITE_KEYS:
                if t not in writes:
                    writes.append(t)
            elif t not in reads:
                reads.append(t)
        return self._record(Op(eng, meth, args, kw), reads, writes)

    def dma(self, q, out, in_, **kw):
        to, ti = self._tr(out), self._tr(in_)
        dtrk = None
        for ap, t in ((out, to), (in_, ti)):
            if t is not None and not type(ap.tensor).__name__.startswith("DRam"):
                dtrk = t
        if dtrk is None:
            dtrk = to if to is not None else ti
        assert dtrk is not None, "dma with no tracked side"
        o = Op(q, "dma_start", (), dict(out=out, in_=in_, **kw), isdma=True, dtrk=dtrk)
        return self._record(o, [ti] if ti is not None else [], [to] if to is not None else [])

    def finalize(self, final_wait_eng="sp"):
        nc = self.nc
        esem = {e: nc.alloc_semaphore("es_" + e) for e in self.engs}
        ecnt = {e: 0 for e in self.engs}
        known = {e: {} for e in self.engs}
        nwait = 0
        nroll = 0
        for op in self.ops:
            eng = self.engs[op.eng]
            kn = known[op.eng]
            need = {}
            for d in op.deps:
                sem, val = self.ops[d].ev
                if isinstance(sem, SemSlot):
                    if sem.sem is None:
                        sem.sem = nc.alloc_semaphore("ds%d" % id(sem))
                    sem = sem.sem
                k = id(sem)
                if kn.get(k, 0) >= val:
                    continue
                if k not in need or need[k][1] < val:
                    need[k] = (sem, val)
            for k, (sem, val) in need.items():
                eng.wait_ge(sem, val)
                kn[k] = val
                nwait += 1
            ins = getattr(eng, op.meth)(*op.args, **op.kw)
            if op.isdma:
                slot = op.ev[0]
                if slot.sem is None:
                    slot.sem = nc.alloc_semaphore("ds%d" % id(slot))
                ins.then_inc(slot.sem, 16)
            elif op.needinc:
                if ecnt[op.eng] >= 30000:
                    nroll += 1
                    esem[op.eng] = nc.alloc_semaphore("es_%s_%d" % (op.eng, nroll))
                    ecnt[op.eng] = 0
                ecnt[op.eng] += 1
                ins.then_inc(esem[op.eng], 1)
                op.ev = (esem[op.eng], ecnt[op.eng])
        eng = self.engs[final_wait_eng]
        seen = set()
        for t in self.all_trks:
            for sl in t.slot.values():
                if sl.sem is not None and id(sl) not in seen:
                    seen.add(id(sl))
                    eng.wait_ge(sl.sem, sl.v)
        for e in self.engs:
            if ecnt[e] and e != final_wait_eng:
                eng.wait_ge(esem[e], ecnt[e])
        self.stats = dict(n_ops=len(self.ops), n_wait=nwait, ecnt=dict(ecnt), sb_peak=self.sb_peak, nsem=len(seen) + 5)
        return self.stats


D = 1024
KC = 8
LS = 4096
LP = 256
NPR = 2
T = LS + NPR * LP
NT = T // 512
NB = T // 128
PAST = 512
DEPTH = 2
FH = 2816
FC = FH // 128
EPS = 1e-6
O_AQ, O_AK, O_AV = 0, 512, 1024
O_BQ, O_BK, O_BV, O_BG, O_BA, O_BB = 1536, 1792, 2048, 2304, 2560, 2568
O_CQ, O_CK, O_CV, O_CO, O_CI, O_CF = 2576, 2832, 3088, 3344, 3600, 3608
O_AQS, O_AKS = 3616, 4128
WIN = 4640
FM_CHUNKS = ([O_AQ + 128 * i for i in range(4)] + [O_AK + 128 * i for i in range(4)]
             + [O_BQ + 128 * i for i in range(6)] + [O_CQ + 128 * i for i in range(4)]
             + [O_AQS + 128 * i for i in range(4)] + [O_AKS + 128 * i for i in range(4)])
TM_GROUPS = [
    ("av", [(O_AV, 512)]),
    ("ckv", [(O_CK, 256), (O_CV, 256)]),
    ("og", [(O_BG, 256), (O_CO, 256)]),
    ("gt", [(O_BA, 16), (O_CI, 16)]),
    ("ak", [(O_AK, 512)]),
]
TM_OFF = {}
_o = 0
for _n, _cols in TM_GROUPS:
    TM_OFF[_n] = _o
    _o += sum(w for _, w in _cols)
TM_W = _o


class StopBuild(Exception):
    pass


DBG = {}


def chk(name):
    if DBG.get("stop") == name:
        raise StopBuild(name)


def lam_init_of(l):
    return 0.8 - 0.6 * math.exp(-0.3 * l)


def host_consts():
    i = np.arange(128)
    same = (i[:, None] // 64) == (i[None, :] // 64)
    c = {}
    c["ident"] = np.eye(128, dtype=np.float32)
    c["ones"] = np.ones((128, 128), np.float32)
    c["bd"] = same.astype(np.float32)
    c["m_ig"] = (same & (i[:, None] > i[None, :])).astype(np.float32)
    c["m_il"] = (same & (i[:, None] < i[None, :])).astype(np.float32)
    c["m_le"] = (same & (i[:, None] <= i[None, :])).astype(np.float32)
    c["m_ge"] = (same & (i[:, None] >= i[None, :])).astype(np.float32)
    order = ["ident", "ones", "bd", "m_ig", "m_il", "m_le", "m_ge"]
    cm = np.stack([c[k] for k in order], axis=1)
    t = np.arange(LS)
    rows = (t // 64).astype(np.float64)
    cols = (t % 64).astype(np.float64)
    nf = 16
    inv = 10000.0 ** (-np.arange(nf, dtype=np.float64) / nf)
    ang = np.concatenate([rows[:, None] * inv, cols[:, None] * inv], axis=-1)
    ang = ang.astype(np.float32).astype(np.float64)
    cos = np.cos(ang).astype(np.float32)
    sin = np.sin(ang).astype(np.float32)
    ct = np.zeros((128, LS), np.float32)
    st = np.zeros((128, LS), np.float32)
    for m in range(2):
        for d in range(64):
            ct[m * 64 + d] = cos[:, d // 2]
            st[m * 64 + d] = sin[:, d // 2] * (-1.0 if d % 2 == 0 else 1.0)
    return np.ascontiguousarray(cm), ct, st


CONST_ORDER = {"ident": 0, "ones": 1, "bd": 2, "m_ig": 3, "m_il": 4, "m_le": 5, "m_ge": 6}


class K:
    pass


def build(debug_outs=()):
    nc = bass.Bass("TRN2", target_bir_lowering=False)
    S = Sched(nc)
    k = K()
    k.nc, k.S = nc, S

    def din(name, shape):
        return nc.dram_tensor(name, list(shape), F32, kind="ExternalInput")

    def dout(name, shape):
        return S.dram(name, shape, F32, kind="ExternalOutput")

    k.x_all = din("x_all", [T, D])
    k.cache_k = din("cache_k", [DEPTH, PAST, 512])
    k.cache_v = din("cache_v", [DEPTH, PAST, 512])
    k.st_d = din("st_d", [DEPTH, 2, 4, 64, 64])
    k.st_c = din("st_c", [DEPTH, 2, 4, 64, 64])
    k.st_n = din("st_n", [DEPTH, 2, 4, 64])
    k.st_m = din("st_m", [DEPTH, 2, 4])
    k.cT = din("cT", [128, KC, 2])
    k.w_mod = din("w_mod", [DEPTH, D, 6 * D])
    k.w_in = din("w_in", [DEPTH, D, WIN])
    k.w_out = din("w_out", [DEPTH, D, D])
    k.w_gu = din("w_gu", [DEPTH, D, 2 * FH])
    k.w_dn = din("w_dn", [DEPTH, FH, D])
    k.consts = din("consts", [128, 7, 128])
    k.rope_c = din("rope_c", [128, LS])
    k.rope_s = din("rope_s", [128, LS])
    k.n1g = din("n1g", [128, DEPTH, KC])
    k.n2g = din("n2g", [128, DEPTH, KC])
    k.fng = din("fng", [1, D])
    k.b_mod = din("b_mod", [128, DEPTH, 48])
    k.b_fm = din("b_fm", [128, DEPTH, len(FM_CHUNKS)])
    k.b_tm = din("b_tm", [1, DEPTH, TM_W])
    k.conv_w = din("conv_w", [128, DEPTH, 6, 5])
    k.lam_qk = din("lam_qk", [DEPTH, 256])
    k.subln_g = din("subln_g", [128, DEPTH])
    k.dn_g = din("dn_g", [DEPTH, 64])
    k.mn_g = din("mn_g", [DEPTH, 64])
    k.a_log = din("a_log", [DEPTH, 8])
    k.dt_b = din("dt_b", [DEPTH, 8])
    k.f_b = din("f_b", [DEPTH, 8])
    k.y_s = dout("y_s", [LS, D])
    k.y_p = dout("y_p", [NPR * LP, D])
    k.o_k = dout("o_k", [NPR, DEPTH, LP, 512])
    k.o_v = dout("o_v", [NPR, DEPTH, LP, 512])
    k.o_S = dout("o_S", [NPR, DEPTH, 2, 4, 64, 64])
    k.o_C = dout("o_C", [NPR, DEPTH, 2, 4, 64, 64])
    k.o_n = dout("o_n", [NPR, DEPTH, 2, 4, 64])
    k.o_m = dout("o_m", [NPR, DEPTH, 2, 4])
    k.XT = S.dram("XT", [KC, 128, T])
    k.QT = S.dram("QT", [4, 128, T], BF16)
    k.KT = S.dram("KT", [4, 128, T + PAST], BF16)
    k.VV = S.dram("VV", [T + PAST, 512], BF16)
    k.BT = S.dram("BT", [6, 128, T])
    k.CQK = S.dram("CQK", [4, 128, T])
    k.TMS = S.dram("TMS", [T, TM_W])
    k.DQK = S.dram("DQK", [8, 64, T])
    k.DKV = S.dram("DKV", [T, 512])
    k.OD = S.dram("OD", [T, 256])
    k.OD2 = S.dram("OD2", [T, 256])
    k.OM2 = S.dram("OM2", [T, 256])
    k.OM = S.dram("OM", [T, 256])
    k.HT = S.dram("HT", [FC, 128, T], BF16)
    k.dbg = {}
    for name, shape in debug_outs:
        k.dbg[name] = dout("dbg_" + name, shape)

    k.cst = S.sb("cst", [128, 7, 128])
    k.cstb = S.sb("cstb", [128, 7, 128], BF16)
    S.dma("sp", k.cst[:], k.consts[:])
    S.op("dve", "tensor_copy", out=k.cstb[:], in_=k.cst[:])
    k.C = lambda name: k.cst[:, CONST_ORDER[name], :]
    k.Cb = lambda name: k.cstb[:, CONST_ORDER[name], :]
    k.BIG = S.sb("BIG", [128, KC, T], BF16)
    k.ps = [S.ps("ps%d" % i, [128, 512]) for i in range(8)]
    k.mod = S.sb("mod", [128, DEPTH, 48, 2])
    k.g1 = S.sb("g1", [128, DEPTH, KC, 2])
    k.g2 = S.sb("g2", [128, DEPTH, KC, 2])
    k.bfm = S.sb("bfm", [128, DEPTH, len(FM_CHUNKS)])
    S.dma("sp", k.bfm[:], k.b_fm[:])
    k.btm = S.sb("btm", [1, DEPTH, TM_W], BF16)
    k.ones1 = S.sb("ones1", [1, 128], BF16)
    S.op("dve", "memset", k.ones1[:], 1.0)
    return k


def V(k, meth, **kw):
    return k.S.op("dve", meth, **kw)


def A(k, **kw):
    return k.S.op("act", "activation", **kw)


def G(k, meth, *a, **kw):
    return k.S.op("pool", meth, *a, **kw)


def MM(k, out, lhsT, rhs, start=True, stop=True):
    return k.S.op("pe", "matmul", out, lhsT=lhsT, rhs=rhs, start=start, stop=stop)


def phase_setup(k):
    with k.S.scope():
        _phase_setup(k)


def _phase_setup(k):
    S = k.S
    csil = S.sb("csil", [128, KC, 2])
    ctmp = S.sb("ctmp", [128, KC, 2])
    S.dma("sp", ctmp[:], k.cT[:])
    A(k, out=csil[:], in_=ctmp[:], func=AF.Silu)
    bm = S.sb("bm", [128, DEPTH, 48])
    S.dma("sp", bm[:], k.b_mod[:])
    n1 = S.sb("n1", [128, DEPTH, KC])
    n2 = S.sb("n2", [128, DEPTH, KC])
    S.dma("sp", n1[:], k.n1g[:])
    S.dma("sp", n2[:], k.n2g[:])
    btmf = S.sb("btmf", [1, DEPTH, TM_W])
    S.dma("sp", btmf[:], k.b_tm[:])
    V(k, "tensor_copy", out=k.btm[:], in_=btmf[:])
    xin = [S.sb("xin%d" % i, [128, D]) for i in range(2)]
    xto = [S.sb("xto%d" % i, [128, KC, 128]) for i in range(2)]

    def xblock(b):
        xi = xin[b % 2]
        xo = xto[b % 2]
        S.dma("sp", xi[:], k.x_all[b * 128:(b + 1) * 128, :])
        for half in range(2):
            p = k.ps[1 + (b % 2) * 2 + half]
            for j in range(4):
                kc = half * 4 + j
                S.op("pe", "transpose", out=p[:, j * 128:(j + 1) * 128], in_=xi[:, kc * 128:(kc + 1) * 128],
                     identity=k.C("ident"))
            if half == 0:
                V(k, "tensor_copy", out=xo[:, 0:4, :], in_=p[:].rearrange("p (c n) -> p c n", n=128))
            else:
                A(k, out=xo[:, 4:8, :], in_=p[:].rearrange("p (c n) -> p c n", n=128), func=AF.Copy)
        S.dma("pool", k.XT[:, :, b * 128:(b + 1) * 128].rearrange("c p n -> p c n"), xo[:])

    wst = [S.sb("wmst%d" % i, [128, KC, 768]) for i in range(2)]
    pm = k.ps[0]
    n = 0
    NG = DEPTH * 8
    for l in range(DEPTH):
        for g in range(8):
            w = wst[n % 2]
            n += 1
            S.dma("sp", w[:], k.w_mod[l, :, g * 768:(g + 1) * 768].rearrange("(c p) n -> p c n", p=128))
            for b in range((n - 1) * NB // NG, n * NB // NG):
                xblock(b)
            for j in range(6):
                mc = g * 6 + j
                for kc in range(KC):
                    MM(k, pm[:, mc * 2:mc * 2 + 2], lhsT=w[:, kc, j * 128:(j + 1) * 128], rhs=csil[:, kc, :],
                       start=(kc == 0), stop=(kc == KC - 1))
        for r in range(2):
            V(k, "tensor_tensor", out=k.mod[:, l, :, r], in0=pm[:, 0:96].rearrange("p (c r) -> p c r", r=2)[:, :, r],
              in1=bm[:, l, :], op=ALU.add)
        for r in range(2):
            V(k, "scalar_tensor_tensor", out=k.g1[:, l, :, r], in0=k.mod[:, l, 8:16, r], scalar=1.0, in1=n1[:, l, :],
              op0=ALU.add, op1=ALU.mult)
            V(k, "scalar_tensor_tensor", out=k.g2[:, l, :, r], in0=k.mod[:, l, 32:40, r], scalar=1.0, in1=n2[:, l, :],
              op0=ALU.add, op1=ALU.mult)


def seq_r(tile):
    return 0 if tile < LS // 512 else 1


def phase_norm(k, l, which):
    with k.S.scope():
        S = k.S
        k.xt_buf = [(S.sb("xta%d" % i, [128, 4, 512]), S.sb("xtb%d" % i, [128, 4, 512])) for i in range(2)]
        k.sq_buf = S.sb("sq", [128, 2, 512])
        k.rstd_buf = S.sb("rstd", [128, 512])
        k.tmp_buf = [S.sb("tmp%d" % i, [128, 512]) for i in range(2)]
        _phase_norm(k, l, which)


def _phase_norm(k, l, which):
    S = k.S
    gg = k.g1 if which == 1 else k.g2
    sh0 = 0 if which == 1 else 24
    for t in range(NT):
        r = seq_r(t)
        xta, xtb = k.xt_buf[t % 2]
        S.dma("sp", xta[:], k.XT[0:4, :, t * 512:(t + 1) * 512].rearrange("c p n -> p c n"))
        S.dma("pool", xtb[:], k.XT[4:8, :, t * 512:(t + 1) * 512].rearrange("c p n -> p c n"))

        def xsl(kc):
            return (xta if kc < 4 else xtb)[:, kc % 4, :]
        sq = k.sq_buf
        pss = k.ps[t % 2]
        for kc in range(KC):
            A(k, out=sq[:, kc % 2, :], in_=xsl(kc), func=AF.Square)
            MM(k, pss[:], lhsT=k.C("ones"), rhs=sq[:, kc % 2, :], start=(kc == 0), stop=(kc == KC - 1))
        rstd = k.rstd_buf
        A(k, out=k.tmp_buf[0][:], in_=pss[:], func=AF.Sqrt, bias=EPS, scale=1.0 / D)
        V(k, "reciprocal", out=rstd[:], in_=k.tmp_buf[0][:])
        for kc in range(KC):
            tmp = k.tmp_buf[kc % 2]
            V(k, "scalar_tensor_tensor", out=tmp[:], in0=xsl(kc), scalar=gg[:, l, kc, r:r + 1], in1=rstd[:],
              op0=ALU.mult, op1=ALU.mult)
            A(k, out=k.BIG[:, kc, t * 512:(t + 1) * 512], in_=tmp[:], func=AF.Identity,
              bias=k.mod[:, l, sh0 + kc, r:r + 1], scale=1.0)


def load_w_bf16(k, dst, src_ap, stage, eng_i):
    S = k.S
    n = src_ap.shape[-1]
    S.dma("sp", stage[:, :, 0:n], src_ap.rearrange("(c p) n -> p c n", p=128))
    if eng_i % 2 == 0:
        V(k, "tensor_copy", out=dst, in_=stage[:, :, 0:n])
    else:
        G(k, "tensor_copy", out=dst, in_=stage[:, :, 0:n])


def phase_inproj(k, l):
    with k.S.scope():
        S = k.S
        k.tmp_buf = [S.sb("tmp%d" % i, [128, 512]) for i in range(2)]
        k.ob_buf = [S.sb("ob%d" % i, [128, 512], BF16) for i in range(2)]
        k.obf_buf = [S.sb("obf%d" % i, [128, 512]) for i in range(2)]
        k.wfm = [S.sb("wfm%d" % i, [128, KC, 128], BF16) for i in range(2)]
        k.wstage = [S.sb("wstage%d" % i, [128, KC, 256]) for i in range(2)]
        k.wtm = S.sb("wtm", [128, KC, TM_W], BF16)
        k.vb_buf = [S.sb("vb%d" % i, [128, 512], BF16) for i in range(2)]
        k.tmf_buf = [S.sb("tmf%d" % i, [128, 512]) for i in range(2)]
        k.ropec = S.sb("ropec", [128, LS])
        k.ropes = S.sb("ropes", [128, LS])
        S.dma("sp", k.ropec[:], k.rope_c[:])
        S.dma("sp", k.ropes[:], k.rope_s[:])
        _phase_inproj(k, l)


def _phase_inproj(k, l):
    S = k.S
    nfm = len(FM_CHUNKS)
    wfm = k.wfm
    stage = k.wstage
    fm_index = {c: i for i, c in enumerate(FM_CHUNKS)}

    def fm_matmul(col, t, ps):
        for kc in range(KC):
            MM(k, ps[:], lhsT=wcur[:, kc, :], rhs=k.BIG[:, kc, t * 512:(t + 1) * 512], start=(kc == 0), stop=(kc == KC - 1))

    cnt = 0
    for which, o_main, o_sw, dst in (("q", O_AQ, O_AQS, k.QT), ("k", O_AK, O_AKS, k.KT)):
        for h in range(4):
            wm = wfm[0]
            ws = wfm[1]
            load_w_bf16(k, wm[:], k.w_in[l, :, o_main + h * 128:o_main + (h + 1) * 128], stage[0], 0)
            load_w_bf16(k, ws[:], k.w_in[l, :, o_sw + h * 128:o_sw + (h + 1) * 128], stage[1], 1)
            bm = k.bfm[:, l, fm_index[o_main + h * 128]:fm_index[o_main + h * 128] + 1]
            bs = k.bfm[:, l, fm_index[o_sw + h * 128]:fm_index[o_sw + h * 128] + 1]
            for t in range(NT):
                p1 = k.ps[(cnt % 2) * 2]
                p2 = k.ps[(cnt % 2) * 2 + 1]
                ob = k.ob_buf[cnt % 2]
                cnt += 1
                for kc in range(KC):
                    MM(k, p1[:], lhsT=wm[:, kc, :], rhs=k.BIG[:, kc, t * 512:(t + 1) * 512], start=(kc == 0), stop=(kc == KC - 1))
                if seq_r(t) == 0:
                    for kc in range(KC):
                        MM(k, p2[:], lhsT=ws[:, kc, :], rhs=k.BIG[:, kc, t * 512:(t + 1) * 512], start=(kc == 0), stop=(kc == KC - 1))
                    t1 = k.tmp_buf[0]
                    t2 = k.tmp_buf[1]
                    V(k, "scalar_tensor_tensor", out=t1[:], in0=p1[:], scalar=bm, in1=k.ropec[:, t * 512:(t + 1) * 512],
                      op0=ALU.add, op1=ALU.mult)
                    V(k, "scalar_tensor_tensor", out=t2[:], in0=p2[:], scalar=bs, in1=k.ropes[:, t * 512:(t + 1) * 512],
                      op0=ALU.add, op1=ALU.mult)
                    G(k, "tensor_tensor", out=ob[:], in0=t1[:], in1=t2[:], op=ALU.add)
                else:
                    A(k, out=ob[:], in_=p1[:], func=AF.Identity, bias=bm, scale=1.0)
                S.dma("pool", dst[h, :, t * 512:(t + 1) * 512], ob[:])
    for o_main, nch, dst in ((O_BQ, 6, k.BT), (O_CQ, 4, k.CQK)):
        for c in range(nch):
            wm = wfm[cnt % 2]
            load_w_bf16(k, wm[:], k.w_in[l, :, o_main + c * 128:o_main + (c + 1) * 128], stage[cnt % 2], cnt)
            bm = k.bfm[:, l, fm_index[o_main + c * 128]:fm_index[o_main + c * 128] + 1]
            for t in range(NT):
                p1 = k.ps[(cnt % 2) * 2]
                ob = k.obf_buf[cnt % 2]
                cnt += 1
                for kc in range(KC):
                    MM(k, p1[:], lhsT=wm[:, kc, :], rhs=k.BIG[:, kc, t * 512:(t + 1) * 512], start=(kc == 0), stop=(kc == KC - 1))
                A(k, out=ob[:], in_=p1[:], func=AF.Identity, bias=bm, scale=1.0)
                S.dma("pool", dst[c, :, t * 512:(t + 1) * 512], ob[:])
    wtm = k.wtm
    for name, cols in TM_GROUPS:
        o = TM_OFF[name]
        for (c0, w) in cols:
            done = 0
            while done < w:
                ww = min(256, w - done)
                st = stage[cnt % 2]
                cnt += 1
                S.dma("sp", st[:, :, 0:ww], k.w_in[l, :, c0 + done:c0 + done + ww].rearrange("(c p) n -> p c n", p=128))
                V(k, "tensor_copy", out=wtm[:, :, o + done:o + done + ww], in_=st[:, :, 0:ww])
                done += ww
            o += w
    for b in range(NB):
        isprompt = b >= LS // 128
        for gi, (name, cols) in enumerate(TM_GROUPS):
            if name == "ak" and not isprompt:
                continue
            o = TM_OFF[name]
            w = sum(x for _, x in cols)
            p = k.ps[4 + (cnt % 2)]
            cnt += 1
            for kc in range(KC):
                MM(k, p[:, 0:w], lhsT=k.BIG[:, kc, b * 128:(b + 1) * 128], rhs=wtm[:, kc, o:o + w], start=(kc == 0), stop=False)
            MM(k, p[:, 0:w], lhsT=k.ones1[:, :], rhs=k.btm[:, l, o:o + w], start=False, stop=True)
            if name == "av":
                vb = k.vb_buf[b % 2]
                V(k, "tensor_copy", out=vb[:], in_=p[:])
                S.dma("pool", k.VV[b * 128:(b + 1) * 128, :], vb[:])
                if isprompt:
                    vf = k.tmf_buf[cnt % 2]
                    A(k, out=vf[:], in_=p[:], func=AF.Copy)
                    pb = b - LS // 128
                    S.dma("pool", k.o_v[pb // 2, l, (pb % 2) * 128:(pb % 2 + 1) * 128, :], vf[:])
            elif name == "ak":
                vf = k.tmf_buf[cnt % 2]
                A(k, out=vf[:], in_=p[:], func=AF.Copy)
                pb = b - LS // 128
                S.dma("pool", k.o_k[pb // 2, l, (pb % 2) * 128:(pb % 2 + 1) * 128, :], vf[:])
            else:
                vf = k.tmf_buf[cnt % 2]
                A(k, out=vf[:, 0:w], in_=p[:, 0:w], func=AF.Copy)
                S.dma("pool", k.TMS[b * 128:(b + 1) * 128, o:o + w], vf[:, 0:w])


def phase_outproj(k, l):
    S = k.S
    with S.scope():
        wo = S.sb("wo", [128, KC, D], BF16)
        stage = [S.sb("ostage%d" % i, [128, KC, 256]) for i in range(2)]
        for j in range(4):
            load_w_bf16(k, wo[:, :, j * 256:(j + 1) * 256], k.w_out[l, :, j * 256:(j + 1) * 256], stage[j % 2], j)
        xt = [S.sb("oxt%d" % i, [128, KC, 512]) for i in range(2)]
        for t in range(NT):
            r = seq_r(t)
            x = xt[t % 2]
            S.dma("sp", x[:], k.XT[:, :, t * 512:(t + 1) * 512].rearrange("c p n -> p c n"))
            for mc in range(KC):
                p = k.ps[mc % 2]
                for kc in range(KC):
                    MM(k, p[:], lhsT=wo[:, kc, mc * 128:(mc + 1) * 128], rhs=k.BIG[:, kc, t * 512:(t + 1) * 512],
                       start=(kc == 0), stop=(kc == KC - 1))
                V(k, "scalar_tensor_tensor", out=x[:, mc, :], in0=p[:], scalar=k.mod[:, l, 16 + mc, r:r + 1], in1=x[:, mc, :],
                  op0=ALU.mult, op1=ALU.add)
            S.dma("pool", k.XT[:, :, t * 512:(t + 1) * 512].rearrange("c p n -> p c n"), x[:])


def phase_ffn(k, l):
    S = k.S
    with S.scope():
        wg = [S.sb("wg%d" % i, [128, KC, 128], BF16) for i in range(2)]
        wu = [S.sb("wu%d" % i, [128, KC, 128], BF16) for i in range(2)]
        stage = [S.sb("fstage%d" % i, [128, KC, 256]) for i in range(2)]
        sil = [S.sb("sil%d" % i, [128, 512]) for i in range(2)]
        hb = [S.sb("hb%d" % i, [128, 512], BF16) for i in range(2)]
        cnt = 0
        for j in range(FC):
            g, u = wg[j % 2], wu[j % 2]
            load_w_bf16(k, g[:], k.w_gu[l, :, j * 128:(j + 1) * 128], stage[0], 0)
            load_w_bf16(k, u[:], k.w_gu[l, :, FH + j * 128:FH + (j + 1) * 128], stage[1], 1)
            for t in range(NT):
                pg = k.ps[(cnt % 2) * 2]
                pu = k.ps[(cnt % 2) * 2 + 1]
                for kc in range(KC):
                    MM(k, pg[:], lhsT=g[:, kc, :], rhs=k.BIG[:, kc, t * 512:(t + 1) * 512], start=(kc == 0), stop=(kc == KC - 1))
                for kc in range(KC):
                    MM(k, pu[:], lhsT=u[:, kc, :], rhs=k.BIG[:, kc, t * 512:(t + 1) * 512], start=(kc == 0), stop=(kc == KC - 1))
                A(k, out=sil[cnt % 2][:], in_=pg[:], func=AF.Silu)
                V(k, "tensor_tensor", out=hb[cnt % 2][:], in0=sil[cnt % 2][:], in1=pu[:], op=ALU.mult)
                S.dma("pool", k.HT[j, :, t * 512:(t + 1) * 512], hb[cnt % 2][:])
                cnt += 1
    with S.scope():
        wd = S.sb("wd", [128, FC, D], BF16)
        stage = S.sb("dstage", [128, KC, 256])
        n = 0
        for cp in range(4):
            for kr in (0, 8, 16):
                nn = min(8, FC - kr)
                S.dma("sp", stage[:, 0:nn, :], k.w_dn[l, kr * 128:(kr + nn) * 128, cp * 256:(cp + 1) * 256].rearrange("(c p) n -> p c n", p=128))
                if n % 2 == 0:
                    V(k, "tensor_copy", out=wd[:, kr:kr + nn, cp * 256:(cp + 1) * 256], in_=stage[:, 0:nn, :])
                else:
                    G(k, "tensor_copy", out=wd[:, kr:kr + nn, cp * 256:(cp + 1) * 256], in_=stage[:, 0:nn, :])
                n += 1
        hts = [S.sb("ht%d" % i, [128, FC, 512], BF16) for i in range(2)]
        x = S.sb("fxt", [128, KC, 512])
        for t in range(NT):
            r = seq_r(t)
            ht = hts[t % 2]
            for c4 in range(0, FC, 6):
                c5 = min(FC, c4 + 6)
                S.dma("sp", ht[:, c4:c5, :], k.HT[c4:c5, :, t * 512:(t + 1) * 512].rearrange("c p n -> p c n"))
            S.dma("sp", x[:], k.XT[:, :, t * 512:(t + 1) * 512].rearrange("c p n -> p c n"))
            for mc in range(KC):
                p = k.ps[mc % 2]
                for kc in range(FC):
                    MM(k, p[:], lhsT=wd[:, kc, mc * 128:(mc + 1) * 128], rhs=ht[:, kc, :], start=(kc == 0), stop=(kc == FC - 1))
                V(k, "scalar_tensor_tensor", out=x[:, mc, :], in0=p[:], scalar=k.mod[:, l, 40 + mc, r:r + 1], in1=x[:, mc, :],
                  op0=ALU.mult, op1=ALU.add)
            S.dma("pool", k.XT[:, :, t * 512:(t + 1) * 512].rearrange("c p n -> p c n"), x[:])


def phase_final(k):
    S = k.S
    with S.scope():
        fr = S.sb("fr", [1, D])
        S.dma("sp", fr[:], k.fng[:])
        fb = S.sb("fb", [128, D])
        for j in range(2):
            MM(k, k.ps[j][:], lhsT=k.cst[0:1, 1, :], rhs=fr[:, j * 512:(j + 1) * 512])
            V(k, "tensor_copy", out=fb[:, j * 512:(j + 1) * 512], in_=k.ps[j][:])
        xb = [S.sb("yx%d" % i, [128, KC, 128]) for i in range(2)]
        xk = [S.sb("yk%d" % i, [128, D]) for i in range(2)]
        junk = S.sb("yjunk", [128, D])
        ss = S.sb("yss", [128, 2])
        yo = [S.sb("yo%d" % i, [128, D]) for i in range(2)]
        for b in range(NB):
            x = xb[b % 2]
            xt = xk[b % 2]
            S.dma("sp", x[:], k.XT[:, :, b * 128:(b + 1) * 128].rearrange("c p n -> p c n"))
            for half in range(2):
                p = k.ps[2 + (b % 2) * 2 + half]
                for j in range(4):
                    S.op("pe", "transpose", out=p[:, j * 128:(j + 1) * 128], in_=x[:, half * 4 + j, :], identity=k.C("ident"))
                if half == 0:
                    V(k, "tensor_copy", out=xt[:, 0:512], in_=p[:])
                else:
                    A(k, out=xt[:, 512:1024], in_=p[:], func=AF.Copy)
            k.S.op("dve", "memset", ss[:, 0:1], 0.0)
            A(k, out=junk[:], in_=xt[:], func=AF.Square, accum_out=ss[:, 0:1])
            A(k, out=ss[:, 1:2], in_=ss[:, 0:1], func=AF.Sqrt, bias=EPS, scale=1.0 / D)
            V(k, "reciprocal", out=ss[:, 1:2], in_=ss[:, 1:2])
            y = yo[b % 2]
            V(k, "scalar_tensor_tensor", out=y[:], in0=xt[:], scalar=ss[:, 1:2], in1=fb[:], op0=ALU.mult, op1=ALU.mult)
            if b < LS // 128:
                S.dma("pool", k.y_s[b * 128:(b + 1) * 128, :], y[:])
            else:
                pb = b - LS // 128
                S.dma("pool", k.y_p[pb * 128:(pb + 1) * 128, :], y[:])


SEQS = [dict(tok0=0, L=LS, sample=True, p=-1)] + [dict(tok0=LS + i * LP, L=LP, sample=False, p=i) for i in range(NPR)]


def phase_attn(k, l):
    S = k.S
    with S.scope():
        lt = S.sb("lamt", [128, 256])
        S.dma("sp", lt[:], k.lam_qk[l:l + 1, :].partition_broadcast(128))
        lp = S.sb("lamp", [128, 256])
        ls = S.sb("lams", [128, 4])
        V(k, "tensor_tensor", out=lp[:, 0:64], in0=lt[:, 0:64], in1=lt[:, 64:128], op=ALU.mult)
        V(k, "tensor_tensor", out=lp[:, 64:128], in0=lt[:, 128:192], in1=lt[:, 192:256], op=ALU.mult)
        V(k, "reduce_sum", out=ls[:, 0:1], in_=lp[:, 0:64], axis=AX.X)
        V(k, "reduce_sum", out=ls[:, 1:2], in_=lp[:, 64:128], axis=AX.X)
        A(k, out=ls[:, 0:2], in_=ls[:, 0:2], func=AF.Exp)
        V(k, "tensor_tensor", out=ls[:, 2:3], in0=ls[:, 1:2], in1=ls[:, 0:1], op=ALU.subtract)
        V(k, "tensor_scalar", out=ls[:, 3:4], in0=ls[:, 2:3], scalar1=-lam_init_of(l), scalar2=None, op0=ALU.add)
        nlam = ls[:, 3:4]
        sg = S.sb("sublg", [128, DEPTH])
        S.dma("sp", sg[:], k.subln_g[:])
        sgl = S.sb("sublgl", [128, 1])
        V(k, "tensor_scalar", out=sgl[:], in0=sg[:, l:l + 1], scalar1=1.0 - lam_init_of(l), scalar2=None, op0=ALU.mult)
        ckf = S.sb("ckf", [128, 512])
        ckb = S.sb("ckb", [128, 4, 128], BF16)
        cvf = S.sb("cvf", [128, 512])
        cvb = S.sb("cvb", [128, 512], BF16)
        for b in range(PAST // 128):
            S.dma("sp", ckf[:], k.cache_k[l, b * 128:(b + 1) * 128, :])
            for h in range(4):
                S.op("pe", "transpose", out=k.ps[7][:, h * 128:(h + 1) * 128], in_=ckf[:, h * 128:(h + 1) * 128], identity=k.C("ident"))
            V(k, "tensor_copy", out=ckb[:], in_=k.ps[7][:].rearrange("p (h n) -> p h n", n=128))
            S.dma("pool", k.KT[:, :, T + b * 128:T + (b + 1) * 128].rearrange("h p n -> p h n"), ckb[:])
            S.dma("sp", cvf[:], k.cache_v[l, b * 128:(b + 1) * 128, :])
            V(k, "tensor_copy", out=cvb[:], in_=cvf[:])
            S.dma("pool", k.VV[T + b * 128:T + (b + 1) * 128, :], cvb[:])
        ktb = S.sb("ktb", [128, LS + PAST], BF16)
        vsb = S.sb("vsb", [128, (LS + PAST) // 128, 128], BF16)
        qsb = [S.sb("qsb%d" % i, [128, LS], BF16) for i in range(2)]
        for i in range(2):
            S.op("dve", "memset", qsb[i][:], 0.0)
        ptb = [S.sb("ptb%d" % i, [128, 512], BF16) for i in range(6)]
        zacc = [S.sb("zacc%d" % i, [128, 512]) for i in range(4)]
        sbank = [k.ps[0], k.ps[1], k.ps[6], k.ps[7]]
        rz = S.sb("rz", [128, 512])
        a0 = S.sb("a0", [128, 512])
        a1 = S.sb("a1", [128, 512])
        sq = S.sb("asq", [128, 512])
        for sq_ in SEQS:
            tok0, L = sq_["tok0"], sq_["L"]
            QW = min(512, L)
            nkt_own = L // 128
            nkt = nkt_own + (PAST // 128 if sq_["sample"] else 0)
            for h in range(4):
                S.dma("sp", ktb[:, 0:L], k.KT[h, :, tok0:tok0 + L])
                for t4 in range(0, nkt_own, 4):
                    t5 = min(nkt_own, t4 + 4)
                    S.dma("sp", vsb[:, t4:t5, :], k.VV[tok0 + t4 * 128:tok0 + t5 * 128, h * 128:(h + 1) * 128].rearrange("(t p) e -> p t e", p=128))
                if sq_["sample"]:
                    S.dma("sp", ktb[:, L:L + PAST], k.KT[h, :, T:T + PAST])
                    S.dma("sp", vsb[:, nkt_own:nkt, :], k.VV[T:T + PAST, h * 128:(h + 1) * 128].rearrange("(t p) e -> p t e", p=128))
                for m in range(2):
                    S.dma("sp", qsb[m][m * 64:(m + 1) * 64, 0:L], k.QT[h, m * 64:(m + 1) * 64, tok0:tok0 + L])
                for qt in range(L // QW):
                    units = [(m, kt) for m in range(2) for kt in range(nkt)]
                    NS = len(sbank)
                    NU = len(units)

                    def qk_mm(i):
                        m, kt = units[i]
                        MM(k, sbank[i % NS][:, 0:QW], lhsT=ktb[:, kt * 128:(kt + 1) * 128], rhs=qsb[m][:, qt * QW:(qt + 1) * QW])

                    for i in range(min(NS, NU)):
                        qk_mm(i)
                    first_pv = [True, True]
                    zused = set()
                    npv = [0, 0]
                    for i0 in range(0, NU, 2):
                        grp = [i for i in (i0, i0 + 1) if i < NU]
                        for i in grp:
                            m, kt = units[i]
                            pt = ptb[i % len(ptb)]
                            A(k, out=pt[:, 0:QW], in_=sbank[i % NS][:, 0:QW], func=AF.Exp, scale=0.125)
                            par = kt % 3
                            if par == 0:
                                MM(k, k.ps[4 + m][:, 0:QW], lhsT=k.Cb("ones"), rhs=pt[:, 0:QW], start=(kt == 0), stop=False)
                            else:
                                eng = "dve" if par == 1 else "pool"
                                za = zacc[m * 2 + par - 1]
                                if kt < 3:
                                    S.op(eng, "tensor_copy", out=za[:, 0:QW], in_=pt[:, 0:QW])
                                    zused.add(m * 2 + par - 1)
                                else:
                                    S.op(eng, "tensor_tensor", out=za[:, 0:QW], in0=za[:, 0:QW], in1=pt[:, 0:QW], op=ALU.add)
                        for i in reversed(grp):
                            m, kt = units[i]
                            pt = ptb[i % len(ptb)]
                            npv[m] += 1
                            MM(k, k.ps[2 + m][:, 0:QW], lhsT=vsb[:, kt, :], rhs=pt[:, 0:QW], start=first_pv[m], stop=(npv[m] == nkt))
                            first_pv[m] = False
                        for i in grp:
                            if i + NS < NU:
                                qk_mm(i + NS)
                    for m in range(2):
                        zl = [z for z in (m * 2, m * 2 + 1) if z in zused]
                        for zi, z in enumerate(zl):
                            MM(k, k.ps[4 + m][:, 0:QW], lhsT=k.C("ones"), rhs=zacc[z][:, 0:QW], start=False, stop=(zi == len(zl) - 1))
                    V(k, "reciprocal", out=rz[:, 0:QW], in_=k.ps[4][:, 0:QW])
                    V(k, "tensor_tensor", out=a0[:, 0:QW], in0=k.ps[2][:, 0:QW], in1=rz[:, 0:QW], op=ALU.mult)
                    V(k, "reciprocal", out=rz[:, 0:QW], in_=k.ps[5][:, 0:QW])
                    V(k, "tensor_tensor", out=a1[:, 0:QW], in0=k.ps[3][:, 0:QW], in1=rz[:, 0:QW], op=ALU.mult)
                    V(k, "scalar_tensor_tensor", out=a0[:, 0:QW], in0=a1[:, 0:QW], scalar=nlam, in1=a0[:, 0:QW], op0=ALU.mult, op1=ALU.add)
                    G(k, "tensor_tensor", out=sq[:, 0:QW], in0=a0[:, 0:QW], in1=a0[:, 0:QW], op=ALU.mult)
                    MM(k, k.ps[6][:, 0:QW], lhsT=k.C("ones"), rhs=sq[:, 0:QW])
                    A(k, out=sq[:, 0:QW], in_=k.ps[6][:, 0:QW], func=AF.Sqrt, bias=EPS, scale=1.0 / 128)
                    V(k, "reciprocal", out=rz[:, 0:QW], in_=sq[:, 0:QW])
                    V(k, "scalar_tensor_tensor", out=k.BIG[:, h, tok0 + qt * QW:tok0 + (qt + 1) * QW], in0=a0[:, 0:QW], scalar=sgl[:, 0:1],
                      in1=rz[:, 0:QW], op0=ALU.mult, op1=ALU.mult)


def run_streams(gens):
    gens = list(gens)
    while gens:
        for g in list(gens):
            try:
                next(g)
            except StopIteration:
                gens.remove(g)


class NS_:
    pass


DT_REC = F32


DT_ML = BF16


def phase_mlstm(k, l):
    S = k.S
    LN8 = math.log(0.125)
    I64 = k.cst[0:64, CONST_ORDER["ident"], 0:64]
    O64 = k.cst[0:64, CONST_ORDER["ones"], 0:64]

    def bc3(ap2, n):
        return ap2.unsqueeze(2).to_broadcast([ap2.shape[0], ap2.shape[1], n])

    def bcm(ap2, n):
        return ap2.unsqueeze(1).to_broadcast([ap2.shape[0], n, ap2.shape[1]])

    def r3(ap):
        return ap.rearrange("p (a n) -> p a n", n=64)

    def alloc(tag):
        t = NS_()
        for nm in ("diag", "X", "ET", "ebq"):
            setattr(t, nm, S.sb("m%s%s" % (nm, tag), [64, 8, 64]))
        for nm in ("qb", "sT", "kw"):
            setattr(t, nm, S.sb("m%s%s" % (nm, tag), [64, 8, 64], DT_ML))
        t.stb = S.sb("mstb" + tag, [64, 4, 128], DT_ML)
        for nm in ("nlf", "tg", "nbs", "t4", "t4b", "colE", "wk", "dec"):
            setattr(t, nm, S.sb("m%s%s" % (nm, tag), [64, 8]))
        t.gt2 = S.sb("mgt2" + tag, [64, 2, 32])
        t.ckv = S.sb("mckv" + tag, [64, 2, 512])
        t.og = S.sb("mog" + tag, [64, 512])
        t.qk = S.sb("mcqk" + tag, [64, 4, 2, 128])
        t.v1 = S.sb("mv1" + tag, [64, 8, 128], DT_ML)
        S.op("dve", "memset", t.v1[:], 0.0)
        S.op("dve", "memset", t.v1[:, :, 64:65], 1.0)
        t.state = S.sb("mstate" + tag, [64, 4, 128])
        for nm in ("hm", "omf", "tt", "sig"):
            setattr(t, nm, S.sb("m%s%s" % (nm, tag), [64, 256]))
        t.den = S.sb("mden" + tag, [64, 4])
        t.ss4 = S.sb("mss4" + tag, [64, 4])
        t.junk = S.sb("mjunk" + tag, [64, 64])
        t.em0 = S.sb("mem0" + tag, [64, 4])
        t.n0r = S.sb("mn0r" + tag, [4, 64])
        t.kvs = S.sb("mkvs" + tag, [4, 256])
        t.nls = S.sb("mnls" + tag, [4, 256])
        t.nblc = S.sb("mnblc" + tag, [4, 4])
        t.off = S.sb("moff" + tag, [4, 4])
        t.mx = S.sb("mmx" + tag, [4, 2])
        t.mfin = S.sb("mmfin" + tag, [4, 1])
        t.mrow = S.sb("mmrow" + tag, [1, 4])
        t.d4 = S.sb("md4" + tag, [4, 4])
        t.emf = S.sb("memf" + tag, [64, 4])
        t.so = S.sb("mso" + tag, [64, 4, 64])
        t.ncol = S.sb("mncol" + tag, [64, 4])
        t.nrow = S.sb("mnrow" + tag, [4, 64])
        return t

    def stream(sq_, dr, t, B, fbb, mgb):
        tok0, L = sq_["tok0"], sq_["L"]
        nblk = L // 128
        ci = CONST_ORDER["m_le"] if dr == 0 else CONST_ORDER["m_ge"]
        cum64 = k.cst[0:64, ci, 0:64]
        OWN, OTH = (k.OM, k.OM2) if dr == 0 else (k.OM2, k.OM)
        state = t.state
        S.op("dve", "memset", state[:], 0.0)
        if sq_["sample"]:
            S.dma("sp", t.em0[:], k.st_m[l, dr:dr + 1, :].partition_broadcast(64))
            A(k, out=t.em0[:], in_=t.em0[:], func=AF.Exp)
            S.dma("sp", state[:, :, 0:64], k.st_c[l, dr, :, :, :].rearrange("h a b -> a h b"))
            S.dma("sp", t.n0r[:], k.st_n[l, dr, :, :])
            S.op("pe", "transpose", out=B[0][0:64, 32:36], in_=t.n0r[:], identity=k.cst[0:4, CONST_ORDER["ident"], 0:4])
            V(k, "tensor_copy", out=state[:, :, 64], in_=B[0][0:64, 32:36])
            V(k, "tensor_tensor", out=state[:], in0=state[:], in1=bc3(t.em0[:], 128), op=ALU.mult)
        A(k, out=t.stb[:], in_=state[:], func=AF.Copy)
        blocks = list(range(nblk)) if dr == 0 else list(range(nblk - 1, -1, -1))
        corder = (0, 1) if dr == 0 else (1, 0)
        for step, bi in enumerate(blocks):
            r0 = tok0 + bi * 128
            for c in range(2):
                S.dma("sp", t.gt2[:, c, :], k.TMS[r0 + c * 64:r0 + (c + 1) * 64, TM_OFF["gt"]:TM_OFF["gt"] + 32])
                S.dma("sp", t.ckv[:, c, :], k.TMS[r0 + c * 64:r0 + (c + 1) * 64, TM_OFF["ckv"]:TM_OFF["ckv"] + 512])
            S.dma("sp", t.qk[:], k.CQK[:, :, r0:r0 + 128].rearrange("c (hh p) n -> p c hh n", p=64))
            yield
            kv4 = t.ckv[:].rearrange("p c (x h e) -> p c x h e", x=2, e=64)
            G(k, "tensor_copy", out=t.v1[:, :, 0:64].rearrange("p (c h) e -> p c h e", c=2), in_=kv4[:, :, 1, :, :])
            t3 = t.tg[:].rearrange("p (c h) -> p c h", c=2)
            V(k, "tensor_tensor", out=t3, in0=t.gt2[:, :, 24 + dr * 4:28 + dr * 4], in1=bcm(fbb[:, dr * 4:dr * 4 + 4], 2), op=ALU.add)
            A(k, out=t.tg[:], in_=t.tg[:], func=AF.Exp, scale=-1.0)
            A(k, out=t.nlf[:], in_=t.tg[:], func=AF.Ln, bias=1.0, scale=1.0)
            yield
            pa = B[0]
            for c in range(2):
                MM(k, pa[0:64, c * 4:(c + 1) * 4], lhsT=cum64, rhs=t.nlf[:, c * 4:(c + 1) * 4])
                MM(k, pa[0:64, 8 + c * 4:12 + c * 4], lhsT=O64, rhs=t.nlf[:, c * 4:(c + 1) * 4])
            yield
            V(k, "tensor_copy", out=t.nbs[:], in_=pa[0:64, 0:8])
            V(k, "tensor_tensor", out=t.t4[:], in0=t.nbs[:], in1=pa[0:64, 8:16], op=ALU.subtract)
            ig3 = t.gt2[:, :, 16 + dr * 4:20 + dr * 4]
            V(k, "tensor_tensor", out=t.t4b[:].rearrange("p (c h) -> p c h", c=2), in0=t.t4[:].rearrange("p (c h) -> p c h", c=2), in1=ig3, op=ALU.add)
            A(k, out=t.wk[:], in_=t.t4b[:], func=AF.Exp, bias=LN8, scale=1.0)
            V(k, "tensor_tensor", out=t.colE[:].rearrange("p (c h) -> p c h", c=2), in0=t.nbs[:].rearrange("p (c h) -> p c h", c=2), in1=ig3, op=ALU.add)
            V(k, "tensor_copy", out=t.dec[:], in_=pa[0:64, 8:16])
            A(k, out=t.dec[:], in_=t.dec[:], func=AF.Exp, scale=-1.0)
            V(k, "tensor_tensor", out=t.diag[:], in0=bcm(I64, 8), in1=bc3(t.nbs[:], 64), op=ALU.mult)
            yield
            if not sq_["sample"]:
                for c in range(2):
                    cc = bi * 2 + c
                    S.op("pe", "transpose", out=B[0][0:4, 256:320], in_=t.t4b[:, c * 4:(c + 1) * 4], identity=I64)
                    S.op("pe", "transpose", out=B[0][0:4, 384:448], in_=t.nlf[:, c * 4:(c + 1) * 4], identity=I64)
                    V(k, "tensor_copy", out=t.kvs[:, cc * 64:(cc + 1) * 64], in_=B[0][0:4, 256:320])
                    V(k, "tensor_copy", out=t.nls[:, cc * 64:(cc + 1) * 64], in_=B[0][0:4, 384:448])
            pb, pc = B[1], B[2]
            for pi in range(8):
                MM(k, pb[0:64, pi * 64:(pi + 1) * 64], lhsT=O64, rhs=t.diag[:, pi, :])
            for c in range(2):
                for h in range(4):
                    pi = c * 4 + h
                    MM(k, pc[0:64, pi * 64:(pi + 1) * 64], lhsT=t.qk[:, 2 + h // 2, h % 2, c * 64:(c + 1) * 64],
                       rhs=t.qk[:, h // 2, h % 2, c * 64:(c + 1) * 64])
            yield
            pb3 = r3(pb[0:64, :])
            V(k, "tensor_tensor", out=t.X[:], in0=bc3(t.colE[:], 64), in1=pb3, op=ALU.subtract)
            A(k, out=t.ebq[:], in_=pb3, func=AF.Exp, scale=-1.0)
            yield
            A(k, out=t.ET[:], in_=t.X[:], func=AF.Exp)
            q4 = t.qk[:, 0:2, :, :].rearrange("p a hh (c n) -> p c (a hh) n", c=2)
            G(k, "tensor_tensor", out=t.qb[:].rearrange("p (c h) n -> p c h n", c=2), in0=q4,
              in1=t.ebq[:].rearrange("p (c h) n -> p c h n", c=2), op=ALU.mult)
            V(k, "tensor_tensor", out=t.kw[:].rearrange("p (c h) e -> p c h e", c=2), in0=kv4[:, :, 0, :, :],
              in1=bc3(t.wk[:], 64).rearrange("p (c h) e -> p c h e", c=2), op=ALU.mult)
            yield
            G(k, "tensor_tensor", out=t.ET[:], in0=t.ET[:], in1=bcm(cum64, 8), op=ALU.mult)
            yield
            V(k, "scalar_tensor_tensor", out=t.sT[:], in0=r3(pc[0:64, :]), scalar=0.125, in1=t.ET[:], op0=ALU.mult, op1=ALU.mult)
            yield
            for c in corder:
                po, pst = B[3], B[1]
                for h in range(4):
                    pi = c * 4 + h
                    for (a0_, a1_) in ((0, 64), (64, 128)):
                        MM(k, po[0:64, h * 128 + a0_:h * 128 + a1_], lhsT=t.qb[:, pi, :], rhs=t.stb[:, h, a0_:a1_], start=True, stop=False)
                        MM(k, po[0:64, h * 128 + a0_:h * 128 + a1_], lhsT=t.sT[:, pi, :], rhs=t.v1[:, pi, a0_:a1_], start=False, stop=True)
                    MM(k, pst[0:64, h * 128:(h + 1) * 128], lhsT=t.kw[:, pi, :], rhs=t.v1[:, pi, :])
                yield
                V(k, "tensor_tensor", out=state[:], in0=state[:], in1=bc3(t.dec[:, c * 4:(c + 1) * 4], 128), op=ALU.mult)
                V(k, "tensor_tensor", out=state[:], in0=state[:], in1=pst[0:64, :].rearrange("p (h e) -> p h e", e=128), op=ALU.add)
                A(k, out=t.stb[:], in_=state[:], func=AF.Copy)
                rc = r0 + c * 64
                po3 = po[0:64, :].rearrange("p (h e) -> p h e", e=128)
                A(k, out=t.den[:], in_=po3[:, :, 64], func=AF.Abs)
                yield
                V(k, "tensor_scalar", out=t.den[:], in0=t.den[:], scalar1=1.0, scalar2=None, op0=ALU.max)
                V(k, "reciprocal", out=t.den[:], in_=t.den[:])
                V(k, "tensor_tensor", out=t.hm[:].rearrange("p (h e) -> p h e", e=64), in0=po3[:, :, 0:64], in1=bc3(t.den[:], 64), op=ALU.mult)
                if step < nblk // 2:
                    S.dma("pool", OWN[rc:rc + 64, :], t.hm[:])
                    yield
                else:
                    S.dma("sp", t.omf[:], OTH[rc:rc + 64, :])
                    S.dma("sp", t.og[:], k.TMS[rc:rc + 64, TM_OFF["og"]:TM_OFF["og"] + 512])
                    yield
                    V(k, "tensor_tensor", out=t.hm[:], in0=t.hm[:], in1=t.omf[:], op=ALU.add)
                    S.op("dve", "memset", t.ss4[:], 0.0)
                    for h in range(4):
                        A(k, out=t.junk[:], in_=t.hm[:, h * 64:(h + 1) * 64], func=AF.Square, accum_out=t.ss4[:, h:h + 1])
                    A(k, out=t.ss4[:], in_=t.ss4[:], func=AF.Sqrt, bias=EPS, scale=1.0 / 64)
                    V(k, "reciprocal", out=t.ss4[:], in_=t.ss4[:])
                    yield
                    for h in range(4):
                        V(k, "scalar_tensor_tensor", out=t.tt[:, h * 64:(h + 1) * 64], in0=t.hm[:, h * 64:(h + 1) * 64],
                          scalar=t.ss4[:, h:h + 1], in1=mgb[:], op0=ALU.mult, op1=ALU.mult)
                    A(k, out=t.sig[:], in_=t.og[:, 256:512], func=AF.Sigmoid)
                    V(k, "tensor_tensor", out=t.tt[:], in0=t.tt[:], in1=t.sig[:], op=ALU.mult)
                    yield
                    for j in range(2):
                        S.op("pe", "transpose", out=B[2][:, j * 64:(j + 1) * 64], in_=t.tt[:, j * 128:(j + 1) * 128], identity=I64)
                    yield
                    V(k, "tensor_copy", out=k.BIG[:, 6:8, rc:rc + 64], in_=B[2][:, 0:128].rearrange("p (j n) -> p j n", n=64))
        if not sq_["sample"]:
            p = sq_["p"]
            V(k, "reduce_sum", out=t.nblc[:], in_=t.nls[:].rearrange("p (c s) -> p c s", s=64), axis=AX.X)
            nch = L // 64
            S.op("dve", "memset", t.off[:], 0.0)
            if dr == 0:
                for c in range(nch - 2, -1, -1):
                    V(k, "tensor_tensor", out=t.off[:, c:c + 1], in0=t.off[:, c + 1:c + 2], in1=t.nblc[:, c + 1:c + 2], op=ALU.subtract)
            else:
                for c in range(1, nch):
                    V(k, "tensor_tensor", out=t.off[:, c:c + 1], in0=t.off[:, c - 1:c], in1=t.nblc[:, c - 1:c], op=ALU.subtract)
            V(k, "tensor_tensor", out=t.kvs[:].rearrange("p (c s) -> p c s", s=64), in0=t.kvs[:].rearrange("p (c s) -> p c s", s=64),
              in1=bc3(t.off[:], 64), op=ALU.add)
            V(k, "reduce_max", out=t.mx[:, 0:1], in_=t.kvs[:], axis=AX.X)
            V(k, "reduce_sum", out=t.mx[:, 1:2], in_=t.nblc[:], axis=AX.X)
            V(k, "scalar_tensor_tensor", out=t.mfin[:], in0=t.mx[:, 1:2], scalar=-1.0, in1=t.mx[:, 0:1], op0=ALU.mult, op1=ALU.max)
            yield
            S.op("pe", "transpose", out=B[0][0:1, 40:44], in_=t.mfin[:], identity=k.cst[0:4, CONST_ORDER["ident"], 0:4])
            V(k, "tensor_copy", out=t.mrow[:], in_=B[0][0:1, 40:44])
            S.dma("pool", k.o_m[p, l, dr:dr + 1, :], t.mrow[:])
            V(k, "tensor_scalar", out=t.d4[:], in0=k.cst[0:4, CONST_ORDER["ident"], 0:4], scalar1=t.mfin[:, 0:1], scalar2=None, op0=ALU.mult)
            MM(k, B[0][0:64, 16:20], lhsT=k.cst[0:4, CONST_ORDER["ones"], 0:64], rhs=t.d4[:])
            yield
            V(k, "tensor_copy", out=t.emf[:], in_=B[0][0:64, 16:20])
            A(k, out=t.emf[:], in_=t.emf[:], func=AF.Exp, scale=-1.0)
            V(k, "tensor_tensor", out=t.so[:], in0=state[:, :, 0:64], in1=bc3(t.emf[:], 64), op=ALU.mult)
            S.dma("pool", k.o_C[p, l, dr, :, :, :].rearrange("h a b -> a h b"), t.so[:])
            V(k, "tensor_tensor", out=t.ncol[:], in0=state[:, :, 64], in1=t.emf[:], op=ALU.mult)
            S.op("pe", "transpose", out=B[0][0:4, 64:128], in_=t.ncol[:], identity=I64)
            yield
            V(k, "tensor_copy", out=t.nrow[:], in_=B[0][0:4, 64:128])
            S.dma("pool", k.o_n[p, l, dr, :, :], t.nrow[:])

    with S.scope():
        fbb = S.sb("fbb", [64, 8])
        S.dma("sp", fbb[:], k.f_b[l:l + 1, :].partition_broadcast(64))
        mgb = S.sb("mgb", [64, 64])
        S.dma("sp", mgb[:], k.mn_g[l:l + 1, :].partition_broadcast(64))
        tiles = [alloc("f"), alloc("b")]
        for sq_ in SEQS:
            run_streams([stream(sq_, dr, tiles[dr], k.ps[dr * 4:dr * 4 + 4], fbb, mgb) for dr in range(2)])


def phase_delta(k, l):
    phase_delta_prep(k, l)
    phase_delta_scan(k, l)


def phase_delta_prep(k, l):
    S = k.S
    with S.scope():
        cw = S.sb("cw", [128, DEPTH, 6, 5])
        S.dma("sp", cw[:], k.conv_w[:])
        xin = S.sb("dxin", [128, 6, 516])
        acc = S.sb("dacc", [128, 6, 512])
        sact = S.sb("dsact", [128, 6, 512])
        sqb = S.sb("dsq", [128, 512])
        rin = S.sb("drin", [128, 512])
        tok = S.sb("dtok", [128, 512])
        for sq_ in SEQS:
            tok0, L = sq_["tok0"], sq_["L"]
            W = min(512, L)
            for ti in range(L // W):
                t0 = tok0 + ti * W
                lo, hi = max(tok0, t0 - 2), min(tok0 + L, t0 + W + 2)
                k.S.op("dve", "memset", xin[:], 0.0)
                S.dma("sp", xin[:, :, lo - (t0 - 2):hi - (t0 - 2)], k.BT[:, :, lo:hi].rearrange("c p n -> p c n"))
                for c in range(6):
                    eng = "dve"
                    S.op(eng, "tensor_scalar", out=acc[:, c, 0:W], in0=xin[:, c, 0:W], scalar1=cw[:, l, c, 0:1], scalar2=None, op0=ALU.mult)
                    for j in range(1, 5):
                        S.op(eng, "scalar_tensor_tensor", out=acc[:, c, 0:W], in0=xin[:, c, j:j + W], scalar=cw[:, l, c, j:j + 1],
                             in1=acc[:, c, 0:W], op0=ALU.mult, op1=ALU.add)
                for c in range(6):
                    A(k, out=sact[:, c, 0:W], in_=acc[:, c, 0:W], func=AF.Silu)
                for c in range(4):
                    G(k, "tensor_tensor", out=sqb[:, 0:W], in0=sact[:, c, 0:W], in1=sact[:, c, 0:W], op=ALU.mult)
                    p = k.ps[c % 2]
                    MM(k, p[:, 0:W], lhsT=k.C("bd"), rhs=sqb[:, 0:W])
                    A(k, out=rin[:, 0:W], in_=p[:, 0:W], func=AF.Sqrt, bias=EPS, scale=1.0)
                    V(k, "reciprocal", out=rin[:, 0:W], in_=rin[:, 0:W])
                    V(k, "scalar_tensor_tensor", out=sact[:, c, 0:W], in0=sact[:, c, 0:W], scalar=(0.125 if c < 2 else 1.0), in1=rin[:, 0:W],
                      op0=ALU.mult, op1=ALU.mult)
                    h0 = (c // 2) * 4 + (c % 2) * 2
                    S.dma("pool", k.DQK[h0:h0 + 2, :, t0:t0 + W].rearrange("h p n -> (h p) n"), sact[:, c, 0:W])
                for b in range(W // 128):
                    p = k.ps[2 + b % 2]
                    for j, c in enumerate((2, 3, 4, 5)):
                        S.op("pe", "transpose", out=p[:, j * 128:(j + 1) * 128], in_=sact[:, c, b * 128:(b + 1) * 128], identity=k.C("ident"))
                    V(k, "tensor_copy", out=tok[:], in_=p[:])
                    S.dma("pool", k.DKV[t0 + b * 128:t0 + (b + 1) * 128, :], tok[:])


def phase_delta_scan(k, l):
    S = k.S
    I64 = k.cst[0:64, CONST_ORDER["ident"], 0:64]
    O64 = k.cst[0:64, CONST_ORDER["ones"], 0:64]

    def bc3(ap2, n):
        return ap2.unsqueeze(2).to_broadcast([ap2.shape[0], ap2.shape[1], n])

    def bcm(ap2, n):
        return ap2.unsqueeze(1).to_broadcast([ap2.shape[0], n, ap2.shape[1]])

    def r3(ap):
        return ap.rearrange("p (a n) -> p a n", n=64)

    def alloc(tag):
        t = NS_()
        for nm in ("diag", "X", "N", "TT", "u", "ebq"):
            setattr(t, nm, S.sb("d%s%s" % (nm, tag), [64, 8, 64]))
        for nm in ("P0", "P1", "PT0", "PT1", "TTb", "aT", "vb", "kbg", "kdec", "wT", "qd"):
            setattr(t, nm, S.sb("d%s%s" % (nm, tag), [64, 8, 64], DT_REC))
        t.qkb = S.sb("dqkb" + tag, [64, 8, 128], DT_REC)
        t.S4b = S.sb("dS4b" + tag, [64, 4, 64], DT_REC)
        for nm in ("beta", "nbeta", "ng", "tg", "ngc", "egc", "bgs", "kds", "gl"):
            setattr(t, nm, S.sb("d%s%s" % (nm, tag), [64, 8]))
        t.gt2 = S.sb("dgt2" + tag, [64, 2, 32])
        t.dkv = S.sb("ddkv" + tag, [64, 2, 512])
        t.qk = S.sb("dqk" + tag, [64, 8, 128])
        t.vnew = S.sb("dvnew" + tag, [64, 4, 64], DT_REC)
        t.S4 = S.sb("dS4" + tag, [64, 4, 64])
        t.oc = S.sb("doc" + tag, [64, 256])
        t.of = S.sb("dof" + tag, [64, 256])
        t.og = S.sb("dog" + tag, [64, 512])
        t.ss4 = S.sb("dss4" + tag, [64, 4])
        t.junk = S.sb("djunk" + tag, [64, 64])
        t.tt = S.sb("dtt" + tag, [64, 256])
        t.sig = S.sb("dsig" + tag, [64, 256])
        if DT_REC == F32:
            t.qkb, t.P0, t.TTb, t.S4b = t.qk, t.N, t.TT, t.S4
        return t

    def stream(sq_, dr, t, B, alb, dtb, dgb):
        tok0, L = sq_["tok0"], sq_["L"]
        nblk = L // 128
        ci = CONST_ORDER["m_le"] if dr == 0 else CONST_ORDER["m_ge"]
        si = CONST_ORDER["m_ig"] if dr == 0 else CONST_ORDER["m_il"]
        cum64 = k.cst[0:64, ci, 0:64]
        str64 = k.cst[0:64, si, 0:64]
        OWN, OTH = (k.OD, k.OD2) if dr == 0 else (k.OD2, k.OD)
        if sq_["sample"]:
            S.dma("sp", t.S4[:], k.st_d[l, dr, :, :, :].rearrange("h a b -> a h b"))
        else:
            S.op("dve", "memset", t.S4[:], 0.0)
        if DT_REC != F32:
            V(k, "tensor_copy", out=t.S4b[:], in_=t.S4[:])
        blocks = list(range(nblk)) if dr == 0 else list(range(nblk - 1, -1, -1))
        corder = (0, 1) if dr == 0 else (1, 0)
        for step, bi in enumerate(blocks):
            r0 = tok0 + bi * 128
            for c in range(2):
                S.dma("sp", t.gt2[:, c, :], k.TMS[r0 + c * 64:r0 + (c + 1) * 64, TM_OFF["gt"]:TM_OFF["gt"] + 32])
                S.dma("sp", t.dkv[:, c, :], k.DKV[r0 + c * 64:r0 + (c + 1) * 64, :])
            S.dma("sp", t.qk[:], k.DQK[:, :, r0:r0 + 128].rearrange("h p n -> p h n"))
            yield
            if DT_REC != F32:
                G(k, "tensor_copy", out=t.qkb[:], in_=t.qk[:])
            b3 = t.beta[:].rearrange("p (c h) -> p c h", c=2)
            A(k, out=b3, in_=t.gt2[:, :, 8 + dr * 4:12 + dr * 4], func=AF.Sigmoid)
            V(k, "tensor_scalar", out=t.nbeta[:], in0=t.beta[:], scalar1=-1.0, scalar2=None, op0=ALU.mult)
            t3 = t.tg[:].rearrange("p (c h) -> p c h", c=2)
            V(k, "tensor_tensor", out=t3, in0=t.gt2[:, :, dr * 4:dr * 4 + 4], in1=bcm(dtb[:, dr * 4:dr * 4 + 4], 2), op=ALU.add)
            A(k, out=t.tg[:], in_=t.tg[:], func=AF.Exp)
            A(k, out=t.tg[:], in_=t.tg[:], func=AF.Ln, bias=1.0, scale=1.0)
            V(k, "tensor_tensor", out=t.ng[:].rearrange("p (c h) -> p c h", c=2), in0=t3, in1=bcm(alb[:, dr * 4:dr * 4 + 4], 2), op=ALU.mult)
            yield
            pa = B[0]
            for c in range(2):
                MM(k, pa[0:64, c * 4:(c + 1) * 4], lhsT=cum64, rhs=t.ng[:, c * 4:(c + 1) * 4])
                MM(k, pa[0:64, 8 + c * 4:12 + c * 4], lhsT=O64, rhs=t.ng[:, c * 4:(c + 1) * 4])
            yield
            V(k, "tensor_copy", out=t.ngc[:], in_=pa[0:64, 0:8])
            A(k, out=t.egc[:], in_=t.ngc[:], func=AF.Exp, scale=-1.0)
            V(k, "tensor_tensor", out=t.bgs[:], in0=t.beta[:], in1=t.egc[:], op=ALU.mult)
            V(k, "tensor_tensor", out=t.kds[:], in0=t.ngc[:], in1=pa[0:64, 8:16], op=ALU.subtract)
            A(k, out=t.kds[:], in_=t.kds[:], func=AF.Exp)
            V(k, "tensor_copy", out=t.gl[:], in_=pa[0:64, 8:16])
            A(k, out=t.gl[:], in_=t.gl[:], func=AF.Exp, scale=-1.0)
            V(k, "tensor_tensor", out=t.diag[:], in0=bcm(I64, 8), in1=bc3(t.ngc[:], 64), op=ALU.mult)
            yield
            pb, pc, pd = B[1], B[2], B[3]
            for pi in range(8):
                MM(k, pb[0:64, pi * 64:(pi + 1) * 64], lhsT=O64, rhs=t.diag[:, pi, :])
            for c in range(2):
                for h in range(4):
                    pi = c * 4 + h
                    kTc = t.qkb[:, 4 + h, c * 64:(c + 1) * 64]
                    qTc = t.qkb[:, h, c * 64:(c + 1) * 64]
                    MM(k, pc[0:64, pi * 64:(pi + 1) * 64], lhsT=kTc, rhs=kTc)
                    MM(k, pd[0:64, pi * 64:(pi + 1) * 64], lhsT=kTc, rhs=qTc)
            yield
            pb3 = r3(pb[0:64, :])
            V(k, "tensor_tensor", out=t.X[:], in0=pb3, in1=bc3(t.ngc[:], 64), op=ALU.subtract)
            A(k, out=t.ebq[:], in_=pb3, func=AF.Exp, scale=-1.0)
            V(k, "tensor_scalar", out=t.diag[:], in0=t.X[:], scalar1=0.0, scalar2=None, op0=ALU.min)
            G(k, "tensor_scalar", out=t.X[:], in0=t.X[:], scalar1=0.0, scalar2=None, op0=ALU.max)
            yield
            A(k, out=t.diag[:], in_=t.diag[:], func=AF.Exp)
            A(k, out=t.X[:], in_=t.X[:], func=AF.Exp, scale=-1.0)
            G(k, "tensor_tensor", out=t.diag[:], in0=t.diag[:], in1=bcm(str64, 8), op=ALU.mult)
            G(k, "tensor_tensor", out=t.X[:], in0=t.X[:], in1=bcm(cum64, 8), op=ALU.mult)
            yield
            V(k, "tensor_tensor", out=t.N[:], in0=r3(pc[0:64, :]), in1=bc3(t.nbeta[:], 64), op=ALU.mult)
            V(k, "tensor_tensor", out=t.N[:], in0=t.N[:], in1=t.diag[:], op=ALU.mult)
            if DT_REC != F32:
                A(k, out=t.P0[:], in_=t.N[:], func=AF.Copy)
            V(k, "tensor_tensor", out=t.aT[:], in0=r3(pd[0:64, :]), in1=t.X[:], op=ALU.mult)
            yield
            pt = B[1]
            for pi in range(8):
                S.op("pe", "transpose", out=pt[0:64, pi * 64:(pi + 1) * 64], in_=t.N[:, pi, :], identity=I64)
            yield
            A(k, out=t.PT0[:], in_=r3(pt[0:64, :]), func=AF.Copy)
            V(k, "tensor_tensor", out=t.TT[:], in0=r3(pt[0:64, :]), in1=bcm(I64, 8), op=ALU.add)
            if DT_REC != F32:
                if DT_REC != F32:
                    A(k, out=t.TTb[:], in_=t.TT[:], func=AF.Copy)
            yield
            Pb, PTb = (t.P0, t.P1), (t.PT0, t.PT1)
            for stp in range(5):
                P, PT = Pb[stp % 2], PTb[stp % 2]
                Pn, PTn = Pb[(stp + 1) % 2], PTb[(stp + 1) % 2]
                p1, p2, p3 = B[2], B[3], B[1]
                for pi in range(8):
                    MM(k, p1[0:64, pi * 64:(pi + 1) * 64], lhsT=PT[:, pi, :], rhs=P[:, pi, :])
                if stp < 4:
                    for pi in range(8):
                        MM(k, p2[0:64, pi * 64:(pi + 1) * 64], lhsT=P[:, pi, :], rhs=PT[:, pi, :])
                yield
                A(k, out=Pn[:], in_=r3(p1[0:64, :]), func=AF.Copy)
                if stp < 4:
                    V(k, "tensor_copy", out=PTn[:], in_=r3(p2[0:64, :]))
                yield
                for pi in range(8):
                    MM(k, p3[0:64, pi * 64:(pi + 1) * 64], lhsT=Pn[:, pi, :], rhs=t.TTb[:, pi, :])
                yield
                V(k, "tensor_tensor", out=t.TT[:], in0=t.TT[:], in1=r3(p3[0:64, :]), op=ALU.add)
                if DT_REC != F32:
                    A(k, out=t.TTb[:], in_=t.TT[:], func=AF.Copy)
                yield
            kv4 = t.dkv[:].rearrange("p c (x h e) -> p c x h e", x=2, e=64)
            V(k, "tensor_tensor", out=t.vb[:].rearrange("p (c h) e -> p c h e", c=2), in0=kv4[:, :, 1, :, :],
              in1=bc3(t.beta[:], 64).rearrange("p (c h) e -> p c h e", c=2), op=ALU.mult)
            V(k, "tensor_tensor", out=t.kbg[:].rearrange("p (c h) e -> p c h e", c=2), in0=kv4[:, :, 0, :, :],
              in1=bc3(t.bgs[:], 64).rearrange("p (c h) e -> p c h e", c=2), op=ALU.mult)
            G(k, "tensor_tensor", out=t.kdec[:].rearrange("p (c h) e -> p c h e", c=2), in0=kv4[:, :, 0, :, :],
              in1=bc3(t.kds[:], 64).rearrange("p (c h) e -> p c h e", c=2), op=ALU.mult)
            G(k, "tensor_tensor", out=t.qd[:].rearrange("p (c h) n -> p c h n", c=2),
              in0=t.qk[:, 0:4, :].rearrange("p h (c n) -> p c h n", c=2), in1=t.ebq[:].rearrange("p (c h) n -> p c h n", c=2), op=ALU.mult)
            yield
            p1, p2 = B[2], B[3]
            for pi in range(8):
                MM(k, p1[0:64, pi * 64:(pi + 1) * 64], lhsT=t.TTb[:, pi, :], rhs=t.vb[:, pi, :])
                MM(k, p2[0:64, pi * 64:(pi + 1) * 64], lhsT=t.kbg[:, pi, :], rhs=t.TTb[:, pi, :])
            yield
            A(k, out=t.u[:], in_=r3(p1[0:64, :]), func=AF.Copy)
            V(k, "tensor_copy", out=t.wT[:], in_=r3(p2[0:64, :]))
            yield
            for c in corder:
                px, py, pz = B[1], B[2], B[3]
                for h in range(4):
                    MM(k, px[0:64, h * 64:(h + 1) * 64], lhsT=t.wT[:, c * 4 + h, :], rhs=t.S4b[:, h, :])
                yield
                V(k, "tensor_tensor", out=t.vnew[:], in0=t.u[:, c * 4:(c + 1) * 4, :], in1=r3(px[0:64, 0:256]), op=ALU.subtract)
                yield
                for h in range(4):
                    MM(k, py[0:64, h * 64:(h + 1) * 64], lhsT=t.qd[:, c * 4 + h, :], rhs=t.S4b[:, h, :], start=True, stop=False)
                    MM(k, py[0:64, h * 64:(h + 1) * 64], lhsT=t.aT[:, c * 4 + h, :], rhs=t.vnew[:, h, :], start=False, stop=True)
                    MM(k, pz[0:64, h * 64:(h + 1) * 64], lhsT=t.kdec[:, c * 4 + h, :], rhs=t.vnew[:, h, :])
                yield
                V(k, "tensor_tensor", out=t.S4[:], in0=t.S4[:], in1=bc3(t.gl[:, c * 4:(c + 1) * 4], 64), op=ALU.mult)
                V(k, "tensor_tensor", out=t.S4[:], in0=t.S4[:], in1=r3(pz[0:64, 0:256]), op=ALU.add)
                if DT_REC != F32:
                    G(k, "tensor_copy", out=t.S4b[:], in_=t.S4[:])
                rc = r0 + c * 64
                A(k, out=t.oc[:], in_=py[0:64, 0:256], func=AF.Copy)
                if step < nblk // 2:
                    S.dma("pool", OWN[rc:rc + 64, :], t.oc[:])
                    yield
                else:
                    S.dma("sp", t.of[:], OTH[rc:rc + 64, :])
                    S.dma("sp", t.og[:], k.TMS[rc:rc + 64, TM_OFF["og"]:TM_OFF["og"] + 512])
                    yield
                    V(k, "tensor_tensor", out=t.oc[:], in0=t.oc[:], in1=t.of[:], op=ALU.add)
                    S.op("dve", "memset", t.ss4[:], 0.0)
                    for h in range(4):
                        A(k, out=t.junk[:], in_=t.oc[:, h * 64:(h + 1) * 64], func=AF.Square, accum_out=t.ss4[:, h:h + 1])
                    A(k, out=t.ss4[:], in_=t.ss4[:], func=AF.Sqrt, bias=EPS, scale=1.0 / 64)
                    V(k, "reciprocal", out=t.ss4[:], in_=t.ss4[:])
                    yield
                    for h in range(4):
                        V(k, "scalar_tensor_tensor", out=t.tt[:, h * 64:(h + 1) * 64], in0=t.oc[:, h * 64:(h + 1) * 64],
                          scalar=t.ss4[:, h:h + 1], in1=dgb[:], op0=ALU.mult, op1=ALU.mult)
                    A(k, out=t.sig[:], in_=t.og[:, 0:256], func=AF.Silu)
                    V(k, "tensor_tensor", out=t.tt[:], in0=t.tt[:], in1=t.sig[:], op=ALU.mult)
                    yield
                    for j in range(2):
                        S.op("pe", "transpose", out=B[0][:, 128 + j * 64:128 + (j + 1) * 64], in_=t.tt[:, j * 128:(j + 1) * 128], identity=I64)
                    yield
                    V(k, "tensor_copy", out=k.BIG[:, 4:6, rc:rc + 64], in_=B[0][:, 128:256].rearrange("p (j n) -> p j n", n=64))
        if not sq_["sample"]:
            S.dma("pool", k.o_S[sq_["p"], l, dr, :, :, :].rearrange("h a b -> a h b"), t.S4[:])

    with S.scope():
        alb = S.sb("alb", [64, 8])
        S.dma("sp", alb[:], k.a_log[l:l + 1, :].partition_broadcast(64))
        A(k, out=alb[:], in_=alb[:], func=AF.Exp)
        dtb = S.sb("dtb", [64, 8])
        S.dma("sp", dtb[:], k.dt_b[l:l + 1, :].partition_broadcast(64))
        dgb = S.sb("dgb", [64, 64])
        S.dma("sp", dgb[:], k.dn_g[l:l + 1, :].partition_broadcast(64))
        tiles = [alloc("f"), alloc("b")]
        for sq_ in SEQS:
            run_streams([stream(sq_, dr, tiles[dr], k.ps[dr * 4:dr * 4 + 4], alb, dtb, dgb) for dr in range(2)])


def swap_pairs(n):
    idx = np.arange(n)
    return idx ^ 1


def prep_shared(inp):
    f = np.float32
    sh = {}
    w_in = inp["w_in"]
    b_in = inp["b_in"]
    sw = swap_pairs(512)
    w_ext = np.concatenate([w_in, w_in[:, :, O_AQ:O_AQ + 512][:, :, sw], w_in[:, :, O_AK:O_AK + 512][:, :, sw]], axis=2)
    b_ext = np.concatenate([b_in, b_in[:, O_AQ:O_AQ + 512][:, sw], b_in[:, O_AK:O_AK + 512][:, sw]], axis=1)
    sh["w_in"] = np.ascontiguousarray(w_ext, dtype=f)
    sh["w_mod"] = inp["w_mod"]
    sh["w_out"] = inp["w_out"]
    sh["w_gu"] = inp["w_gate_up"]
    sh["w_dn"] = inp["w_down"]
    cm, ct, st = host_consts()
    sh["consts"], sh["rope_c"], sh["rope_s"] = cm, ct, st

    def fm(v):
        d, n = v.shape
        return np.ascontiguousarray(v.reshape(d, n // 128, 128).transpose(2, 0, 1), dtype=f)

    sh["n1g"] = fm(inp["norm1_g"])
    sh["n2g"] = fm(inp["norm2_g"])
    sh["fng"] = np.ascontiguousarray(inp["final_norm_g"].reshape(1, D), dtype=f)
    sh["b_mod"] = fm(inp["b_mod"])
    sh["b_fm"] = np.ascontiguousarray(
        np.stack([b_ext[:, c:c + 128] for c in FM_CHUNKS], axis=1).transpose(2, 0, 1), dtype=f)
    tm_cols = np.concatenate([np.arange(c0, c0 + w) for _, cols in TM_GROUPS for (c0, w) in cols])
    sh["b_tm"] = np.ascontiguousarray(b_ext[:, tm_cols].reshape(1, DEPTH, TM_W), dtype=f)
    cw = inp["delta_conv_w"]
    sh["conv_w"] = np.ascontiguousarray(cw.reshape(DEPTH, 5, 6, 128).transpose(3, 0, 2, 1), dtype=f)
    sh["lam_qk"] = np.ascontiguousarray(inp["lambda_qk"].reshape(DEPTH, 256), dtype=f)
    sh["subln_g"] = np.ascontiguousarray(inp["attn_subln_g"].T, dtype=f)
    sh["dn_g"] = np.ascontiguousarray(inp["delta_norm_g"], dtype=f)
    sh["mn_g"] = np.ascontiguousarray(inp["mlstm_norm_g"], dtype=f)
    sh["a_log"] = np.ascontiguousarray(inp["delta_A_log"].reshape(DEPTH, 8), dtype=f)
    sh["dt_b"] = np.ascontiguousarray(inp["delta_dt_bias"].reshape(DEPTH, 8), dtype=f)
    sh["f_b"] = np.ascontiguousarray(inp["mlstm_f_bias"].reshape(DEPTH, 8), dtype=f)
    return sh


def prep_core(inp, sh, i):
    f = np.float32
    b = i % 4
    m = dict(sh)
    xp = inp["x_prompt"][2 * i:2 * i + 2].reshape(NPR * LP, D)
    m["x_all"] = np.ascontiguousarray(np.concatenate([inp["x_sample"][b], xp], axis=0), dtype=f)
    m["cache_k"] = np.ascontiguousarray(inp["cache_attn_k"][b].reshape(DEPTH, PAST, 512), dtype=f)
    m["cache_v"] = np.ascontiguousarray(inp["cache_attn_v"][b].reshape(DEPTH, PAST, 512), dtype=f)
    m["st_d"] = np.ascontiguousarray(inp["state_delta"][b], dtype=f)
    m["st_c"] = np.ascontiguousarray(inp["state_mlstm_C"][b], dtype=f)
    m["st_n"] = np.ascontiguousarray(inp["state_mlstm_n"][b], dtype=f)
    m["st_m"] = np.ascontiguousarray(inp["state_mlstm_m"][b], dtype=f)
    cc = np.stack([inp["c"][b], inp["c_ctx"]], axis=-1)
    m["cT"] = np.ascontiguousarray(cc.reshape(KC, 128, 2).transpose(1, 0, 2), dtype=f)
    return m


def build_program(stop_after=None, debug_outs=()):
    k = build(debug_outs)
    DBG["stop"] = stop_after
    try:
        phase_setup(k)
        chk("setup")
        for l in range(DEPTH):
            phase_norm(k, l, 1)
            chk("norm%d" % l)
            phase_inproj(k, l)
            chk("inproj%d" % l)
            phase_attn(k, l)
            chk("attn%d" % l)
            phase_mlstm(k, l)
            chk("mlstm%d" % l)
            phase_delta_prep(k, l)
            chk("dprep%d" % l)
            phase_delta_scan(k, l)
            chk("delta%d" % l)
            phase_outproj(k, l)
            chk("outproj%d" % l)
            phase_norm(k, l, 2)
            phase_ffn(k, l)
            chk("ffn%d" % l)
        phase_final(k)
    except StopBuild:
        pass
    st = k.S.finalize()
    return k, st


_CACHE = {}


def kernel(**inputs):
    inp = {n: np.asarray(v) for n, v in inputs.items()}
    if "prog" not in _CACHE:
        _CACHE["prog"] = build_program()
    k, st = _CACHE["prog"]
    sh = prep_shared(inp)
    in_maps = [prep_core(inp, sh, i) for i in range(8)]
    res = run_bass_kernel_spmd(k.nc, in_maps, core_ids=list(range(8)))
    R = res.results
    B = 16
    y_prompt = np.concatenate([R[i]["y_p"].reshape(NPR, LP, D) for i in range(8)], axis=0)
    y_sample = np.stack([R[b]["y_s"] for b in range(4)], axis=0)
    nk = np.concatenate([R[i]["o_k"] for i in range(8)], axis=0).reshape(B, DEPTH, LP, 4, 2, 64)
    nv = np.concatenate([R[i]["o_v"] for i in range(8)], axis=0).reshape(B, DEPTH, LP, 4, 128)
    nS = np.concatenate([R[i]["o_S"] for i in range(8)], axis=0)
    nC = np.concatenate([R[i]["o_C"] for i in range(8)], axis=0)
    nn = np.concatenate([R[i]["o_n"] for i in range(8)], axis=0)
    nm = np.concatenate([R[i]["o_m"] for i in range(8)], axis=0)
    return (y_prompt.astype(np.float32), y_sample.astype(np.float32), nk.astype(np.float32), nv.astype(np.float32),
            nS.astype(np.float32), nC.astype(np.float32), nn.astype(np.float32), nm.astype(np.float32))
```

```python
import contextlib
import math
import numpy as np
import concourse.bass as bass
import concourse.mybir as mybir
from concourse.bass_utils import run_bass_kernel_spmd

F32 = mybir.dt.float32
BF16 = mybir.dt.bfloat16
AF = mybir.ActivationFunctionType
ALU = mybir.AluOpType
AX = mybir.AxisListType


class SemSlot:
    __slots__ = ("sem", "v")

    def __init__(self):
        self.sem = None
        self.v = 0


class Trk:
    __slots__ = ("name", "lw", "rd", "ldma", "slot", "psum")

    def __init__(self, name):
        self.name = name
        self.psum = False
        self.lw = None
        self.rd = []
        self.ldma = None
        self.slot = {}


class Op:
    __slots__ = ("eng", "meth", "args", "kw", "deps", "isdma", "dtrk", "needinc", "ev")

    def __init__(self, eng, meth, args, kw, isdma=False, dtrk=None):
        self.eng, self.meth, self.args, self.kw = eng, meth, args, kw
        self.deps = []
        self.isdma = isdma
        self.dtrk = dtrk
        self.needinc = isdma
        self.ev = None


WRITE_KEYS = ("out", "accum_out")


class Sched:
    def __init__(self, nc):
        self.nc = nc
        self.ops = []
        self.trk = {}
        self.stack = contextlib.ExitStack()
        self.engs = {"pe": nc.tensor, "dve": nc.vector, "act": nc.scalar,
                     "pool": nc.gpsimd, "sp": nc.sync}
        self.sb_bytes = 0
        self.sb_peak = 0
        self.uid = 0
        self.all_trks = []
        self.scope_trks = [[]]
        self.free_slots = {"hw": [], "sw": []}
        self.bar = []
        self.bar_pending = {e: False for e in self.engs}
        self.last_eng_op = {e: None for e in self.engs}

    def _newtrk(self, tname, name):
        t = Trk(name)
        self.trk[tname] = t
        self.all_trks.append(t)
        self.scope_trks[-1].append(t)
        return t

    def sb(self, name, shape, dtype=F32):
        self.uid += 1
        t = self.stack.enter_context(self.nc.sbuf_tensor("%s_%d" % (name, self.uid), list(shape), dtype))
        self._newtrk(t.name, name)
        n = 1
        for s in shape[1:]:
            n *= s
        self.sb_bytes += n * (2 if dtype == BF16 else 4)
        self.sb_peak = max(self.sb_peak, self.sb_bytes)
        return t

    def ps(self, name, shape, dtype=F32):
        t = self.stack.enter_context(self.nc.psum_tensor(name, list(shape), dtype))
        self._newtrk(t.name, name).psum = True
        return t

    @contextlib.contextmanager
    def scope(self):
        old = self.stack
        self.stack = contextlib.ExitStack()
        self.scope_trks.append([])
        b0 = self.sb_bytes
        try:
            yield
        finally:
            self.stack.close()
            self.stack = old
            self.sb_bytes = b0
            self.barrier()
            for t in self.scope_trks.pop():
                for cls, sl in t.slot.items():
                    self.free_slots[cls].append(sl)

    def barrier(self):
        bar = [o for o in self.last_eng_op.values() if o is not None]
        bar += [t.ldma for t in self.all_trks if t.ldma is not None]
        self.bar = sorted(set(bar))
        for e in self.bar_pending:
            self.bar_pending[e] = True

    def dram(self, name, shape, dtype=F32, kind="Internal", track=True):
        t = self.nc.dram_tensor(name, list(shape), dtype, kind=kind)
        if track:
            self.trk[t.name] = Trk(name)
        return t

    def _tr(self, ap):
        try:
            return self.trk.get(ap.tensor.name)
        except AttributeError:
            return None

    def _record(self, op, reads, writes):
        oid = len(self.ops)
        deps = set()
        for t in reads:
            if t.lw is not None:
                deps.add((t.lw, "raw"))
            if t.psum:
                for r in t.rd:
                    deps.add((r, "rar"))
        for t in writes:
            if t.lw is not None:
                deps.add((t.lw, "waw"))
            for r in t.rd:
                deps.add((r, "war"))
        if op.isdma and op.dtrk.ldma is not None:
            deps.add((op.dtrk.ldma, "raw"))
        if self.bar_pending[op.eng]:
            self.bar_pending[op.eng] = False
            for b in self.bar:
                deps.add((b, "bar"))
        final = {}
        for d, kind in deps:
            dop = self.ops[d]
            if not dop.isdma and not op.isdma and dop.eng == op.eng:
                if op.eng == "pe":
                    continue
            final[d] = True
        latest = {}
        for d in list(final):
            dop = self.ops[d]
            if not dop.isdma and dop.eng in ("pe", "act", "dve"):
                if dop.eng in latest:
                    lo = min(latest[dop.eng], d)
                    latest[dop.eng] = max(latest[dop.eng], d)
                    del final[lo]
                else:
                    latest[dop.eng] = d
        op.deps = sorted(final)
        for d in op.deps:
            self.ops[d].needinc = True
        self.ops.append(op)
        for t in reads:
            t.rd.append(oid)
        for t in writes:
            t.lw = oid
            t.rd = []
        if op.isdma:
            op.dtrk.ldma = oid
            cls = "sw" if op.eng == "pool" else "hw"
            if cls not in op.dtrk.slot:
                op.dtrk.slot[cls] = self.free_slots[cls].pop() if self.free_slots[cls] else SemSlot()
            sl = op.dtrk.slot[cls]
            if sl.v >= 30000:
                sl = op.dtrk.slot[cls] = SemSlot()
            sl.v += 16
            op.ev = (sl, sl.v)
        else:
            self.last_eng_op[op.eng] = oid
        return oid

    def op(self, eng, meth, *args, **kw):
        reads, writes = [], []
        names = list(kw.items())
        for i, a in enumerate(args):
            names.append(("out" if i == 0 else "in", a))
        for k, v in names:
            t = self._tr(v) if hasattr(v, "tensor") else None
            if t is None:
                continue
            if k in WRITE_KEYS:
                if t not in writes:
                    writes.append(t)
            elif t not in reads:
                reads.append(t)
        return self._record(Op(eng, meth, args, kw), reads, writes)

    def dma(self, q, out, in_, **kw):
        to, ti = self._tr(out), self._tr(in_)
        dtrk = None
        for ap, t in ((out, to), (in_, ti)):
            if t is not None and not type(ap.tensor).__name__.startswith("DRam"):
                dtrk = t
        if dtrk is None:
            dtrk = to if to is not None else ti
        assert dtrk is not None, "dma with no tracked side"
        o = Op(q, "dma_start", (), dict(out=out, in_=in_, **kw), isdma=True, dtrk=dtrk)
        return self._record(o, [ti] if ti is not None else [], [to] if to is not None else [])

    def finalize(self, final_wait_eng="sp"):
        nc = self.nc
        esem = {e: nc.alloc_semaphore("es_" + e) for e in self.engs}
        ecnt = {e: 0 for e in self.engs}
        known = {e: {} for e in self.engs}
        nwait = 0
        nroll = 0
        for op in self.ops:
            eng = self.engs[op.eng]
            kn = known[op.eng]
            need = {}
            for d in op.deps:
                sem, val = self.ops[d].ev
                if isinstance(sem, SemSlot):
                    if sem.sem is None:
                        sem.sem = nc.alloc_semaphore("ds%d" % id(sem))
                    sem = sem.sem
                k = id(sem)
                if kn.get(k, 0) >= val:
                    continue
                if k not in need or need[k][1] < val:
                    need[k] = (sem, val)
            for k, (sem, val) in need.items():
                eng.wait_ge(sem, val)
                kn[k] = val
                nwait += 1
            ins = getattr(eng, op.meth)(*op.args, **op.kw)
            if op.isdma:
                slot = op.ev[0]
                if slot.sem is None:
                    slot.sem = nc.alloc_semaphore("ds%d" % id(slot))
                ins.then_inc(slot.sem, 16)
            elif op.needinc:
                if ecnt[op.eng] >= 30000:
                    nroll += 1
                    esem[op.eng] = nc.alloc_semaphore("es_%s_%d" % (op.eng, nroll))
                    ecnt[op.eng] = 0
                ecnt[op.eng] += 1
                ins.then_inc(esem[op.eng], 1)
                op.ev = (esem[op.eng], ecnt[op.eng])
        eng = self.engs[final_wait_eng]
        seen = set()
        for t in self.all_trks:
            for sl in t.slot.values():
                if sl.sem is not None and id(sl) not in seen:
                    seen.add(id(sl))
                    eng.wait_ge(sl.sem, sl.v)
        for e in self.engs:
            if ecnt[e] and e != final_wait_eng:
                eng.wait_ge(esem[e], ecnt[e])
        self.stats = dict(n_ops=len(self.ops), n_wait=nwait, ecnt=dict(ecnt), sb_peak=self.sb_peak, nsem=len(seen) + 5)
        return self.stats


D = 1024
KC = 8
LS = 4096
LP = 256
NPR = 2
T = LS + NPR * LP
NT = T // 512
NB = T // 128
PAST = 512
DEPTH = 2
FH = 2816
FC = FH // 128
EPS = 1e-6
O_AQ, O_AK, O_AV = 0, 512, 1024
O_BQ, O_BK, O_BV, O_BG, O_BA, O_BB = 1536, 1792, 2048, 2304, 2560, 2568
O_CQ, O_CK, O_CV, O_CO, O_CI, O_CF = 2576, 2832, 3088, 3344, 3600, 3608
O_AQS, O_AKS = 3616, 4128
WIN = 4640
FM_CHUNKS = ([O_AQ + 128 * i for i in range(4)] + [O_AK + 128 * i for i in range(4)]
             + [O_BQ + 128 * i for i in range(6)] + [O_CQ + 128 * i for i in range(4)]
             + [O_AQS + 128 * i for i in range(4)] + [O_AKS + 128 * i for i in range(4)])
TM_GROUPS = [
    ("av", [(O_AV, 512)]),
    ("ckv", [(O_CK, 256), (O_CV, 256)]),
    ("og", [(O_BG, 256), (O_CO, 256)]),
    ("gt", [(O_BA, 16), (O_CI, 16)]),
    ("ak", [(O_AK, 512)]),
]
TM_OFF = {}
_o = 0
for _n, _cols in TM_GROUPS:
    TM_OFF[_n] = _o
    _o += sum(w for _, w in _cols)
TM_W = _o


class StopBuild(Exception):
    pass


DBG = {}


def chk(name):
    if DBG.get("stop") == name:
        raise StopBuild(name)


def lam_init_of(l):
    return 0.8 - 0.6 * math.exp(-0.3 * l)


def host_consts():
    i = np.arange(128)
    same = (i[:, None] // 64) == (i[None, :] // 64)
    c = {}
    c["ident"] = np.eye(128, dtype=np.float32)
    c["ones"] = np.ones((128, 128), np.float32)
    c["bd"] = same.astype(np.float32)
    c["m_ig"] = (same & (i[:, None] > i[None, :])).astype(np.float32)
    c["m_il"] = (same & (i[:, None] < i[None, :])).astype(np.float32)
    c["m_le"] = (same & (i[:, None] <= i[None, :])).astype(np.float32)
    c["m_ge"] = (same & (i[:, None] >= i[None, :])).astype(np.float32)
    order = ["ident", "ones", "bd", "m_ig", "m_il", "m_le", "m_ge"]
    cm = np.stack([c[k] for k in order], axis=1)
    t = np.arange(LS)
    rows = (t // 64).astype(np.float64)
    cols = (t % 64).astype(np.float64)
    nf = 16
    inv = 10000.0 ** (-np.arange(nf, dtype=np.float64) / nf)
    ang = np.concatenate([rows[:, None] * inv, cols[:, None] * inv], axis=-1)
    ang = ang.astype(np.float32).astype(np.float64)
    cos = np.cos(ang).astype(np.float32)
    sin = np.sin(ang).astype(np.float32)
    ct = np.zeros((128, LS), np.float32)
    st = np.zeros((128, LS), np.float32)
    for m in range(2):
        for d in range(64):
            ct[m * 64 + d] = cos[:, d // 2]
            st[m * 64 + d] = sin[:, d // 2] * (-1.0 if d % 2 == 0 else 1.0)
    return np.ascontiguousarray(cm), ct, st


CONST_ORDER = {"ident": 0, "ones": 1, "bd": 2, "m_ig": 3, "m_il": 4, "m_le": 5, "m_ge": 6}


class K:
    pass


def build(debug_outs=()):
    nc = bass.Bass("TRN2", target_bir_lowering=False)
    S = Sched(nc)
    k = K()
    k.nc, k.S = nc, S

    def din(name, shape):
        return nc.dram_tensor(name, list(shape), F32, kind="ExternalInput")

    def dout(name, shape):
        return S.dram(name, shape, F32, kind="ExternalOutput")

    k.x_all = din("x_all", [T, D])
    k.cache_k = din("cache_k", [DEPTH, PAST, 512])
    k.cache_v = din("cache_v", [DEPTH, PAST, 512])
    k.st_d = din("st_d", [DEPTH, 2, 4, 64, 64])
    k.st_c = din("st_c", [DEPTH, 2, 4, 64, 64])
    k.st_n = din("st_n", [DEPTH, 2, 4, 64])
    k.st_m = din("st_m", [DEPTH, 2, 4])
    k.cT = din("cT", [128, KC, 2])
    k.w_mod = din("w_mod", [DEPTH, D, 6 * D])
    k.w_in = din("w_in", [DEPTH, D, WIN])
    k.w_out = din("w_out", [DEPTH, D, D])
    k.w_gu = din("w_gu", [DEPTH, D, 2 * FH])
    k.w_dn = din("w_dn", [DEPTH, FH, D])
    k.consts = din("consts", [128, 7, 128])
    k.rope_c = din("rope_c", [128, LS])
    k.rope_s = din("rope_s", [128, LS])
    k.n1g = din("n1g", [128, DEPTH, KC])
    k.n2g = din("n2g", [128, DEPTH, KC])
    k.fng = din("fng", [1, D])
    k.b_mod = din("b_mod", [128, DEPTH, 48])
    k.b_fm = din("b_fm", [128, DEPTH, len(FM_CHUNKS)])
    k.b_tm = din("b_tm", [1, DEPTH, TM_W])
    k.conv_w = din("conv_w", [128, DEPTH, 6, 5])
    k.lam_qk = din("lam_qk", [DEPTH, 256])
    k.subln_g = din("subln_g", [128, DEPTH])
    k.dn_g = din("dn_g", [DEPTH, 64])
    k.mn_g = din("mn_g", [DEPTH, 64])
    k.a_log = din("a_log", [DEPTH, 8])
    k.dt_b = din("dt_b", [DEPTH, 8])
    k.f_b = din("f_b", [DEPTH, 8])
    k.y_s = dout("y_s", [LS, D])
    k.y_p = dout("y_p", [NPR * LP, D])
    k.o_k = dout("o_k", [NPR, DEPTH, LP, 512])
    k.o_v = dout("o_v", [NPR, DEPTH, LP, 512])
    k.o_S = dout("o_S", [NPR, DEPTH, 2, 4, 64, 64])
    k.o_C = dout("o_C", [NPR, DEPTH, 2, 4, 64, 64])
    k.o_n = dout("o_n", [NPR, DEPTH, 2, 4, 64])
    k.o_m = dout("o_m", [NPR, DEPTH, 2, 4])
    k.XT = S.dram("XT", [KC, 128, T])
    k.QT = S.dram("QT", [4, 128, T], BF16)
    k.KT = S.dram("KT", [4, 128, T + PAST], BF16)
    k.VV = S.dram("VV", [T + PAST, 512], BF16)
    k.BT = S.dram("BT", [6, 128, T])
    k.CQK = S.dram("CQK", [4, 128, T])
    k.TMS = S.dram("TMS", [T, TM_W])
    k.DQK = S.dram("DQK", [8, 64, T])
    k.DKV = S.dram("DKV", [T, 512])
    k.OD = S.dram("OD", [T, 256])
    k.OD2 = S.dram("OD2", [T, 256])
    k.OM2 = S.dram("OM2", [T, 256])
    k.OM = S.dram("OM", [T, 256])
    k.HT = S.dram("HT", [FC, 128, T], BF16)
    k.dbg = {}
    for name, shape in debug_outs:
        k.dbg[name] = dout("dbg_" + name, shape)

    k.cst = S.sb("cst", [128, 7, 128])
    k.cstb = S.sb("cstb", [128, 7, 128], BF16)
    S.dma("sp", k.cst[:], k.consts[:])
    S.op("dve", "tensor_copy", out=k.cstb[:], in_=k.cst[:])
    k.C = lambda name: k.cst[:, CONST_ORDER[name], :]
    k.Cb = lambda name: k.cstb[:, CONST_ORDER[name], :]
    k.BIG = S.sb("BIG", [128, KC, T], BF16)
    k.ps = [S.ps("ps%d" % i, [128, 512]) for i in range(8)]
    k.mod = S.sb("mod", [128, DEPTH, 48, 2])
    k.g1 = S.sb("g1", [128, DEPTH, KC, 2])
    k.g2 = S.sb("g2", [128, DEPTH, KC, 2])
    k.bfm = S.sb("bfm", [128, DEPTH, len(FM_CHUNKS)])
    S.dma("sp", k.bfm[:], k.b_fm[:])
    k.btm = S.sb("btm", [1, DEPTH, TM_W], BF16)
    k.ones1 = S.sb("ones1", [1, 128], BF16)
    S.op("dve", "memset", k.ones1[:], 1.0)
    return k


def V(k, meth, **kw):
    return k.S.op("dve", meth, **kw)


def A(k, **kw):
    return k.S.op("act", "activation", **kw)


def G(k, meth, *a, **kw):
    return k.S.op("pool", meth, *a, **kw)


def MM(k, out, lhsT, rhs, start=True, stop=True):
    return k.S.op("pe", "matmul", out, lhsT=lhsT, rhs=rhs, start=start, stop=stop)


def phase_setup(k):
    with k.S.scope():
        _phase_setup(k)


def _phase_setup(k):
    S = k.S
    csil = S.sb("csil", [128, KC, 2])
    ctmp = S.sb("ctmp", [128, KC, 2])
    S.dma("sp", ctmp[:], k.cT[:])
    A(k, out=csil[:], in_=ctmp[:], func=AF.Silu)
    bm = S.sb("bm", [128, DEPTH, 48])
    S.dma("sp", bm[:], k.b_mod[:])
    n1 = S.sb("n1", [128, DEPTH, KC])
    n2 = S.sb("n2", [128, DEPTH, KC])
    S.dma("sp", n1[:], k.n1g[:])
    S.dma("sp", n2[:], k.n2g[:])
    btmf = S.sb("btmf", [1, DEPTH, TM_W])
    S.dma("sp", btmf[:], k.b_tm[:])
    V(k, "tensor_copy", out=k.btm[:], in_=btmf[:])
    xin = [S.sb("xin%d" % i, [128, D]) for i in range(2)]
    xto = [S.sb("xto%d" % i, [128, KC, 128]) for i in range(2)]

    def xblock(b):
        xi = xin[b % 2]
        xo = xto[b % 2]
        S.dma("sp", xi[:], k.x_all[b * 128:(b + 1) * 128, :])
        for half in range(2):
            p = k.ps[1 + (b % 2) * 2 + half]
            for j in range(4):
                kc = half * 4 + j
                S.op("pe", "transpose", out=p[:, j * 128:(j + 1) * 128], in_=xi[:, kc * 128:(kc + 1) * 128],
                     identity=k.C("ident"))
            if half == 0:
                V(k, "tensor_copy", out=xo[:, 0:4, :], in_=p[:].rearrange("p (c n) -> p c n", n=128))
            else:
                A(k, out=xo[:, 4:8, :], in_=p[:].rearrange("p (c n) -> p c n", n=128), func=AF.Copy)
        S.dma("pool", k.XT[:, :, b * 128:(b + 1) * 128].rearrange("c p n -> p c n"), xo[:])

    wst = [S.sb("wmst%d" % i, [128, KC, 768]) for i in range(2)]
    pm = k.ps[0]
    n = 0
    NG = DEPTH * 8
    for l in range(DEPTH):
        for g in range(8):
            w = wst[n % 2]
            n += 1
            S.dma("sp", w[:], k.w_mod[l, :, g * 768:(g + 1) * 768].rearrange("(c p) n -> p c n", p=128))
            for b in range((n - 1) * NB // NG, n * NB // NG):
                xblock(b)
            for j in range(6):
                mc = g * 6 + j
                for kc in range(KC):
                    MM(k, pm[:, mc * 2:mc * 2 + 2], lhsT=w[:, kc, j * 128:(j + 1) * 128], rhs=csil[:, kc, :],
                       start=(kc == 0), stop=(kc == KC - 1))
        for r in range(2):
            V(k, "tensor_tensor", out=k.mod[:, l, :, r], in0=pm[:, 0:96].rearrange("p (c r) -> p c r", r=2)[:, :, r],
              in1=bm[:, l, :], op=ALU.add)
        for r in range(2):
            V(k, "scalar_tensor_tensor", out=k.g1[:, l, :, r], in0=k.mod[:, l, 8:16, r], scalar=1.0, in1=n1[:, l, :],
              op0=ALU.add, op1=ALU.mult)
            V(k, "scalar_tensor_tensor", out=k.g2[:, l, :, r], in0=k.mod[:, l, 32:40, r], scalar=1.0, in1=n2[:, l, :],
              op0=ALU.add, op1=ALU.mult)


def seq_r(tile):
    return 0 if tile < LS // 512 else 1


def phase_norm(k, l, which):
    with k.S.scope():
        S = k.S
        k.xt_buf = [S.sb("xt%d" % i, [128, KC, 512]) for i in range(2)]
        k.sq_buf = S.sb("sq", [128, 2, 512])
        k.rstd_buf = S.sb("rstd", [128, 512])
        k.tmp_buf = [S.sb("tmp%d" % i, [128, 512]) for i in range(2)]
        _phase_norm(k, l, which)


def _phase_norm(k, l, which):
    S = k.S
    gg = k.g1 if which == 1 else k.g2
    sh0 = 0 if which == 1 else 24
    for t in range(NT):
        r = seq_r(t)
        xt = k.xt_buf[t % 2]
        S.dma("sp", xt[:], k.XT[:, :, t * 512:(t + 1) * 512].rearrange("c p n -> p c n"))
        sq = k.sq_buf
        pss = k.ps[t % 2]
        for kc in range(KC):
            G(k, "tensor_tensor", out=sq[:, kc % 2, :], in0=xt[:, kc, :], in1=xt[:, kc, :], op=ALU.mult)
            MM(k, pss[:], lhsT=k.C("ones"), rhs=sq[:, kc % 2, :], start=(kc == 0), stop=(kc == KC - 1))
        rstd = k.rstd_buf
        A(k, out=k.tmp_buf[0][:], in_=pss[:], func=AF.Sqrt, bias=EPS, scale=1.0 / D)
        V(k, "reciprocal", out=rstd[:], in_=k.tmp_buf[0][:])
        for kc in range(KC):
            tmp = k.tmp_buf[kc % 2]
            V(k, "scalar_tensor_tensor", out=tmp[:], in0=xt[:, kc, :], scalar=gg[:, l, kc, r:r + 1], in1=rstd[:],
              op0=ALU.mult, op1=ALU.mult)
            A(k, out=k.BIG[:, kc, t * 512:(t + 1) * 512], in_=tmp[:], func=AF.Identity,
              bias=k.mod[:, l, sh0 + kc, r:r + 1], scale=1.0)


def load_w_bf16(k, dst, src_ap, stage, eng_i):
    S = k.S
    n = src_ap.shape[-1]
    S.dma("sp", stage[:, :, 0:n], src_ap.rearrange("(c p) n -> p c n", p=128))
    if eng_i % 2 == 0:
        V(k, "tensor_copy", out=dst, in_=stage[:, :, 0:n])
    else:
        G(k, "tensor_copy", out=dst, in_=stage[:, :, 0:n])


def phase_inproj(k, l):
    with k.S.scope():
        S = k.S
        k.tmp_buf = [S.sb("tmp%d" % i, [128, 512]) for i in range(2)]
        k.ob_buf = [S.sb("ob%d" % i, [128, 512], BF16) for i in range(2)]
        k.obf_buf = [S.sb("obf%d" % i, [128, 512]) for i in range(2)]
        k.wfm = [S.sb("wfm%d" % i, [128, KC, 128], BF16) for i in range(2)]
        k.wstage = [S.sb("wstage%d" % i, [128, KC, 256]) for i in range(2)]
        k.wtm = S.sb("wtm", [128, KC, TM_W], BF16)
        k.vb_buf = [S.sb("vb%d" % i, [128, 512], BF16) for i in range(2)]
        k.tmf_buf = [S.sb("tmf%d" % i, [128, 512]) for i in range(2)]
        k.ropec = S.sb("ropec", [128, LS])
        k.ropes = S.sb("ropes", [128, LS])
        S.dma("sp", k.ropec[:], k.rope_c[:])
        S.dma("sp", k.ropes[:], k.rope_s[:])
        _phase_inproj(k, l)


def _phase_inproj(k, l):
    S = k.S
    nfm = len(FM_CHUNKS)
    wfm = k.wfm
    stage = k.wstage
    fm_index = {c: i for i, c in enumerate(FM_CHUNKS)}

    def fm_matmul(col, t, ps):
        for kc in range(KC):
            MM(k, ps[:], lhsT=wcur[:, kc, :], rhs=k.BIG[:, kc, t * 512:(t + 1) * 512], start=(kc == 0), stop=(kc == KC - 1))

    cnt = 0
    for which, o_main, o_sw, dst in (("q", O_AQ, O_AQS, k.QT), ("k", O_AK, O_AKS, k.KT)):
        for h in range(4):
            wm = wfm[0]
            ws = wfm[1]
            load_w_bf16(k, wm[:], k.w_in[l, :, o_main + h * 128:o_main + (h + 1) * 128], stage[0], 0)
            load_w_bf16(k, ws[:], k.w_in[l, :, o_sw + h * 128:o_sw + (h + 1) * 128], stage[1], 1)
            bm = k.bfm[:, l, fm_index[o_main + h * 128]:fm_index[o_main + h * 128] + 1]
            bs = k.bfm[:, l, fm_index[o_sw + h * 128]:fm_index[o_sw + h * 128] + 1]
            for t in range(NT):
                p1 = k.ps[(cnt % 2) * 2]
                p2 = k.ps[(cnt % 2) * 2 + 1]
                ob = k.ob_buf[cnt % 2]
                cnt += 1
                for kc in range(KC):
                    MM(k, p1[:], lhsT=wm[:, kc, :], rhs=k.BIG[:, kc, t * 512:(t + 1) * 512], start=(kc == 0), stop=(kc == KC - 1))
                if seq_r(t) == 0:
                    for kc in range(KC):
                        MM(k, p2[:], lhsT=ws[:, kc, :], rhs=k.BIG[:, kc, t * 512:(t + 1) * 512], start=(kc == 0), stop=(kc == KC - 1))
                    t1 = k.tmp_buf[0]
                    t2 = k.tmp_buf[1]
                    V(k, "scalar_tensor_tensor", out=t1[:], in0=p1[:], scalar=bm, in1=k.ropec[:, t * 512:(t + 1) * 512],
                      op0=ALU.add, op1=ALU.mult)
                    V(k, "scalar_tensor_tensor", out=t2[:], in0=p2[:], scalar=bs, in1=k.ropes[:, t * 512:(t + 1) * 512],
                      op0=ALU.add, op1=ALU.mult)
                    G(k, "tensor_tensor", out=ob[:], in0=t1[:], in1=t2[:], op=ALU.add)
                else:
                    A(k, out=ob[:], in_=p1[:], func=AF.Identity, bias=bm, scale=1.0)
                S.dma("pool", dst[h, :, t * 512:(t + 1) * 512], ob[:])
    for o_main, nch, dst in ((O_BQ, 6, k.BT), (O_CQ, 4, k.CQK)):
        for c in range(nch):
            wm = wfm[cnt % 2]
            load_w_bf16(k, wm[:], k.w_in[l, :, o_main + c * 128:o_main + (c + 1) * 128], stage[cnt % 2], cnt)
            bm = k.bfm[:, l, fm_index[o_main + c * 128]:fm_index[o_main + c * 128] + 1]
            for t in range(NT):
                p1 = k.ps[(cnt % 2) * 2]
                ob = k.obf_buf[cnt % 2]
                cnt += 1
                for kc in range(KC):
                    MM(k, p1[:], lhsT=wm[:, kc, :], rhs=k.BIG[:, kc, t * 512:(t + 1) * 512], start=(kc == 0), stop=(kc == KC - 1))
                A(k, out=ob[:], in_=p1[:], func=AF.Identity, bias=bm, scale=1.0)
                S.dma("pool", dst[c, :, t * 512:(t + 1) * 512], ob[:])
    wtm = k.wtm
    for name, cols in TM_GROUPS:
        o = TM_OFF[name]
        for (c0, w) in cols:
            done = 0
            while done < w:
                ww = min(256, w - done)
                st = stage[cnt % 2]
                cnt += 1
                S.dma("sp", st[:, :, 0:ww], k.w_in[l, :, c0 + done:c0 + done + ww].rearrange("(c p) n -> p c n", p=128))
                V(k, "tensor_copy", out=wtm[:, :, o + done:o + done + ww], in_=st[:, :, 0:ww])
                done += ww
            o += w
    for b in range(NB):
        isprompt = b >= LS // 128
        for gi, (name, cols) in enumerate(TM_GROUPS):
            if name == "ak" and not isprompt:
                continue
            o = TM_OFF[name]
            w = sum(x for _, x in cols)
            p = k.ps[4 + (cnt % 2)]
            cnt += 1
            for kc in range(KC):
                MM(k, p[:, 0:w], lhsT=k.BIG[:, kc, b * 128:(b + 1) * 128], rhs=wtm[:, kc, o:o + w], start=(kc == 0), stop=False)
            MM(k, p[:, 0:w], lhsT=k.ones1[:, :], rhs=k.btm[:, l, o:o + w], start=False, stop=True)
            if name == "av":
                vb = k.vb_buf[b % 2]
                V(k, "tensor_copy", out=vb[:], in_=p[:])
                S.dma("pool", k.VV[b * 128:(b + 1) * 128, :], vb[:])
                if isprompt:
                    vf = k.tmf_buf[cnt % 2]
                    A(k, out=vf[:], in_=p[:], func=AF.Copy)
                    pb = b - LS // 128
                    S.dma("pool", k.o_v[pb // 2, l, (pb % 2) * 128:(pb % 2 + 1) * 128, :], vf[:])
            elif name == "ak":
                vf = k.tmf_buf[cnt % 2]
                A(k, out=vf[:], in_=p[:], func=AF.Copy)
                pb = b - LS // 128
                S.dma("pool", k.o_k[pb // 2, l, (pb % 2) * 128:(pb % 2 + 1) * 128, :], vf[:])
            else:
                vf = k.tmf_buf[cnt % 2]
                A(k, out=vf[:, 0:w], in_=p[:, 0:w], func=AF.Copy)
                S.dma("pool", k.TMS[b * 128:(b + 1) * 128, o:o + w], vf[:, 0:w])


def phase_outproj(k, l):
    S = k.S
    with S.scope():
        wo = S.sb("wo", [128, KC, D], BF16)
        stage = [S.sb("ostage%d" % i, [128, KC, 256]) for i in range(2)]
        for j in range(4):
            load_w_bf16(k, wo[:, :, j * 256:(j + 1) * 256], k.w_out[l, :, j * 256:(j + 1) * 256], stage[j % 2], j)
        xt = [S.sb("oxt%d" % i, [128, KC, 512]) for i in range(2)]
        for t in range(NT):
            r = seq_r(t)
            x = xt[t % 2]
            S.dma("sp", x[:], k.XT[:, :, t * 512:(t + 1) * 512].rearrange("c p n -> p c n"))
            for mc in range(KC):
                p = k.ps[mc % 2]
                for kc in range(KC):
                    MM(k, p[:], lhsT=wo[:, kc, mc * 128:(mc + 1) * 128], rhs=k.BIG[:, kc, t * 512:(t + 1) * 512],
                       start=(kc == 0), stop=(kc == KC - 1))
                V(k, "scalar_tensor_tensor", out=x[:, mc, :], in0=p[:], scalar=k.mod[:, l, 16 + mc, r:r + 1], in1=x[:, mc, :],
                  op0=ALU.mult, op1=ALU.add)
            S.dma("pool", k.XT[:, :, t * 512:(t + 1) * 512].rearrange("c p n -> p c n"), x[:])


def phase_ffn(k, l):
    S = k.S
    with S.scope():
        wg = [S.sb("wg%d" % i, [128, KC, 128], BF16) for i in range(2)]
        wu = [S.sb("wu%d" % i, [128, KC, 128], BF16) for i in range(2)]
        stage = [S.sb("fstage%d" % i, [128, KC, 256]) for i in range(2)]
        sil = [S.sb("sil%d" % i, [128, 512]) for i in range(2)]
        hb = [S.sb("hb%d" % i, [128, 512], BF16) for i in range(2)]
        cnt = 0
        for j in range(FC):
            g, u = wg[j % 2], wu[j % 2]
            load_w_bf16(k, g[:], k.w_gu[l, :, j * 128:(j + 1) * 128], stage[0], 0)
            load_w_bf16(k, u[:], k.w_gu[l, :, FH + j * 128:FH + (j + 1) * 128], stage[1], 1)
            for t in range(NT):
                pg = k.ps[(cnt % 2) * 2]
                pu = k.ps[(cnt % 2) * 2 + 1]
                for kc in range(KC):
                    MM(k, pg[:], lhsT=g[:, kc, :], rhs=k.BIG[:, kc, t * 512:(t + 1) * 512], start=(kc == 0), stop=(kc == KC - 1))
                for kc in range(KC):
                    MM(k, pu[:], lhsT=u[:, kc, :], rhs=k.BIG[:, kc, t * 512:(t + 1) * 512], start=(kc == 0), stop=(kc == KC - 1))
                A(k, out=sil[cnt % 2][:], in_=pg[:], func=AF.Silu)
                V(k, "tensor_tensor", out=hb[cnt % 2][:], in0=sil[cnt % 2][:], in1=pu[:], op=ALU.mult)
                S.dma("pool", k.HT[j, :, t * 512:(t + 1) * 512], hb[cnt % 2][:])
                cnt += 1
    with S.scope():
        wd = S.sb("wd", [128, FC, D], BF16)
        stage = S.sb("dstage", [128, KC, 256])
        n = 0
        for cp in range(4):
            for kr in (0, 8, 16):
                nn = min(8, FC - kr)
                S.dma("sp", stage[:, 0:nn, :], k.w_dn[l, kr * 128:(kr + nn) * 128, cp * 256:(cp + 1) * 256].rearrange("(c p) n -> p c n", p=128))
                if n % 2 == 0:
                    V(k, "tensor_copy", out=wd[:, kr:kr + nn, cp * 256:(cp + 1) * 256], in_=stage[:, 0:nn, :])
                else:
                    G(k, "tensor_copy", out=wd[:, kr:kr + nn, cp * 256:(cp + 1) * 256], in_=stage[:, 0:nn, :])
                n += 1
        hts = [S.sb("ht%d" % i, [128, FC, 512], BF16) for i in range(2)]
        x = S.sb("fxt", [128, KC, 512])
        for t in range(NT):
            r = seq_r(t)
            ht = hts[t % 2]
            for c4 in range(0, FC, 6):
                c5 = min(FC, c4 + 6)
                S.dma("sp", ht[:, c4:c5, :], k.HT[c4:c5, :, t * 512:(t + 1) * 512].rearrange("c p n -> p c n"))
            S.dma("sp", x[:], k.XT[:, :, t * 512:(t + 1) * 512].rearrange("c p n -> p c n"))
            for mc in range(KC):
                p = k.ps[mc % 2]
                for kc in range(FC):
                    MM(k, p[:], lhsT=wd[:, kc, mc * 128:(mc + 1) * 128], rhs=ht[:, kc, :], start=(kc == 0), stop=(kc == FC - 1))
                V(k, "scalar_tensor_tensor", out=x[:, mc, :], in0=p[:], scalar=k.mod[:, l, 40 + mc, r:r + 1], in1=x[:, mc, :],
                  op0=ALU.mult, op1=ALU.add)
            S.dma("pool", k.XT[:, :, t * 512:(t + 1) * 512].rearrange("c p n -> p c n"), x[:])


def phase_final(k):
    S = k.S
    with S.scope():
        fr = S.sb("fr", [1, D])
        S.dma("sp", fr[:], k.fng[:])
        fb = S.sb("fb", [128, D])
        for j in range(2):
            MM(k, k.ps[j][:], lhsT=k.cst[0:1, 1, :], rhs=fr[:, j * 512:(j + 1) * 512])
            V(k, "tensor_copy", out=fb[:, j * 512:(j + 1) * 512], in_=k.ps[j][:])
        xb = [S.sb("yx%d" % i, [128, KC, 128]) for i in range(2)]
        xk = [S.sb("yk%d" % i, [128, D]) for i in range(2)]
        junk = S.sb("yjunk", [128, D])
        ss = S.sb("yss", [128, 2])
        yo = [S.sb("yo%d" % i, [128, D]) for i in range(2)]
        for b in range(NB):
            x = xb[b % 2]
            xt = xk[b % 2]
            S.dma("sp", x[:], k.XT[:, :, b * 128:(b + 1) * 128].rearrange("c p n -> p c n"))
            for half in range(2):
                p = k.ps[2 + (b % 2) * 2 + half]
                for j in range(4):
                    S.op("pe", "transpose", out=p[:, j * 128:(j + 1) * 128], in_=x[:, half * 4 + j, :], identity=k.C("ident"))
                if half == 0:
                    V(k, "tensor_copy", out=xt[:, 0:512], in_=p[:])
                else:
                    A(k, out=xt[:, 512:1024], in_=p[:], func=AF.Copy)
            k.S.op("dve", "memset", ss[:, 0:1], 0.0)
            A(k, out=junk[:], in_=xt[:], func=AF.Square, accum_out=ss[:, 0:1])
            A(k, out=ss[:, 1:2], in_=ss[:, 0:1], func=AF.Sqrt, bias=EPS, scale=1.0 / D)
            V(k, "reciprocal", out=ss[:, 1:2], in_=ss[:, 1:2])
            y = yo[b % 2]
            V(k, "scalar_tensor_tensor", out=y[:], in0=xt[:], scalar=ss[:, 1:2], in1=fb[:], op0=ALU.mult, op1=ALU.mult)
            if b < LS // 128:
                S.dma("pool", k.y_s[b * 128:(b + 1) * 128, :], y[:])
            else:
                pb = b - LS // 128
                S.dma("pool", k.y_p[pb * 128:(pb + 1) * 128, :], y[:])


SEQS = [dict(tok0=0, L=LS, sample=True, p=-1)] + [dict(tok0=LS + i * LP, L=LP, sample=False, p=i) for i in range(NPR)]


def phase_attn(k, l):
    S = k.S
    with S.scope():
        lt = S.sb("lamt", [128, 256])
        S.dma("sp", lt[:], k.lam_qk[l:l + 1, :].partition_broadcast(128))
        lp = S.sb("lamp", [128, 256])
        ls = S.sb("lams", [128, 4])
        V(k, "tensor_tensor", out=lp[:, 0:64], in0=lt[:, 0:64], in1=lt[:, 64:128], op=ALU.mult)
        V(k, "tensor_tensor", out=lp[:, 64:128], in0=lt[:, 128:192], in1=lt[:, 192:256], op=ALU.mult)
        V(k, "reduce_sum", out=ls[:, 0:1], in_=lp[:, 0:64], axis=AX.X)
        V(k, "reduce_sum", out=ls[:, 1:2], in_=lp[:, 64:128], axis=AX.X)
        A(k, out=ls[:, 0:2], in_=ls[:, 0:2], func=AF.Exp)
        V(k, "tensor_tensor", out=ls[:, 2:3], in0=ls[:, 1:2], in1=ls[:, 0:1], op=ALU.subtract)
        V(k, "tensor_scalar", out=ls[:, 3:4], in0=ls[:, 2:3], scalar1=-lam_init_of(l), scalar2=None, op0=ALU.add)
        nlam = ls[:, 3:4]
        sg = S.sb("sublg", [128, DEPTH])
        S.dma("sp", sg[:], k.subln_g[:])
        sgl = S.sb("sublgl", [128, 1])
        V(k, "tensor_scalar", out=sgl[:], in0=sg[:, l:l + 1], scalar1=1.0 - lam_init_of(l), scalar2=None, op0=ALU.mult)
        ckf = S.sb("ckf", [128, 512])
        ckb = S.sb("ckb", [128, 4, 128], BF16)
        cvf = S.sb("cvf", [128, 512])
        cvb = S.sb("cvb", [128, 512], BF16)
        for b in range(PAST // 128):
            S.dma("sp", ckf[:], k.cache_k[l, b * 128:(b + 1) * 128, :])
            for h in range(4):
                S.op("pe", "transpose", out=k.ps[7][:, h * 128:(h + 1) * 128], in_=ckf[:, h * 128:(h + 1) * 128], identity=k.C("ident"))
            V(k, "tensor_copy", out=ckb[:], in_=k.ps[7][:].rearrange("p (h n) -> p h n", n=128))
            S.dma("pool", k.KT[:, :, T + b * 128:T + (b + 1) * 128].rearrange("h p n -> p h n"), ckb[:])
            S.dma("sp", cvf[:], k.cache_v[l, b * 128:(b + 1) * 128, :])
            V(k, "tensor_copy", out=cvb[:], in_=cvf[:])
            S.dma("pool", k.VV[T + b * 128:T + (b + 1) * 128, :], cvb[:])
        ktb = S.sb("ktb", [128, LS + PAST], BF16)
        vsb = S.sb("vsb", [128, (LS + PAST) // 128, 128], BF16)
        qsb = [S.sb("qsb%d" % i, [128, LS], BF16) for i in range(2)]
        for i in range(2):
            S.op("dve", "memset", qsb[i][:], 0.0)
        ptb = [S.sb("ptb%d" % i, [128, 512], BF16) for i in range(6)]
        zacc = [S.sb("zacc%d" % i, [128, 512]) for i in range(4)]
        sbank = [k.ps[0], k.ps[1], k.ps[6], k.ps[7]]
        rz = S.sb("rz", [128, 512])
        a0 = S.sb("a0", [128, 512])
        a1 = S.sb("a1", [128, 512])
        sq = S.sb("asq", [128, 512])
        for sq_ in SEQS:
            tok0, L = sq_["tok0"], sq_["L"]
            QW = min(512, L)
            nkt_own = L // 128
            nkt = nkt_own + (PAST // 128 if sq_["sample"] else 0)
            for h in range(4):
                S.dma("sp", ktb[:, 0:L], k.KT[h, :, tok0:tok0 + L])
                for t4 in range(0, nkt_own, 4):
                    t5 = min(nkt_own, t4 + 4)
                    S.dma("sp", vsb[:, t4:t5, :], k.VV[tok0 + t4 * 128:tok0 + t5 * 128, h * 128:(h + 1) * 128].rearrange("(t p) e -> p t e", p=128))
                if sq_["sample"]:
                    S.dma("sp", ktb[:, L:L + PAST], k.KT[h, :, T:T + PAST])
                    S.dma("sp", vsb[:, nkt_own:nkt, :], k.VV[T:T + PAST, h * 128:(h + 1) * 128].rearrange("(t p) e -> p t e", p=128))
                for m in range(2):
                    S.dma("sp", qsb[m][m * 64:(m + 1) * 64, 0:L], k.QT[h, m * 64:(m + 1) * 64, tok0:tok0 + L])
                for qt in range(L // QW):
                    units = [(m, kt) for m in range(2) for kt in range(nkt)]
                    NS = len(sbank)
                    NU = len(units)

                    def qk_mm(i):
                        m, kt = units[i]
                        MM(k, sbank[i % NS][:, 0:QW], lhsT=ktb[:, kt * 128:(kt + 1) * 128], rhs=qsb[m][:, qt * QW:(qt + 1) * QW])

                    for i in range(min(NS, NU)):
                        qk_mm(i)
                    first_pv = [True, True]
                    zused = set()
                    npv = [0, 0]
                    for i0 in range(0, NU, 2):
                        grp = [i for i in (i0, i0 + 1) if i < NU]
                        for i in grp:
                            m, kt = units[i]
                            pt = ptb[i % len(ptb)]
                            A(k, out=pt[:, 0:QW], in_=sbank[i % NS][:, 0:QW], func=AF.Exp, scale=0.125)
                            par = kt % 3
                            if par == 0:
                                MM(k, k.ps[4 + m][:, 0:QW], lhsT=k.Cb("ones"), rhs=pt[:, 0:QW], start=(kt == 0), stop=False)
                            else:
                                eng = "dve" if par == 1 else "pool"
                                za = zacc[m * 2 + par - 1]
                                if kt < 3:
                                    S.op(eng, "tensor_copy", out=za[:, 0:QW], in_=pt[:, 0:QW])
                                    zused.add(m * 2 + par - 1)
                                else:
                                    S.op(eng, "tensor_tensor", out=za[:, 0:QW], in0=za[:, 0:QW], in1=pt[:, 0:QW], op=ALU.add)
                        for i in reversed(grp):
                            m, kt = units[i]
                            pt = ptb[i % len(ptb)]
                            npv[m] += 1
                            MM(k, k.ps[2 + m][:, 0:QW], lhsT=vsb[:, kt, :], rhs=pt[:, 0:QW], start=first_pv[m], stop=(npv[m] == nkt))
                            first_pv[m] = False
                        for i in grp:
                            if i + NS < NU:
                                qk_mm(i + NS)
                    for m in range(2):
                        zl = [z for z in (m * 2, m * 2 + 1) if z in zused]
                        for zi, z in enumerate(zl):
                            MM(k, k.ps[4 + m][:, 0:QW], lhsT=k.C("ones"), rhs=zacc[z][:, 0:QW], start=False, stop=(zi == len(zl) - 1))
                    V(k, "reciprocal", out=rz[:, 0:QW], in_=k.ps[4][:, 0:QW])
                    V(k, "tensor_tensor", out=a0[:, 0:QW], in0=k.ps[2][:, 0:QW], in1=rz[:, 0:QW], op=ALU.mult)
                    V(k, "reciprocal", out=rz[:, 0:QW], in_=k.ps[5][:, 0:QW])
                    V(k, "tensor_tensor", out=a1[:, 0:QW], in0=k.ps[3][:, 0:QW], in1=rz[:, 0:QW], op=ALU.mult)
                    V(k, "scalar_tensor_tensor", out=a0[:, 0:QW], in0=a1[:, 0:QW], scalar=nlam, in1=a0[:, 0:QW], op0=ALU.mult, op1=ALU.add)
                    G(k, "tensor_tensor", out=sq[:, 0:QW], in0=a0[:, 0:QW], in1=a0[:, 0:QW], op=ALU.mult)
                    MM(k, k.ps[6][:, 0:QW], lhsT=k.C("ones"), rhs=sq[:, 0:QW])
                    A(k, out=sq[:, 0:QW], in_=k.ps[6][:, 0:QW], func=AF.Sqrt, bias=EPS, scale=1.0 / 128)
                    V(k, "reciprocal", out=rz[:, 0:QW], in_=sq[:, 0:QW])
                    V(k, "scalar_tensor_tensor", out=k.BIG[:, h, tok0 + qt * QW:tok0 + (qt + 1) * QW], in0=a0[:, 0:QW], scalar=sgl[:, 0:1],
                      in1=rz[:, 0:QW], op0=ALU.mult, op1=ALU.mult)


def run_streams(gens):
    gens = list(gens)
    while gens:
        for g in list(gens):
            try:
                next(g)
            except StopIteration:
                gens.remove(g)


class NS_:
    pass


DT_REC = F32


DT_ML = BF16


def phase_mlstm(k, l):
    S = k.S
    LN8 = math.log(0.125)
    I64 = k.cst[0:64, CONST_ORDER["ident"], 0:64]
    O64 = k.cst[0:64, CONST_ORDER["ones"], 0:64]

    def bc3(ap2, n):
        return ap2.unsqueeze(2).to_broadcast([ap2.shape[0], ap2.shape[1], n])

    def bcm(ap2, n):
        return ap2.unsqueeze(1).to_broadcast([ap2.shape[0], n, ap2.shape[1]])

    def r3(ap):
        return ap.rearrange("p (a n) -> p a n", n=64)

    def alloc(tag):
        t = NS_()
        for nm in ("diag", "X", "ET", "ebq"):
            setattr(t, nm, S.sb("m%s%s" % (nm, tag), [64, 8, 64]))
        for nm in ("qb", "sT", "kw"):
            setattr(t, nm, S.sb("m%s%s" % (nm, tag), [64, 8, 64], DT_ML))
        t.stb = S.sb("mstb" + tag, [64, 4, 128], DT_ML)
        for nm in ("nlf", "tg", "nbs", "t4", "t4b", "colE", "wk", "dec"):
            setattr(t, nm, S.sb("m%s%s" % (nm, tag), [64, 8]))
        t.gt2 = S.sb("mgt2" + tag, [64, 2, 32])
        t.ckv = S.sb("mckv" + tag, [64, 2, 512])
        t.og = S.sb("mog" + tag, [64, 512])
        t.qk = S.sb("mcqk" + tag, [64, 4, 2, 128])
        t.v1 = S.sb("mv1" + tag, [64, 8, 128], DT_ML)
        S.op("dve", "memset", t.v1[:], 0.0)
        S.op("dve", "memset", t.v1[:, :, 64:65], 1.0)
        t.state = S.sb("mstate" + tag, [64, 4, 128])
        for nm in ("hm", "omf", "tt", "sig"):
            setattr(t, nm, S.sb("m%s%s" % (nm, tag), [64, 256]))
        t.den = S.sb("mden" + tag, [64, 4])
        t.ss4 = S.sb("mss4" + tag, [64, 4])
        t.junk = S.sb("mjunk" + tag, [64, 64])
        t.em0 = S.sb("mem0" + tag, [64, 4])
        t.n0r = S.sb("mn0r" + tag, [4, 64])
        t.kvs = S.sb("mkvs" + tag, [4, 256])
        t.nls = S.sb("mnls" + tag, [4, 256])
        t.nblc = S.sb("mnblc" + tag, [4, 4])
        t.off = S.sb("moff" + tag, [4, 4])
        t.mx = S.sb("mmx" + tag, [4, 2])
        t.mfin = S.sb("mmfin" + tag, [4, 1])
        t.mrow = S.sb("mmrow" + tag, [1, 4])
        t.d4 = S.sb("md4" + tag, [4, 4])
        t.emf = S.sb("memf" + tag, [64, 4])
        t.so = S.sb("mso" + tag, [64, 4, 64])
        t.ncol = S.sb("mncol" + tag, [64, 4])
        t.nrow = S.sb("mnrow" + tag, [4, 64])
        return t

    def stream(sq_, dr, t, B, fbb, mgb):
        tok0, L = sq_["tok0"], sq_["L"]
        nblk = L // 128
        ci = CONST_ORDER["m_le"] if dr == 0 else CONST_ORDER["m_ge"]
        cum64 = k.cst[0:64, ci, 0:64]
        OWN, OTH = (k.OM, k.OM2) if dr == 0 else (k.OM2, k.OM)
        state = t.state
        S.op("dve", "memset", state[:], 0.0)
        if sq_["sample"]:
            S.dma("sp", t.em0[:], k.st_m[l, dr:dr + 1, :].partition_broadcast(64))
            A(k, out=t.em0[:], in_=t.em0[:], func=AF.Exp)
            S.dma("sp", state[:, :, 0:64], k.st_c[l, dr, :, :, :].rearrange("h a b -> a h b"))
            S.dma("sp", t.n0r[:], k.st_n[l, dr, :, :])
            S.op("pe", "transpose", out=B[0][0:64, 32:36], in_=t.n0r[:], identity=k.cst[0:4, CONST_ORDER["ident"], 0:4])
            V(k, "tensor_copy", out=state[:, :, 64], in_=B[0][0:64, 32:36])
            V(k, "tensor_tensor", out=state[:], in0=state[:], in1=bc3(t.em0[:], 128), op=ALU.mult)
        A(k, out=t.stb[:], in_=state[:], func=AF.Copy)
        blocks = list(range(nblk)) if dr == 0 else list(range(nblk - 1, -1, -1))
        corder = (0, 1) if dr == 0 else (1, 0)
        for step, bi in enumerate(blocks):
            r0 = tok0 + bi * 128
            for c in range(2):
                S.dma("sp", t.gt2[:, c, :], k.TMS[r0 + c * 64:r0 + (c + 1) * 64, TM_OFF["gt"]:TM_OFF["gt"] + 32])
                S.dma("sp", t.ckv[:, c, :], k.TMS[r0 + c * 64:r0 + (c + 1) * 64, TM_OFF["ckv"]:TM_OFF["ckv"] + 512])
            S.dma("sp", t.qk[:], k.CQK[:, :, r0:r0 + 128].rearrange("c (hh p) n -> p c hh n", p=64))
            yield
            kv4 = t.ckv[:].rearrange("p c (x h e) -> p c x h e", x=2, e=64)
            G(k, "tensor_copy", out=t.v1[:, :, 0:64].rearrange("p (c h) e -> p c h e", c=2), in_=kv4[:, :, 1, :, :])
            t3 = t.tg[:].rearrange("p (c h) -> p c h", c=2)
            V(k, "tensor_tensor", out=t3, in0=t.gt2[:, :, 24 + dr * 4:28 + dr * 4], in1=bcm(fbb[:, dr * 4:dr * 4 + 4], 2), op=ALU.add)
            A(k, out=t.tg[:], in_=t.tg[:], func=AF.Exp, scale=-1.0)
            A(k, out=t.nlf[:], in_=t.tg[:], func=AF.Ln, bias=1.0, scale=1.0)
            yield
            pa = B[0]
            for c in range(2):
                MM(k, pa[0:64, c * 4:(c + 1) * 4], lhsT=cum64, rhs=t.nlf[:, c * 4:(c + 1) * 4])
                MM(k, pa[0:64, 8 + c * 4:12 + c * 4], lhsT=O64, rhs=t.nlf[:, c * 4:(c + 1) * 4])
            yield
            V(k, "tensor_copy", out=t.nbs[:], in_=pa[0:64, 0:8])
            V(k, "tensor_tensor", out=t.t4[:], in0=t.nbs[:], in1=pa[0:64, 8:16], op=ALU.subtract)
            ig3 = t.gt2[:, :, 16 + dr * 4:20 + dr * 4]
            V(k, "tensor_tensor", out=t.t4b[:].rearrange("p (c h) -> p c h", c=2), in0=t.t4[:].rearrange("p (c h) -> p c h", c=2), in1=ig3, op=ALU.add)
            A(k, out=t.wk[:], in_=t.t4b[:], func=AF.Exp, bias=LN8, scale=1.0)
            V(k, "tensor_tensor", out=t.colE[:].rearrange("p (c h) -> p c h", c=2), in0=t.nbs[:].rearrange("p (c h) -> p c h", c=2), in1=ig3, op=ALU.add)
            V(k, "tensor_copy", out=t.dec[:], in_=pa[0:64, 8:16])
            A(k, out=t.dec[:], in_=t.dec[:], func=AF.Exp, scale=-1.0)
            V(k, "tensor_tensor", out=t.diag[:], in0=bcm(I64, 8), in1=bc3(t.nbs[:], 64), op=ALU.mult)
            yield
            if not sq_["sample"]:
                for c in range(2):
                    cc = bi * 2 + c
                    S.op("pe", "transpose", out=B[0][0:4, 256:320], in_=t.t4b[:, c * 4:(c + 1) * 4], identity=I64)
                    S.op("pe", "transpose", out=B[0][0:4, 384:448], in_=t.nlf[:, c * 4:(c + 1) * 4], identity=I64)
                    V(k, "tensor_copy", out=t.kvs[:, cc * 64:(cc + 1) * 64], in_=B[0][0:4, 256:320])
                    V(k, "tensor_copy", out=t.nls[:, cc * 64:(cc + 1) * 64], in_=B[0][0:4, 384:448])
            pb, pc = B[1], B[2]
            for pi in range(8):
                MM(k, pb[0:64, pi * 64:(pi + 1) * 64], lhsT=O64, rhs=t.diag[:, pi, :])
            for c in range(2):
                for h in range(4):
                    pi = c * 4 + h
                    MM(k, pc[0:64, pi * 64:(pi + 1) * 64], lhsT=t.qk[:, 2 + h // 2, h % 2, c * 64:(c + 1) * 64],
                       rhs=t.qk[:, h // 2, h % 2, c * 64:(c + 1) * 64])
            yield
            pb3 = r3(pb[0:64, :])
            V(k, "tensor_tensor", out=t.X[:], in0=bc3(t.colE[:], 64), in1=pb3, op=ALU.subtract)
            A(k, out=t.ebq[:], in_=pb3, func=AF.Exp, scale=-1.0)
            yield
            A(k, out=t.ET[:], in_=t.X[:], func=AF.Exp)
            q4 = t.qk[:, 0:2, :, :].rearrange("p a hh (c n) -> p c (a hh) n", c=2)
            G(k, "tensor_tensor", out=t.qb[:].rearrange("p (c h) n -> p c h n", c=2), in0=q4,
              in1=t.ebq[:].rearrange("p (c h) n -> p c h n", c=2), op=ALU.mult)
            V(k, "tensor_tensor", out=t.kw[:].rearrange("p (c h) e -> p c h e", c=2), in0=kv4[:, :, 0, :, :],
              in1=bc3(t.wk[:], 64).rearrange("p (c h) e -> p c h e", c=2), op=ALU.mult)
            yield
            G(k, "tensor_tensor", out=t.ET[:], in0=t.ET[:], in1=bcm(cum64, 8), op=ALU.mult)
            yield
            V(k, "scalar_tensor_tensor", out=t.sT[:], in0=r3(pc[0:64, :]), scalar=0.125, in1=t.ET[:], op0=ALU.mult, op1=ALU.mult)
            yield
            for c in corder:
                po, pst = B[3], B[1]
                for h in range(4):
                    pi = c * 4 + h
                    for (a0_, a1_) in ((0, 64), (64, 128)):
                        MM(k, po[0:64, h * 128 + a0_:h * 128 + a1_], lhsT=t.qb[:, pi, :], rhs=t.stb[:, h, a0_:a1_], start=True, stop=False)
                        MM(k, po[0:64, h * 128 + a0_:h * 128 + a1_], lhsT=t.sT[:, pi, :], rhs=t.v1[:, pi, a0_:a1_], start=False, stop=True)
                    MM(k, pst[0:64, h * 128:(h + 1) * 128], lhsT=t.kw[:, pi, :], rhs=t.v1[:, pi, :])
                yield
                V(k, "tensor_tensor", out=state[:], in0=state[:], in1=bc3(t.dec[:, c * 4:(c + 1) * 4], 128), op=ALU.mult)
                V(k, "tensor_tensor", out=state[:], in0=state[:], in1=pst[0:64, :].rearrange("p (h e) -> p h e", e=128), op=ALU.add)
                A(k, out=t.stb[:], in_=state[:], func=AF.Copy)
                rc = r0 + c * 64
                po3 = po[0:64, :].rearrange("p (h e) -> p h e", e=128)
                A(k, out=t.den[:], in_=po3[:, :, 64], func=AF.Abs)
                yield
                V(k, "tensor_scalar", out=t.den[:], in0=t.den[:], scalar1=1.0, scalar2=None, op0=ALU.max)
                V(k, "reciprocal", out=t.den[:], in_=t.den[:])
                V(k, "tensor_tensor", out=t.hm[:].rearrange("p (h e) -> p h e", e=64), in0=po3[:, :, 0:64], in1=bc3(t.den[:], 64), op=ALU.mult)
                if step < nblk // 2:
                    S.dma("pool", OWN[rc:rc + 64, :], t.hm[:])
                    yield
                else:
                    S.dma("sp", t.omf[:], OTH[rc:rc + 64, :])
                    S.dma("sp", t.og[:], k.TMS[rc:rc + 64, TM_OFF["og"]:TM_OFF["og"] + 512])
                    yield
                    V(k, "tensor_tensor", out=t.hm[:], in0=t.hm[:], in1=t.omf[:], op=ALU.add)
                    S.op("dve", "memset", t.ss4[:], 0.0)
                    for h in range(4):
                        A(k, out=t.junk[:], in_=t.hm[:, h * 64:(h + 1) * 64], func=AF.Square, accum_out=t.ss4[:, h:h + 1])
                    A(k, out=t.ss4[:], in_=t.ss4[:], func=AF.Sqrt, bias=EPS, scale=1.0 / 64)
                    V(k, "reciprocal", out=t.ss4[:], in_=t.ss4[:])
                    yield
                    for h in range(4):
                        V(k, "scalar_tensor_tensor", out=t.tt[:, h * 64:(h + 1) * 64], in0=t.hm[:, h * 64:(h + 1) * 64],
                          scalar=t.ss4[:, h:h + 1], in1=mgb[:], op0=ALU.mult, op1=ALU.mult)
                    A(k, out=t.sig[:], in_=t.og[:, 256:512], func=AF.Sigmoid)
                    V(k, "tensor_tensor", out=t.tt[:], in0=t.tt[:], in1=t.sig[:], op=ALU.mult)
                    yield
                    for j in range(2):
                        S.op("pe", "transpose", out=B[2][:, j * 64:(j + 1) * 64], in_=t.tt[:, j * 128:(j + 1) * 128], identity=I64)
                    yield
                    V(k, "tensor_copy", out=k.BIG[:, 6:8, rc:rc + 64], in_=B[2][:, 0:128].rearrange("p (j n) -> p j n", n=64))
        if not sq_["sample"]:
            p = sq_["p"]
            V(k, "reduce_sum", out=t.nblc[:], in_=t.nls[:].rearrange("p (c s) -> p c s", s=64), axis=AX.X)
            nch = L // 64
            S.op("dve", "memset", t.off[:], 0.0)
            if dr == 0:
                for c in range(nch - 2, -1, -1):
                    V(k, "tensor_tensor", out=t.off[:, c:c + 1], in0=t.off[:, c + 1:c + 2], in1=t.nblc[:, c + 1:c + 2], op=ALU.subtract)
            else:
                for c in range(1, nch):
                    V(k, "tensor_tensor", out=t.off[:, c:c + 1], in0=t.off[:, c - 1:c], in1=t.nblc[:, c - 1:c], op=ALU.subtract)
            V(k, "tensor_tensor", out=t.kvs[:].rearrange("p (c s) -> p c s", s=64), in0=t.kvs[:].rearrange("p (c s) -> p c s", s=64),
              in1=bc3(t.off[:], 64), op=ALU.add)
            V(k, "reduce_max", out=t.mx[:, 0:1], in_=t.kvs[:], axis=AX.X)
            V(k, "reduce_sum", out=t.mx[:, 1:2], in_=t.nblc[:], axis=AX.X)
            V(k, "scalar_tensor_tensor", out=t.mfin[:], in0=t.mx[:, 1:2], scalar=-1.0, in1=t.mx[:, 0:1], op0=ALU.mult, op1=ALU.max)
            yield
            S.op("pe", "transpose", out=B[0][0:1, 40:44], in_=t.mfin[:], identity=k.cst[0:4, CONST_ORDER["ident"], 0:4])
            V(k, "tensor_copy", out=t.mrow[:], in_=B[0][0:1, 40:44])
            S.dma("pool", k.o_m[p, l, dr:dr + 1, :], t.mrow[:])
            V(k, "tensor_scalar", out=t.d4[:], in0=k.cst[0:4, CONST_ORDER["ident"], 0:4], scalar1=t.mfin[:, 0:1], scalar2=None, op0=ALU.mult)
            MM(k, B[0][0:64, 16:20], lhsT=k.cst[0:4, CONST_ORDER["ones"], 0:64], rhs=t.d4[:])
            yield
            V(k, "tensor_copy", out=t.emf[:], in_=B[0][0:64, 16:20])
            A(k, out=t.emf[:], in_=t.emf[:], func=AF.Exp, scale=-1.0)
            V(k, "tensor_tensor", out=t.so[:], in0=state[:, :, 0:64], in1=bc3(t.emf[:], 64), op=ALU.mult)
            S.dma("pool", k.o_C[p, l, dr, :, :, :].rearrange("h a b -> a h b"), t.so[:])
            V(k, "tensor_tensor", out=t.ncol[:], in0=state[:, :, 64], in1=t.emf[:], op=ALU.mult)
            S.op("pe", "transpose", out=B[0][0:4, 64:128], in_=t.ncol[:], identity=I64)
            yield
            V(k, "tensor_copy", out=t.nrow[:], in_=B[0][0:4, 64:128])
            S.dma("pool", k.o_n[p, l, dr, :, :], t.nrow[:])

    with S.scope():
        fbb = S.sb("fbb", [64, 8])
        S.dma("sp", fbb[:], k.f_b[l:l + 1, :].partition_broadcast(64))
        mgb = S.sb("mgb", [64, 64])
        S.dma("sp", mgb[:], k.mn_g[l:l + 1, :].partition_broadcast(64))
        tiles = [alloc("f"), alloc("b")]
        for sq_ in SEQS:
            run_streams([stream(sq_, dr, tiles[dr], k.ps[dr * 4:dr * 4 + 4], fbb, mgb) for dr in range(2)])


def phase_delta(k, l):
    phase_delta_prep(k, l)
    phase_delta_scan(k, l)


def phase_delta_prep(k, l):
    S = k.S
    with S.scope():
        cw = S.sb("cw", [128, DEPTH, 6, 5])
        S.dma("sp", cw[:], k.conv_w[:])
        xin = S.sb("dxin", [128, 6, 516])
        acc = S.sb("dacc", [128, 6, 512])
        sact = S.sb("dsact", [128, 6, 512])
        sqb = S.sb("dsq", [128, 512])
        rin = S.sb("drin", [128, 512])
        tok = S.sb("dtok", [128, 512])
        for sq_ in SEQS:
            tok0, L = sq_["tok0"], sq_["L"]
            W = min(512, L)
            for ti in range(L // W):
                t0 = tok0 + ti * W
                lo, hi = max(tok0, t0 - 2), min(tok0 + L, t0 + W + 2)
                k.S.op("dve", "memset", xin[:], 0.0)
                S.dma("sp", xin[:, :, lo - (t0 - 2):hi - (t0 - 2)], k.BT[:, :, lo:hi].rearrange("c p n -> p c n"))
                for c in range(6):
                    eng = "dve"
                    S.op(eng, "tensor_scalar", out=acc[:, c, 0:W], in0=xin[:, c, 0:W], scalar1=cw[:, l, c, 0:1], scalar2=None, op0=ALU.mult)
                    for j in range(1, 5):
                        S.op(eng, "scalar_tensor_tensor", out=acc[:, c, 0:W], in0=xin[:, c, j:j + W], scalar=cw[:, l, c, j:j + 1],
                             in1=acc[:, c, 0:W], op0=ALU.mult, op1=ALU.add)
                for c in range(6):
                    A(k, out=sact[:, c, 0:W], in_=acc[:, c, 0:W], func=AF.Silu)
                for c in range(4):
                    G(k, "tensor_tensor", out=sqb[:, 0:W], in0=sact[:, c, 0:W], in1=sact[:, c, 0:W], op=ALU.mult)
                    p = k.ps[c % 2]
                    MM(k, p[:, 0:W], lhsT=k.C("bd"), rhs=sqb[:, 0:W])
                    A(k, out=rin[:, 0:W], in_=p[:, 0:W], func=AF.Sqrt, bias=EPS, scale=1.0)
                    V(k, "reciprocal", out=rin[:, 0:W], in_=rin[:, 0:W])
                    V(k, "scalar_tensor_tensor", out=sact[:, c, 0:W], in0=sact[:, c, 0:W], scalar=(0.125 if c < 2 else 1.0), in1=rin[:, 0:W],
                      op0=ALU.mult, op1=ALU.mult)
                    h0 = (c // 2) * 4 + (c % 2) * 2
                    S.dma("pool", k.DQK[h0:h0 + 2, :, t0:t0 + W].rearrange("h p n -> (h p) n"), sact[:, c, 0:W])
                for b in range(W // 128):
                    p = k.ps[2 + b % 2]
                    for j, c in enumerate((2, 3, 4, 5)):
                        S.op("pe", "transpose", out=p[:, j * 128:(j + 1) * 128], in_=sact[:, c, b * 128:(b + 1) * 128], identity=k.C("ident"))
                    V(k, "tensor_copy", out=tok[:], in_=p[:])
                    S.dma("pool", k.DKV[t0 + b * 128:t0 + (b + 1) * 128, :], tok[:])


def phase_delta_scan(k, l):
    S = k.S
    I64 = k.cst[0:64, CONST_ORDER["ident"], 0:64]
    O64 = k.cst[0:64, CONST_ORDER["ones"], 0:64]

    def bc3(ap2, n):
        return ap2.unsqueeze(2).to_broadcast([ap2.shape[0], ap2.shape[1], n])

    def bcm(ap2, n):
        return ap2.unsqueeze(1).to_broadcast([ap2.shape[0], n, ap2.shape[1]])

    def r3(ap):
        return ap.rearrange("p (a n) -> p a n", n=64)

    def alloc(tag):
        t = NS_()
        for nm in ("diag", "X", "N", "TT", "u", "ebq"):
            setattr(t, nm, S.sb("d%s%s" % (nm, tag), [64, 8, 64]))
        for nm in ("P0", "P1", "PT0", "PT1", "TTb", "aT", "vb", "kbg", "kdec", "wT", "qd"):
            setattr(t, nm, S.sb("d%s%s" % (nm, tag), [64, 8, 64], DT_REC))
        t.qkb = S.sb("dqkb" + tag, [64, 8, 128], DT_REC)
        t.S4b = S.sb("dS4b" + tag, [64, 4, 64], DT_REC)
        for nm in ("beta", "nbeta", "ng", "tg", "ngc", "egc", "bgs", "kds", "gl"):
            setattr(t, nm, S.sb("d%s%s" % (nm, tag), [64, 8]))
        t.gt2 = S.sb("dgt2" + tag, [64, 2, 32])
        t.dkv = S.sb("ddkv" + tag, [64, 2, 512])
        t.qk = S.sb("dqk" + tag, [64, 8, 128])
        t.vnew = S.sb("dvnew" + tag, [64, 4, 64], DT_REC)
        t.S4 = S.sb("dS4" + tag, [64, 4, 64])
        t.oc = S.sb("doc" + tag, [64, 256])
        t.of = S.sb("dof" + tag, [64, 256])
        t.og = S.sb("dog" + tag, [64, 512])
        t.ss4 = S.sb("dss4" + tag, [64, 4])
        t.junk = S.sb("djunk" + tag, [64, 64])
        t.tt = S.sb("dtt" + tag, [64, 256])
        t.sig = S.sb("dsig" + tag, [64, 256])
        if DT_REC == F32:
            t.qkb, t.P0, t.TTb, t.S4b = t.qk, t.N, t.TT, t.S4
        return t

    def stream(sq_, dr, t, B, alb, dtb, dgb):
        tok0, L = sq_["tok0"], sq_["L"]
        nblk = L // 128
        ci = CONST_ORDER["m_le"] if dr == 0 else CONST_ORDER["m_ge"]
        si = CONST_ORDER["m_ig"] if dr == 0 else CONST_ORDER["m_il"]
        cum64 = k.cst[0:64, ci, 0:64]
        str64 = k.cst[0:64, si, 0:64]
        OWN, OTH = (k.OD, k.OD2) if dr == 0 else (k.OD2, k.OD)
        if sq_["sample"]:
            S.dma("sp", t.S4[:], k.st_d[l, dr, :, :, :].rearrange("h a b -> a h b"))
        else:
            S.op("dve", "memset", t.S4[:], 0.0)
        if DT_REC != F32:
            V(k, "tensor_copy", out=t.S4b[:], in_=t.S4[:])
        blocks = list(range(nblk)) if dr == 0 else list(range(nblk - 1, -1, -1))
        corder = (0, 1) if dr == 0 else (1, 0)
        for step, bi in enumerate(blocks):
            r0 = tok0 + bi * 128
            for c in range(2):
                S.dma("sp", t.gt2[:, c, :], k.TMS[r0 + c * 64:r0 + (c + 1) * 64, TM_OFF["gt"]:TM_OFF["gt"] + 32])
                S.dma("sp", t.dkv[:, c, :], k.DKV[r0 + c * 64:r0 + (c + 1) * 64, :])
            S.dma("sp", t.qk[:], k.DQK[:, :, r0:r0 + 128].rearrange("h p n -> p h n"))
            yield
            if DT_REC != F32:
                G(k, "tensor_copy", out=t.qkb[:], in_=t.qk[:])
            b3 = t.beta[:].rearrange("p (c h) -> p c h", c=2)
            A(k, out=b3, in_=t.gt2[:, :, 8 + dr * 4:12 + dr * 4], func=AF.Sigmoid)
            V(k, "tensor_scalar", out=t.nbeta[:], in0=t.beta[:], scalar1=-1.0, scalar2=None, op0=ALU.mult)
            t3 = t.tg[:].rearrange("p (c h) -> p c h", c=2)
            V(k, "tensor_tensor", out=t3, in0=t.gt2[:, :, dr * 4:dr * 4 + 4], in1=bcm(dtb[:, dr * 4:dr * 4 + 4], 2), op=ALU.add)
            A(k, out=t.tg[:], in_=t.tg[:], func=AF.Exp)
            A(k, out=t.tg[:], in_=t.tg[:], func=AF.Ln, bias=1.0, scale=1.0)
            V(k, "tensor_tensor", out=t.ng[:].rearrange("p (c h) -> p c h", c=2), in0=t3, in1=bcm(alb[:, dr * 4:dr * 4 + 4], 2), op=ALU.mult)
            yield
            pa = B[0]
            for c in range(2):
                MM(k, pa[0:64, c * 4:(c + 1) * 4], lhsT=cum64, rhs=t.ng[:, c * 4:(c + 1) * 4])
                MM(k, pa[0:64, 8 + c * 4:12 + c * 4], lhsT=O64, rhs=t.ng[:, c * 4:(c + 1) * 4])
            yield
            V(k, "tensor_copy", out=t.ngc[:], in_=pa[0:64, 0:8])
            A(k, out=t.egc[:], in_=t.ngc[:], func=AF.Exp, scale=-1.0)
            V(k, "tensor_tensor", out=t.bgs[:], in0=t.beta[:], in1=t.egc[:], op=ALU.mult)
            V(k, "tensor_tensor", out=t.kds[:], in0=t.ngc[:], in1=pa[0:64, 8:16], op=ALU.subtract)
            A(k, out=t.kds[:], in_=t.kds[:], func=AF.Exp)
            V(k, "tensor_copy", out=t.gl[:], in_=pa[0:64, 8:16])
            A(k, out=t.gl[:], in_=t.gl[:], func=AF.Exp, scale=-1.0)
            V(k, "tensor_tensor", out=t.diag[:], in0=bcm(I64, 8), in1=bc3(t.ngc[:], 64), op=ALU.mult)
            yield
            pb, pc, pd = B[1], B[2], B[3]
            for pi in range(8):
                MM(k, pb[0:64, pi * 64:(pi + 1) * 64], lhsT=O64, rhs=t.diag[:, pi, :])
            for c in range(2):
                for h in range(4):
                    pi = c * 4 + h
                    kTc = t.qkb[:, 4 + h, c * 64:(c + 1) * 64]
                    qTc = t.qkb[:, h, c * 64:(c + 1) * 64]
                    MM(k, pc[0:64, pi * 64:(pi + 1) * 64], lhsT=kTc, rhs=kTc)
                    MM(k, pd[0:64, pi * 64:(pi + 1) * 64], lhsT=kTc, rhs=qTc)
            yield
            pb3 = r3(pb[0:64, :])
            V(k, "tensor_tensor", out=t.X[:], in0=pb3, in1=bc3(t.ngc[:], 64), op=ALU.subtract)
            A(k, out=t.ebq[:], in_=pb3, func=AF.Exp, scale=-1.0)
            V(k, "tensor_scalar", out=t.diag[:], in0=t.X[:], scalar1=0.0, scalar2=None, op0=ALU.min)
            G(k, "tensor_scalar", out=t.X[:], in0=t.X[:], scalar1=0.0, scalar2=None, op0=ALU.max)
            yield
            A(k, out=t.diag[:], in_=t.diag[:], func=AF.Exp)
            A(k, out=t.X[:], in_=t.X[:], func=AF.Exp, scale=-1.0)
            G(k, "tensor_tensor", out=t.diag[:], in0=t.diag[:], in1=bcm(str64, 8), op=ALU.mult)
            G(k, "tensor_tensor", out=t.X[:], in0=t.X[:], in1=bcm(cum64, 8), op=ALU.mult)
            yield
            V(k, "tensor_tensor", out=t.N[:], in0=r3(pc[0:64, :]), in1=bc3(t.nbeta[:], 64), op=ALU.mult)
            V(k, "tensor_tensor", out=t.N[:], in0=t.N[:], in1=t.diag[:], op=ALU.mult)
            if DT_REC != F32:
                A(k, out=t.P0[:], in_=t.N[:], func=AF.Copy)
            V(k, "tensor_tensor", out=t.aT[:], in0=r3(pd[0:64, :]), in1=t.X[:], op=ALU.mult)
            yield
            pt = B[1]
            for pi in range(8):
                S.op("pe", "transpose", out=pt[0:64, pi * 64:(pi + 1) * 64], in_=t.N[:, pi, :], identity=I64)
            yield
            A(k, out=t.PT0[:], in_=r3(pt[0:64, :]), func=AF.Copy)
            V(k, "tensor_tensor", out=t.TT[:], in0=r3(pt[0:64, :]), in1=bcm(I64, 8), op=ALU.add)
            if DT_REC != F32:
                if DT_REC != F32:
                    A(k, out=t.TTb[:], in_=t.TT[:], func=AF.Copy)
            yield
            Pb, PTb = (t.P0, t.P1), (t.PT0, t.PT1)
            for stp in range(5):
                P, PT = Pb[stp % 2], PTb[stp % 2]
                Pn, PTn = Pb[(stp + 1) % 2], PTb[(stp + 1) % 2]
                p1, p2, p3 = B[2], B[3], B[1]
                for pi in range(8):
                    MM(k, p1[0:64, pi * 64:(pi + 1) * 64], lhsT=PT[:, pi, :], rhs=P[:, pi, :])
                if stp < 4:
                    for pi in range(8):
                        MM(k, p2[0:64, pi * 64:(pi + 1) * 64], lhsT=P[:, pi, :], rhs=PT[:, pi, :])
                yield
                A(k, out=Pn[:], in_=r3(p1[0:64, :]), func=AF.Copy)
                if stp < 4:
                    V(k, "tensor_copy", out=PTn[:], in_=r3(p2[0:64, :]))
                yield
                for pi in range(8):
                    MM(k, p3[0:64, pi * 64:(pi + 1) * 64], lhsT=Pn[:, pi, :], rhs=t.TTb[:, pi, :])
                yield
                V(k, "tensor_tensor", out=t.TT[:], in0=t.TT[:], in1=r3(p3[0:64, :]), op=ALU.add)
                if DT_REC != F32:
                    A(k, out=t.TTb[:], in_=t.TT[:], func=AF.Copy)
                yield
            kv4 = t.dkv[:].rearrange("p c (x h e) -> p c x h e", x=2, e=64)
            V(k, "tensor_tensor", out=t.vb[:].rearrange("p (c h) e -> p c h e", c=2), in0=kv4[:, :, 1, :, :],
              in1=bc3(t.beta[:], 64).rearrange("p (c h) e -> p c h e", c=2), op=ALU.mult)
            V(k, "tensor_tensor", out=t.kbg[:].rearrange("p (c h) e -> p c h e", c=2), in0=kv4[:, :, 0, :, :],
              in1=bc3(t.bgs[:], 64).rearrange("p (c h) e -> p c h e", c=2), op=ALU.mult)
            G(k, "tensor_tensor", out=t.kdec[:].rearrange("p (c h) e -> p c h e", c=2), in0=kv4[:, :, 0, :, :],
              in1=bc3(t.kds[:], 64).rearrange("p (c h) e -> p c h e", c=2), op=ALU.mult)
            G(k, "tensor_tensor", out=t.qd[:].rearrange("p (c h) n -> p c h n", c=2),
              in0=t.qk[:, 0:4, :].rearrange("p h (c n) -> p c h n", c=2), in1=t.ebq[:].rearrange("p (c h) n -> p c h n", c=2), op=ALU.mult)
            yield
            p1, p2 = B[2], B[3]
            for pi in range(8):
                MM(k, p1[0:64, pi * 64:(pi + 1) * 64], lhsT=t.TTb[:, pi, :], rhs=t.vb[:, pi, :])
                MM(k, p2[0:64, pi * 64:(pi + 1) * 64], lhsT=t.kbg[:, pi, :], rhs=t.TTb[:, pi, :])
            yield
            A(k, out=t.u[:], in_=r3(p1[0:64, :]), func=AF.Copy)
            V(k, "tensor_copy", out=t.wT[:], in_=r3(p2[0:64, :]))
            yield
            for c in corder:
                px, py, pz = B[1], B[2], B[3]
                for h in range(4):
                    MM(k, px[0:64, h * 64:(h + 1) * 64], lhsT=t.wT[:, c * 4 + h, :], rhs=t.S4b[:, h, :])
                yield
                V(k, "tensor_tensor", out=t.vnew[:], in0=t.u[:, c * 4:(c + 1) * 4, :], in1=r3(px[0:64, 0:256]), op=ALU.subtract)
                yield
                for h in range(4):
                    MM(k, py[0:64, h * 64:(h + 1) * 64], lhsT=t.qd[:, c * 4 + h, :], rhs=t.S4b[:, h, :], start=True, stop=False)
                    MM(k, py[0:64, h * 64:(h + 1) * 64], lhsT=t.aT[:, c * 4 + h, :], rhs=t.vnew[:, h, :], start=False, stop=True)
                    MM(k, pz[0:64, h * 64:(h + 1) * 64], lhsT=t.kdec[:, c * 4 + h, :], rhs=t.vnew[:, h, :])
                yield
                V(k, "tensor_tensor", out=t.S4[:], in0=t.S4[:], in1=bc3(t.gl[:, c * 4:(c + 1) * 4], 64), op=ALU.mult)
                V(k, "tensor_tensor", out=t.S4[:], in0=t.S4[:], in1=r3(pz[0:64, 0:256]), op=ALU.add)
                if DT_REC != F32:
                    G(k, "tensor_copy", out=t.S4b[:], in_=t.S4[:])
                rc = r0 + c * 64
                A(k, out=t.oc[:], in_=py[0:64, 0:256], func=AF.Copy)
                if step < nblk // 2:
                    S.dma("pool", OWN[rc:rc + 64, :], t.oc[:])
                    yield
                else:
                    S.dma("sp", t.of[:], OTH[rc:rc + 64, :])
                    S.dma("sp", t.og[:], k.TMS[rc:rc + 64, TM_OFF["og"]:TM_OFF["og"] + 512])
                    yield
                    V(k, "tensor_tensor", out=t.oc[:], in0=t.oc[:], in1=t.of[:], op=ALU.add)
                    S.op("dve", "memset", t.ss4[:], 0.0)
                    for h in range(4):
                        A(k, out=t.junk[:], in_=t.oc[:, h * 64:(h + 1) * 64], func=AF.Square, accum_out=t.ss4[:, h:h + 1])
                    A(k, out=t.ss4[:], in_=t.ss4[:], func=AF.Sqrt, bias=EPS, scale=1.0 / 64)
                    V(k, "reciprocal", out=t.ss4[:], in_=t.ss4[:])
                    yield
                    for h in range(4):
                        V(k, "scalar_tensor_tensor", out=t.tt[:, h * 64:(h + 1) * 64], in0=t.oc[:, h * 64:(h + 1) * 64],
                          scalar=t.ss4[:, h:h + 1], in1=dgb[:], op0=ALU.mult, op1=ALU.mult)
                    A(k, out=t.sig[:], in_=t.og[:, 0:256], func=AF.Silu)
                    V(k, "tensor_tensor", out=t.tt[:], in0=t.tt[:], in1=t.sig[:], op=ALU.mult)
                    yield
                    for j in range(2):
                        S.op("pe", "transpose", out=B[0][:, 128 + j * 64:128 + (j + 1) * 64], in_=t.tt[:, j * 128:(j + 1) * 128], identity=I64)
                    yield
                    V(k, "tensor_copy", out=k.BIG[:, 4:6, rc:rc + 64], in_=B[0][:, 128:256].rearrange("p (j n) -> p j n", n=64))
        if not sq_["sample"]:
            S.dma("pool", k.o_S[sq_["p"], l, dr, :, :, :].rearrange("h a b -> a h b"), t.S4[:])

    with S.scope():
        alb = S.sb("alb", [64, 8])
        S.dma("sp", alb[:], k.a_log[l:l + 1, :].partition_broadcast(64))
        A(k, out=alb[:], in_=alb[:], func=AF.Exp)
        dtb = S.sb("dtb", [64, 8])
        S.dma("sp", dtb[:], k.dt_b[l:l + 1, :].partition_broadcast(64))
        dgb = S.sb("dgb", [64, 64])
        S.dma("sp", dgb[:], k.dn_g[l:l + 1, :].partition_broadcast(64))
        tiles = [alloc("f"), alloc("b")]
        for sq_ in SEQS:
            run_streams([stream(sq_, dr, tiles[dr], k.ps[dr * 4:dr * 4 + 4], alb, dtb, dgb) for dr in range(2)])


def swap_pairs(n):
    idx = np.arange(n)
    return idx ^ 1


def prep_shared(inp):
    f = np.float32
    sh = {}
    w_in = inp["w_in"]
    b_in = inp["b_in"]
    sw = swap_pairs(512)
    w_ext = np.concatenate([w_in, w_in[:, :, O_AQ:O_AQ + 512][:, :, sw], w_in[:, :, O_AK:O_AK + 512][:, :, sw]], axis=2)
    b_ext = np.concatenate([b_in, b_in[:, O_AQ:O_AQ + 512][:, sw], b_in[:, O_AK:O_AK + 512][:, sw]], axis=1)
    sh["w_in"] = np.ascontiguousarray(w_ext, dtype=f)
    sh["w_mod"] = inp["w_mod"]
    sh["w_out"] = inp["w_out"]
    sh["w_gu"] = inp["w_gate_up"]
    sh["w_dn"] = inp["w_down"]
    cm, ct, st = host_consts()
    sh["consts"], sh["rope_c"], sh["rope_s"] = cm, ct, st

    def fm(v):
        d, n = v.shape
        return np.ascontiguousarray(v.reshape(d, n // 128, 128).transpose(2, 0, 1), dtype=f)

    sh["n1g"] = fm(inp["norm1_g"])
    sh["n2g"] = fm(inp["norm2_g"])
    sh["fng"] = np.ascontiguousarray(inp["final_norm_g"].reshape(1, D), dtype=f)
    sh["b_mod"] = fm(inp["b_mod"])
    sh["b_fm"] = np.ascontiguousarray(
        np.stack([b_ext[:, c:c + 128] for c in FM_CHUNKS], axis=1).transpose(2, 0, 1), dtype=f)
    tm_cols = np.concatenate([np.arange(c0, c0 + w) for _, cols in TM_GROUPS for (c0, w) in cols])
    sh["b_tm"] = np.ascontiguousarray(b_ext[:, tm_cols].reshape(1, DEPTH, TM_W), dtype=f)
    cw = inp["delta_conv_w"]
    sh["conv_w"] = np.ascontiguousarray(cw.reshape(DEPTH, 5, 6, 128).transpose(3, 0, 2, 1), dtype=f)
    sh["lam_qk"] = np.ascontiguousarray(inp["lambda_qk"].reshape(DEPTH, 256), dtype=f)
    sh["subln_g"] = np.ascontiguousarray(inp["attn_subln_g"].T, dtype=f)
    sh["dn_g"] = np.ascontiguousarray(inp["delta_norm_g"], dtype=f)
    sh["mn_g"] = np.ascontiguousarray(inp["mlstm_norm_g"], dtype=f)
    sh["a_log"] = np.ascontiguousarray(inp["delta_A_log"].reshape(DEPTH, 8), dtype=f)
    sh["dt_b"] = np.ascontiguousarray(inp["delta_dt_bias"].reshape(DEPTH, 8), dtype=f)
    sh["f_b"] = np.ascontiguousarray(inp["mlstm_f_bias"].reshape(DEPTH, 8), dtype=f)
    return sh


def prep_core(inp, sh, i):
    f = np.float32
    b = i % 4
    m = dict(sh)
    xp = inp["x_prompt"][2 * i:2 * i + 2].reshape(NPR * LP, D)
    m["x_all"] = np.ascontiguousarray(np.concatenate([inp["x_sample"][b], xp], axis=0), dtype=f)
    m["cache_k"] = np.ascontiguousarray(inp["cache_attn_k"][b].reshape(DEPTH, PAST, 512), dtype=f)
    m["cache_v"] = np.ascontiguousarray(inp["cache_attn_v"][b].reshape(DEPTH, PAST, 512), dtype=f)
    m["st_d"] = np.ascontiguousarray(inp["state_delta"][b], dtype=f)
    m["st_c"] = np.ascontiguousarray(inp["state_mlstm_C"][b], dtype=f)
    m["st_n"] = np.ascontiguousarray(inp["state_mlstm_n"][b], dtype=f)
    m["st_m"] = np.ascontiguousarray(inp["state_mlstm_m"][b], dtype=f)
    cc = np.stack([inp["c"][b], inp["c_ctx"]], axis=-1)
    m["cT"] = np.ascontiguousarray(cc.reshape(KC, 128, 2).transpose(1, 0, 2), dtype=f)
    return m


def build_program(stop_after=None, debug_outs=()):
    k = build(debug_outs)
    DBG["stop"] = stop_after
    try:
        phase_setup(k)
        chk("setup")
        for l in range(DEPTH):
            phase_norm(k, l, 1)
            chk("norm%d" % l)
            phase_inproj(k, l)
            chk("inproj%d" % l)
            phase_attn(k, l)
            chk("attn%d" % l)
            phase_mlstm(k, l)
            chk("mlstm%d" % l)
            phase_delta_prep(k, l)
            chk("dprep%d" % l)
            phase_delta_scan(k, l)
            chk("delta%d" % l)
            phase_outproj(k, l)
            chk("outproj%d" % l)
            phase_norm(k, l, 2)
            phase_ffn(k, l)
            chk("ffn%d" % l)
        phase_final(k)
    except StopBuild:
        pass
    st = k.S.finalize()
    return k, st


_CACHE = {}


def kernel(**inputs):
    inp = {n: np.asarray(v) for n, v in inputs.items()}
    if "prog" not in _CACHE:
        _CACHE["prog"] = build_program()
    k, st = _CACHE["prog"]
    sh = prep_shared(inp)
    in_maps = [prep_core(inp, sh, i) for i in range(8)]
    res = run_bass_kernel_spmd(k.nc, in_maps, core_ids=list(range(8)))
    R = res.results
    B = 16
    y_prompt = np.concatenate([R[i]["y_p"].reshape(NPR, LP, D) for i in range(8)], axis=0)
    y_sample = np.stack([R[b]["y_s"] for b in range(4)], axis=0)
    nk = np.concatenate([R[i]["o_k"] for i in range(8)], axis=0).reshape(B, DEPTH, LP, 4, 2, 64)
    nv = np.concatenate([R[i]["o_v"] for i in range(8)], axis=0).reshape(B, DEPTH, LP, 4, 128)
    nS = np.concatenate([R[i]["o_S"] for i in range(8)], axis=0)
    nC = np.concatenate([R[i]["o_C"] for i in range(8)], axis=0)
    nn = np.concatenate([R[i]["o_n"] for i in range(8)], axis=0)
    nm = np.concatenate([R[i]["o_m"] for i in range(8)], axis=0)
    return (y_prompt.astype(np.float32), y_sample.astype(np.float32), nk.astype(np.float32), nv.astype(np.float32),
            nS.astype(np.float32), nC.astype(np.float32), nn.astype(np.float32), nm.astype(np.float32))
```
